# Optimizing a Trainium2 kernel written in Bass

```python
import jax, jax.numpy as jnp
from jax import lax
import numpy as np

D_MODEL = 2048
BATCH = 4
SEQ = 2048
DEPTH = 1
DEC_BATCH = 8
DEC_SEQ = 8
PAST_LEN = 16384
PAGE_SIZE = 128

DH_A = 128
H_A = (D_MODEL // 2) // DH_A
W_A = H_A * DH_A
H_IDX = 16
D_IDX = 64
TOPK_MAX = 256
Q_BLOCK = 32
DK_B = 128
DV_B = 128
H_B = (D_MODEL // 4) // DV_B
W_B = H_B * DV_B
HGRN_CHUNK = 64
DH_M = 128
H_M = (D_MODEL // 4) // DH_M
W_M = H_M * DH_M
N_MEM = 256
ROPE_THETA = 500000.0
ROPE_FRAC = 4
LN_EPS = 1e-5
RMS_EPS = 1e-6
ALPHA = (2.0 * DEPTH) ** 0.25
BETA = (8.0 * DEPTH) ** -0.25
IN_SPLITS = (W_A, W_A, W_A, W_A, H_IDX * D_IDX, D_IDX, H_IDX,
             H_B * DK_B, H_B * DK_B, W_B, W_B, W_M, W_M)
N_IN = sum(IN_SPLITS)

kernel_name = "hymba_dsa_hgrn2_memxattn_step"


def _rope(x, pos):
    d = x.shape[-1]
    r = d // ROPE_FRAC
    half = r // 2
    inv = ROPE_THETA ** (-jnp.arange(half, dtype=jnp.float32) / half)
    ang = pos.astype(jnp.float32)[:, None] * inv[None, :]
    cos = jnp.cos(ang)[:, None, :]
    sin = jnp.sin(ang)[:, None, :]
    xr = x[..., :r].astype(jnp.float32)
    x1, x2 = xr[..., :half], xr[..., half:]
    rot = jnp.concatenate([x1 * cos - x2 * sin, x2 * cos + x1 * sin], axis=-1)
    return jnp.concatenate([rot.astype(x.dtype), x[..., r:]], axis=-1)


def _layernorm(x, g, b):
    xf = x.astype(jnp.float32)
    xc = xf - jnp.mean(xf, axis=-1, keepdims=True)
    var = jnp.mean(xc * xc, axis=-1, keepdims=True)
    return (xc * lax.rsqrt(var + LN_EPS) * g.astype(jnp.float32) + b.astype(jnp.float32)).astype(x.dtype)


def _in_proj(h, w, pos):
    n_b, n_t, _ = h.shape
    z = jnp.einsum('btd,dn->btn', h, w)
    cuts = [int(c) for c in np.cumsum(IN_SPLITS)[:-1]]
    aq, ak, av, ag, iq, ik, iw, bq, bf, bi, bg, mq, mg = jnp.split(z, cuts, axis=-1)
    aq = _rope(aq.reshape(n_b, n_t, H_A, DH_A), pos)
    ak = _rope(ak.reshape(n_b, n_t, H_A, DH_A), pos)
    av = av.reshape(n_b, n_t, H_A, DH_A)
    iq = _rope(iq.reshape(n_b, n_t, H_IDX, D_IDX), pos)
    ik = _rope(ik.reshape(n_b, n_t, 1, D_IDX), pos)[:, :, 0]
    mq = mq.reshape(n_b, n_t, H_M, DH_M)
    return aq, ak, av, ag, iq, ik, iw, bq, bf, bi, bg, mq, mg


def _indexer_scores(iq, iw, ik):
    dots = jnp.einsum('bthd,bld->bthl', iq.astype(jnp.float32), ik.astype(jnp.float32)) * (D_IDX ** -0.5)
    return jnp.einsum('bth,bthl->btl', iw.astype(jnp.float32) * (H_IDX ** -0.5), jax.nn.relu(dots))


def _sparse_attend(q, k_sel, v_sel, valid):
    s = jnp.einsum('bthd,btkhd->bthk', q.astype(jnp.float32), k_sel.astype(jnp.float32)) * (DH_A ** -0.5)
    s = jnp.where(valid[:, :, None, :], s, -jnp.inf)
    p = jax.nn.softmax(s, axis=-1)
    return jnp.einsum('bthk,btkhd->bthd', p, v_sel.astype(jnp.float32))


def _take_rows(a, idx):
    return jax.vmap(lambda ab, ib: ab[ib])(a, idx)


def _dsa_prompt(q, k, v, iq, iw, ik):
    n_b, n_t = q.shape[:2]
    n_sel = min(TOPK_MAX, n_t // 4)
    n_blk = n_t // Q_BLOCK
    kpos = jnp.arange(n_t)

    def block(i):
        t0 = i * Q_BLOCK
        qb = lax.dynamic_slice_in_dim(q, t0, Q_BLOCK, axis=1)
        iqb = lax.dynamic_slice_in_dim(iq, t0, Q_BLOCK, axis=1)
        iwb = lax.dynamic_slice_in_dim(iw, t0, Q_BLOCK, axis=1)
        qpos = t0 + jnp.arange(Q_BLOCK)
        sc = _indexer_scores(iqb, iwb, ik)
        sc = jnp.where(kpos[None, None, :] <= qpos[None, :, None], sc, -jnp.inf)
        _, sel = lax.top_k(sc, n_sel)
        valid = sel <= qpos[None, :, None]
        return _sparse_attend(qb, _take_rows(k, sel), _take_rows(v, sel), valid)

    out = lax.map(block, jnp.arange(n_blk))
    return jnp.moveaxis(out, 0, 1).reshape(n_b, n_t, H_A, DH_A)


def _dsa_sample(q, k_new, v_new, iq, iw, ik_new, cache_k, cache_v, cache_idx_k, page_table):
    n_b, n_t = q.shape[:2]
    past = page_table.shape[1] * PAGE_SIZE
    n_keys = past + n_t
    n_sel = min(TOPK_MAX, n_keys // 4)
    ik_past = cache_idx_k[page_table].reshape(n_b, past, D_IDX)
    ik_all = jnp.concatenate([ik_past.astype(ik_new.dtype), ik_new], axis=1)
    qpos = past + jnp.arange(n_t)
    kpos = jnp.arange(n_keys)
    sc = _indexer_scores(iq, iw, ik_all)
    sc = jnp.where(kpos[None, None, :] <= qpos[None, :, None], sc, -jnp.inf)
    _, sel = lax.top_k(sc, n_sel)
    valid = sel <= qpos[None, :, None]
    sel_past = jnp.minimum(sel, past - 1)
    phys = (page_table[jnp.arange(n_b)[:, None, None], sel_past // PAGE_SIZE] * PAGE_SIZE
            + sel_past % PAGE_SIZE)
    pool_k = cache_k.reshape(-1, H_A, DH_A)
    pool_v = cache_v.reshape(-1, H_A, DH_A)
    sel_new = jnp.clip(sel - past, 0, n_t - 1)
    is_new = (sel >= past)[..., None, None]
    k_sel = jnp.where(is_new, _take_rows(k_new, sel_new), pool_k[phys].astype(k_new.dtype))
    v_sel = jnp.where(is_new, _take_rows(v_new, sel_new), pool_v[phys].astype(v_new.dtype))
    return _sparse_attend(q, k_sel, v_sel, valid)


def _hgrn2(bq, bf, bi, lb, s0, chunk):
    n_b, n_t, _ = bq.shape
    q = jax.nn.silu(bq.astype(jnp.float32)).reshape(n_b, n_t, H_B, DK_B)
    f = lb + (1.0 - lb) * jax.nn.sigmoid(bf.astype(jnp.float32))
    log_f = jnp.log(f).reshape(n_b, n_t, H_B, DK_B)
    kk = (1.0 - f).reshape(n_b, n_t, H_B, DK_B)
    v = bi.astype(jnp.float32).reshape(n_b, n_t, H_B, DV_B)
    n_c = n_t // chunk
    tri = jnp.tril(jnp.ones((chunk, chunk), dtype=bool))

    def resh(a):
        return jnp.moveaxis(a.reshape(n_b, n_c, chunk, H_B, a.shape[-1]), 1, 0)

    def step(S, inp):
        qc, gc, kc, vc = inp
        G = jnp.cumsum(gc, axis=1)
        diff = G[:, :, None] - G[:, None, :]
        dec = jnp.exp(jnp.where(tri[None, :, :, None, None], diff, -jnp.inf))
        A = jnp.einsum('bthd,btshd->bhts', qc, kc[:, None] * dec)
        o = (jnp.einsum('bhts,bshv->bthv', A, vc)
             + jnp.einsum('bthd,bhdv->bthv', qc * jnp.exp(G), S))
        G_last = G[:, -1]
        k_dec = kc * jnp.exp(G_last[:, None] - G)
        S_new = jnp.exp(G_last)[..., None] * S + jnp.einsum('bshd,bshv->bhdv', k_dec, vc)
        return S_new, o

    S, o = lax.scan(step, s0.astype(jnp.float32), (resh(q), resh(log_f), resh(kk), resh(v)))
    return jnp.moveaxis(o, 0, 1).reshape(n_b, n_t, H_B, DV_B), S


def _mem_attend(q, mk, mv):
    s = jnp.einsum('bthd,bnhd->bhtn', q.astype(jnp.float32), mk.astype(jnp.float32)) * (DH_M ** -0.5)
    p = jax.nn.softmax(s, axis=-1)
    return jnp.einsum('bhtn,bnhd->bthd', p, mv.astype(jnp.float32))


def _merge(h, a_out, ag, b_o, bg, m_out, mg, norm_g, w_out, ln_g, ln_b):
    n_b, n_t, _ = h.shape
    a = a_out.reshape(n_b, n_t, W_A) * jax.nn.silu(ag.astype(jnp.float32))
    bn = b_o * lax.rsqrt(jnp.mean(b_o * b_o, axis=-1, keepdims=True) + RMS_EPS) * norm_g.astype(jnp.float32)
    b = bn.reshape(n_b, n_t, W_B) * jax.nn.silu(bg.astype(jnp.float32))
    m = m_out.reshape(n_b, n_t, W_M) * jax.nn.silu(mg.astype(jnp.float32))
    cat = jnp.concatenate([a, b, m], axis=-1).astype(h.dtype)
    y = jnp.einsum('btn,nd->btd', cat, w_out)
    return _layernorm(ALPHA * h + y, ln_g, ln_b)


def setup_inputs(seed: int = 0) -> dict:
    key = jax.random.key(seed)
    ks = jax.random.split(key, 20)
    n_pages = PAST_LEN // PAGE_SIZE
    n_phys = (DEC_BATCH * n_pages * 5 + 3) // 4
    nrm = jax.random.normal
    x_prompt = nrm(ks[0], (BATCH, SEQ, D_MODEL), jnp.float32)
    x_sample = nrm(ks[1], (DEC_BATCH, DEC_SEQ, D_MODEL), jnp.float32)
    mem_prompt = nrm(ks[2], (BATCH, N_MEM, D_MODEL), jnp.float32)
    cache_k = nrm(ks[3], (DEPTH, n_phys, PAGE_SIZE, H_A, DH_A), jnp.float32)
    cache_v = nrm(ks[4], (DEPTH, n_phys, PAGE_SIZE, H_A, DH_A), jnp.float32)
    cache_idx_k = nrm(ks[5], (DEPTH, n_phys, PAGE_SIZE, D_IDX), jnp.float32)
    state_hgrn = 0.5 * nrm(ks[6], (DEPTH, DEC_BATCH, H_B, DK_B, DV_B), jnp.float32)
    cache_mem_k = nrm(ks[7], (DEPTH, DEC_BATCH, N_MEM, H_M, DH_M), jnp.float32)
    cache_mem_v = nrm(ks[8], (DEPTH, DEC_BATCH, N_MEM, H_M, DH_M), jnp.float32)
    page_table = jax.random.permutation(ks[9], n_phys)[:DEC_BATCH * n_pages].reshape(
        DEC_BATCH, n_pages).astype(jnp.int32)
    off = np.concatenate([[0], np.cumsum(IN_SPLITS)])
    col_scale = np.ones((N_IN,), np.float32)
    col_scale[off[2]:off[3]] = BETA
    col_scale[off[9]:off[10]] = BETA
    w_in = nrm(ks[10], (DEPTH, D_MODEL, N_IN), jnp.float32) * (D_MODEL ** -0.5) * jnp.asarray(col_scale)
    lb_logits = 0.1 * nrm(ks[11], (DEPTH + 1, H_B * DK_B), jnp.float32)
    hgrn_norm_g = 1.0 + 0.02 * nrm(ks[12], (DEPTH, DV_B), jnp.float32)
    w_mem_k = nrm(ks[13], (DEPTH, D_MODEL, W_M), jnp.float32) * (D_MODEL ** -0.5)
    w_mem_v = nrm(ks[14], (DEPTH, D_MODEL, W_M), jnp.float32) * (D_MODEL ** -0.5) * BETA
    w_out = nrm(ks[15], (DEPTH, D_MODEL, D_MODEL), jnp.float32) * (D_MODEL ** -0.5) * BETA
    ln_g = 1.0 + 0.02 * nrm(ks[16], (DEPTH, D_MODEL), jnp.float32)
    ln_b = 0.02 * nrm(ks[17], (DEPTH, D_MODEL), jnp.float32)
    return {"x_prompt": x_prompt, "x_sample": x_sample, "mem_prompt": mem_prompt,
            "cache_k": cache_k, "cache_v": cache_v, "cache_idx_k": cache_idx_k,
            "state_hgrn": state_hgrn, "cache_mem_k": cache_mem_k, "cache_mem_v": cache_mem_v,
            "page_table": page_table, "w_in": w_in, "lb_logits": lb_logits,
            "hgrn_norm_g": hgrn_norm_g, "w_mem_k": w_mem_k, "w_mem_v": w_mem_v,
            "w_out": w_out, "ln_g": ln_g, "ln_b": ln_b}


def reference(x_prompt, x_sample, mem_prompt, cache_k, cache_v, cache_idx_k, state_hgrn,
              cache_mem_k, cache_mem_v, page_table, w_in, lb_logits, hgrn_norm_g,
              w_mem_k, w_mem_v, w_out, ln_g, ln_b):
    lb_all = jnp.cumsum(jax.nn.softmax(lb_logits.astype(jnp.float32), axis=0), axis=0)
    n_bp, t_p, _ = x_prompt.shape
    n_bs, t_s, _ = x_sample.shape
    past = page_table.shape[1] * PAGE_SIZE
    pos_p = jnp.arange(t_p, dtype=jnp.int32)
    pos_s = past + jnp.arange(t_s, dtype=jnp.int32)
    hp, hs = x_prompt, x_sample
    kp_l, vp_l, ikp_l, sp_l, mkp_l, mvp_l = [], [], [], [], [], []
    ks_l, vs_l, iks_l, ss_l = [], [], [], []
    for l in range(DEPTH):
        lb = lb_all[l]
        aq, ak, av, ag, iq, ik, iw, bq, bf, bi, bg, mq, mg = _in_proj(hp, w_in[l], pos_p)
        a_out = _dsa_prompt(aq, ak, av, iq, iw, ik)
        s0 = jnp.zeros((n_bp, H_B, DK_B, DV_B), jnp.float32)
        b_o, s_p = _hgrn2(bq, bf, bi, lb, s0, min(HGRN_CHUNK, t_p))
        mk = jnp.einsum('bnd,dm->bnm', mem_prompt, w_mem_k[l]).reshape(n_bp, N_MEM, H_M, DH_M)
        mv = jnp.einsum('bnd,dm->bnm', mem_prompt, w_mem_v[l]).reshape(n_bp, N_MEM, H_M, DH_M)
        m_out = _mem_attend(mq, mk, mv)
        hp_next = _merge(hp, a_out, ag, b_o, bg, m_out, mg, hgrn_norm_g[l], w_out[l], ln_g[l], ln_b[l])
        kp_l.append(ak); vp_l.append(av); ikp_l.append(ik); sp_l.append(s_p)
        mkp_l.append(mk); mvp_l.append(mv)
        aq, ak, av, ag, iq, ik, iw, bq, bf, bi, bg, mq, mg = _in_proj(hs, w_in[l], pos_s)
        a_out = _dsa_sample(aq, ak, av, iq, iw, ik, cache_k[l], cache_v[l], cache_idx_k[l], page_table)
        b_o, s_s = _hgrn2(bq, bf, bi, lb, state_hgrn[l], t_s)
        m_out = _mem_attend(mq, cache_mem_k[l], cache_mem_v[l])
        hs_next = _merge(hs, a_out, ag, b_o, bg, m_out, mg, hgrn_norm_g[l], w_out[l], ln_g[l], ln_b[l])
        ks_l.append(ak); vs_l.append(av); iks_l.append(ik); ss_l.append(s_s)
        hp, hs = hp_next, hs_next
    return (hp, hs, jnp.stack(kp_l), jnp.stack(vp_l), jnp.stack(ikp_l), jnp.stack(sp_l),
            jnp.stack(mkp_l), jnp.stack(mvp_l), jnp.stack(ks_l), jnp.stack(vs_l),
            jnp.stack(iks_l), jnp.stack(ss_l))
```

```python
from contextlib import ExitStack
import numpy as np
import concourse.bass as bass
import concourse.mybir as mybir
from concourse.bass_utils import run_bass_kernel_spmd

F32 = mybir.dt.float32
BF16 = mybir.dt.bfloat16
I32 = mybir.dt.int32
AF = mybir.ActivationFunctionType
ALU = mybir.AluOpType
AX = mybir.AxisListType

D = 2048
KC = 16
SEQ = 2048
NT = 16
NO = 8
ROPE_THETA = 500000.0
PAST = 16384
ALPHA = 2.0 ** 0.25
LN_EPS = 1e-5
RMS_EPS = 1e-6
NEG = -1.0e30

O_AQ, O_AK, O_AV, O_AG, O_IQ, O_IK, O_IW, O_BQ, O_BF, O_BI, O_BG, O_MQ, O_MG, O_END = (
    0, 1024, 2048, 3072, 4096, 5120, 5184, 5200, 5712, 6224, 6736, 7248, 7760, 8272)


class Buf:
    __slots__ = ("name", "w", "r", "dsem", "dval", "excl")

    def __init__(self, name):
        self.name = name
        self.excl = False
        self.w = None
        self.r = {}
        self.dsem = None
        self.dval = 0


class TL:
    def __init__(self, t, name):
        self.t = t
        self.b = Buf(name)

    def __getitem__(self, k):
        return self.t[k]


class TLV(TL):
    def __init__(self, base, fn):
        self.t = None
        self.base = base
        self.fn = fn
        self.b = base.b

    def __getitem__(self, k):
        return self.fn(self.base.t)[k]


class Sched:
    ENG = ("pe", "act", "dve", "pool", "sp")

    def __init__(self, nc, es):
        self.nc = nc
        self.es = es
        self.q = {e: [] for e in self.ENG}
        self.cnt = {e: 0 for e in self.ENG}
        self.known = {e: {} for e in self.ENG}
        self.sems = {}
        for e in self.ENG:
            self.sems["e:" + e] = es.enter_context(nc.semaphore("s_" + e))
        self.ndsem = 0
        self.ninstr = 0
        self.dvals = {}

    def _dsem(self, buf):
        if buf.dsem is None:
            key = "d:%d" % self.ndsem
            self.ndsem += 1
            self.sems[key] = self.es.enter_context(self.nc.semaphore("sd%d" % self.ndsem))
            buf.dsem = key
        return buf.dsem

    def _deps(self, eng, reads, writes, skip_self_pe=False):
        deps = {}

        def add(k, v):
            if deps.get(k, 0) < v:
                deps[k] = v
        for b in reads:
            if b.w is not None:
                add(*b.w)
            if b.excl:
                for k, v in b.r.items():
                    if k != "e:" + eng:
                        add(k, v)
        for b in writes:
            if b.w is not None:
                add(*b.w)
            for k, v in b.r.items():
                add(k, v)
        waits = []
        for k, v in deps.items():
            if skip_self_pe and k == "e:pe":
                continue
            if self.known[eng].get(k, 0) < v:
                self.known[eng][k] = v
                waits.append((k, v))
        return waits

    def _mark(self, ev, reads, writes):
        for b in reads:
            if b.r.get(ev[0], 0) < ev[1]:
                b.r[ev[0]] = ev[1]
        for b in writes:
            b.w = ev
            b.r = {}

    LIMIT = 10 ** 9

    def op(self, eng, fn, reads=(), writes=(), pe_acc=False):
        if self.ninstr >= Sched.LIMIT:
            return None
        reads = [x.b if isinstance(x, TL) else x for x in reads]
        writes = [x.b if isinstance(x, TL) else x for x in writes]
        waits = self._deps(eng, reads, writes, skip_self_pe=(eng == "pe" and pe_acc))
        self.cnt[eng] += 1
        ev = ("e:" + eng, self.cnt[eng])
        self.q[eng].append((waits, fn, ev[0], 1))
        self._mark(ev, reads, writes)
        self.ninstr += 1
        return ev

    def dma(self, queue, fns, reads=(), writes=(), owner=None):
        if self.ninstr >= Sched.LIMIT:
            return None
        reads = [x.b if isinstance(x, TL) else x for x in reads]
        writes = [x.b if isinstance(x, TL) else x for x in writes]
        if not isinstance(fns, (list, tuple)):
            fns = [fns]
        if owner is None:
            owner = (list(writes) + list(reads))[0]
        elif isinstance(owner, TL):
            owner = owner.b
        key = self._dsem(owner)
        waits = self._deps(queue, reads, writes)
        for i, fn in enumerate(fns):
            owner.dval += 16
            self.q[queue].append((waits if i == 0 else [], fn, key, 16))
            self.ninstr += 1
        ev = (key, owner.dval)
        self.dvals[key] = owner.dval
        self._mark(ev, reads, writes)
        return ev

    def barrier(self):
        tgt = {"e:" + e: self.cnt[e] for e in self.ENG if self.cnt[e] > 0}
        for k, v in self.dvals.items():
            tgt[k] = v
        for e in self.ENG:
            waits = []
            for k, v in tgt.items():
                if k == "e:" + e:
                    continue
                if self.known[e].get(k, 0) < v:
                    self.known[e][k] = v
                    waits.append((k, v))
            self.q[e].append((waits, None, None, 0))

    def finish_wait(self, bufs, eng="sp"):
        deps = {}
        for b in bufs:
            b = b.b if isinstance(b, TL) else b
            for k, v in ([b.w] if b.w else []) + list(b.r.items()):
                if deps.get(k, 0) < v:
                    deps[k] = v
        self.q[eng].append((list(deps.items()), None, None, 0))

    def emit(self):
        nc = self.nc
        engobj = {"pe": "tensor", "act": "scalar", "dve": "vector", "pool": "gpsimd", "sp": "sync"}
        with nc.Block() as block:
            for e in self.ENG:
                items = self.q[e]
                if not items:
                    continue

                def body(eng, items=items):
                    for waits, fn, semkey, inc in items:
                        for k, v in waits:
                            eng.wait_ge(self.sems[k], v)
                        if fn is not None:
                            ins = fn(eng)
                            ins.then_inc(self.sems[semkey], inc)
                getattr(block, engobj[e])(body)
        self.q = {e: [] for e in self.ENG}


def build_program(stage=99):
    nc = bass.Bass("TRN2", target_bir_lowering=False)

    def din(name, shape, dt=F32):
        return nc.dram_tensor(name, list(shape), dt, kind="ExternalInput").ap()

    def dout(name, shape, dt=F32):
        return nc.dram_tensor(name, list(shape), dt, kind="ExternalOutput").ap()

    xTn = din("xTn", [D, SEQ])
    memT = din("memT", [D, 256])
    Wk = din("Wk", [D, 1024])
    Wv = din("Wv", [D, 1024])
    W3 = din("W3", [D, 1088])
    Wmk = din("Wmk", [D, 512])
    Wmv = din("Wmv", [D, 512])
    lbl = din("lbl", [2, 512])
    ropeN = din("ropeN", [SEQ, 96])
    c_tri2 = din("c_tri2", [128, 128])
    c_blk2 = din("c_blk2", [128, 128])
    c_ident = din("c_ident", [128, 128])
    posq = din("posq", [128, NO + 1])
    xTo = din("xTo", [D, 1024])
    xo = din("xo", [1024, D])
    Wi = din("Wi", [D, 1040])
    Wq = din("Wq", [D, 1024])
    Wg = din("Wg", [D, 1024])
    Wb = din("Wb", [D, 2048])
    Wm = din("Wm", [D, 1024])
    Wo = din("Wo", [D, D])
    ropeO = din("ropeO", [1024, 96])
    c_j = din("c_j", [128, 256])
    c_pow = din("c_pow", [128, 48])
    ng = din("ng", [1, 128])
    lng = din("lng", [1, D])
    lnb = din("lnb", [1, D])
    o_k = dout("o_k", [SEQ, 1024])
    o_v = dout("o_v", [SEQ, 1024])
    o_ik = dout("o_ik", [SEQ, 64])
    o_hg = dout("o_hg", [128, 4, 128])
    o_mk = dout("o_mk", [256, 512])
    o_mv = dout("o_mv", [256, 512])
    o_y = dout("o_y", [1024, D])
    xsT = din("xsT", [D, 8])
    xs = din("xs", [8, D])
    ropeS = din("ropeS", [8, 160])
    st_h = din("st_h", [4, 128, 128])
    cmk_d = din("cmk_d", [256, 512])
    cmv_d = din("cmv_d", [256, 512])
    ptab = din("ptab", [1, 128], I32)
    cidx = din("cidx", [1280, 8192])
    ck = din("ck", [163840, 1024])
    cv = din("cv", [163840, 1024])
    cin = {}
    for nm_, shp_, dt__ in (("c_sel8", [8, 128], F32), ("c_hm", [128, 16], F32), ("c_bq8", [128, 8], F32),
                            ("c_bq16", [128, 8], F32), ("c_BQ1", [128, 128], F32), ("c_BQm", [128, 128], F32),
                            ("c_LT", [128, 128], F32), ("c_negm", [128, 8], F32), ("c_aiota", [128, 1024], F32),
                            ("c_iota512", [128, 512], F32), ("c_esel", [8, 1024], F32), ("c_eq", [8, 64], F32),
                            ("c_bd8", [8, 1024], F32), ("c_pow2", [128, 60], F32), ("c_i32", [128, 768], I32)):
        cin[nm_] = din(nm_, shp_, dt__)
    o_ks = dout("o_ks", [8, 1024])
    o_vs = dout("o_vs", [8, 1024])
    o_iks = dout("o_iks", [8, 64])
    o_hgs = dout("o_hgs", [128, 4, 128])
    o_ys = dout("o_ys", [8, D])
    import os
    DBG = os.environ.get("KDBG", "0") == "1"
    if DBG:
        o_cat = dout("o_cat", [128, KC * 1024], BF16)

    with ExitStack() as es:
        S = Sched(nc, es)
        outbufs = []

        def sb(st, name, shape, dt):
            return TL(st.enter_context(nc.sbuf_tensor(name, list(shape), dt)), name)

        def ps(st, name, shape, dt):
            t = TL(st.enter_context(nc.psum_tensor(name, list(shape), dt)), name)
            t.b.excl = True
            return t

        ident_f = sb(es, "ident_f", [128, 128], F32)
        ident_b = sb(es, "ident_b", [128, 128], BF16)
        tri2 = sb(es, "tri2", [128, 128], F32)
        blk2 = sb(es, "blk2", [128, 128], F32)
        ones_f = sb(es, "ones_f", [128, 2], F32)
        lb_bc = sb(es, "lb_bc", [128, 512], F32)
        oml_bc = sb(es, "oml_bc", [128, 512], F32)
        posq_sb = sb(es, "posq_sb", [128, NO + 1], F32)
        hflag = sb(es, "hflag", [128, 1], F32)
        mkT = sb(es, "mkT", [128, 4, 256], BF16)
        mv = sb(es, "mv", [128, 2, 4, 130], BF16)
        Sown = sb(es, "Sown", [128, NO, 512], BF16)
        R16 = sb(es, "R16", [128, 8192], BF16)
        es_A = ExitStack()
        KT = sb(es_A, "KT", [128, 8, SEQ], BF16)
        V = sb(es_A, "V", [128, NT, 8, 130], BF16)
        ikT = sb(es_A, "ikT", [64, SEQ], BF16)

        PZ = [ps(es, "PZ%d" % i, [128, 512], F32) for i in range(2)]
        PT = ps(es, "PT", [128, 1024], BF16)
        PG = ps(es, "PG", [128, 512], F32)
        PGL = ps(es, "PGL", [128, 512], F32)
        PU = [ps(es, "PU%d" % i, [128, 512], F32) for i in range(2)]
        PD = ps(es, "PD", [128, 512], F32)

        S.dma("sp", lambda e: e.dma_start(out=ident_f[:], in_=c_ident[:, :]), writes=[ident_f])
        S.dma("sp", lambda e: e.dma_start(out=tri2[:], in_=c_tri2[:, :]), writes=[tri2])
        S.dma("sp", lambda e: e.dma_start(out=blk2[:], in_=c_blk2[:, :]), writes=[blk2])
        S.dma("sp", lambda e: e.dma_start(out=posq_sb[:], in_=posq[:, :]), writes=[posq_sb])
        tmp_es = ExitStack()
        lb2 = sb(tmp_es, "lb2", [128, 2, 512], F32)
        S.dma("sp", [lambda e, l=l: e.dma_start(out=lb2[:, l, :], in_=lbl[l:l + 1, :].broadcast_to([128, 512]))
                     for l in range(2)], writes=[lb2])
        S.op("pool", lambda e: e.memset(ones_f[:], 1.0), writes=[ones_f])
        S.op("dve", lambda e: e.tensor_copy(out=ident_b[:], in_=ident_f[:]), reads=[ident_f], writes=[ident_b])
        S.op("dve", lambda e: e.tensor_tensor(out=oml_bc[:], in0=lb2[:, 0, :], in1=lb2[:, 1, :], op=ALU.subtract),
             reads=[lb2], writes=[oml_bc])
        S.op("act", lambda e: e.activation(out=lb_bc[:], in_=oml_bc[:], func=AF.Sigmoid), reads=[oml_bc], writes=[lb_bc])
        S.op("dve", lambda e: e.tensor_scalar(out=oml_bc[:], in0=lb_bc[:], scalar1=-1.0, scalar2=1.0,
                                              op0=ALU.mult, op1=ALU.add), reads=[lb_bc], writes=[oml_bc])
        S.op("dve", lambda e: e.tensor_copy(out=hflag[:], in_=posq_sb[:, NO:NO + 1]), reads=[posq_sb], writes=[hflag])
        S.op("pool", lambda e: e.memset(V[:, :, :, 128:129], 1.0), writes=[V])
        S.op("pool", lambda e: e.memset(mv[:, :, :, 128:129], 1.0), writes=[mv])
        S.barrier()
        S.emit()
        tmp_es.close()

        def rope(src3, dst3, nh, half, CC, SS, tA, tB, np_=128):
            r = 2 * half
            ccb = CC.unsqueeze(1).to_broadcast([np_, nh, r])
            s1b = SS[:, 0:half].unsqueeze(1).to_broadcast([np_, nh, half])
            s2b = SS[:, half:r].unsqueeze(1).to_broadcast([np_, nh, half])
            S.op("dve", lambda e: e.tensor_tensor(out=tA[0:np_, 0:nh, 0:r], in0=src3(0, r), in1=ccb, op=ALU.mult),
                 reads=src3.deps, writes=[tA])
            S.op("dve", lambda e: e.tensor_tensor(out=tB[0:np_, 0:nh, 0:half], in0=src3(half, r), in1=s1b, op=ALU.mult),
                 reads=src3.deps, writes=[tB])
            S.op("dve", lambda e: e.tensor_tensor(out=tB[0:np_, 0:nh, half:r], in0=src3(0, half), in1=s2b, op=ALU.mult),
                 reads=src3.deps + [tB], writes=[tB])
            S.op("dve", lambda e: e.tensor_tensor(out=dst3(0, r), in0=tA[0:np_, 0:nh, 0:r], in1=tB[0:np_, 0:nh, 0:r],
                                                  op=ALU.add), reads=[tA, tB], writes=dst3.deps)

        class V3:
            def __init__(self, fn, deps):
                self.fn = fn
                self.deps = deps

            def __call__(self, lo, hi):
                return self.fn(lo, hi)

        with ExitStack() as pm:
            memTb = sb(pm, "memTb", [128, KC, 256], BF16)
            wmk = sb(pm, "wmk", [128, KC, 512], BF16)
            wmv = sb(pm, "wmv", [128, KC, 512], BF16)
            mf = [sb(pm, "mf%d" % i, [128, 512], F32) for i in range(2)]
            mb = sb(pm, "mb", [128, 512], BF16)
            S.dma("pool", [lambda e, k=k: e.dma_start(out=memTb[:, k, :], in_=memT[k * 128:(k + 1) * 128, :])
                           for k in range(KC)], writes=[memTb])
            S.dma("pool", [lambda e, k=k: e.dma_start(out=wmk[:, k, :], in_=Wmk[k * 128:(k + 1) * 128, :])
                           for k in range(KC)], writes=[wmk])
            S.dma("pool", [lambda e, k=k: e.dma_start(out=wmv[:, k, :], in_=Wmv[k * 128:(k + 1) * 128, :])
                           for k in range(KC)], writes=[wmv])
            cnt = 0
            for nt in range(2):
                for which in range(2):
                    wsb = wmk if which == 0 else wmv
                    pz = PZ[cnt % 2]
                    f = mf[cnt % 2]
                    cnt += 1
                    for k in range(KC):
                        S.op("pe", lambda e, k=k, pz=pz, wsb=wsb, nt=nt: e.matmul(
                            pz[:], lhsT=memTb[:, k, nt * 128:(nt + 1) * 128], rhs=wsb[:, k, :],
                            start=(k == 0), stop=(k == KC - 1)), reads=[memTb, wsb], writes=[pz], pe_acc=(k > 0))
                    S.op("act", lambda e, pz=pz, f=f: e.activation(out=f[:], in_=pz[:], func=AF.Copy),
                         reads=[pz], writes=[f])
                    dst = o_mk if which == 0 else o_mv
                    S.dma("sp", lambda e, f=f, dst=dst, nt=nt: e.dma_start(out=dst[nt * 128:(nt + 1) * 128, :], in_=f[:]),
                          reads=[f])
                    outbufs.append(f)
                    if which == 0:
                        S.op("dve", lambda e, pz=pz: e.tensor_copy(out=mb[:], in_=pz[:]), reads=[pz], writes=[mb])
                        for hd in range(4):
                            S.op("pe", lambda e, hd=hd: e.transpose(out=PT[:, hd * 128:(hd + 1) * 128],
                                                                    in_=mb[:, hd * 128:(hd + 1) * 128],
                                                                    identity=ident_b[:]),
                                 reads=[mb, ident_b], writes=[PT])
                        S.op("act", lambda e, nt=nt: e.activation(
                            out=mkT[:, :, nt * 128:(nt + 1) * 128],
                            in_=PT[:, 0:512].rearrange("p (h t) -> p h t", h=4), func=AF.Copy),
                            reads=[PT], writes=[mkT])
                    else:
                        S.op("dve", lambda e, pz=pz, nt=nt: e.tensor_copy(
                            out=mv[:, nt, :, 0:128], in_=pz[:].rearrange("p (h d) -> p h d", h=4)),
                            reads=[pz], writes=[mv])
            S.barrier()
            S.emit()
        if stage == 1:
            return nc

        with ExitStack() as pn:
            xh = sb(pn, "xh", [128, KC, 1024], BF16)
            W0 = sb(pn, "W0", [128, KC, 1088], BF16)
            W1 = TLV(R16, lambda t: t[:].rearrange("p (k c) -> p k c", k=KC))
            ropeN_sb = sb(pn, "ropeN_sb", [128, NT, 96], F32)
            Kf = [sb(pn, "Kf%d" % i, [128, 512], F32) for i in range(2)]
            Kb = [sb(pn, "Kb%d" % i, [128, 512], BF16) for i in range(2)]
            tA = sb(pn, "tA", [128, 4, 32], F32)
            tB = sb(pn, "tB", [128, 4, 32], F32)
            sg = sb(pn, "sg", [128, 512], F32)
            ff = sg
            gg = sb(pn, "gg", [128, 512], F32)
            omf = sb(pn, "omf", [128, 512], F32)
            Gs = sb(pn, "Gs", [128, 512], F32)
            dlt = sb(pn, "dlt", [128, 512], F32)
            EE = dlt
            kdec = sb(pn, "kdec", [128, 512], BF16)
            vb = sb(pn, "vb", [128, 512], BF16)
            ikf = sb(pn, "ikf", [128, 64], F32)
            ikb = sb(pn, "ikb", [128, 64], BF16)
            Dc = sb(pn, "Dc", [128, 16], F32)
            St = sb(pn, "St", [128, 4, 128], F32)
            Sev = sb(pn, "Sev", [128, 512], F32)
            Stmp = sb(pn, "Stmp", [128, 512], F32)

            S.dma("sp", lambda e: e.dma_start(out=ropeN_sb[:], in_=ropeN.rearrange("(n p) c -> p n c", p=128)),
                  writes=[ropeN_sb])
            S.op("pool", lambda e: e.memset(St[:], 0.0), writes=[St])

            def load_w(dst, src, c0, ncol):
                S.dma("pool", [lambda e, k=k: e.dma_start(out=dst[:, k, 0:ncol], in_=src[k * 128:(k + 1) * 128, c0:c0 + ncol])
                               for k in range(KC)], writes=[dst])

            def mm_chunk(pz, tl, wsb, c0, ncol):
                for k in range(KC):
                    S.op("pe", lambda e, k=k: e.matmul(pz[:, 0:ncol], lhsT=xh[:, k, tl * 128:(tl + 1) * 128],
                                                       rhs=wsb[:, k, c0:c0 + ncol], start=(k == 0), stop=(k == KC - 1)),
                         reads=[xh, wsb], writes=[pz], pe_acc=(k > 0))

            ctr = [0]
            for half in range(2 if stage != 2 else 1):
                S.dma("pool", [lambda e, k=k, half=half: e.dma_start(out=xh[:, k, :],
                                                          in_=xTn[k * 128:(k + 1) * 128, half * 1024:(half + 1) * 1024])
                               for k in range(KC)], writes=[xh])
                for blk in range(2):
                    wsb = W0 if blk == 0 else W1
                    load_w(wsb, Wk, blk * 512, 512)
                    for tl in range(8):
                        n = half * 8 + tl
                        i = ctr[0] % 2
                        ctr[0] += 1
                        pz, kf, kb = PZ[i], Kf[i], Kb[i]
                        mm_chunk(pz, tl, wsb, 0, 512)
                        S.op("act", lambda e, pz=pz, kf=kf: e.activation(out=kf[:], in_=pz[:], func=AF.Copy),
                             reads=[pz], writes=[kf])
                        src3 = V3(lambda lo, hi, pz=pz: pz[:].rearrange("p (h d) -> p h d", h=4)[:, :, lo:hi], [pz])
                        dst3 = V3(lambda lo, hi, kf=kf: kf[:].rearrange("p (h d) -> p h d", h=4)[:, :, lo:hi], [kf])
                        rope(src3, dst3, 4, 16, ropeN_sb[:, n, 0:32], ropeN_sb[:, n, 32:64], tA, tB)
                        S.dma("sp", lambda e, kf=kf, n=n, blk=blk: e.dma_start(
                            out=o_k[n * 128:(n + 1) * 128, blk * 512:(blk + 1) * 512], in_=kf[:]), reads=[kf])
                        S.op("pool", lambda e, kf=kf, kb=kb: e.tensor_copy(out=kb[:], in_=kf[:]), reads=[kf], writes=[kb])
                        for hd in range(4):
                            S.op("pe", lambda e, hd=hd, kb=kb: e.transpose(out=PT[:, hd * 128:(hd + 1) * 128],
                                                                           in_=kb[:, hd * 128:(hd + 1) * 128],
                                                                           identity=ident_b[:]),
                                 reads=[kb, ident_b], writes=[PT])
                        S.op("act", lambda e, n=n, blk=blk: e.activation(
                            out=KT[:, blk * 4:(blk + 1) * 4, n * 128:(n + 1) * 128],
                            in_=PT[:, 0:512].rearrange("p (h t) -> p h t", h=4), func=AF.Copy),
                            reads=[PT], writes=[KT])
                for blk in range(2):
                    wsb = W0 if blk == 0 else W1
                    load_w(wsb, Wv, blk * 512, 512)
                    for tl in range(8):
                        n = half * 8 + tl
                        i = ctr[0] % 2
                        ctr[0] += 1
                        pz, kf = PZ[i], Kf[i]
                        mm_chunk(pz, tl, wsb, 0, 512)
                        S.op("act", lambda e, pz=pz, kf=kf: e.activation(out=kf[:], in_=pz[:], func=AF.Copy),
                             reads=[pz], writes=[kf])
                        S.dma("sp", lambda e, kf=kf, n=n, blk=blk: e.dma_start(
                            out=o_v[n * 128:(n + 1) * 128, blk * 512:(blk + 1) * 512], in_=kf[:]), reads=[kf])
                        S.op("dve", lambda e, pz=pz, n=n, blk=blk: e.tensor_copy(
                            out=V[:, n, blk * 4:(blk + 1) * 4, 0:128], in_=pz[:].rearrange("p (h d) -> p h d", h=4)),
                            reads=[pz], writes=[V])
                load_w(W0, W3, 0, 1088)
                for tl in range(8):
                    n = half * 8 + tl
                    pza, pzb = PZ[0], PZ[1]
                    mm_chunk(pza, tl, W0, 0, 512)
                    S.op("act", lambda e: e.activation(out=sg[:], in_=pza[:], func=AF.Sigmoid), reads=[pza], writes=[sg])
                    mm_chunk(pzb, tl, W0, 512, 512)
                    S.op("act", lambda e: e.activation(out=vb[:], in_=pzb[:], func=AF.Copy), reads=[pzb], writes=[vb])
                    mm_chunk(pza, tl, W0, 1024, 64)
                    S.op("act", lambda e: e.activation(out=ikf[:], in_=pza[:, 0:64], func=AF.Copy), reads=[pza], writes=[ikf])
                    src3 = V3(lambda lo, hi: pza[:, 0:64].rearrange("p (h d) -> p h d", h=1)[:, :, lo:hi], [pza])
                    dst3 = V3(lambda lo, hi: ikf[:].rearrange("p (h d) -> p h d", h=1)[:, :, lo:hi], [ikf])
                    rope(src3, dst3, 1, 8, ropeN_sb[:, n, 64:80], ropeN_sb[:, n, 80:96], tA, tB)
                    S.dma("sp", lambda e, n=n: e.dma_start(out=o_ik[n * 128:(n + 1) * 128, :], in_=ikf[:]), reads=[ikf])
                    S.op("pool", lambda e: e.tensor_copy(out=ikb[:], in_=ikf[:]), reads=[ikf], writes=[ikb])
                    S.op("pe", lambda e: e.transpose(out=PT[0:64, 512:640], in_=ikb[:], identity=ident_b[:]),
                         reads=[ikb, ident_b], writes=[PT])
                    S.op("act", lambda e, n=n: e.activation(out=ikT[:, n * 128:(n + 1) * 128], in_=PT[0:64, 512:640],
                                                            func=AF.Copy), reads=[PT], writes=[ikT])
                    S.op("dve", lambda e: e.tensor_tensor(out=ff[:], in0=sg[:], in1=oml_bc[:], op=ALU.mult),
                         reads=[sg, oml_bc], writes=[sg])
                    S.op("dve", lambda e: e.tensor_tensor(out=ff[:], in0=ff[:], in1=lb_bc[:], op=ALU.add),
                         reads=[ff, lb_bc], writes=[ff])
                    S.op("act", lambda e: e.activation(out=gg[:], in_=ff[:], func=AF.Ln), reads=[ff], writes=[gg])
                    S.op("pool", lambda e: e.tensor_scalar(out=omf[:], in0=ff[:], scalar1=-1.0, scalar2=1.0,
                                                           op0=ALU.mult, op1=ALU.add), reads=[ff], writes=[omf])
                    S.op("pe", lambda e: e.matmul(PG[:], lhsT=tri2[:], rhs=gg[:], start=True, stop=True),
                         reads=[tri2, gg], writes=[PG])
                    S.op("pe", lambda e: e.matmul(PGL[:], lhsT=blk2[:], rhs=gg[:], start=True, stop=True),
                         reads=[blk2, gg], writes=[PGL])
                    S.op("act", lambda e: e.activation(out=Gs[:], in_=PG[:], func=AF.Copy), reads=[PG], writes=[Gs])
                    S.op("dve", lambda e: e.tensor_tensor(out=dlt[:], in0=PGL[:], in1=Gs[:], op=ALU.subtract),
                         reads=[PGL, Gs], writes=[dlt])
                    S.op("act", lambda e: e.activation(out=EE[:], in_=dlt[:], func=AF.Exp), reads=[dlt], writes=[EE])
                    S.op("dve", lambda e: e.tensor_tensor(out=kdec[:], in0=omf[:], in1=EE[:], op=ALU.mult),
                         reads=[omf, EE], writes=[kdec])
                    for c in range(2):
                        for hd in range(4):
                            j = c * 4 + hd
                            S.op("pe", lambda e, c=c, hd=hd, j=j: e.matmul(
                                PD[:, 2 * j:2 * j + 2], lhsT=gg[c * 64:(c + 1) * 64, hd * 128:(hd + 1) * 128],
                                rhs=ones_f[c * 64:(c + 1) * 64, :], start=True, stop=True),
                                reads=[gg, ones_f], writes=[PD])
                    S.op("act", lambda e: e.activation(out=Dc[:], in_=PD[:, 0:16], func=AF.Exp), reads=[PD], writes=[Dc])
                    if n % 2 == 0:
                        S.op("pool", lambda e: e.tensor_copy(out=Sev[:], in_=St[:].rearrange("p h d -> p (h d)")),
                             reads=[St], writes=[Sev])
                    else:
                        S.op("dve", lambda e: e.tensor_tensor(out=Stmp[:], in0=St[:].rearrange("p h d -> p (h d)"),
                                                              in1=Sev[:], op=ALU.subtract), reads=[St, Sev], writes=[Stmp])
                        S.op("dve", lambda e, n=n: e.scalar_tensor_tensor(
                            out=Sown[:, n // 2, :], in0=Stmp[:], scalar=hflag[:, 0:1], in1=Sev[:],
                            op0=ALU.mult, op1=ALU.add), reads=[Stmp, hflag, Sev], writes=[Sown])
                    for c in range(2):
                        pu = PU[c]
                        for hd in range(4):
                            S.op("pe", lambda e, c=c, hd=hd, pu=pu: e.matmul(
                                pu[:, hd * 128:(hd + 1) * 128], lhsT=kdec[c * 64:(c + 1) * 64, hd * 128:(hd + 1) * 128],
                                rhs=vb[c * 64:(c + 1) * 64, hd * 128:(hd + 1) * 128], start=True, stop=True),
                                reads=[kdec, vb], writes=[pu])
                        for hd in range(4):
                            j = c * 4 + hd
                            S.op("dve", lambda e, hd=hd, j=j, pu=pu: e.scalar_tensor_tensor(
                                out=St[:, hd, :], in0=St[:, hd, :], scalar=Dc[:, 2 * j:2 * j + 1],
                                in1=pu[:, hd * 128:(hd + 1) * 128], op0=ALU.mult, op1=ALU.add),
                                reads=[St, Dc, pu], writes=[St])
            S.dma("sp", lambda e: e.dma_start(out=o_hg[:, :, :], in_=St[:]), reads=[St])
            S.barrier()
            S.emit()
        if stage <= 3:
            return nc
        QS = 128.0 ** -0.5
        KB = 24
        PA = [PG, PGL]
        PI = PU
        moff = [sum(2 * j + 2 for j in range(i)) for i in range(NO)]

        def load_wres(dst, src, ncol):
            S.dma("pool", [lambda e, k=k: e.dma_start(out=dst[:, k, 0:ncol], in_=src[k * 128:(k + 1) * 128, 0:ncol])
                           for k in range(KC)], writes=[dst])

        def load_xt(dst, i):
            S.dma("pool", lambda e: e.dma_start(out=dst[:], in_=xTo[:, i * 128:(i + 1) * 128].rearrange("(k p) t -> p k t", p=128)),
                  writes=[dst])

        def mm_x(pz, xt_, wsb, c0, ncol):
            for k in range(KC):
                S.op("pe", lambda e, k=k: e.matmul(pz[:, 0:ncol], lhsT=xt_[:, k, :], rhs=wsb[:, k, c0:c0 + ncol],
                                                   start=(k == 0), stop=(k == KC - 1)),
                     reads=[xt_, wsb], writes=[pz], pe_acc=(k > 0))

        def transpose4(src_fn, srcdeps, col0=0, n=4):
            for hd in range(n):
                S.op("pe", lambda e, hd=hd: e.transpose(out=PT[:, col0 + hd * 128:col0 + (hd + 1) * 128], in_=src_fn(hd),
                                                        identity=ident_b[:]), reads=srcdeps + [ident_b], writes=[PT])

        es_B = ExitStack()
        maskT = sb(es_B, "maskT", [128, 72, 128], BF16)
        ropeO_sb = sb(es_B, "ropeO_sb", [128, NO, 96], F32)
        S.dma("sp", lambda e: e.dma_start(out=ropeO_sb[:], in_=ropeO.rearrange("(n p) c -> p n c", p=128)), writes=[ropeO_sb])

        with ExitStack() as p1:
            Wi_sb = sb(p1, "Wi_sb", [128, KC, 1040], BF16)
            xt = [sb(p1, "xt1_%d" % j, [128, KC, 128], BF16) for j in range(2)]
            cb = sb(p1, "cb", [128, 256], F32)
            cj = sb(p1, "cj", [128, 256], F32)
            pw = sb(p1, "pw", [128, 2 * KB], F32)
            spw = sb(p1, "spw", [128, 2 * KB], F32)
            iq_b = sb(p1, "iq_b", [128, 16, 64], BF16)
            w_s = sb(p1, "w_s", [128, 16], F32)
            iqT = sb(p1, "iqT", [64, 16, 128], BF16)
            Dg = sb(p1, "Dg", [128, 16, 128], BF16)
            Tb = [sb(p1, "Tb%d" % j, [128, 512], BF16) for j in range(2)]
            I_sb = sb(p1, "I_sb", [128, 2048], F32)
            junk = sb(p1, "junk", [128, 2048], BF16)
            tA1 = sb(p1, "tA1", [128, 8, 16], F32)
            tB1 = sb(p1, "tB1", [128, 8, 16], F32)
            sm = sb(p1, "sm", [128, 8], F32)
            lo, hi, rng, mid, cnt, wv, thr0 = [sm[:, j:j + 1] for j in range(7)]
            load_wres(Wi_sb, Wi, 1040)
            S.dma("sp", lambda e: e.dma_start(out=cj[:], in_=c_j[:, :]), writes=[cj])
            S.dma("sp", lambda e: e.dma_start(out=pw[:], in_=c_pow[:, :]), writes=[pw])
            S.op("dve", lambda e: e.tensor_scalar(out=cb[:], in0=cj[:], scalar1=posq_sb[:, 0:1], scalar2=NEG,
                                                  op0=ALU.is_gt, op1=ALU.mult), reads=[cj, posq_sb], writes=[cb])
            S.op("pool", lambda e: e.memset(sm[:, 6:7], -1.0e29), writes=[sm])
            for i in range(NO):
                nk = (2 * i + 2) * 128
                x = xt[i % 2]
                load_xt(x, i)
                for c in range(2):
                    pz = PZ[c]
                    mm_x(pz, x, Wi_sb, c * 512, 512)
                    S.op("act", lambda e, pz=pz, c=c: e.activation(
                        out=iq_b[:, c * 8:(c + 1) * 8, :], in_=pz[:].rearrange("p (h d) -> p h d", h=8), func=AF.Copy),
                        reads=[pz], writes=[iq_b])
                    src3 = V3(lambda lo_, hi_, pz=pz: pz[:].rearrange("p (h d) -> p h d", h=8)[:, :, lo_:hi_], [pz])
                    dst3 = V3(lambda lo_, hi_, c=c: iq_b[:, c * 8:(c + 1) * 8, lo_:hi_], [iq_b])
                    rope(src3, dst3, 8, 8, ropeO_sb[:, i, 64:80], ropeO_sb[:, i, 80:96], tA1, tB1)
                pz = PZ[0]
                mm_x(pz, x, Wi_sb, 1024, 16)
                S.op("act", lambda e, pz=pz: e.activation(out=w_s[:], in_=pz[:, 0:16], func=AF.Copy, scale=1.0 / 32.0),
                     reads=[pz], writes=[w_s])
                for r in range(2):
                    for hh in range(8):
                        S.op("pe", lambda e, r=r, hh=hh: e.transpose(out=PT[0:64, hh * 128:(hh + 1) * 128],
                                                                     in_=iq_b[:, r * 8 + hh, :], identity=ident_b[:]),
                             reads=[iq_b, ident_b], writes=[PT])
                    S.op("act", lambda e, r=r: e.activation(out=iqT[:, r * 8:(r + 1) * 8, :],
                                                            in_=PT[0:64, :].rearrange("p (h t) -> p h t", h=8), func=AF.Copy),
                         reads=[PT], writes=[iqT])
                for h in range(16):
                    S.op("pool", lambda e, h=h: e.tensor_scalar(out=Dg[:, h, :], in0=ident_b[:], scalar1=w_s[:, h:h + 1],
                                                                scalar2=None, op0=ALU.mult),
                         reads=[ident_b, w_s], writes=[Dg])
                nch = (nk + 511) // 512
                lastlo = nk - 256
                for c in range(nch):
                    kw = min(512, nk - 512 * c)
                    pi = PI[c % 2]

                    def mm1(h, c=c, kw=kw):
                        pa = PA[h % 2]
                        S.op("pe", lambda e, h=h, pa=pa: e.matmul(pa[:, 0:kw], lhsT=iqT[:, h, :],
                                                                  rhs=ikT[:, c * 512:c * 512 + kw], start=True, stop=True),
                             reads=[iqT, ikT], writes=[pa])
                    mm1(0)
                    for h in range(16):
                        if h + 1 < 16:
                            mm1(h + 1)
                        pa = PA[h % 2]
                        tb = Tb[h % 2]
                        S.op("act", lambda e, pa=pa, tb=tb, kw=kw: e.activation(out=tb[:, 0:kw], in_=pa[:, 0:kw], func=AF.Relu),
                             reads=[pa], writes=[tb])
                        S.op("pe", lambda e, h=h, tb=tb, pi=pi, kw=kw: e.matmul(pi[:, 0:kw], lhsT=Dg[:, h, :], rhs=tb[:, 0:kw],
                                                                                start=(h == 0), stop=(h == 15)),
                             reads=[Dg, tb], writes=[pi], pe_acc=(h > 0))
                    a0 = 512 * c
                    a1 = a0 + kw
                    if a1 <= lastlo:
                        S.op("dve", lambda e, pi=pi, a0=a0, a1=a1, kw=kw: e.tensor_copy(out=I_sb[:, a0:a1], in_=pi[:, 0:kw]),
                             reads=[pi], writes=[I_sb])
                    else:
                        if a0 < lastlo:
                            S.op("dve", lambda e, pi=pi, a0=a0, lastlo=lastlo: e.tensor_copy(
                                out=I_sb[:, a0:lastlo], in_=pi[:, 0:lastlo - a0]), reads=[pi], writes=[I_sb])
                        S.op("dve", lambda e, pi=pi, a0=a0, lastlo=lastlo, kw=kw, nk=nk: e.tensor_tensor(
                            out=I_sb[:, lastlo:nk], in0=pi[:, lastlo - a0:kw], in1=cb[:], op=ALU.add),
                            reads=[pi, cb], writes=[I_sb])
                if i >= 1:
                    S.op("dve", lambda e, nk=nk: e.tensor_reduce(out=lo, in_=I_sb[:, 0:nk - 256], axis=AX.X, op=ALU.min),
                         reads=[I_sb], writes=[sm])
                    S.op("dve", lambda e, nk=nk: e.tensor_reduce(out=hi, in_=I_sb[:, 0:nk], axis=AX.X, op=ALU.max),
                         reads=[I_sb], writes=[sm])
                    S.op("dve", lambda e: e.tensor_tensor(out=rng, in0=hi, in1=lo, op=ALU.subtract), reads=[sm], writes=[sm])
                    S.op("dve", lambda e: e.tensor_scalar(out=spw[:], in0=pw[:], scalar1=rng, scalar2=None, op0=ALU.mult),
                         reads=[pw, sm], writes=[spw])
                    S.op("dve", lambda e: e.tensor_tensor(out=mid, in0=lo, in1=spw[:, 0:1], op=ALU.add),
                         reads=[sm, spw], writes=[sm])
                    for k in range(KB):
                        S.op("dve", lambda e, nk=nk: e.tensor_scalar(out=junk[:, 0:nk], in0=I_sb[:, 0:nk], scalar1=mid,
                                                                     scalar2=0.0, op0=ALU.is_ge, op1=ALU.add, accum_out=cnt),
                             reads=[I_sb, sm], writes=[junk, sm])
                        S.op("dve", lambda e, k=k: e.scalar_tensor_tensor(out=wv, in0=cnt, scalar=256.0, in1=spw[:, k:k + 1],
                                                                          op0=ALU.is_ge, op1=ALU.mult),
                             reads=[sm, spw], writes=[sm])
                        S.op("dve", lambda e, k=k: e.scalar_tensor_tensor(out=mid, in0=mid, scalar=spw[:, KB + k:KB + k + 1],
                                                                          in1=wv, op0=ALU.subtract, op1=ALU.add),
                             reads=[sm, spw], writes=[sm])
                    thr = mid
                else:
                    thr = thr0
                S.op("dve", lambda e, nk=nk, thr=thr: e.tensor_scalar(out=junk[:, 0:nk], in0=I_sb[:, 0:nk], scalar1=thr,
                                                                      scalar2=None, op0=ALU.is_ge),
                     reads=[I_sb, sm], writes=[junk])
                nb = 2 * i + 2
                for g0 in range(0, nb, 8):
                    gn = min(8, nb - g0)
                    for j in range(gn):
                        S.op("pe", lambda e, j=j, g0=g0: e.transpose(out=PT[:, j * 128:(j + 1) * 128],
                                                                     in_=junk[:, (g0 + j) * 128:(g0 + j + 1) * 128],
                                                                     identity=ident_b[:]),
                             reads=[junk, ident_b], writes=[PT])
                    S.op("act", lambda e, i=i, g0=g0, gn=gn: e.activation(
                        out=maskT[:, moff[i] + g0:moff[i] + g0 + gn, :],
                        in_=PT[:, 0:gn * 128].rearrange("p (h t) -> p h t", h=gn), func=AF.Copy),
                        reads=[PT], writes=[maskT])
            S.barrier()
            S.emit()

        es_B2 = ExitStack()
        qT = sb(es_B2, "qT", [128, 8, 1024], BF16)
        with ExitStack() as p2:
            Wq_sb = sb(p2, "Wq_sb", [128, KC, 1024], BF16)
            xt = [sb(p2, "xt2_%d" % j, [128, KC, 128], BF16) for j in range(2)]
            q_b = sb(p2, "q_b", [128, 1024], BF16)
            tA2 = sb(p2, "tA2", [128, 4, 32], F32)
            tB2 = sb(p2, "tB2", [128, 4, 32], F32)
            load_wres(Wq_sb, Wq, 1024)
            for i in range(NO):
                x = xt[i % 2]
                load_xt(x, i)
                for c in range(2):
                    pz = PZ[c]
                    mm_x(pz, x, Wq_sb, c * 512, 512)
                    S.op("act", lambda e, pz=pz, c=c: e.activation(out=q_b[:, c * 512:(c + 1) * 512], in_=pz[:], func=AF.Copy,
                                                                   scale=QS), reads=[pz], writes=[q_b])
                    src3 = V3(lambda lo_, hi_, pz=pz: pz[:].rearrange("p (h d) -> p h d", h=4)[:, :, lo_:hi_], [pz])
                    dst3 = V3(lambda lo_, hi_, c=c: q_b[:, c * 512:(c + 1) * 512].rearrange("p (h d) -> p h d", h=4)[:, :, lo_:hi_],
                              [q_b])
                    rope(src3, dst3, 4, 16, ropeO_sb[:, i, 0:32], ropeO_sb[:, i, 32:64], tA2, tB2)
                    transpose4(lambda hd, c=c: q_b[:, c * 512 + hd * 128:c * 512 + (hd + 1) * 128], [q_b])
                    S.op("act", lambda e, c=c, i=i: e.activation(out=qT[:, c * 4:(c + 1) * 4, i * 128:(i + 1) * 128],
                                                                 in_=PT[:, 0:512].rearrange("p (h t) -> p h t", h=4), func=AF.Copy),
                         reads=[PT], writes=[qT])
            S.barrier()
            S.emit()

        a_out = TLV(R16, lambda t: t[:].rearrange("p (i c) -> p i c", i=NO))
        with ExitStack() as p3:
            Eb = [sb(p3, "Eb%d" % j, [128, 4, 128], BF16) for j in range(2)]
            Pb = [sb(p3, "Pb%d" % j, [128, 4, 128], BF16) for j in range(2)]
            rden = sb(p3, "rden", [128, 8], F32)
            PSb = [PG, PGL]
            PO = [PU[0], PU[1], PD]
            for i in range(NO):
                nb = 2 * i + 2
                groups = [(j, g) for j in range(nb) for g in range(2)]
                first = [True, True, True]

                def st_mm(n, i=i):
                    j, g = groups[n]
                    ps_ = PSb[n % 2]
                    for hh in range(4):
                        hd = g * 4 + hh
                        S.op("pe", lambda e, hh=hh, hd=hd, j=j, ps_=ps_: e.matmul(
                            ps_[:, hh * 128:(hh + 1) * 128], lhsT=KT[:, hd, j * 128:(j + 1) * 128],
                            rhs=qT[:, hd, i * 128:(i + 1) * 128], start=True, stop=True),
                            reads=[KT, qT], writes=[ps_])
                st_mm(0)
                for n in range(len(groups)):
                    j, g = groups[n]
                    if n + 1 < len(groups):
                        st_mm(n + 1)
                    ps_, eb, pb = PSb[n % 2], Eb[n % 2], Pb[n % 2]
                    S.op("act", lambda e, ps_=ps_, eb=eb: e.activation(out=eb[:].rearrange("p h t -> p (h t)"), in_=ps_[:],
                                                                       func=AF.Exp), reads=[ps_], writes=[eb])
                    S.op("pool", lambda e, eb=eb, pb=pb, i=i, j=j: e.tensor_tensor(
                        out=pb[:], in0=eb[:], in1=maskT[:, moff[i] + j, :].unsqueeze(1).to_broadcast([128, 4, 128]),
                        op=ALU.mult), reads=[eb, maskT], writes=[pb])
                    for hh in range(4):
                        hd = g * 4 + hh
                        bank = hd // 3
                        col = (hd % 3) * 130
                        st = first[bank]
                        first[bank] = False
                        S.op("pe", lambda e, hh=hh, hd=hd, j=j, pb=pb, bank=bank, col=col, st=st, nb=nb: e.matmul(
                            PO[bank][:, col:col + 130], lhsT=pb[:, hh, :], rhs=V[:, j, hd, :],
                            start=st, stop=(j == nb - 1), skip_group_check=True),
                            reads=[pb, V], writes=[PO[bank]], pe_acc=(not st))
                for bank in range(3):
                    nh = 3 if bank < 2 else 2
                    S.op("dve", lambda e, bank=bank, nh=nh: e.reciprocal(
                        out=rden[:, bank * 3:bank * 3 + nh],
                        in_=PO[bank][:, 0:nh * 130].rearrange("p (h c) -> p h c", c=130)[:, :, 128]),
                        reads=[PO[bank]], writes=[rden])
                    S.op("dve", lambda e, bank=bank, nh=nh, i=i: e.tensor_tensor(
                        out=a_out[:, i, bank * 384:bank * 384 + nh * 128].rearrange("p (h d) -> p h d", h=nh),
                        in0=PO[bank][:, 0:nh * 130].rearrange("p (h c) -> p h c", c=130)[:, :, 0:128],
                        in1=rden[:, bank * 3:bank * 3 + nh].unsqueeze(2).to_broadcast([128, nh, 128]), op=ALU.mult),
                        reads=[PO[bank], rden], writes=[a_out])
            S.barrier()
            S.emit()
        es_B2.close()
        es_B.close()
        es_A.close()
        if stage <= 4:
            return nc

        es_C = ExitStack()
        catT = sb(es_C, "catT", [128, KC, 1024], BF16)

        with ExitStack() as p4:
            Wg_sb = sb(p4, "Wg_sb", [128, KC, 1024], BF16)
            xt = [sb(p4, "xt4_%d" % j, [128, KC, 128], BF16) for j in range(2)]
            ga = [sb(p4, "ga%d" % j, [128, 512], F32) for j in range(2)]
            cab = [sb(p4, "cab%d" % j, [128, 512], BF16) for j in range(2)]
            load_wres(Wg_sb, Wg, 1024)
            for i in range(NO):
                x = xt[i % 2]
                load_xt(x, i)
                for c in range(2):
                    pz, g_, cb_ = PZ[c], ga[c], cab[c]
                    mm_x(pz, x, Wg_sb, c * 512, 512)
                    S.op("act", lambda e, pz=pz, g_=g_: e.activation(out=g_[:], in_=pz[:], func=AF.Silu), reads=[pz], writes=[g_])
                    S.op("dve", lambda e, g_=g_, cb_=cb_, i=i, c=c: e.tensor_tensor(
                        out=cb_[:], in0=a_out[:, i, c * 512:(c + 1) * 512], in1=g_[:], op=ALU.mult),
                        reads=[a_out, g_], writes=[cb_])
                    transpose4(lambda hd, cb_=cb_: cb_[:, hd * 128:(hd + 1) * 128], [cb_])
                    S.op("act", lambda e, c=c, i=i: e.activation(out=catT[:, c * 4:(c + 1) * 4, i * 128:(i + 1) * 128],
                                                                 in_=PT[:, 0:512].rearrange("p (h t) -> p h t", h=4), func=AF.Copy),
                         reads=[PT], writes=[catT])
            S.barrier()
            S.emit()

        with ExitStack() as p5:
            Wb_sb = sb(p5, "Wb_sb", [128, KC, 2048], BF16)
            xt = [sb(p5, "xt5_%d" % j, [128, KC, 128], BF16) for j in range(2)]
            ng_bc = sb(p5, "ng_bc", [128, 4, 128], F32)
            qs = sb(p5, "qs", [128, 512], F32)
            sg = sb(p5, "sg5", [128, 512], F32)
            gg = sb(p5, "gg5", [128, 512], F32)
            omf = sb(p5, "omf5", [128, 512], F32)
            Gs = sb(p5, "Gs5", [128, 512], F32)
            dlt = sb(p5, "dlt5", [128, 512], F32)
            eG = sb(p5, "eG", [128, 512], F32)
            enG = sb(p5, "enG", [128, 512], F32)
            gb = sb(p5, "gb5", [128, 512], F32)
            t1 = sb(p5, "t15", [128, 512], F32)
            kdec = sb(p5, "kdec5", [128, 512], BF16)
            vb = sb(p5, "vb5", [128, 512], BF16)
            kg = sb(p5, "kg", [128, 512], BF16)
            qg = sb(p5, "qg", [128, 512], BF16)
            qgT = sb(p5, "qgT", [128, 4, 128], BF16)
            qgT0 = sb(p5, "qgT0", [128, 4, 128], BF16)
            qgT1 = sb(p5, "qgT1", [128, 4, 128], BF16)
            kgT = sb(p5, "kgT", [128, 4, 128], BF16)
            ATb = sb(p5, "ATb", [128, 4, 128], BF16)
            S1b = sb(p5, "S1b", [128, 4, 128], BF16)
            cbb = sb(p5, "cbb", [128, 512], BF16)
            Dc0 = sb(p5, "Dc0", [128, 8], F32)
            ss = sb(p5, "ss5", [128, 4], F32)
            rstd = sb(p5, "rstd5", [128, 4], F32)
            jk5 = sb(p5, "jk5", [128, 128], F32)
            load_wres(Wb_sb, Wb, 2048)
            S.dma("sp", [lambda e, hd=hd: e.dma_start(out=ng_bc[:, hd, :], in_=ng[0:1, :].broadcast_to([128, 128]))
                         for hd in range(4)], writes=[ng_bc])
            S.op("pool", lambda e: e.memset(qgT0[:], 0.0), writes=[qgT0])
            S.op("pool", lambda e: e.memset(qgT1[:], 0.0), writes=[qgT1])
            for i in range(NO):
                x = xt[i % 2]
                load_xt(x, i)
                mm_x(PZ[0], x, Wb_sb, 0, 512)
                S.op("act", lambda e: e.activation(out=qs[:], in_=PZ[0][:], func=AF.Silu), reads=[PZ[0]], writes=[qs])
                mm_x(PZ[1], x, Wb_sb, 512, 512)
                S.op("act", lambda e: e.activation(out=sg[:], in_=PZ[1][:], func=AF.Sigmoid), reads=[PZ[1]], writes=[sg])
                mm_x(PZ[0], x, Wb_sb, 1024, 512)
                S.op("act", lambda e: e.activation(out=vb[:], in_=PZ[0][:], func=AF.Copy), reads=[PZ[0]], writes=[vb])
                mm_x(PZ[1], x, Wb_sb, 1536, 512)
                S.op("act", lambda e: e.activation(out=gb[:], in_=PZ[1][:], func=AF.Silu), reads=[PZ[1]], writes=[gb])
                S.op("dve", lambda e: e.tensor_tensor(out=sg[:], in0=sg[:], in1=oml_bc[:], op=ALU.mult),
                     reads=[sg, oml_bc], writes=[sg])
                S.op("dve", lambda e: e.tensor_tensor(out=sg[:], in0=sg[:], in1=lb_bc[:], op=ALU.add),
                     reads=[sg, lb_bc], writes=[sg])
                S.op("act", lambda e: e.activation(out=gg[:], in_=sg[:], func=AF.Ln), reads=[sg], writes=[gg])
                S.op("pool", lambda e: e.tensor_scalar(out=omf[:], in0=sg[:], scalar1=-1.0, scalar2=1.0,
                                                       op0=ALU.mult, op1=ALU.add), reads=[sg], writes=[omf])
                S.op("pe", lambda e: e.matmul(PG[:], lhsT=tri2[:], rhs=gg[:], start=True, stop=True),
                     reads=[tri2, gg], writes=[PG])
                S.op("pe", lambda e: e.matmul(PGL[:], lhsT=blk2[:], rhs=gg[:], start=True, stop=True),
                     reads=[blk2, gg], writes=[PGL])
                S.op("act", lambda e: e.activation(out=Gs[:], in_=PG[:], func=AF.Copy), reads=[PG], writes=[Gs])
                S.op("dve", lambda e: e.tensor_tensor(out=dlt[:], in0=PGL[:], in1=Gs[:], op=ALU.subtract),
                     reads=[PGL, Gs], writes=[dlt])
                S.op("act", lambda e: e.activation(out=dlt[:], in_=dlt[:], func=AF.Exp), reads=[dlt], writes=[dlt])
                S.op("dve", lambda e: e.tensor_tensor(out=kdec[:], in0=omf[:], in1=dlt[:], op=ALU.mult),
                     reads=[omf, dlt], writes=[kdec])
                S.op("act", lambda e: e.activation(out=eG[:], in_=Gs[:], func=AF.Exp), reads=[Gs], writes=[eG])
                S.op("act", lambda e: e.activation(out=enG[:], in_=Gs[:], func=AF.Exp, scale=-1.0), reads=[Gs], writes=[enG])
                S.op("dve", lambda e: e.tensor_tensor(out=qg[:], in0=qs[:], in1=eG[:], op=ALU.mult), reads=[qs, eG], writes=[qg])
                S.op("pool", lambda e: e.tensor_tensor(out=kg[:], in0=omf[:], in1=enG[:], op=ALU.mult),
                     reads=[omf, enG], writes=[kg])
                for hd in range(4):
                    S.op("pe", lambda e, hd=hd: e.matmul(PU[0][:, hd * 128:(hd + 1) * 128],
                                                         lhsT=kdec[0:64, hd * 128:(hd + 1) * 128],
                                                         rhs=vb[0:64, hd * 128:(hd + 1) * 128], start=True, stop=True),
                         reads=[kdec, vb], writes=[PU[0]])
                for hd in range(4):
                    S.op("pe", lambda e, hd=hd: e.matmul(PD[:, 2 * hd:2 * hd + 2], lhsT=gg[0:64, hd * 128:(hd + 1) * 128],
                                                         rhs=ones_f[0:64, :], start=True, stop=True),
                         reads=[gg, ones_f], writes=[PD])
                S.op("act", lambda e: e.activation(out=Dc0[:], in_=PD[:, 0:8], func=AF.Exp), reads=[PD], writes=[Dc0])
                for hd in range(4):
                    S.op("dve", lambda e, hd=hd, i=i: e.scalar_tensor_tensor(
                        out=S1b[:, hd, :], in0=Sown[:, i, hd * 128:(hd + 1) * 128], scalar=Dc0[:, 2 * hd:2 * hd + 1],
                        in1=PU[0][:, hd * 128:(hd + 1) * 128], op0=ALU.mult, op1=ALU.add),
                        reads=[Sown, Dc0, PU[0]], writes=[S1b])
                transpose4(lambda hd: qg[:, hd * 128:(hd + 1) * 128], [qg])
                transpose4(lambda hd: kg[:, hd * 128:(hd + 1) * 128], [kg], col0=512)
                ptq = lambda: PT[:, 0:512].rearrange("p (h t) -> p h t", h=4)
                S.op("act", lambda e: e.activation(out=qgT[:], in_=ptq(), func=AF.Copy), reads=[PT], writes=[qgT])
                S.op("dve", lambda e: e.tensor_copy(out=qgT0[:, :, 0:64], in_=ptq()[:, :, 0:64]), reads=[PT], writes=[qgT0])
                S.op("dve", lambda e: e.tensor_copy(out=qgT1[:, :, 64:128], in_=ptq()[:, :, 64:128]), reads=[PT], writes=[qgT1])
                S.op("act", lambda e: e.activation(out=kgT[:], in_=PT[:, 512:1024].rearrange("p (h t) -> p h t", h=4),
                                                   func=AF.Copy), reads=[PT], writes=[kgT])
                for hd in range(4):
                    S.op("pe", lambda e, hd=hd: e.matmul(PU[1][:, hd * 128:(hd + 1) * 128], lhsT=kgT[:, hd, :], rhs=qgT[:, hd, :],
                                                         start=True, stop=True), reads=[kgT, qgT], writes=[PU[1]])
                S.op("dve", lambda e: e.tensor_tensor(out=ATb[:], in0=PU[1][:].rearrange("p (h t) -> p h t", h=4),
                                                      in1=tri2[:].unsqueeze(1).to_broadcast([128, 4, 128]), op=ALU.mult),
                     reads=[PU[1], tri2], writes=[ATb])
                for hd in range(4):
                    cs_ = slice(hd * 128, (hd + 1) * 128)
                    S.op("pe", lambda e, hd=hd, cs_=cs_: e.matmul(PG[:, cs_], lhsT=ATb[:, hd, :], rhs=vb[:, cs_],
                                                                  start=True, stop=False), reads=[ATb, vb], writes=[PG])
                    S.op("pe", lambda e, hd=hd, cs_=cs_, i=i: e.matmul(PG[:, cs_], lhsT=qgT0[:, hd, :], rhs=Sown[:, i, cs_],
                                                                       start=False, stop=False),
                         reads=[qgT0, Sown], writes=[PG], pe_acc=True)
                    S.op("pe", lambda e, hd=hd, cs_=cs_: e.matmul(PG[:, cs_], lhsT=qgT1[:, hd, :], rhs=S1b[:, hd, :],
                                                                  start=False, stop=True),
                         reads=[qgT1, S1b], writes=[PG], pe_acc=True)
                for hd in range(4):
                    S.op("act", lambda e, hd=hd: e.activation(out=jk5[:], in_=PG[:, hd * 128:(hd + 1) * 128], func=AF.Square,
                                                              accum_out=ss[:, hd:hd + 1]), reads=[PG], writes=[jk5, ss])
                S.op("dve", lambda e: e.tensor_scalar(out=rstd[:], in0=ss[:], scalar1=1.0 / 128.0, scalar2=RMS_EPS,
                                                      op0=ALU.mult, op1=ALU.add), reads=[ss], writes=[rstd])
                S.op("act", lambda e: e.activation(out=rstd[:], in_=rstd[:], func=AF.Sqrt), reads=[rstd], writes=[rstd])
                S.op("dve", lambda e: e.reciprocal(out=rstd[:], in_=rstd[:]), reads=[rstd], writes=[rstd])
                S.op("dve", lambda e: e.tensor_tensor(out=t1[:].rearrange("p (h d) -> p h d", h=4),
                                                      in0=PG[:].rearrange("p (h d) -> p h d", h=4),
                                                      in1=rstd[:, 0:4].unsqueeze(2).to_broadcast([128, 4, 128]), op=ALU.mult),
                     reads=[PG, rstd], writes=[t1])
                S.op("pool", lambda e: e.tensor_tensor(out=t1[:], in0=t1[:], in1=ng_bc[:].rearrange("p h d -> p (h d)"),
                                                       op=ALU.mult), reads=[t1, ng_bc], writes=[t1])
                S.op("pool", lambda e: e.tensor_tensor(out=cbb[:], in0=t1[:], in1=gb[:], op=ALU.mult),
                     reads=[t1, gb], writes=[cbb])
                transpose4(lambda hd: cbb[:, hd * 128:(hd + 1) * 128], [cbb])
                S.op("act", lambda e, i=i: e.activation(out=catT[:, 8:12, i * 128:(i + 1) * 128],
                                                        in_=PT[:, 0:512].rearrange("p (h t) -> p h t", h=4), func=AF.Copy),
                     reads=[PT], writes=[catT])
            S.barrier()
            S.emit()

        with ExitStack() as p6:
            Wm_sb = sb(p6, "Wm_sb", [128, KC, 1024], BF16)
            xt = [sb(p6, "xt6_%d" % j, [128, KC, 128], BF16) for j in range(2)]
            mq_b = sb(p6, "mq_b", [128, 512], BF16)
            gm = sb(p6, "gm", [128, 512], F32)
            mqT = sb(p6, "mqT", [128, 4, 128], BF16)
            Em = [sb(p6, "Em%d" % j, [128, 4, 128], BF16) for j in range(2)]
            rdm = sb(p6, "rdm", [128, 4], F32)
            tm = sb(p6, "tm", [128, 512], F32)
            cmb = sb(p6, "cmb", [128, 512], BF16)
            POm = [PU[0], PU[1]]
            load_wres(Wm_sb, Wm, 1024)
            for i in range(NO):
                x = xt[i % 2]
                load_xt(x, i)
                mm_x(PZ[0], x, Wm_sb, 0, 512)
                S.op("act", lambda e: e.activation(out=mq_b[:], in_=PZ[0][:], func=AF.Copy, scale=QS), reads=[PZ[0]], writes=[mq_b])
                mm_x(PZ[1], x, Wm_sb, 512, 512)
                S.op("act", lambda e: e.activation(out=gm[:], in_=PZ[1][:], func=AF.Silu), reads=[PZ[1]], writes=[gm])
                transpose4(lambda hd: mq_b[:, hd * 128:(hd + 1) * 128], [mq_b])
                S.op("act", lambda e: e.activation(out=mqT[:], in_=PT[:, 0:512].rearrange("p (h t) -> p h t", h=4), func=AF.Copy),
                     reads=[PT], writes=[mqT])
                for nt in range(2):
                    ps_ = PA[nt]
                    for hd in range(4):
                        S.op("pe", lambda e, hd=hd, nt=nt, ps_=ps_: e.matmul(
                            ps_[:, hd * 128:(hd + 1) * 128], lhsT=mkT[:, hd, nt * 128:(nt + 1) * 128], rhs=mqT[:, hd, :],
                            start=True, stop=True), reads=[mkT, mqT], writes=[ps_])
                    S.op("act", lambda e, nt=nt, ps_=ps_: e.activation(out=Em[nt][:].rearrange("p h t -> p (h t)"), in_=ps_[:],
                                                                       func=AF.Exp), reads=[ps_], writes=[Em[nt]])
                for nt in range(2):
                    for hd in range(4):
                        bank = hd // 3
                        col = (hd % 3) * 130
                        st = (nt == 0 and hd % 3 == 0)
                        S.op("pe", lambda e, hd=hd, nt=nt, bank=bank, col=col, st=st: e.matmul(
                            POm[bank][:, col:col + 130], lhsT=Em[nt][:, hd, :], rhs=mv[:, nt, hd, :],
                            start=st, stop=(nt == 1), skip_group_check=True),
                            reads=[Em[nt], mv], writes=[POm[bank]], pe_acc=(not st))
                for bank in range(2):
                    nh = 3 if bank == 0 else 1
                    S.op("dve", lambda e, bank=bank, nh=nh: e.reciprocal(
                        out=rdm[:, bank * 3:bank * 3 + nh],
                        in_=POm[bank][:, 0:nh * 130].rearrange("p (h c) -> p h c", c=130)[:, :, 128]),
                        reads=[POm[bank]], writes=[rdm])
                    S.op("dve", lambda e, bank=bank, nh=nh: e.tensor_tensor(
                        out=tm[:, bank * 384:bank * 384 + nh * 128].rearrange("p (h d) -> p h d", h=nh),
                        in0=POm[bank][:, 0:nh * 130].rearrange("p (h c) -> p h c", c=130)[:, :, 0:128],
                        in1=rdm[:, bank * 3:bank * 3 + nh].unsqueeze(2).to_broadcast([128, nh, 128]), op=ALU.mult),
                        reads=[POm[bank], rdm], writes=[tm])
                S.op("pool", lambda e: e.tensor_tensor(out=cmb[:], in0=tm[:], in1=gm[:], op=ALU.mult), reads=[tm, gm], writes=[cmb])
                transpose4(lambda hd: cmb[:, hd * 128:(hd + 1) * 128], [cmb])
                S.op("act", lambda e, i=i: e.activation(out=catT[:, 12:16, i * 128:(i + 1) * 128],
                                                        in_=PT[:, 0:512].rearrange("p (h t) -> p h t", h=4), func=AF.Copy),
                     reads=[PT], writes=[catT])
            S.barrier()
            S.emit()

        print("ninstr at S start", S.ninstr, flush=True)
        es_S = ExitStack()
        cat_s = sb(es_S, "cat_s", [8, D], BF16)
        aq_sb = sb(es_S, "aq_sb", [8, 1024], BF16)
        ak_sb = sb(es_S, "ak_sb", [8, 1024], BF16)
        av_sb = sb(es_S, "av_sb", [8, 1024], BF16)
        ga_s = sb(es_S, "ga_s", [8, 1024], F32)
        iq_sb = sb(es_S, "iq_sb", [8, 16, 64], BF16)
        ik_sb = sb(es_S, "ik_sb", [8, 64], BF16)
        iw_s = sb(es_S, "iw_s", [8, 16], F32)
        ropeS_sb = sb(es_S, "ropeS_sb", [8, 160], F32)
        tri8 = sb(es_S, "tri8", [8, 8], F32)
        one8 = sb(es_S, "one8", [8, 8], F32)
        S.dma("sp", lambda e: e.dma_start(out=ropeS_sb[:], in_=ropeS[:, :]), writes=[ropeS_sb])
        S.dma("sp", lambda e: e.dma_start(out=tri8[:], in_=c_tri2[0:8, 0:8]), writes=[tri8])
        S.op("pool", lambda e: e.memset(one8[:], 1.0), writes=[one8])
        with ExitStack() as s1:
            xsT_b = sb(s1, "xsT_b", [128, KC, 8], BF16)
            Wblk = [sb(s1, "Wblk%d" % j, [128, KC, 512], BF16) for j in range(2)]
            zq = sb(s1, "zq", [8, 1024], F32)
            zk = sb(s1, "zk", [8, 1024], F32)
            zv = sb(s1, "zv", [8, 1024], F32)
            ziq = sb(s1, "ziq", [8, 1024], F32)
            zik = sb(s1, "zik", [8, 64], F32)
            zb = sb(s1, "zb", [8, 2048], F32)
            zm = sb(s1, "zm", [8, 1024], F32)
            tAs = sb(s1, "tAs", [8, 16, 32], F32)
            tBs = sb(s1, "tBs", [8, 16, 32], F32)
            S.dma("pool", lambda e: e.dma_start(out=xsT_b[:], in_=xsT.rearrange("(k p) t -> p k t", p=128)), writes=[xsT_b])
            plan = [(Wq, 0, 1024, zq, 0), (Wk, 0, 1024, zk, 0), (Wv, 0, 1024, zv, 0), (Wg, 0, 1024, ga_s, 0),
                    (Wi, 0, 1024, ziq, 0), (Wi, 1024, 16, iw_s, 0), (W3, 1024, 64, zik, 0), (Wb, 0, 2048, zb, 0),
                    (Wm, 0, 1024, zm, 0)]
            bj = 0
            for (src, c0, ncols, dst, d0) in plan:
                for o in range(0, ncols, 512):
                    n_ = min(512, ncols - o)
                    wb_, pz = Wblk[bj % 2], PZ[bj % 2]
                    bj += 1
                    S.dma("pool", [lambda e, k=k, wb_=wb_, src=src, c0=c0, o=o, n_=n_: e.dma_start(
                        out=wb_[:, k, 0:n_], in_=src[k * 128:(k + 1) * 128, c0 + o:c0 + o + n_]) for k in range(KC)],
                        writes=[wb_])
                    for k in range(KC):
                        S.op("pe", lambda e, k=k, wb_=wb_, pz=pz, n_=n_: e.matmul(
                            pz[0:8, 0:n_], lhsT=xsT_b[:, k, :], rhs=wb_[:, k, 0:n_], start=(k == 0), stop=(k == KC - 1)),
                            reads=[xsT_b, wb_], writes=[pz], pe_acc=(k > 0))
                    S.op("act", lambda e, pz=pz, dst=dst, d0=d0, o=o, n_=n_: e.activation(
                        out=dst[:, d0 + o:d0 + o + n_], in_=pz[0:8, 0:n_], func=AF.Copy), reads=[pz], writes=[dst])
            CCk, SSk = ropeS_sb[:, 0:32], ropeS_sb[:, 32:64]
            CCi, SSi = ropeS_sb[:, 64:80], ropeS_sb[:, 80:96]
            CCq, SSq = ropeS_sb[:, 96:128], ropeS_sb[:, 128:160]
            v3 = lambda t, nh: V3(lambda lo_, hi_: t[:].rearrange("p (h d) -> p h d", h=nh)[:, :, lo_:hi_], [t])
            rope(v3(zk, 8), v3(zk, 8), 8, 16, CCk, SSk, tAs, tBs, np_=8)
            S.dma("sp", lambda e: e.dma_start(out=o_ks[:, :], in_=zk[:]), reads=[zk])
            S.dma("sp", lambda e: e.dma_start(out=o_vs[:, :], in_=zv[:]), reads=[zv])
            S.op("dve", lambda e: e.tensor_copy(out=ak_sb[:], in_=zk[:]), reads=[zk], writes=[ak_sb])
            S.op("dve", lambda e: e.tensor_copy(out=av_sb[:], in_=zv[:]), reads=[zv], writes=[av_sb])
            rope(v3(zik, 1), v3(zik, 1), 1, 8, CCi, SSi, tAs, tBs, np_=8)
            S.dma("sp", lambda e: e.dma_start(out=o_iks[:, :], in_=zik[:]), reads=[zik])
            S.op("dve", lambda e: e.tensor_copy(out=ik_sb[:], in_=zik[:]), reads=[zik], writes=[ik_sb])
            S.op("act", lambda e: e.activation(out=aq_sb[:], in_=zq[:], func=AF.Copy, scale=QS), reads=[zq], writes=[aq_sb])
            rope(v3(zq, 8), v3(aq_sb, 8), 8, 16, CCq, SSq, tAs, tBs, np_=8)
            S.op("act", lambda e: e.activation(out=iq_sb[:].rearrange("p h d -> p (h d)"), in_=ziq[:], func=AF.Copy),
                 reads=[ziq], writes=[iq_sb])
            rope(v3(ziq, 16), V3(lambda lo_, hi_: iq_sb[:, :, lo_:hi_], [iq_sb]), 16, 8, CCi, SSi, tAs, tBs, np_=8)
            S.op("act", lambda e: e.activation(out=ga_s[:], in_=ga_s[:], func=AF.Silu), reads=[ga_s], writes=[ga_s])

            S0f = sb(s1, "S0f", [128, 4, 128], F32)
            S0b = sb(s1, "S0b", [128, 4, 128], BF16)
            S1f = sb(s1, "S1f", [128, 4, 128], F32)
            hs = [sb(s1, "hs%d" % j, [8, 512], F32) for j in range(8)]
            qs_, sg_, gg_, omf_, Gs_, dl_, eG_, enG_ = hs
            gb_ = sb(s1, "gb_s", [8, 512], F32)
            t1_ = sb(s1, "t1_s", [8, 512], F32)
            kdec_ = sb(s1, "kdec_s", [8, 512], BF16)
            vb_ = sb(s1, "vb_s", [8, 512], BF16)
            kg_ = sb(s1, "kg_s", [8, 512], BF16)
            qg_ = sb(s1, "qg_s", [8, 512], BF16)
            qgT_ = sb(s1, "qgT_s", [128, 4, 8], BF16)
            kgT_ = sb(s1, "kgT_s", [128, 4, 8], BF16)
            ATb_ = sb(s1, "ATb_s", [8, 4, 8], BF16)
            Dcs = sb(s1, "Dcs", [128, 8], F32)
            ss_ = sb(s1, "ss_s", [8, 4], F32)
            rstd_ = sb(s1, "rstd_s", [8, 4], F32)
            jk_ = sb(s1, "jk_s", [8, 128], F32)
            ngs = sb(s1, "ngs", [8, 4, 128], F32)
            lb8 = sb(s1, "lb8", [8, 2, 512], F32)
            S.dma("sp", lambda e: e.dma_start(out=S0f[:], in_=st_h.rearrange("h k v -> k h v")), writes=[S0f])
            S.dma("sp", [lambda e, hd=hd: e.dma_start(out=ngs[:, hd, :], in_=ng[0:1, :].broadcast_to([8, 128]))
                         for hd in range(4)], writes=[ngs])
            S.op("dve", lambda e: e.tensor_copy(out=S0b[:], in_=S0f[:]), reads=[S0f], writes=[S0b])
            S.op("act", lambda e: e.activation(out=qs_[:], in_=zb[:, 0:512], func=AF.Silu), reads=[zb], writes=[qs_])
            S.op("act", lambda e: e.activation(out=sg_[:], in_=zb[:, 512:1024], func=AF.Sigmoid), reads=[zb], writes=[sg_])
            S.op("act", lambda e: e.activation(out=vb_[:], in_=zb[:, 1024:1536], func=AF.Copy), reads=[zb], writes=[vb_])
            S.op("act", lambda e: e.activation(out=gb_[:], in_=zb[:, 1536:2048], func=AF.Silu), reads=[zb], writes=[gb_])
            S.op("dve", lambda e: e.tensor_tensor(out=sg_[:], in0=sg_[:], in1=oml_bc[0:8, :], op=ALU.mult),
                 reads=[sg_, oml_bc], writes=[sg_])
            S.op("dve", lambda e: e.tensor_tensor(out=sg_[:], in0=sg_[:], in1=lb_bc[0:8, :], op=ALU.add),
                 reads=[sg_, lb_bc], writes=[sg_])
            S.op("act", lambda e: e.activation(out=gg_[:], in_=sg_[:], func=AF.Ln), reads=[sg_], writes=[gg_])
            S.op("dve", lambda e: e.tensor_scalar(out=omf_[:], in0=sg_[:], scalar1=-1.0, scalar2=1.0, op0=ALU.mult, op1=ALU.add),
                 reads=[sg_], writes=[omf_])
            S.op("pe", lambda e: e.matmul(PG[0:8, :], lhsT=tri8[:], rhs=gg_[:], start=True, stop=True),
                 reads=[tri8, gg_], writes=[PG])
            S.op("pe", lambda e: e.matmul(PGL[0:8, :], lhsT=one8[:], rhs=gg_[:], start=True, stop=True),
                 reads=[one8, gg_], writes=[PGL])
            S.op("act", lambda e: e.activation(out=Gs_[:], in_=PG[0:8, :], func=AF.Copy), reads=[PG], writes=[Gs_])
            S.op("dve", lambda e: e.tensor_tensor(out=dl_[:], in0=PGL[0:8, :], in1=Gs_[:], op=ALU.subtract),
                 reads=[PGL, Gs_], writes=[dl_])
            S.op("act", lambda e: e.activation(out=dl_[:], in_=dl_[:], func=AF.Exp), reads=[dl_], writes=[dl_])
            S.op("dve", lambda e: e.tensor_tensor(out=kdec_[:], in0=omf_[:], in1=dl_[:], op=ALU.mult),
                 reads=[omf_, dl_], writes=[kdec_])
            S.op("act", lambda e: e.activation(out=eG_[:], in_=Gs_[:], func=AF.Exp), reads=[Gs_], writes=[eG_])
            S.op("act", lambda e: e.activation(out=enG_[:], in_=Gs_[:], func=AF.Exp, scale=-1.0), reads=[Gs_], writes=[enG_])
            S.op("dve", lambda e: e.tensor_tensor(out=qg_[:], in0=qs_[:], in1=eG_[:], op=ALU.mult), reads=[qs_, eG_], writes=[qg_])
            S.op("dve", lambda e: e.tensor_tensor(out=kg_[:], in0=omf_[:], in1=enG_[:], op=ALU.mult),
                 reads=[omf_, enG_], writes=[kg_])
            for hd in range(4):
                cs_ = slice(hd * 128, (hd + 1) * 128)
                S.op("pe", lambda e, cs_=cs_: e.matmul(PU[0][:, cs_], lhsT=kdec_[:, cs_], rhs=vb_[:, cs_], start=True, stop=True),
                     reads=[kdec_, vb_], writes=[PU[0]])
            for hd in range(4):
                S.op("pe", lambda e, hd=hd: e.matmul(PD[:, 2 * hd:2 * hd + 2], lhsT=gg_[:, hd * 128:(hd + 1) * 128],
                                                     rhs=ones_f[0:8, :], start=True, stop=True),
                     reads=[gg_, ones_f], writes=[PD])
            S.op("act", lambda e: e.activation(out=Dcs[:], in_=PD[:, 0:8], func=AF.Exp), reads=[PD], writes=[Dcs])
            for hd in range(4):
                S.op("dve", lambda e, hd=hd: e.scalar_tensor_tensor(
                    out=S1f[:, hd, :], in0=S0f[:, hd, :], scalar=Dcs[:, 2 * hd:2 * hd + 1],
                    in1=PU[0][:, hd * 128:(hd + 1) * 128], op0=ALU.mult, op1=ALU.add),
                    reads=[S0f, Dcs, PU[0]], writes=[S1f])
            S.dma("sp", lambda e: e.dma_start(out=o_hgs[:, :, :], in_=S1f[:]), reads=[S1f])
            for hd in range(4):
                S.op("pe", lambda e, hd=hd: e.transpose(out=PT[:, hd * 8:(hd + 1) * 8], in_=qg_[:, hd * 128:(hd + 1) * 128],
                                                        identity=ident_b[0:8, 0:8]), reads=[qg_, ident_b], writes=[PT])
                S.op("pe", lambda e, hd=hd: e.transpose(out=PT[:, 32 + hd * 8:32 + (hd + 1) * 8], in_=kg_[:, hd * 128:(hd + 1) * 128],
                                                        identity=ident_b[0:8, 0:8]), reads=[kg_, ident_b], writes=[PT])
            S.op("act", lambda e: e.activation(out=qgT_[:].rearrange("p h t -> p (h t)"), in_=PT[:, 0:32], func=AF.Copy),
                 reads=[PT], writes=[qgT_])
            S.op("act", lambda e: e.activation(out=kgT_[:].rearrange("p h t -> p (h t)"), in_=PT[:, 32:64], func=AF.Copy),
                 reads=[PT], writes=[kgT_])
            for hd in range(4):
                S.op("pe", lambda e, hd=hd: e.matmul(PU[1][0:8, hd * 8:(hd + 1) * 8], lhsT=kgT_[:, hd, :], rhs=qgT_[:, hd, :],
                                                     start=True, stop=True), reads=[kgT_, qgT_], writes=[PU[1]])
            S.op("dve", lambda e: e.tensor_tensor(out=ATb_[:], in0=PU[1][0:8, 0:32].rearrange("p (h t) -> p h t", h=4),
                                                  in1=tri8[:].unsqueeze(1).to_broadcast([8, 4, 8]), op=ALU.mult),
                 reads=[PU[1], tri8], writes=[ATb_])
            for hd in range(4):
                cs_ = slice(hd * 128, (hd + 1) * 128)
                S.op("pe", lambda e, hd=hd, cs_=cs_: e.matmul(PG[0:8, cs_], lhsT=ATb_[:, hd, :], rhs=vb_[:, cs_],
                                                              start=True, stop=False), reads=[ATb_, vb_], writes=[PG])
                S.op("pe", lambda e, hd=hd, cs_=cs_: e.matmul(PG[0:8, cs_], lhsT=qgT_[:, hd, :], rhs=S0b[:, hd, :],
                                                              start=False, stop=True), reads=[qgT_, S0b], writes=[PG], pe_acc=True)
            for hd in range(4):
                S.op("act", lambda e, hd=hd: e.activation(out=jk_[:], in_=PG[0:8, hd * 128:(hd + 1) * 128], func=AF.Square,
                                                          accum_out=ss_[:, hd:hd + 1]), reads=[PG], writes=[jk_, ss_])
            S.op("dve", lambda e: e.tensor_scalar(out=rstd_[:], in0=ss_[:], scalar1=1.0 / 128.0, scalar2=RMS_EPS,
                                                  op0=ALU.mult, op1=ALU.add), reads=[ss_], writes=[rstd_])
            S.op("act", lambda e: e.activation(out=rstd_[:], in_=rstd_[:], func=AF.Sqrt), reads=[rstd_], writes=[rstd_])
            S.op("dve", lambda e: e.reciprocal(out=rstd_[:], in_=rstd_[:]), reads=[rstd_], writes=[rstd_])
            S.op("dve", lambda e: e.tensor_tensor(out=t1_[:].rearrange("p (h d) -> p h d", h=4),
                                                  in0=PG[0:8, :].rearrange("p (h d) -> p h d", h=4),
                                                  in1=rstd_[:, 0:4].unsqueeze(2).to_broadcast([8, 4, 128]), op=ALU.mult),
                 reads=[PG, rstd_], writes=[t1_])
            S.op("dve", lambda e: e.tensor_tensor(out=t1_[:], in0=t1_[:], in1=ngs[:].rearrange("p h d -> p (h d)"), op=ALU.mult),
                 reads=[t1_, ngs], writes=[t1_])
            S.op("dve", lambda e: e.tensor_tensor(out=cat_s[:, 1024:1536], in0=t1_[:], in1=gb_[:], op=ALU.mult),
                 reads=[t1_, gb_], writes=[cat_s])

            cmk = sb(s1, "cmk", [128, 2, 512], BF16)
            mvs = sb(s1, "mvs", [128, 2, 4, 130], BF16)
            mkTs = sb(s1, "mkTs", [128, 4, 256], BF16)
            mq_s = sb(s1, "mq_s", [8, 512], BF16)
            gm_s = sb(s1, "gm_s", [8, 512], F32)
            mqTs = sb(s1, "mqTs", [128, 4, 8], BF16)
            Ems = [sb(s1, "Ems%d" % j, [128, 4, 8], BF16) for j in range(2)]
            rdms = sb(s1, "rdms", [8, 4], F32)
            tms = sb(s1, "tms", [8, 512], F32)
            S.dma("pool", [lambda e, nt=nt: e.dma_start(out=cmk[:, nt, :], in_=cmk_d[nt * 128:(nt + 1) * 128, :])
                           for nt in range(2)], writes=[cmk])
            S.op("pool", lambda e: e.memset(mvs[:, :, :, 128:129], 1.0), writes=[mvs])
            S.dma("pool", [lambda e, nt=nt: e.dma_start(out=mvs[:, nt, :, 0:128],
                                                        in_=cmv_d[nt * 128:(nt + 1) * 128, :].rearrange("p (h d) -> p h d", h=4))
                           for nt in range(2)], writes=[mvs])
            for nt in range(2):
                transpose4(lambda hd, nt=nt: cmk[:, nt, hd * 128:(hd + 1) * 128], [cmk])
                S.op("act", lambda e, nt=nt: e.activation(out=mkTs[:, :, nt * 128:(nt + 1) * 128],
                                                          in_=PT[:, 0:512].rearrange("p (h t) -> p h t", h=4), func=AF.Copy),
                     reads=[PT], writes=[mkTs])
            S.op("act", lambda e: e.activation(out=mq_s[:], in_=zm[:, 0:512], func=AF.Copy, scale=QS), reads=[zm], writes=[mq_s])
            S.op("act", lambda e: e.activation(out=gm_s[:], in_=zm[:, 512:1024], func=AF.Silu), reads=[zm], writes=[gm_s])
            for hd in range(4):
                S.op("pe", lambda e, hd=hd: e.transpose(out=PT[:, hd * 8:(hd + 1) * 8], in_=mq_s[:, hd * 128:(hd + 1) * 128],
                                                        identity=ident_b[0:8, 0:8]), reads=[mq_s, ident_b], writes=[PT])
            S.op("act", lambda e: e.activation(out=mqTs[:].rearrange("p h t -> p (h t)"), in_=PT[:, 0:32], func=AF.Copy),
                 reads=[PT], writes=[mqTs])
            Es_ = sb(s1, "Es_s", [8, 2, 512], BF16)
            for nt in range(2):
                ps_ = PA[nt]
                for hd in range(4):
                    S.op("pe", lambda e, hd=hd, nt=nt, ps_=ps_: e.matmul(
                        ps_[0:8, hd * 128:(hd + 1) * 128], lhsT=mqTs[:, hd, :], rhs=mkTs[:, hd, nt * 128:(nt + 1) * 128],
                        start=True, stop=True), reads=[mkTs, mqTs], writes=[ps_])
                S.op("act", lambda e, nt=nt, ps_=ps_: e.activation(out=Es_[:, nt, :], in_=ps_[0:8, :], func=AF.Exp),
                     reads=[ps_], writes=[Es_])
                for hd in range(4):
                    S.op("pe", lambda e, hd=hd, nt=nt: e.transpose(out=PT[:, hd * 8:(hd + 1) * 8],
                                                                   in_=Es_[:, nt, hd * 128:(hd + 1) * 128],
                                                                   identity=ident_b[0:8, 0:8]), reads=[Es_, ident_b], writes=[PT])
                S.op("act", lambda e, nt=nt: e.activation(out=Ems[nt][:].rearrange("p h t -> p (h t)"), in_=PT[:, 0:32],
                                                          func=AF.Copy), reads=[PT], writes=[Ems[nt]])
            POs = [PU[0], PU[1]]
            for nt in range(2):
                for hd in range(4):
                    bank, col = hd // 3, (hd % 3) * 130
                    st = (nt == 0 and hd % 3 == 0)
                    S.op("pe", lambda e, hd=hd, nt=nt, bank=bank, col=col, st=st: e.matmul(
                        POs[bank][0:8, col:col + 130], lhsT=Ems[nt][:, hd, :], rhs=mvs[:, nt, hd, :],
                        start=st, stop=(nt == 1), skip_group_check=True),
                        reads=[Ems[nt], mvs], writes=[POs[bank]], pe_acc=(not st))
            for bank in range(2):
                nh = 3 if bank == 0 else 1
                S.op("dve", lambda e, bank=bank, nh=nh: e.reciprocal(
                    out=rdms[:, bank * 3:bank * 3 + nh],
                    in_=POs[bank][0:8, 0:nh * 130].rearrange("p (h c) -> p h c", c=130)[:, :, 128]),
                    reads=[POs[bank]], writes=[rdms])
                S.op("dve", lambda e, bank=bank, nh=nh: e.tensor_tensor(
                    out=tms[:, bank * 384:bank * 384 + nh * 128].rearrange("p (h d) -> p h d", h=nh),
                    in0=POs[bank][0:8, 0:nh * 130].rearrange("p (h c) -> p h c", c=130)[:, :, 0:128],
                    in1=rdms[:, bank * 3:bank * 3 + nh].unsqueeze(2).to_broadcast([8, nh, 128]), op=ALU.mult),
                    reads=[POs[bank], rdms], writes=[tms])
            S.op("dve", lambda e: e.tensor_tensor(out=cat_s[:, 1536:2048], in0=tms[:], in1=gm_s[:], op=ALU.mult),
                 reads=[tms, gm_s], writes=[cat_s])
            S.barrier()
            S.emit()
        print("ninstr at S1 end", S.ninstr, flush=True)
        KB2 = 30
        with ExitStack() as s3:
            ci = {}
            for nm, shp, dt_ in (("c_sel8", [8, 128], F32), ("c_hm", [128, 16], F32), ("c_bq8", [128, 8], F32),
                                 ("c_bq16", [128, 8], F32), ("c_BQ1", [128, 128], F32), ("c_BQm", [128, 128], F32),
                                 ("c_LT", [128, 128], F32), ("c_negm", [128, 8], F32), ("c_aiota", [128, 1024], F32),
                                 ("c_iota512", [128, 512], F32), ("c_esel", [8, 1024], F32), ("c_eq", [8, 64], F32),
                                 ("c_bd8", [8, 1024], F32), ("c_pow2", [128, 2 * KB2], F32), ("c_i32", [128, 768], I32)):
                t_ = sb(s3, "k" + nm, shp, dt_)
                S.dma("sp", lambda e, t_=t_, nm=nm: e.dma_start(out=t_[:], in_=cin[nm][:, :]), writes=[t_])
                ci[nm] = t_
            esel_b = sb(s3, "esel_b", [8, 8, 128], BF16)
            eq_b = sb(s3, "eq_b", [8, 8, 8], BF16)
            ones_b = sb(s3, "ones_b", [128, 2], BF16)
            S.op("dve", lambda e: e.tensor_copy(out=esel_b[:].rearrange("p a b -> p (a b)"), in_=ci["c_esel"][:]),
                 reads=[ci["c_esel"]], writes=[esel_b])
            S.op("dve", lambda e: e.tensor_copy(out=eq_b[:].rearrange("p a b -> p (a b)"), in_=ci["c_eq"][:]),
                 reads=[ci["c_eq"]], writes=[eq_b])
            S.op("pool", lambda e: e.memset(ones_b[:], 1.0), writes=[ones_b])
            I_s = sb(s3, "I_s", [128, 1032], F32)
            junk_s = sb(s3, "junk_s", [128, 1032], BF16)
            iqT_s = sb(s3, "iqT_s", [64, 128], BF16)
            ikTn = sb(s3, "ikTn", [64, 8], BF16)
            wtmp = sb(s3, "wtmp", [128, 16], F32)
            wcol = sb(s3, "wcol", [128, 1], F32)
            wdb = sb(s3, "wdb", [128, 8], F32)
            Wd_all = sb(s3, "Wd_all", [128, 16, 128], BF16)
            Tbs = [sb(s3, "Tbs%d" % j, [128, 512], BF16) for j in range(2)]
            Tn = sb(s3, "Tn", [128, 8], BF16)
            pt_i = sb(s3, "pt_i", [128, 1], I32)
            ptrow_i = sb(s3, "ptrow_i", [128, 128], I32)
            ptrow_f = sb(s3, "ptrow_f", [128, 128], F32)
            st2 = sb(s3, "st2", [128, 4], F32)
            sm2 = sb(s3, "sm2", [128, 8], F32)
            lo2, hi2, rng2, mid2, wv2 = [sm2[:, j:j + 1] for j in range(5)]
            cnt2 = sb(s3, "cnt2", [128, 2], F32)
            spw2 = sb(s3, "spw2", [128, 2 * KB2], F32)
            off_s = sb(s3, "off_s", [128, 1], F32)
            seln = sb(s3, "seln", [128, 8], F32)
            selnT = sb(s3, "selnT", [8, 128], F32)
            idx_i = sb(s3, "idx_i", [128, 2, 8], I32)
            vmask = sb(s3, "vmask", [128, 2, 8], F32)
            S.dma("sp", lambda e: e.dma_start(out=pt_i[:], in_=ptab.rearrange("o p -> p o")), writes=[pt_i])
            S.dma("sp", lambda e: e.dma_start(out=ptrow_i[:], in_=ptab[0:1, :].broadcast_to([128, 128])), writes=[ptrow_i])
            S.op("pool", lambda e: e.memset(cnt2[:], 0.0), writes=[cnt2])
            for h in range(16):
                S.op("pe", lambda e, h=h: e.transpose(out=PT[0:64, h * 8:(h + 1) * 8], in_=iq_sb[:, h, :],
                                                      identity=ident_b[0:8, 0:8]), reads=[iq_sb, ident_b], writes=[PT])
            S.op("pe", lambda e: e.transpose(out=PT[0:64, 128:136], in_=ik_sb[:], identity=ident_b[0:8, 0:8]),
                 reads=[ik_sb, ident_b], writes=[PT])
            S.op("act", lambda e: e.activation(out=iqT_s[:], in_=PT[0:64, 0:128], func=AF.Copy), reads=[PT], writes=[iqT_s])
            S.op("act", lambda e: e.activation(out=ikTn[:], in_=PT[0:64, 128:136], func=AF.Copy), reads=[PT], writes=[ikTn])
            S.op("pe", lambda e: e.matmul(PD[:, 0:16], lhsT=ci["c_sel8"][:], rhs=iw_s[:], start=True, stop=True),
                 reads=[ci["c_sel8"], iw_s], writes=[PD])
            S.op("dve", lambda e: e.tensor_tensor(out=wtmp[:], in0=PD[:, 0:16], in1=ci["c_hm"][:], op=ALU.mult),
                 reads=[PD, ci["c_hm"]], writes=[wtmp])
            S.op("dve", lambda e: e.tensor_reduce(out=wcol[:], in_=wtmp[:], axis=AX.X, op=ALU.add), reads=[wtmp], writes=[wcol])
            S.op("dve", lambda e: e.tensor_scalar(out=wcol[:], in0=wcol[:], scalar1=1.0 / 32.0, scalar2=None, op0=ALU.mult),
                 reads=[wcol], writes=[wcol])
            S.op("dve", lambda e: e.tensor_scalar(out=wdb[:], in0=ci["c_bq8"][:], scalar1=wcol[:, 0:1], scalar2=None, op0=ALU.mult),
                 reads=[ci["c_bq8"], wcol], writes=[wdb])
            S.op("pool", lambda e: e.memset(Wd_all[:], 0.0), writes=[Wd_all])
            for seg in range(16):
                S.op("dve", lambda e, seg=seg: e.tensor_copy(
                    out=Wd_all[:, seg, :].rearrange("p (q s) -> p q s", s=16)[:, :, seg], in_=wdb[:]),
                    reads=[wdb], writes=[Wd_all])
            with ExitStack() as sA:
                ikT_s = sb(sA, "ikT_s", [64, PAST], BF16)
                with ExitStack() as sA2:
                    ikp = sb(sA2, "ikp", [128, 8192], F32)
                    S.dma("pool", lambda e: e.indirect_dma_start(
                        out=ikp[:], out_offset=None, in_=cidx[:, :],
                        in_offset=bass.IndirectOffsetOnAxis(ap=pt_i[:, 0:1], axis=0)), reads=[pt_i], writes=[ikp])
                    for r in range(128):
                        pz = PZ[(r // 4) % 2]
                        S.op("pe", lambda e, r=r, pz=pz: e.transpose(out=pz[0:64, (r % 4) * 128:(r % 4 + 1) * 128],
                                                                     in_=ikp[:, r * 64:(r + 1) * 64], identity=ident_f[:]),
                             reads=[ikp, ident_f], writes=[pz])
                        if r % 4 == 3:
                            eng_ = "act" if (r // 4) % 2 == 0 else "dve"
                            if eng_ == "act":
                                S.op("act", lambda e, r=r, pz=pz: e.activation(out=ikT_s[:, (r - 3) * 128:(r + 1) * 128],
                                                                               in_=pz[0:64, :], func=AF.Copy),
                                     reads=[pz], writes=[ikT_s])
                            else:
                                S.op("dve", lambda e, r=r, pz=pz: e.tensor_copy(out=ikT_s[:, (r - 3) * 128:(r + 1) * 128],
                                                                                in_=pz[0:64, :]), reads=[pz], writes=[ikT_s])
                    S.barrier()
                    S.emit()
                def mm1s(c):
                    pa = PA[c % 2]
                    S.op("pe", lambda e, c=c, pa=pa: e.matmul(pa[:], lhsT=iqT_s[:], rhs=ikT_s[:, c * 512:(c + 1) * 512],
                                                              start=True, stop=True), reads=[iqT_s, ikT_s], writes=[pa])
                mm1s(0)
                for c in range(32):
                    if c + 1 < 32:
                        mm1s(c + 1)
                    pa, tb = PA[c % 2], Tbs[c % 2]
                    seg, half = c // 2, c % 2
                    S.op("act", lambda e, pa=pa, tb=tb: e.activation(out=tb[:], in_=pa[:], func=AF.Relu), reads=[pa], writes=[tb])
                    S.op("pe", lambda e, seg=seg, half=half, tb=tb: e.matmul(PI[half][:], lhsT=Wd_all[:, seg, :], rhs=tb[:],
                                                                            start=(seg == 0), stop=(seg == 15)),
                         reads=[Wd_all, tb], writes=[PI[half]], pe_acc=(seg > 0))
                for half in range(2):
                    S.op("dve", lambda e, half=half: e.tensor_copy(out=I_s[:, half * 512:(half + 1) * 512], in_=PI[half][:]),
                         reads=[PI[half]], writes=[I_s])
                S.op("pe", lambda e: e.matmul(PA[0][:, 0:8], lhsT=iqT_s[:], rhs=ikTn[:], start=True, stop=True),
                     reads=[iqT_s, ikTn], writes=[PA[0]])
                S.op("act", lambda e: e.activation(out=Tn[:], in_=PA[0][:, 0:8], func=AF.Relu), reads=[PA[0]], writes=[Tn])
                S.op("pe", lambda e: e.matmul(PD[:, 0:8], lhsT=Wd_all[:, 0, :], rhs=Tn[:], start=True, stop=True),
                     reads=[Wd_all, Tn], writes=[PD])
                S.op("dve", lambda e: e.tensor_reduce(out=st2[:, 2:3], in_=PD[:, 0:8], axis=AX.X, op=ALU.max,
                                                      apply_absolute_value=True), reads=[PD], writes=[st2])
                S.op("dve", lambda e: e.tensor_tensor(out=I_s[:, 1024:1032], in0=PD[:, 0:8], in1=ci["c_negm"][:], op=ALU.add),
                     reads=[PD, ci["c_negm"]], writes=[I_s])
                S.barrier()
                S.emit()
            S.op("dve", lambda e: e.tensor_reduce(out=st2[:, 0:1], in_=I_s[:, 0:1024], axis=AX.X, op=ALU.min),
                 reads=[I_s], writes=[st2])
            S.op("dve", lambda e: e.tensor_reduce(out=st2[:, 1:2], in_=I_s[:, 0:1024], axis=AX.X, op=ALU.max,
                                                  apply_absolute_value=True), reads=[I_s], writes=[st2])
            S.op("dve", lambda e: e.tensor_tensor(out=st2[:, 1:2], in0=st2[:, 1:2], in1=st2[:, 2:3], op=ALU.max),
                 reads=[st2], writes=[st2])
            S.op("pe", lambda e: e.matmul(PD[:, 16:18], lhsT=ci["c_BQm"][:], rhs=st2[:, 0:2], start=True, stop=True),
                 reads=[ci["c_BQm"], st2], writes=[PD])
            S.op("pe", lambda e: e.matmul(PD[:, 18:20], lhsT=ci["c_BQ1"][:], rhs=st2[:, 0:2], start=True, stop=True),
                 reads=[ci["c_BQ1"], st2], writes=[PD])
            S.op("dve", lambda e: e.tensor_copy(out=lo2, in_=PD[:, 16:17]), reads=[PD], writes=[sm2])
            S.op("dve", lambda e: e.tensor_copy(out=hi2, in_=PD[:, 19:20]), reads=[PD], writes=[sm2])
            S.op("dve", lambda e: e.tensor_tensor(out=rng2, in0=hi2, in1=lo2, op=ALU.subtract), reads=[sm2], writes=[sm2])
            S.op("dve", lambda e: e.tensor_scalar(out=spw2[:], in0=ci["c_pow2"][:], scalar1=rng2, scalar2=None, op0=ALU.mult),
                 reads=[ci["c_pow2"], sm2], writes=[spw2])
            S.op("dve", lambda e: e.tensor_tensor(out=mid2, in0=lo2, in1=spw2[:, 0:1], op=ALU.add), reads=[sm2, spw2], writes=[sm2])
            for k in range(KB2):
                S.op("dve", lambda e: e.tensor_scalar(out=junk_s[:], in0=I_s[:], scalar1=mid2, scalar2=0.0, op0=ALU.is_ge,
                                                      op1=ALU.add, accum_out=cnt2[:, 0:1]), reads=[I_s, sm2], writes=[junk_s, cnt2])
                S.op("pe", lambda e: e.matmul(PD[:, 32:34], lhsT=ci["c_BQ1"][:], rhs=cnt2[:], start=True, stop=True),
                     reads=[ci["c_BQ1"], cnt2], writes=[PD])
                S.op("dve", lambda e, k=k: e.scalar_tensor_tensor(out=wv2, in0=PD[:, 32:33], scalar=256.0, in1=spw2[:, k:k + 1],
                                                                  op0=ALU.is_ge, op1=ALU.mult), reads=[PD, spw2], writes=[sm2])
                S.op("dve", lambda e, k=k: e.scalar_tensor_tensor(out=mid2, in0=mid2, scalar=spw2[:, KB2 + k:KB2 + k + 1], in1=wv2,
                                                                  op0=ALU.subtract, op1=ALU.add), reads=[sm2, spw2], writes=[sm2])
            with ExitStack() as sB:
                selm = sb(sB, "selm", [128, 1024], F32)
                PH = sb(sB, "PH", [128, 8, 128], F32)
                kp = [sb(sB, "kp%d" % j, [128, 1024], F32) for j in range(2)]
                cand = sb(sB, "cand", [128, 256], F32)
                candi = sb(sB, "candi", [128, 256], I32)
                dgi = sb(sB, "dgi", [128, 256], I32)
                dgb = sb(sB, "dgb", [128, 3, 256], BF16)
                L_all = sb(sB, "L_all", [128, 256, 24], BF16)
                OH = sb(sB, "OH", [128, 512], BF16)
                Cs = sb(sB, "Cs", [24, 256], F32)
                dig_s = sb(sB, "dig_s", [128, 2, 24], F32)
                idxf = sb(sB, "idxf", [128, 2, 8], F32)
                S.op("dve", lambda e: e.tensor_scalar(out=selm[:], in0=I_s[:, 0:1024], scalar1=mid2, scalar2=0.0, op0=ALU.is_ge,
                                                      op1=ALU.add, accum_out=cnt2[:, 0:1]), reads=[I_s, sm2], writes=[selm, cnt2])
                S.op("pe", lambda e: e.matmul(PD[:, 34:36], lhsT=ci["c_LT"][:], rhs=cnt2[:], start=True, stop=True),
                     reads=[ci["c_LT"], cnt2], writes=[PD])
                S.op("dve", lambda e: e.tensor_copy(out=off_s[:], in_=PD[:, 34:35]), reads=[PD], writes=[off_s])
                S.op("dve", lambda e: e.tensor_scalar(out=seln[:], in0=I_s[:, 1024:1032], scalar1=mid2, scalar2=None, op0=ALU.is_ge),
                     reads=[I_s, sm2], writes=[seln])
                S.op("pe", lambda e: e.transpose(out=PZ[0][0:8, 0:128], in_=seln[:], identity=ident_f[:]),
                     reads=[seln, ident_f], writes=[PZ[0]])
                S.op("dve", lambda e: e.tensor_copy(out=selnT[:], in_=PZ[0][0:8, 0:128]), reads=[PZ[0]], writes=[selnT])
                S.op("dve", lambda e: e.tensor_copy(out=ptrow_f[:], in_=ptrow_i[:]), reads=[ptrow_i], writes=[ptrow_f])
                S.op("dve", lambda e: e.tensor_scalar(out=ptrow_f[:], in0=ptrow_f[:], scalar1=128.0, scalar2=None, op0=ALU.mult),
                     reads=[ptrow_f], writes=[ptrow_f])
                S.op("dve", lambda e: e.tensor_tensor(out=PH[:], in0=ci["c_aiota"][:].rearrange("p (a s) -> p a s", a=8),
                                                      in1=ptrow_f[:].unsqueeze(1).to_broadcast([128, 8, 128]), op=ALU.add),
                     reads=[ci["c_aiota"], ptrow_f], writes=[PH])
                S.op("dve", lambda e: e.tensor_tensor(out=kp[0][:], in0=selm[:], in1=PH[:].rearrange("p a s -> p (a s)"), op=ALU.mult),
                     reads=[selm, PH], writes=[kp[0]])
                for r in range(32):
                    a_, b_ = kp[r % 2], kp[(r + 1) % 2]
                    S.op("dve", lambda e, r=r, a_=a_: e.max(out=cand[:, r * 8:(r + 1) * 8], in_=a_[:]), reads=[a_], writes=[cand])
                    if r < 31:
                        S.op("dve", lambda e, r=r, a_=a_, b_=b_: e.match_replace(out=b_[:], in_to_replace=cand[:, r * 8:(r + 1) * 8],
                                                                                 in_values=a_[:], imm_value=0.0),
                             reads=[a_, cand], writes=[b_])
                S.op("dve", lambda e: e.tensor_copy(out=candi[:], in_=cand[:]), reads=[cand], writes=[candi])
                c255, c8, c16 = ci["c_i32"][:, 0:256], ci["c_i32"][:, 256:512], ci["c_i32"][:, 512:768]
                S.op("dve", lambda e: e.tensor_tensor(out=dgi[:], in0=candi[:], in1=c255, op=ALU.bitwise_and),
                     reads=[candi, ci["c_i32"]], writes=[dgi])
                S.op("dve", lambda e: e.tensor_copy(out=dgb[:, 0, :], in_=dgi[:]), reads=[dgi], writes=[dgb])
                S.op("dve", lambda e: e.tensor_tensor(out=dgi[:], in0=candi[:], in1=c8, op=ALU.logical_shift_right),
                     reads=[candi, ci["c_i32"]], writes=[dgi])
                S.op("dve", lambda e: e.tensor_tensor(out=dgi[:], in0=dgi[:], in1=c255, op=ALU.bitwise_and),
                     reads=[dgi, ci["c_i32"]], writes=[dgi])
                S.op("dve", lambda e: e.tensor_copy(out=dgb[:, 1, :], in_=dgi[:]), reads=[dgi], writes=[dgb])
                S.op("dve", lambda e: e.tensor_tensor(out=dgi[:], in0=candi[:], in1=c16, op=ALU.logical_shift_right),
                     reads=[candi, ci["c_i32"]], writes=[dgi])
                S.op("dve", lambda e: e.tensor_copy(out=dgb[:, 2, :], in_=dgi[:]), reads=[dgi], writes=[dgb])
                for dg in range(3):
                    S.op("dve", lambda e, dg=dg: e.tensor_tensor(
                        out=L_all[:, :, dg * 8:(dg + 1) * 8], in0=dgb[:, dg, :].unsqueeze(2).to_broadcast([128, 256, 8]),
                        in1=ci["c_bq16"][:].unsqueeze(1).to_broadcast([128, 256, 8]), op=ALU.mult),
                        reads=[dgb, ci["c_bq16"]], writes=[L_all])
                S.op("dve", lambda e: e.tensor_scalar(out=OH[:], in0=ci["c_iota512"][:], scalar1=off_s[:, 0:1], scalar2=None,
                                                      op0=ALU.is_equal), reads=[ci["c_iota512"], off_s], writes=[OH])
                for r in range(256):
                    S.op("pe", lambda e, r=r: e.matmul(PU[0][0:24, 0:256], lhsT=L_all[:, r, :], rhs=OH[:, 256 - r:512 - r],
                                                       start=(r == 0), stop=(r == 255)), reads=[L_all, OH], writes=[PU[0]],
                         pe_acc=(r > 0))
                S.op("dve", lambda e: e.tensor_copy(out=Cs[:], in_=PU[0][0:24, 0:256]), reads=[PU[0]], writes=[Cs])
                for half in range(2):
                    S.op("pe", lambda e, half=half: e.transpose(out=PZ[1][:, half * 24:(half + 1) * 24],
                                                                in_=Cs[:, half * 128:(half + 1) * 128], identity=ident_f[0:24, 0:24]),
                         reads=[Cs, ident_f], writes=[PZ[1]])
                S.op("dve", lambda e: e.tensor_copy(out=dig_s[:].rearrange("p a b -> p (a b)"), in_=PZ[1][:, 0:48]),
                     reads=[PZ[1]], writes=[dig_s])
                S.op("dve", lambda e: e.scalar_tensor_tensor(out=idxf[:], in0=dig_s[:, :, 16:24], scalar=256.0, in1=dig_s[:, :, 8:16],
                                                             op0=ALU.mult, op1=ALU.add), reads=[dig_s], writes=[idxf])
                S.op("dve", lambda e: e.scalar_tensor_tensor(out=idxf[:], in0=idxf[:], scalar=256.0, in1=dig_s[:, :, 0:8],
                                                             op0=ALU.mult, op1=ALU.add), reads=[idxf, dig_s], writes=[idxf])
                S.op("dve", lambda e: e.tensor_scalar(out=vmask[:], in0=idxf[:], scalar1=0.5, scalar2=None, op0=ALU.is_gt),
                     reads=[idxf], writes=[vmask])
                S.op("dve", lambda e: e.tensor_scalar(out=idxf[:], in0=idxf[:], scalar1=-1.0, scalar2=0.0, op0=ALU.add, op1=ALU.max),
                     reads=[idxf], writes=[idxf])
                S.op("dve", lambda e: e.tensor_copy(out=idx_i[:], in_=idxf[:]), reads=[idxf], writes=[idx_i])
                S.barrier()
                S.emit()
            with ExitStack() as sC:
                Ksel = sb(sC, "Ksel", [128, 16, 1024], BF16)
                Vsel = sb(sC, "Vsel", [128, 16, 1024], BF16)
                prod = sb(sC, "prod", [128, 1024], F32)
                sT = sb(sC, "sT", [128, 2, 8], F32)
                pT = sb(sC, "pT", [128, 2, 8], BF16)
                prodn = sb(sC, "prodn", [8, 1024], F32)
                sTn = sb(sC, "sTn", [8, 8], F32)
                pTn = sb(sC, "pTn", [8, 8], BF16)
                rden_s = sb(sC, "rden_s", [8, 1], F32)
                Mq = sb(sC, "Mq", [8, 1024], BF16)
                for q in range(8):
                    for half in range(2):
                        for dst_, src_ in ((Ksel, ck), (Vsel, cv)):
                            S.dma("pool", lambda e, q=q, half=half, dst_=dst_, src_=src_: e.indirect_dma_start(
                                out=dst_[:, q * 2 + half, :], out_offset=None, in_=src_[:, :],
                                in_offset=bass.IndirectOffsetOnAxis(ap=idx_i[:, half, q:q + 1], axis=0)),
                                reads=[idx_i], writes=[dst_])
                PAs = [PG, PGL]
                for q in range(8):
                    for cb in range(2):
                        S.op("pe", lambda e, q=q, cb=cb: e.matmul(PZ[cb][:], lhsT=esel_b[:, q, :], rhs=aq_sb[:, cb * 512:(cb + 1) * 512],
                                                                  start=True, stop=True), reads=[esel_b, aq_sb], writes=[PZ[cb]])
                    for half in range(2):
                        for cb in range(2):
                            S.op("dve", lambda e, q=q, half=half, cb=cb: e.tensor_tensor(
                                out=prod[:, cb * 512:(cb + 1) * 512], in0=Ksel[:, q * 2 + half, cb * 512:(cb + 1) * 512],
                                in1=PZ[cb][:], op=ALU.mult), reads=[Ksel, PZ[cb]], writes=[prod])
                        S.op("dve", lambda e, half=half: e.tensor_reduce(out=sT[:, half, :], in_=prod[:].rearrange("p (h d) -> p h d", h=8),
                                                                         axis=AX.X, op=ALU.add), reads=[prod], writes=[sT])
                    S.op("act", lambda e: e.activation(out=sT[:].rearrange("p a b -> p (a b)"), in_=sT[:].rearrange("p a b -> p (a b)"),
                                                       func=AF.Exp), reads=[sT], writes=[sT])
                    S.op("dve", lambda e, q=q: e.tensor_tensor(out=pT[:], in0=sT[:], in1=vmask[:, :, q:q + 1].to_broadcast([128, 2, 8]),
                                                               op=ALU.mult), reads=[sT, vmask], writes=[pT])
                    for cb in range(2):
                        S.op("dve", lambda e, cb=cb: e.tensor_tensor(out=prodn[:, cb * 512:(cb + 1) * 512],
                                                                     in0=ak_sb[:, cb * 512:(cb + 1) * 512], in1=PZ[cb][0:8, :],
                                                                     op=ALU.mult), reads=[ak_sb, PZ[cb]], writes=[prodn])
                    S.op("dve", lambda e: e.tensor_reduce(out=sTn[:], in_=prodn[:].rearrange("p (h d) -> p h d", h=8), axis=AX.X,
                                                          op=ALU.add), reads=[prodn], writes=[sTn])
                    S.op("act", lambda e: e.activation(out=sTn[:], in_=sTn[:], func=AF.Exp), reads=[sTn], writes=[sTn])
                    S.op("dve", lambda e, q=q: e.tensor_scalar(out=pTn[:], in0=sTn[:], scalar1=selnT[:, q * 16:q * 16 + 1], scalar2=None,
                                                               op0=ALU.mult), reads=[sTn, selnT], writes=[pTn])
                    for cb in range(2):
                        cs_ = slice(cb * 512, (cb + 1) * 512)
                        S.op("pe", lambda e, q=q, cb=cb, cs_=cs_: e.matmul(PU[cb][0:8, :], lhsT=pT[:, 0, :], rhs=Vsel[:, q * 2, cs_],
                                                                           start=True, stop=False), reads=[pT, Vsel], writes=[PU[cb]])
                        S.op("pe", lambda e, q=q, cb=cb, cs_=cs_: e.matmul(PU[cb][0:8, :], lhsT=pT[:, 1, :], rhs=Vsel[:, q * 2 + 1, cs_],
                                                                           start=False, stop=False), reads=[pT, Vsel], writes=[PU[cb]],
                             pe_acc=True)
                        S.op("pe", lambda e, cb=cb, cs_=cs_: e.matmul(PU[cb][0:8, :], lhsT=pTn[:], rhs=av_sb[:, cs_],
                                                                      start=False, stop=True), reads=[pTn, av_sb], writes=[PU[cb]],
                             pe_acc=True)
                    S.op("pe", lambda e: e.matmul(PD[0:8, 0:2], lhsT=pT[:, 0, :], rhs=ones_b[:], start=True, stop=False),
                         reads=[pT, ones_b], writes=[PD])
                    S.op("pe", lambda e: e.matmul(PD[0:8, 0:2], lhsT=pT[:, 1, :], rhs=ones_b[:], start=False, stop=False),
                         reads=[pT, ones_b], writes=[PD], pe_acc=True)
                    S.op("pe", lambda e: e.matmul(PD[0:8, 0:2], lhsT=pTn[:], rhs=ones_b[0:8, :], start=False, stop=True),
                         reads=[pTn, ones_b], writes=[PD], pe_acc=True)
                    S.op("dve", lambda e: e.reciprocal(out=rden_s[:], in_=PD[0:8, 0:1]), reads=[PD], writes=[rden_s])
                    for cb in range(2):
                        cs_ = slice(cb * 512, (cb + 1) * 512)
                        S.op("dve", lambda e, cb=cb, cs_=cs_: e.scalar_tensor_tensor(
                            out=Mq[:, cs_], in0=PU[cb][0:8, :], scalar=rden_s[:, 0:1], in1=ci["c_bd8"][:, cs_],
                            op0=ALU.mult, op1=ALU.mult), reads=[PU[cb], rden_s, ci["c_bd8"]], writes=[Mq])
                        S.op("pe", lambda e, q=q, cb=cb, cs_=cs_: e.matmul(PAs[cb][0:8, :], lhsT=eq_b[:, q, :], rhs=Mq[:, cs_],
                                                                           start=(q == 0), stop=(q == 7)),
                             reads=[eq_b, Mq], writes=[PAs[cb]], pe_acc=(q > 0))
                for cb in range(2):
                    cs_ = slice(cb * 512, (cb + 1) * 512)
                    S.op("dve", lambda e, cb=cb, cs_=cs_: e.tensor_tensor(out=cat_s[:, cs_], in0=PAs[cb][0:8, :], in1=ga_s[:, cs_],
                                                                          op=ALU.mult), reads=[PAs[cb], ga_s], writes=[cat_s])
                S.barrier()
                S.emit()
        if DBG:
            S.dma("sp", lambda e: e.dma_start(out=o_cat[:, :], in_=catT[:].rearrange("p k t -> p (k t)")), reads=[catT])
        with ExitStack() as p7:
            wo_sb = sb(p7, "wo_sb", [128, KC, 2048], BF16)
            xr = [sb(p7, "xr%d" % j, [128, 2048], F32) for j in range(2)]
            rr = sb(p7, "rr", [128, 2048], F32)
            yo = [sb(p7, "yo%d" % j, [128, 2048], F32) for j in range(2)]
            lng_bc = sb(p7, "lng_bc", [128, 2048], F32)
            lnb_bc = sb(p7, "lnb_bc", [128, 2048], F32)
            stats = sb(p7, "stats", [128, 4, 6], F32)
            mv2 = sb(p7, "mv2", [128, 2], F32)
            rs2 = sb(p7, "rs2", [128, 2], F32)
            load_wres(wo_sb, Wo, 2048)
            S.dma("sp", lambda e: e.dma_start(out=lng_bc[:], in_=lng[0:1, :].broadcast_to([128, 2048])), writes=[lng_bc])
            S.dma("sp", lambda e: e.dma_start(out=lnb_bc[:], in_=lnb[0:1, :].broadcast_to([128, 2048])), writes=[lnb_bc])
            csT = sb(p7, "csT", [128, KC, 8], BF16)
            for kc in range(KC):
                S.op("pe", lambda e, kc=kc: e.transpose(out=PT[:, kc * 8:(kc + 1) * 8], in_=cat_s[:, kc * 128:(kc + 1) * 128],
                                                        identity=ident_b[0:8, 0:8]), reads=[cat_s, ident_b], writes=[PT])
            S.op("act", lambda e: e.activation(out=csT[:].rearrange("p k t -> p (k t)"), in_=PT[:, 0:128], func=AF.Copy),
                 reads=[PT], writes=[csT])

            def merge_rows(np_, j, lhs_fn, lhs_deps, x_src, o_dst):
                x_, y_ = xr[j % 2], yo[j % 2]
                S.dma("sp", lambda e: e.dma_start(out=x_[0:np_, :], in_=x_src), writes=[x_])
                for c in range(4):
                    pz = PZ[c % 2]
                    for k in range(KC):
                        S.op("pe", lambda e, k=k, c=c, pz=pz: e.matmul(
                            pz[0:np_, :], lhsT=lhs_fn(k), rhs=wo_sb[:, k, c * 512:(c + 1) * 512],
                            start=(k == 0), stop=(k == KC - 1)), reads=lhs_deps + [wo_sb], writes=[pz], pe_acc=(k > 0))
                    S.op("dve", lambda e, c=c, pz=pz: e.scalar_tensor_tensor(
                        out=rr[0:np_, c * 512:(c + 1) * 512], in0=x_[0:np_, c * 512:(c + 1) * 512], scalar=ALPHA, in1=pz[0:np_, :],
                        op0=ALU.mult, op1=ALU.add), reads=[x_, pz], writes=[rr])
                    S.op("dve", lambda e, c=c: e.bn_stats(out=stats[0:np_, c, :], in_=rr[0:np_, c * 512:(c + 1) * 512]),
                         reads=[rr], writes=[stats])
                S.op("dve", lambda e: e.bn_aggr(out=mv2[0:np_, :], in_=stats[0:np_].rearrange("p a b -> p (a b)")),
                     reads=[stats], writes=[mv2])
                S.op("dve", lambda e: e.tensor_scalar(out=rs2[0:np_, 0:1], in0=mv2[0:np_, 1:2], scalar1=LN_EPS, scalar2=None,
                                                      op0=ALU.add), reads=[mv2], writes=[rs2])
                S.op("act", lambda e: e.activation(out=rs2[0:np_, 0:1], in_=rs2[0:np_, 0:1], func=AF.Sqrt), reads=[rs2], writes=[rs2])
                S.op("dve", lambda e: e.reciprocal(out=rs2[0:np_, 0:1], in_=rs2[0:np_, 0:1]), reads=[rs2], writes=[rs2])
                S.op("dve", lambda e: e.scalar_tensor_tensor(out=rs2[0:np_, 1:2], in0=mv2[0:np_, 0:1], scalar=-1.0,
                                                             in1=rs2[0:np_, 0:1], op0=ALU.mult, op1=ALU.mult),
                     reads=[mv2, rs2], writes=[rs2])
                S.op("act", lambda e: e.activation(out=y_[0:np_, :], in_=rr[0:np_, :], func=AF.Identity, scale=rs2[0:np_, 0:1],
                                                   bias=rs2[0:np_, 1:2]), reads=[rr, rs2], writes=[y_])
                S.op("pool", lambda e: e.tensor_tensor(out=y_[0:np_, :], in0=y_[0:np_, :], in1=lng_bc[0:np_, :], op=ALU.mult),
                     reads=[y_, lng_bc], writes=[y_])
                S.op("pool", lambda e: e.tensor_tensor(out=y_[0:np_, :], in0=y_[0:np_, :], in1=lnb_bc[0:np_, :], op=ALU.add),
                     reads=[y_, lnb_bc], writes=[y_])
                S.dma("sp", lambda e: e.dma_start(out=o_dst, in_=y_[0:np_, :]), reads=[y_])
            for i in range(NO):
                merge_rows(128, i, lambda k, i=i: catT[:, k, i * 128:(i + 1) * 128], [catT],
                           xo[i * 128:(i + 1) * 128, :], o_y[i * 128:(i + 1) * 128, :])
            merge_rows(8, NO, lambda k: csT[:, k, :], [csT], xs[:, :], o_ys[:, :])
            S.barrier()
            S.emit()
        es_S.close()
        es_C.close()
    return nc


def _rope_tab(pos, half):
    inv = (np.float32(ROPE_THETA) ** (-np.arange(half, dtype=np.float32) / np.float32(half))).astype(np.float32)
    ang = pos.astype(np.float32)[:, None] * inv[None, :]
    c = np.cos(ang).astype(np.float32)
    s = np.sin(ang).astype(np.float32)
    return np.concatenate([c, c], 1), np.concatenate([-s, s], 1)


def _consts():
    s = np.arange(128)
    same = (s[:, None] // 64) == (s[None, :] // 64)
    tri2 = (same & (s[:, None] <= s[None, :])).astype(np.float32)
    blk2 = same.astype(np.float32)
    return dict(c_tri2=tri2, c_blk2=blk2, c_ident=np.eye(128, dtype=np.float32))


_NC_CACHE = {}


def kernel(x_prompt, x_sample, mem_prompt, cache_k, cache_v, cache_idx_k, state_hgrn,
           cache_mem_k, cache_mem_v, page_table, w_in, lb_logits, hgrn_norm_g,
           w_mem_k, w_mem_v, w_out, ln_g, ln_b):
    f32 = np.float32
    x_prompt = np.asarray(x_prompt, f32)
    w = np.asarray(w_in, f32)[0]
    ca = np.ascontiguousarray
    Wk = ca(w[:, O_AK:O_AV])
    Wv = ca(w[:, O_AV:O_AG])
    W3 = ca(np.concatenate([w[:, O_BF:O_BI], w[:, O_BI:O_BG], w[:, O_IK:O_IW]], axis=1))
    pos = np.arange(SEQ)
    cc128, ss128 = _rope_tab(pos, 16)
    cc64, ss64 = _rope_tab(pos, 8)
    ropeN = ca(np.concatenate([cc128, ss128, cc64, ss64], 1).astype(f32))
    consts = _consts()
    consts["c_j"] = np.tile(np.arange(256, dtype=f32)[None, :], (128, 1))
    pw = np.zeros(48, f32)
    for k in range(24):
        pw[k] = 2.0 ** -(k + 1)
        pw[24 + k] = 2.0 ** -(k + 2) if k < 23 else 2.0 ** -24
    consts["c_pow"] = np.tile(pw[None, :], (128, 1))
    Wi_ = ca(np.concatenate([w[:, O_IQ:O_IK], w[:, O_IW:O_BQ]], axis=1))
    Wb_ = ca(np.concatenate([w[:, O_BQ:O_BF], w[:, O_BF:O_BI], w[:, O_BI:O_BG], w[:, O_BG:O_MQ]], axis=1))
    consts.update(Wi=Wi_, Wq=ca(w[:, O_AQ:O_AK]), Wg=ca(w[:, O_AG:O_IQ]), Wb=Wb_, Wm=ca(w[:, O_MQ:O_END]),
                  Wo=ca(np.asarray(w_out, f32)[0]), ng=ca(np.asarray(hgrn_norm_g, f32).reshape(1, 128)),
                  lng=ca(np.asarray(ln_g, f32).reshape(1, D)), lnb=ca(np.asarray(ln_b, f32).reshape(1, D)))
    qs_ = np.float32(128.0 ** -0.5)
    P = np.arange(128)
    hq_h, hq_q = P // 8, P % 8
    qs_q, qs_s = P // 16, P % 16
    consts["c_sel8"] = (np.arange(8)[:, None] == hq_q[None, :]).astype(f32)
    consts["c_hm"] = (hq_h[:, None] == np.arange(16)[None, :]).astype(f32)
    consts["c_bq8"] = (hq_q[:, None] == np.arange(8)[None, :]).astype(f32)
    consts["c_bq16"] = (qs_q[:, None] == np.arange(8)[None, :]).astype(f32)
    sameq = (qs_q[:, None] == qs_q[None, :])
    consts["c_BQ1"] = sameq.astype(f32)
    consts["c_BQm"] = (sameq / 16.0).astype(f32)
    consts["c_LT"] = (sameq & (qs_s[:, None] < qs_s[None, :])).astype(f32)
    consts["c_negm"] = np.where((qs_s[:, None] == 0) & (np.arange(8)[None, :] <= qs_q[:, None]), 0.0, NEG).astype(f32)
    consts["c_aiota"] = (qs_s[:, None] * 8 + (np.arange(1024)[None, :] // 128) + 1).astype(f32)
    consts["c_iota512"] = np.tile((np.arange(512) - 256).astype(f32)[None, :], (128, 1))
    esel = np.zeros((8, 8, 128), f32)
    eq = np.zeros((8, 8, 8), f32)
    for q_ in range(8):
        esel[q_, q_, :] = 1.0
        eq[:, q_, q_] = 1.0
    consts["c_esel"] = esel.reshape(8, 1024)
    consts["c_eq"] = eq.reshape(8, 64)
    consts["c_bd8"] = np.repeat((np.arange(8)[:, None] == np.arange(8)[None, :]).astype(f32), 128, axis=1)
    pw2 = np.zeros(60, f32)
    for k in range(30):
        pw2[k] = 2.0 ** -(k + 1)
        pw2[30 + k] = 2.0 ** -(k + 2) if k < 29 else 2.0 ** -30
    consts["c_pow2"] = np.tile(pw2[None, :], (128, 1))
    consts["c_i32"] = np.concatenate([np.full((128, 256), 255), np.full((128, 256), 8), np.full((128, 256), 16)], 1).astype(np.int32)
    ck_flat = np.asarray(cache_k, f32)[0].reshape(-1, 1024)
    cv_flat = np.asarray(cache_v, f32)[0].reshape(-1, 1024)
    cidx_flat = np.asarray(cache_idx_k, f32)[0].reshape(1280, 8192)
    consts.update(ck=ck_flat, cv=cv_flat, cidx=cidx_flat)
    shared = dict(Wk=Wk, Wv=Wv, W3=W3, Wmk=ca(np.asarray(w_mem_k, f32)[0]), Wmv=ca(np.asarray(w_mem_v, f32)[0]),
                  lbl=ca(np.asarray(lb_logits, f32)), ropeN=ropeN, **consts)
    pos_s = PAST + np.arange(8)
    c128s, s128s = _rope_tab(pos_s, 16)
    c64s, s64s = _rope_tab(pos_s, 8)
    ropeS = ca(np.concatenate([c128s, s128s, c64s, s64s, c128s * qs_, s128s * qs_], 1).astype(f32))
    x_sample = np.asarray(x_sample, f32)
    in_maps = []
    for c in range(8):
        b, h = c // 2, c % 2
        posq = np.zeros((128, NO + 1), f32)
        for i in range(NO):
            posq[:, i] = (2 * i + h) * 128 + np.arange(128)
        posq[:, NO] = h
        m = dict(shared)
        own = np.concatenate([(2 * i + h) * 128 + np.arange(128) for i in range(NO)])
        ropeO = ca(np.concatenate([cc128[own] * qs_, ss128[own] * qs_, cc64[own], ss64[own]], 1).astype(f32))
        m.update(xTn=ca(x_prompt[b].T), memT=ca(np.asarray(mem_prompt, f32)[b].T), posq=posq,
                 xTo=ca(x_prompt[b][own].T), xo=ca(x_prompt[b][own]), ropeO=ropeO,
                 xsT=ca(x_sample[c].T), xs=ca(x_sample[c]), ropeS=ropeS,
                 st_h=ca(np.asarray(state_hgrn, f32)[0, c]),
                 ptab=ca(np.asarray(page_table)[c].astype(np.int32).reshape(1, 128)),
                 cmk_d=ca(np.asarray(cache_mem_k, f32)[0, c].reshape(256, 512)),
                 cmv_d=ca(np.asarray(cache_mem_v, f32)[0, c].reshape(256, 512)))
        in_maps.append(m)
    import os
    stage = int(os.environ.get("KSTAGE", "99"))
    Sched.LIMIT = int(os.environ.get("KLIMIT", str(10 ** 9)))
    ncores = int(os.environ.get("KCORES", "8"))
    if "nc" not in _NC_CACHE:
        _NC_CACHE["nc"] = build_program(stage)
    nc = _NC_CACHE["nc"]
    res = run_bass_kernel_spmd(nc, in_maps[:ncores], core_ids=list(range(ncores)))
    if ncores < 8:
        res.results.extend([res.results[0]] * (8 - ncores))
    R = res.results
    _NC_CACHE["R"] = R
    B = 4
    y_p = np.zeros((B, SEQ, D), f32)
    if "o_y" in R[0]:
        for c in range(8):
            b, h = c // 2, c % 2
            oy = R[c]["o_y"]
            for i in range(NO):
                n = 2 * i + h
                y_p[b, n * 128:(n + 1) * 128] = oy[i * 128:(i + 1) * 128]
    y_s = np.zeros((8, 8, D), f32)
    k_p = np.stack([R[2 * b]["o_k"].reshape(SEQ, 8, 128) for b in range(B)])[None]
    v_p = np.stack([R[2 * b]["o_v"].reshape(SEQ, 8, 128) for b in range(B)])[None]
    ik_p = np.stack([R[2 * b]["o_ik"] for b in range(B)])[None]
    hg_p = np.stack([R[2 * b]["o_hg"].transpose(1, 0, 2) for b in range(B)])[None]
    mk_p = np.stack([R[2 * b]["o_mk"].reshape(256, 4, 128) for b in range(B)])[None]
    mv_p = np.stack([R[2 * b]["o_mv"].reshape(256, 4, 128) for b in range(B)])[None]
    k_s = np.stack([R[c]["o_ks"].reshape(8, 8, 128) for c in range(8)])[None].astype(f32)
    v_s = np.stack([R[c]["o_vs"].reshape(8, 8, 128) for c in range(8)])[None].astype(f32)
    ik_s = np.stack([R[c]["o_iks"] for c in range(8)])[None].astype(f32)
    hg_s = np.stack([R[c]["o_hgs"].transpose(1, 0, 2) for c in range(8)])[None].astype(f32)
    y_s = np.stack([R[c]["o_ys"] for c in range(8)]).astype(f32)
    return (y_p, y_s, k_p.astype(f32), v_p.astype(f32), ik_p.astype(f32), hg_p.astype(f32), mk_p.astype(f32),
            mv_p.astype(f32), k_s, v_s, ik_s, hg_s)
```

```python
from contextlib import ExitStack
import numpy as np
import concourse.bass as bass
import concourse.mybir as mybir
from concourse.bass_utils import run_bass_kernel_spmd

F32 = mybir.dt.float32
BF16 = mybir.dt.bfloat16
I32 = mybir.dt.int32
AF = mybir.ActivationFunctionType
ALU = mybir.AluOpType
AX = mybir.AxisListType

D = 2048
KC = 16
SEQ = 2048
NT = 16
NO = 8
ROPE_THETA = 500000.0
PAST = 16384
ALPHA = 2.0 ** 0.25
LN_EPS = 1e-5
RMS_EPS = 1e-6
NEG = -1.0e30

O_AQ, O_AK, O_AV, O_AG, O_IQ, O_IK, O_IW, O_BQ, O_BF, O_BI, O_BG, O_MQ, O_MG, O_END = (
    0, 1024, 2048, 3072, 4096, 5120, 5184, 5200, 5712, 6224, 6736, 7248, 7760, 8272)


class Buf:
    __slots__ = ("name", "w", "r", "dsem", "dval", "excl")

    def __init__(self, name):
        self.name = name
        self.excl = False
        self.w = None
        self.r = {}
        self.dsem = None
        self.dval = 0


class TL:
    def __init__(self, t, name):
        self.t = t
        self.b = Buf(name)

    def __getitem__(self, k):
        return self.t[k]


class TLV(TL):
    def __init__(self, base, fn):
        self.t = None
        self.base = base
        self.fn = fn
        self.b = base.b

    def __getitem__(self, k):
        return self.fn(self.base.t)[k]


class Sched:
    ENG = ("pe", "act", "dve", "pool", "sp")

    def __init__(self, nc, es):
        self.nc = nc
        self.es = es
        self.q = {e: [] for e in self.ENG}
        self.cnt = {e: 0 for e in self.ENG}
        self.known = {e: {} for e in self.ENG}
        self.sems = {}
        for e in self.ENG:
            self.sems["e:" + e] = es.enter_context(nc.semaphore("s_" + e))
        self.ndsem = 0
        self.ninstr = 0
        self.dvals = {}

    def _dsem(self, buf):
        if buf.dsem is None:
            key = "d:%d" % self.ndsem
            self.ndsem += 1
            self.sems[key] = self.es.enter_context(self.nc.semaphore("sd%d" % self.ndsem))
            buf.dsem = key
        return buf.dsem

    def _deps(self, eng, reads, writes, skip_self_pe=False):
        deps = {}

        def add(k, v):
            if deps.get(k, 0) < v:
                deps[k] = v
        for b in reads:
            if b.w is not None:
                add(*b.w)
            if b.excl:
                for k, v in b.r.items():
                    if k != "e:" + eng:
                        add(k, v)
        for b in writes:
            if b.w is not None:
                add(*b.w)
            for k, v in b.r.items():
                add(k, v)
        waits = []
        for k, v in deps.items():
            if skip_self_pe and k == "e:pe":
                continue
            if self.known[eng].get(k, 0) < v:
                self.known[eng][k] = v
                waits.append((k, v))
        return waits

    def _mark(self, ev, reads, writes):
        for b in reads:
            if b.r.get(ev[0], 0) < ev[1]:
                b.r[ev[0]] = ev[1]
        for b in writes:
            b.w = ev
            b.r = {}

    LIMIT = 10 ** 9

    def op(self, eng, fn, reads=(), writes=(), pe_acc=False):
        if self.ninstr >= Sched.LIMIT:
            return None
        reads = [x.b if isinstance(x, TL) else x for x in reads]
        writes = [x.b if isinstance(x, TL) else x for x in writes]
        waits = self._deps(eng, reads, writes, skip_self_pe=(eng == "pe" and pe_acc))
        self.cnt[eng] += 1
        ev = ("e:" + eng, self.cnt[eng])
        self.q[eng].append((waits, fn, ev[0], 1))
        self._mark(ev, reads, writes)
        self.ninstr += 1
        return ev

    def dma(self, queue, fns, reads=(), writes=(), owner=None):
        if self.ninstr >= Sched.LIMIT:
            return None
        reads = [x.b if isinstance(x, TL) else x for x in reads]
        writes = [x.b if isinstance(x, TL) else x for x in writes]
        if not isinstance(fns, (list, tuple)):
            fns = [fns]
        if owner is None:
            owner = (list(writes) + list(reads))[0]
        elif isinstance(owner, TL):
            owner = owner.b
        key = self._dsem(owner)
        waits = self._deps(queue, reads, writes)
        for i, fn in enumerate(fns):
            owner.dval += 16
            self.q[queue].append((waits if i == 0 else [], fn, key, 16))
            self.ninstr += 1
        ev = (key, owner.dval)
        self.dvals[key] = owner.dval
        self._mark(ev, reads, writes)
        return ev

    def barrier(self):
        tgt = {"e:" + e: self.cnt[e] for e in self.ENG if self.cnt[e] > 0}
        for k, v in self.dvals.items():
            tgt[k] = v
        for e in self.ENG:
            waits = []
            for k, v in tgt.items():
                if k == "e:" + e:
                    continue
                if self.known[e].get(k, 0) < v:
                    self.known[e][k] = v
                    waits.append((k, v))
            self.q[e].append((waits, None, None, 0))

    def finish_wait(self, bufs, eng="sp"):
        deps = {}
        for b in bufs:
            b = b.b if isinstance(b, TL) else b
            for k, v in ([b.w] if b.w else []) + list(b.r.items()):
                if deps.get(k, 0) < v:
                    deps[k] = v
        self.q[eng].append((list(deps.items()), None, None, 0))

    def emit(self):
        nc = self.nc
        engobj = {"pe": "tensor", "act": "scalar", "dve": "vector", "pool": "gpsimd", "sp": "sync"}
        with nc.Block() as block:
            for e in self.ENG:
                items = self.q[e]
                if not items:
                    continue

                def body(eng, items=items):
                    for waits, fn, semkey, inc in items:
                        for k, v in waits:
                            eng.wait_ge(self.sems[k], v)
                        if fn is not None:
                            ins = fn(eng)
                            ins.then_inc(self.sems[semkey], inc)
                getattr(block, engobj[e])(body)
        self.q = {e: [] for e in self.ENG}


def build_program(stage=99):
    nc = bass.Bass("TRN2", target_bir_lowering=False)

    def din(name, shape, dt=F32):
        return nc.dram_tensor(name, list(shape), dt, kind="ExternalInput").ap()

    def dout(name, shape, dt=F32):
        return nc.dram_tensor(name, list(shape), dt, kind="ExternalOutput").ap()

    xTn = din("xTn", [D, SEQ])
    memT = din("memT", [D, 256])
    Wk = din("Wk", [D, 1024])
    Wv = din("Wv", [D, 1024])
    W3 = din("W3", [D, 1088])
    Wmk = din("Wmk", [D, 512])
    Wmv = din("Wmv", [D, 512])
    lbl = din("lbl", [2, 512])
    ropeN = din("ropeN", [SEQ, 96])
    c_tri2 = din("c_tri2", [128, 128])
    c_blk2 = din("c_blk2", [128, 128])
    c_ident = din("c_ident", [128, 128])
    posq = din("posq", [128, NO + 1])
    xTo = din("xTo", [D, 1024])
    xo = din("xo", [1024, D])
    Wi = din("Wi", [D, 1040])
    Wq = din("Wq", [D, 1024])
    Wg = din("Wg", [D, 1024])
    Wb = din("Wb", [D, 2048])
    Wm = din("Wm", [D, 1024])
    Wo = din("Wo", [D, D])
    ropeO = din("ropeO", [1024, 96])
    c_j = din("c_j", [128, 256])
    c_pow = din("c_pow", [128, 48])
    ng = din("ng", [1, 128])
    lng = din("lng", [1, D])
    lnb = din("lnb", [1, D])
    o_k = dout("o_k", [SEQ, 1024])
    o_v = dout("o_v", [SEQ, 1024])
    o_ik = dout("o_ik", [SEQ, 64])
    o_hg = dout("o_hg", [128, 4, 128])
    o_mk = dout("o_mk", [256, 512])
    o_mv = dout("o_mv", [256, 512])
    o_y = dout("o_y", [1024, D])
    xsT = din("xsT", [D, 8])
    xs = din("xs", [8, D])
    ropeS = din("ropeS", [8, 160])
    st_h = din("st_h", [4, 128, 128])
    cmk_d = din("cmk_d", [256, 512])
    cmv_d = din("cmv_d", [256, 512])
    ptab = din("ptab", [1, 128], I32)
    cidx = din("cidx", [1280, 8192])
    ck = din("ck", [163840, 1024])
    cv = din("cv", [163840, 1024])
    cin = {}
    for nm_, shp_, dt__ in (("c_sel8", [8, 128], F32), ("c_hm", [128, 16], F32), ("c_bq8", [128, 8], F32),
                            ("c_bq16", [128, 8], F32), ("c_BQ1", [128, 128], F32), ("c_BQm", [128, 128], F32),
                            ("c_LT", [128, 128], F32), ("c_negm", [128, 8], F32), ("c_aiota", [128, 1024], F32),
                            ("c_iota512", [128, 512], F32), ("c_esel", [8, 1024], F32), ("c_eq", [8, 64], F32),
                            ("c_bd8", [8, 1024], F32), ("c_pow2", [128, 60], F32), ("c_i32", [128, 768], I32)):
        cin[nm_] = din(nm_, shp_, dt__)
    o_ks = dout("o_ks", [8, 1024])
    o_vs = dout("o_vs", [8, 1024])
    o_iks = dout("o_iks", [8, 64])
    o_hgs = dout("o_hgs", [128, 4, 128])
    o_ys = dout("o_ys", [8, D])
    import os
    DBG = os.environ.get("KDBG", "0") == "1"
    if DBG:
        o_cat = dout("o_cat", [128, KC * 1024], BF16)

    with ExitStack() as es:
        S = Sched(nc, es)
        outbufs = []

        def sb(st, name, shape, dt):
            return TL(st.enter_context(nc.sbuf_tensor(name, list(shape), dt)), name)

        def ps(st, name, shape, dt):
            t = TL(st.enter_context(nc.psum_tensor(name, list(shape), dt)), name)
            t.b.excl = True
            return t

        ident_f = sb(es, "ident_f", [128, 128], F32)
        ident_b = sb(es, "ident_b", [128, 128], BF16)
        tri2 = sb(es, "tri2", [128, 128], F32)
        blk2 = sb(es, "blk2", [128, 128], F32)
        ones_f = sb(es, "ones_f", [128, 2], F32)
        lb_bc = sb(es, "lb_bc", [128, 512], F32)
        oml_bc = sb(es, "oml_bc", [128, 512], F32)
        posq_sb = sb(es, "posq_sb", [128, NO + 1], F32)
        hflag = sb(es, "hflag", [128, 1], F32)
        mkT = sb(es, "mkT", [128, 4, 256], BF16)
        mv = sb(es, "mv", [128, 2, 4, 130], BF16)
        Sown = sb(es, "Sown", [128, NO, 512], BF16)
        xsT_b = sb(es, "xsT_b", [128, KC, 8], BF16)
        zst = sb(es, "zst", [8, 512], F32)
        R16 = sb(es, "R16", [128, 8192], BF16)
        es_A = ExitStack()
        KT = sb(es_A, "KT", [128, 8, SEQ], BF16)
        V = sb(es_A, "V", [128, NT, 8, 130], BF16)
        ikT = sb(es_A, "ikT", [64, SEQ], BF16)

        PZ = [ps(es, "PZ%d" % i, [128, 512], F32) for i in range(2)]
        PT = ps(es, "PT", [128, 1024], BF16)
        PG = ps(es, "PG", [128, 512], F32)
        PGL = ps(es, "PGL", [128, 512], F32)
        PU = [ps(es, "PU%d" % i, [128, 512], F32) for i in range(2)]
        PD = ps(es, "PD", [128, 512], F32)

        zs_d = nc.dram_tensor("zs_d", [8, 8448], F32).ap()
        zsd_b = Buf("zs_d")
        S.dma("pool", lambda e: e.dma_start(out=xsT_b[:], in_=xsT.rearrange("(k p) t -> p k t", p=128)), writes=[xsT_b])
        ZOFF = dict(q=0, k=1024, v=2048, g=3072, iq=4096, iw=5120, ik=5136, b=5200, m=7248)

        def zs_chunk(wsb, c0, ncol, dcol):
            for k in range(KC):
                S.op("pe", lambda e, k=k: e.matmul(PD[0:8, 0:ncol], lhsT=xsT_b[:, k, :], rhs=wsb[:, k, c0:c0 + ncol],
                                                   start=(k == 0), stop=(k == KC - 1)),
                     reads=[xsT_b, wsb], writes=[PD], pe_acc=(k > 0))
            S.op("act", lambda e: e.activation(out=zst[:, 0:ncol], in_=PD[0:8, 0:ncol], func=AF.Copy), reads=[PD], writes=[zst])
            S.dma("sp", lambda e: e.dma_start(out=zs_d[:, dcol:dcol + ncol], in_=zst[:, 0:ncol]), reads=[zst], writes=[zsd_b],
                  owner=zst)

        S.dma("sp", lambda e: e.dma_start(out=ident_f[:], in_=c_ident[:, :]), writes=[ident_f])
        S.dma("sp", lambda e: e.dma_start(out=tri2[:], in_=c_tri2[:, :]), writes=[tri2])
        S.dma("sp", lambda e: e.dma_start(out=blk2[:], in_=c_blk2[:, :]), writes=[blk2])
        S.dma("sp", lambda e: e.dma_start(out=posq_sb[:], in_=posq[:, :]), writes=[posq_sb])
        tmp_es = ExitStack()
        lb2 = sb(tmp_es, "lb2", [128, 2, 512], F32)
        S.dma("sp", [lambda e, l=l: e.dma_start(out=lb2[:, l, :], in_=lbl[l:l + 1, :].broadcast_to([128, 512]))
                     for l in range(2)], writes=[lb2])
        S.op("pool", lambda e: e.memset(ones_f[:], 1.0), writes=[ones_f])
        S.op("dve", lambda e: e.tensor_copy(out=ident_b[:], in_=ident_f[:]), reads=[ident_f], writes=[ident_b])
        S.op("dve", lambda e: e.tensor_tensor(out=oml_bc[:], in0=lb2[:, 0, :], in1=lb2[:, 1, :], op=ALU.subtract),
             reads=[lb2], writes=[oml_bc])
        S.op("act", lambda e: e.activation(out=lb_bc[:], in_=oml_bc[:], func=AF.Sigmoid), reads=[oml_bc], writes=[lb_bc])
        S.op("dve", lambda e: e.tensor_scalar(out=oml_bc[:], in0=lb_bc[:], scalar1=-1.0, scalar2=1.0,
                                              op0=ALU.mult, op1=ALU.add), reads=[lb_bc], writes=[oml_bc])
        S.op("dve", lambda e: e.tensor_copy(out=hflag[:], in_=posq_sb[:, NO:NO + 1]), reads=[posq_sb], writes=[hflag])
        S.op("pool", lambda e: e.memset(V[:, :, :, 128:129], 1.0), writes=[V])
        S.op("pool", lambda e: e.memset(mv[:, :, :, 128:129], 1.0), writes=[mv])
        S.barrier()
        S.emit()
        tmp_es.close()

        def rope(src3, dst3, nh, half, CC, SS, tA, tB, np_=128):
            r = 2 * half
            ccb = CC.unsqueeze(1).to_broadcast([np_, nh, r])
            s1b = SS[:, 0:half].unsqueeze(1).to_broadcast([np_, nh, half])
            s2b = SS[:, half:r].unsqueeze(1).to_broadcast([np_, nh, half])
            S.op("dve", lambda e: e.tensor_tensor(out=tA[0:np_, 0:nh, 0:r], in0=src3(0, r), in1=ccb, op=ALU.mult),
                 reads=src3.deps, writes=[tA])
            S.op("dve", lambda e: e.tensor_tensor(out=tB[0:np_, 0:nh, 0:half], in0=src3(half, r), in1=s1b, op=ALU.mult),
                 reads=src3.deps, writes=[tB])
            S.op("dve", lambda e: e.tensor_tensor(out=tB[0:np_, 0:nh, half:r], in0=src3(0, half), in1=s2b, op=ALU.mult),
                 reads=src3.deps + [tB], writes=[tB])
            S.op("dve", lambda e: e.tensor_tensor(out=dst3(0, r), in0=tA[0:np_, 0:nh, 0:r], in1=tB[0:np_, 0:nh, 0:r],
                                                  op=ALU.add), reads=[tA, tB], writes=dst3.deps)

        class V3:
            def __init__(self, fn, deps):
                self.fn = fn
                self.deps = deps

            def __call__(self, lo, hi):
                return self.fn(lo, hi)

        with ExitStack() as pm:
            memTb = sb(pm, "memTb", [128, KC, 256], BF16)
            wmk = sb(pm, "wmk", [128, KC, 512], BF16)
            wmv = sb(pm, "wmv", [128, KC, 512], BF16)
            mf = [sb(pm, "mf%d" % i, [128, 512], F32) for i in range(2)]
            mb = sb(pm, "mb", [128, 512], BF16)
            S.dma("pool", [lambda e, k=k: e.dma_start(out=memTb[:, k, :], in_=memT[k * 128:(k + 1) * 128, :])
                           for k in range(KC)], writes=[memTb])
            S.dma("pool", [lambda e, k=k: e.dma_start(out=wmk[:, k, :], in_=Wmk[k * 128:(k + 1) * 128, :])
                           for k in range(KC)], writes=[wmk])
            S.dma("pool", [lambda e, k=k: e.dma_start(out=wmv[:, k, :], in_=Wmv[k * 128:(k + 1) * 128, :])
                           for k in range(KC)], writes=[wmv])
            cnt = 0
            for nt in range(2):
                for which in range(2):
                    wsb = wmk if which == 0 else wmv
                    pz = PZ[cnt % 2]
                    f = mf[cnt % 2]
                    cnt += 1
                    for k in range(KC):
                        S.op("pe", lambda e, k=k, pz=pz, wsb=wsb, nt=nt: e.matmul(
                            pz[:], lhsT=memTb[:, k, nt * 128:(nt + 1) * 128], rhs=wsb[:, k, :],
                            start=(k == 0), stop=(k == KC - 1)), reads=[memTb, wsb], writes=[pz], pe_acc=(k > 0))
                    S.op("act", lambda e, pz=pz, f=f: e.activation(out=f[:], in_=pz[:], func=AF.Copy),
                         reads=[pz], writes=[f])
                    dst = o_mk if which == 0 else o_mv
                    S.dma("sp", lambda e, f=f, dst=dst, nt=nt: e.dma_start(out=dst[nt * 128:(nt + 1) * 128, :], in_=f[:]),
                          reads=[f])
                    outbufs.append(f)
                    if which == 0:
                        S.op("dve", lambda e, pz=pz: e.tensor_copy(out=mb[:], in_=pz[:]), reads=[pz], writes=[mb])
                        for hd in range(4):
                            S.op("pe", lambda e, hd=hd: e.transpose(out=PT[:, hd * 128:(hd + 1) * 128],
                                                                    in_=mb[:, hd * 128:(hd + 1) * 128],
                                                                    identity=ident_b[:]),
                                 reads=[mb, ident_b], writes=[PT])
                        S.op("act", lambda e, nt=nt: e.activation(
                            out=mkT[:, :, nt * 128:(nt + 1) * 128],
                            in_=PT[:, 0:512].rearrange("p (h t) -> p h t", h=4), func=AF.Copy),
                            reads=[PT], writes=[mkT])
                    else:
                        S.op("dve", lambda e, pz=pz, nt=nt: e.tensor_copy(
                            out=mv[:, nt, :, 0:128], in_=pz[:].rearrange("p (h d) -> p h d", h=4)),
                            reads=[pz], writes=[mv])
            S.barrier()
            S.emit()
        if stage == 1:
            return nc

        with ExitStack() as pn:
            xh = sb(pn, "xh", [128, KC, 1024], BF16)
            W0 = sb(pn, "W0", [128, KC, 1088], BF16)
            W1 = TLV(R16, lambda t: t[:].rearrange("p (k c) -> p k c", k=KC))
            ropeN_sb = sb(pn, "ropeN_sb", [128, NT, 96], F32)
            Kf = [sb(pn, "Kf%d" % i, [128, 512], F32) for i in range(2)]
            Kb = [sb(pn, "Kb%d" % i, [128, 512], BF16) for i in range(2)]
            tA = sb(pn, "tA", [128, 4, 32], F32)
            tB = sb(pn, "tB", [128, 4, 32], F32)
            sg = sb(pn, "sg", [128, 512], F32)
            ff = sg
            gg = sb(pn, "gg", [128, 512], F32)
            omf = sb(pn, "omf", [128, 512], F32)
            Gs = sb(pn, "Gs", [128, 512], F32)
            dlt = sb(pn, "dlt", [128, 512], F32)
            EE = dlt
            kdec = sb(pn, "kdec", [128, 512], BF16)
            vb = sb(pn, "vb", [128, 512], BF16)
            ikf = sb(pn, "ikf", [128, 64], F32)
            ikb = sb(pn, "ikb", [128, 64], BF16)
            Dc = sb(pn, "Dc", [128, 16], F32)
            St = sb(pn, "St", [128, 4, 128], F32)
            Sev = sb(pn, "Sev", [128, 512], F32)
            Stmp = sb(pn, "Stmp", [128, 512], F32)

            S.dma("sp", lambda e: e.dma_start(out=ropeN_sb[:], in_=ropeN.rearrange("(n p) c -> p n c", p=128)),
                  writes=[ropeN_sb])
            S.op("pool", lambda e: e.memset(St[:], 0.0), writes=[St])

            def load_w(dst, src, c0, ncol):
                S.dma("pool", [lambda e, k=k: e.dma_start(out=dst[:, k, 0:ncol], in_=src[k * 128:(k + 1) * 128, c0:c0 + ncol])
                               for k in range(KC)], writes=[dst])

            def mm_chunk(pz, tl, wsb, c0, ncol):
                for k in range(KC):
                    S.op("pe", lambda e, k=k: e.matmul(pz[:, 0:ncol], lhsT=xh[:, k, tl * 128:(tl + 1) * 128],
                                                       rhs=wsb[:, k, c0:c0 + ncol], start=(k == 0), stop=(k == KC - 1)),
                         reads=[xh, wsb], writes=[pz], pe_acc=(k > 0))

            ctr = [0]
            for half in range(2 if stage != 2 else 1):
                S.dma("pool", [lambda e, k=k, half=half: e.dma_start(out=xh[:, k, :],
                                                          in_=xTn[k * 128:(k + 1) * 128, half * 1024:(half + 1) * 1024])
                               for k in range(KC)], writes=[xh])
                for blk in range(2):
                    wsb = W0 if blk == 0 else W1
                    load_w(wsb, Wk, blk * 512, 512)
                    if half == 0:
                        zs_chunk(wsb, 0, 512, ZOFF["k"] + blk * 512)
                    for tl in range(8):
                        n = half * 8 + tl
                        i = ctr[0] % 2
                        ctr[0] += 1
                        pz, kf, kb = PZ[i], Kf[i], Kb[i]
                        mm_chunk(pz, tl, wsb, 0, 512)
                        S.op("act", lambda e, pz=pz, kf=kf: e.activation(out=kf[:], in_=pz[:], func=AF.Copy),
                             reads=[pz], writes=[kf])
                        src3 = V3(lambda lo, hi, pz=pz: pz[:].rearrange("p (h d) -> p h d", h=4)[:, :, lo:hi], [pz])
                        dst3 = V3(lambda lo, hi, kf=kf: kf[:].rearrange("p (h d) -> p h d", h=4)[:, :, lo:hi], [kf])
                        rope(src3, dst3, 4, 16, ropeN_sb[:, n, 0:32], ropeN_sb[:, n, 32:64], tA, tB)
                        S.dma("sp", lambda e, kf=kf, n=n, blk=blk: e.dma_start(
                            out=o_k[n * 128:(n + 1) * 128, blk * 512:(blk + 1) * 512], in_=kf[:]), reads=[kf])
                        S.op("pool", lambda e, kf=kf, kb=kb: e.tensor_copy(out=kb[:], in_=kf[:]), reads=[kf], writes=[kb])
                        for hd in range(4):
                            S.op("pe", lambda e, hd=hd, kb=kb: e.transpose(out=PT[:, hd * 128:(hd + 1) * 128],
                                                                           in_=kb[:, hd * 128:(hd + 1) * 128],
                                                                           identity=ident_b[:]),
                                 reads=[kb, ident_b], writes=[PT])
                        S.op("act", lambda e, n=n, blk=blk: e.activation(
                            out=KT[:, blk * 4:(blk + 1) * 4, n * 128:(n + 1) * 128],
                            in_=PT[:, 0:512].rearrange("p (h t) -> p h t", h=4), func=AF.Copy),
                            reads=[PT], writes=[KT])
                for blk in range(2):
                    wsb = W0 if blk == 0 else W1
                    load_w(wsb, Wv, blk * 512, 512)
                    if half == 0:
                        zs_chunk(wsb, 0, 512, ZOFF["v"] + blk * 512)
                    for tl in range(8):
                        n = half * 8 + tl
                        i = ctr[0] % 2
                        ctr[0] += 1
                        pz, kf = PZ[i], Kf[i]
                        mm_chunk(pz, tl, wsb, 0, 512)
                        S.op("act", lambda e, pz=pz, kf=kf: e.activation(out=kf[:], in_=pz[:], func=AF.Copy),
                             reads=[pz], writes=[kf])
                        S.dma("sp", lambda e, kf=kf, n=n, blk=blk: e.dma_start(
                            out=o_v[n * 128:(n + 1) * 128, blk * 512:(blk + 1) * 512], in_=kf[:]), reads=[kf])
                        S.op("dve", lambda e, pz=pz, n=n, blk=blk: e.tensor_copy(
                            out=V[:, n, blk * 4:(blk + 1) * 4, 0:128], in_=pz[:].rearrange("p (h d) -> p h d", h=4)),
                            reads=[pz], writes=[V])
                load_w(W0, W3, 0, 1088)
                if half == 0:
                    zs_chunk(W0, 1024, 64, ZOFF["ik"])
                for tl in range(8):
                    n = half * 8 + tl
                    pza, pzb = PZ[0], PZ[1]
                    mm_chunk(pza, tl, W0, 0, 512)
                    S.op("act", lambda e: e.activation(out=sg[:], in_=pza[:], func=AF.Sigmoid), reads=[pza], writes=[sg])
                    mm_chunk(pzb, tl, W0, 512, 512)
                    S.op("act", lambda e: e.activation(out=vb[:], in_=pzb[:], func=AF.Copy), reads=[pzb], writes=[vb])
                    mm_chunk(pza, tl, W0, 1024, 64)
                    S.op("act", lambda e: e.activation(out=ikf[:], in_=pza[:, 0:64], func=AF.Copy), reads=[pza], writes=[ikf])
                    src3 = V3(lambda lo, hi: pza[:, 0:64].rearrange("p (h d) -> p h d", h=1)[:, :, lo:hi], [pza])
                    dst3 = V3(lambda lo, hi: ikf[:].rearrange("p (h d) -> p h d", h=1)[:, :, lo:hi], [ikf])
                    rope(src3, dst3, 1, 8, ropeN_sb[:, n, 64:80], ropeN_sb[:, n, 80:96], tA, tB)
                    S.dma("sp", lambda e, n=n: e.dma_start(out=o_ik[n * 128:(n + 1) * 128, :], in_=ikf[:]), reads=[ikf])
                    S.op("pool", lambda e: e.tensor_copy(out=ikb[:], in_=ikf[:]), reads=[ikf], writes=[ikb])
                    S.op("pe", lambda e: e.transpose(out=PT[0:64, 512:640], in_=ikb[:], identity=ident_b[:]),
                         reads=[ikb, ident_b], writes=[PT])
                    S.op("act", lambda e, n=n: e.activation(out=ikT[:, n * 128:(n + 1) * 128], in_=PT[0:64, 512:640],
                                                            func=AF.Copy), reads=[PT], writes=[ikT])
                    S.op("dve", lambda e: e.tensor_tensor(out=ff[:], in0=sg[:], in1=oml_bc[:], op=ALU.mult),
                         reads=[sg, oml_bc], writes=[sg])
                    S.op("dve", lambda e: e.tensor_tensor(out=ff[:], in0=ff[:], in1=lb_bc[:], op=ALU.add),
                         reads=[ff, lb_bc], writes=[ff])
                    S.op("act", lambda e: e.activation(out=gg[:], in_=ff[:], func=AF.Ln), reads=[ff], writes=[gg])
                    S.op("pool", lambda e: e.tensor_scalar(out=omf[:], in0=ff[:], scalar1=-1.0, scalar2=1.0,
                                                           op0=ALU.mult, op1=ALU.add), reads=[ff], writes=[omf])
                    S.op("pe", lambda e: e.matmul(PG[:], lhsT=tri2[:], rhs=gg[:], start=True, stop=True),
                         reads=[tri2, gg], writes=[PG])
                    S.op("pe", lambda e: e.matmul(PGL[:], lhsT=blk2[:], rhs=gg[:], start=True, stop=True),
                         reads=[blk2, gg], writes=[PGL])
                    S.op("act", lambda e: e.activation(out=Gs[:], in_=PG[:], func=AF.Copy), reads=[PG], writes=[Gs])
                    S.op("dve", lambda e: e.tensor_tensor(out=dlt[:], in0=PGL[:], in1=Gs[:], op=ALU.subtract),
                         reads=[PGL, Gs], writes=[dlt])
                    S.op("act", lambda e: e.activation(out=EE[:], in_=dlt[:], func=AF.Exp), reads=[dlt], writes=[EE])
                    S.op("dve", lambda e: e.tensor_tensor(out=kdec[:], in0=omf[:], in1=EE[:], op=ALU.mult),
                         reads=[omf, EE], writes=[kdec])
                    for c in range(2):
                        for hd in range(4):
                            j = c * 4 + hd
                            S.op("pe", lambda e, c=c, hd=hd, j=j: e.matmul(
                                PD[:, 2 * j:2 * j + 2], lhsT=gg[c * 64:(c + 1) * 64, hd * 128:(hd + 1) * 128],
                                rhs=ones_f[c * 64:(c + 1) * 64, :], start=True, stop=True),
                                reads=[gg, ones_f], writes=[PD])
                    S.op("act", lambda e: e.activation(out=Dc[:], in_=PD[:, 0:16], func=AF.Exp), reads=[PD], writes=[Dc])
                    if n % 2 == 0:
                        S.op("pool", lambda e: e.tensor_copy(out=Sev[:], in_=St[:].rearrange("p h d -> p (h d)")),
                             reads=[St], writes=[Sev])
                    else:
                        S.op("dve", lambda e: e.tensor_tensor(out=Stmp[:], in0=St[:].rearrange("p h d -> p (h d)"),
                                                              in1=Sev[:], op=ALU.subtract), reads=[St, Sev], writes=[Stmp])
                        S.op("dve", lambda e, n=n: e.scalar_tensor_tensor(
                            out=Sown[:, n // 2, :], in0=Stmp[:], scalar=hflag[:, 0:1], in1=Sev[:],
                            op0=ALU.mult, op1=ALU.add), reads=[Stmp, hflag, Sev], writes=[Sown])
                    for c in range(2):
                        pu = PU[c]
                        for hd in range(4):
                            S.op("pe", lambda e, c=c, hd=hd, pu=pu: e.matmul(
                                pu[:, hd * 128:(hd + 1) * 128], lhsT=kdec[c * 64:(c + 1) * 64, hd * 128:(hd + 1) * 128],
                                rhs=vb[c * 64:(c + 1) * 64, hd * 128:(hd + 1) * 128], start=True, stop=True),
                                reads=[kdec, vb], writes=[pu])
                        for hd in range(4):
                            j = c * 4 + hd
                            S.op("dve", lambda e, hd=hd, j=j, pu=pu: e.scalar_tensor_tensor(
                                out=St[:, hd, :], in0=St[:, hd, :], scalar=Dc[:, 2 * j:2 * j + 1],
                                in1=pu[:, hd * 128:(hd + 1) * 128], op0=ALU.mult, op1=ALU.add),
                                reads=[St, Dc, pu], writes=[St])
            S.dma("sp", lambda e: e.dma_start(out=o_hg[:, :, :], in_=St[:]), reads=[St])
            S.barrier()
            S.emit()
        if stage <= 3:
            return nc
        QS = 128.0 ** -0.5
        KB = 24
        PA = [PG, PGL]
        PI = PU
        moff = [sum(2 * j + 2 for j in range(i)) for i in range(NO)]

        def load_wres(dst, src, ncol):
            S.dma("pool", [lambda e, k=k: e.dma_start(out=dst[:, k, 0:ncol], in_=src[k * 128:(k + 1) * 128, 0:ncol])
                           for k in range(KC)], writes=[dst])

        def load_xt(dst, i):
            S.dma("pool", lambda e: e.dma_start(out=dst[:], in_=xTo[:, i * 128:(i + 1) * 128].rearrange("(k p) t -> p k t", p=128)),
                  writes=[dst])

        def mm_x(pz, xt_, wsb, c0, ncol):
            for k in range(KC):
                S.op("pe", lambda e, k=k: e.matmul(pz[:, 0:ncol], lhsT=xt_[:, k, :], rhs=wsb[:, k, c0:c0 + ncol],
                                                   start=(k == 0), stop=(k == KC - 1)),
                     reads=[xt_, wsb], writes=[pz], pe_acc=(k > 0))

        def transpose4(src_fn, srcdeps, col0=0, n=4):
            for hd in range(n):
                S.op("pe", lambda e, hd=hd: e.transpose(out=PT[:, col0 + hd * 128:col0 + (hd + 1) * 128], in_=src_fn(hd),
                                                        identity=ident_b[:]), reads=srcdeps + [ident_b], writes=[PT])

        es_B = ExitStack()
        maskT = sb(es_B, "maskT", [128, 72, 128], BF16)
        ropeO_sb = sb(es_B, "ropeO_sb", [128, NO, 96], F32)
        S.dma("sp", lambda e: e.dma_start(out=ropeO_sb[:], in_=ropeO.rearrange("(n p) c -> p n c", p=128)), writes=[ropeO_sb])

        with ExitStack() as p1:
            Wi_sb = sb(p1, "Wi_sb", [128, KC, 1040], BF16)
            xt = [sb(p1, "xt1_%d" % j, [128, KC, 128], BF16) for j in range(2)]
            cb = sb(p1, "cb", [128, 256], F32)
            cj = sb(p1, "cj", [128, 256], F32)
            pw = sb(p1, "pw", [128, 2 * KB], F32)
            spw = sb(p1, "spw", [128, 2 * KB], F32)
            iq_b = sb(p1, "iq_b", [128, 16, 64], BF16)
            w_s = sb(p1, "w_s", [128, 16], F32)
            iqT = sb(p1, "iqT", [64, 16, 128], BF16)
            Dg = sb(p1, "Dg", [128, 16, 128], BF16)
            Tb = [sb(p1, "Tb%d" % j, [128, 512], BF16) for j in range(2)]
            I_sb = sb(p1, "I_sb", [128, 2048], F32)
            junk = sb(p1, "junk", [128, 2048], BF16)
            tA1 = sb(p1, "tA1", [128, 8, 16], F32)
            tB1 = sb(p1, "tB1", [128, 8, 16], F32)
            sm = sb(p1, "sm", [128, 8], F32)
            lo, hi, rng, mid, cnt, wv, thr0 = [sm[:, j:j + 1] for j in range(7)]
            load_wres(Wi_sb, Wi, 1040)
            zs_chunk(Wi_sb, 0, 512, ZOFF["iq"])
            zs_chunk(Wi_sb, 512, 512, ZOFF["iq"] + 512)
            zs_chunk(Wi_sb, 1024, 16, ZOFF["iw"])
            S.dma("sp", lambda e: e.dma_start(out=cj[:], in_=c_j[:, :]), writes=[cj])
            S.dma("sp", lambda e: e.dma_start(out=pw[:], in_=c_pow[:, :]), writes=[pw])
            S.op("dve", lambda e: e.tensor_scalar(out=cb[:], in0=cj[:], scalar1=posq_sb[:, 0:1], scalar2=NEG,
                                                  op0=ALU.is_gt, op1=ALU.mult), reads=[cj, posq_sb], writes=[cb])
            S.op("pool", lambda e: e.memset(sm[:, 6:7], -1.0e29), writes=[sm])
            for i in range(NO):
                nk = (2 * i + 2) * 128
                x = xt[i % 2]
                load_xt(x, i)
                for c in range(2):
                    pz = PZ[c]
                    mm_x(pz, x, Wi_sb, c * 512, 512)
                    S.op("act", lambda e, pz=pz, c=c: e.activation(
                        out=iq_b[:, c * 8:(c + 1) * 8, :], in_=pz[:].rearrange("p (h d) -> p h d", h=8), func=AF.Copy),
                        reads=[pz], writes=[iq_b])
                    src3 = V3(lambda lo_, hi_, pz=pz: pz[:].rearrange("p (h d) -> p h d", h=8)[:, :, lo_:hi_], [pz])
                    dst3 = V3(lambda lo_, hi_, c=c: iq_b[:, c * 8:(c + 1) * 8, lo_:hi_], [iq_b])
                    rope(src3, dst3, 8, 8, ropeO_sb[:, i, 64:80], ropeO_sb[:, i, 80:96], tA1, tB1)
                pz = PZ[0]
                mm_x(pz, x, Wi_sb, 1024, 16)
                S.op("act", lambda e, pz=pz: e.activation(out=w_s[:], in_=pz[:, 0:16], func=AF.Copy, scale=1.0 / 32.0),
                     reads=[pz], writes=[w_s])
                for r in range(2):
                    for hh in range(8):
                        S.op("pe", lambda e, r=r, hh=hh: e.transpose(out=PT[0:64, hh * 128:(hh + 1) * 128],
                                                                     in_=iq_b[:, r * 8 + hh, :], identity=ident_b[:]),
                             reads=[iq_b, ident_b], writes=[PT])
                    S.op("act", lambda e, r=r: e.activation(out=iqT[:, r * 8:(r + 1) * 8, :],
                                                            in_=PT[0:64, :].rearrange("p (h t) -> p h t", h=8), func=AF.Copy),
                         reads=[PT], writes=[iqT])
                for h in range(16):
                    S.op("pool", lambda e, h=h: e.tensor_scalar(out=Dg[:, h, :], in0=ident_b[:], scalar1=w_s[:, h:h + 1],
                                                                scalar2=None, op0=ALU.mult),
                         reads=[ident_b, w_s], writes=[Dg])
                nch = (nk + 511) // 512
                lastlo = nk - 256
                for c in range(nch):
                    kw = min(512, nk - 512 * c)
                    pi = PI[c % 2]

                    def mm1(h, c=c, kw=kw):
                        pa = PA[h % 2]
                        S.op("pe", lambda e, h=h, pa=pa: e.matmul(pa[:, 0:kw], lhsT=iqT[:, h, :],
                                                                  rhs=ikT[:, c * 512:c * 512 + kw], start=True, stop=True),
                             reads=[iqT, ikT], writes=[pa])
                    mm1(0)
                    for h in range(16):
                        if h + 1 < 16:
                            mm1(h + 1)
                        pa = PA[h % 2]
                        tb = Tb[h % 2]
                        S.op("act", lambda e, pa=pa, tb=tb, kw=kw: e.activation(out=tb[:, 0:kw], in_=pa[:, 0:kw], func=AF.Relu),
                             reads=[pa], writes=[tb])
                        S.op("pe", lambda e, h=h, tb=tb, pi=pi, kw=kw: e.matmul(pi[:, 0:kw], lhsT=Dg[:, h, :], rhs=tb[:, 0:kw],
                                                                                start=(h == 0), stop=(h == 15)),
                             reads=[Dg, tb], writes=[pi], pe_acc=(h > 0))
                    a0 = 512 * c
                    a1 = a0 + kw
                    if a1 <= lastlo:
                        S.op("dve", lambda e, pi=pi, a0=a0, a1=a1, kw=kw: e.tensor_copy(out=I_sb[:, a0:a1], in_=pi[:, 0:kw]),
                             reads=[pi], writes=[I_sb])
                    else:
                        if a0 < lastlo:
                            S.op("dve", lambda e, pi=pi, a0=a0, lastlo=lastlo: e.tensor_copy(
                                out=I_sb[:, a0:lastlo], in_=pi[:, 0:lastlo - a0]), reads=[pi], writes=[I_sb])
                        S.op("dve", lambda e, pi=pi, a0=a0, lastlo=lastlo, kw=kw, nk=nk: e.tensor_tensor(
                            out=I_sb[:, lastlo:nk], in0=pi[:, lastlo - a0:kw], in1=cb[:], op=ALU.add),
                            reads=[pi, cb], writes=[I_sb])
                if i >= 1:
                    S.op("dve", lambda e, nk=nk: e.tensor_reduce(out=lo, in_=I_sb[:, 0:nk - 256], axis=AX.X, op=ALU.min),
                         reads=[I_sb], writes=[sm])
                    S.op("dve", lambda e, nk=nk: e.tensor_reduce(out=hi, in_=I_sb[:, 0:nk], axis=AX.X, op=ALU.max),
                         reads=[I_sb], writes=[sm])
                    S.op("dve", lambda e: e.tensor_tensor(out=rng, in0=hi, in1=lo, op=ALU.subtract), reads=[sm], writes=[sm])
                    S.op("dve", lambda e: e.tensor_scalar(out=spw[:], in0=pw[:], scalar1=rng, scalar2=None, op0=ALU.mult),
                         reads=[pw, sm], writes=[spw])
                    S.op("dve", lambda e: e.tensor_tensor(out=mid, in0=lo, in1=spw[:, 0:1], op=ALU.add),
                         reads=[sm, spw], writes=[sm])
                    for k in range(KB):
                        S.op("dve", lambda e, nk=nk: e.tensor_scalar(out=junk[:, 0:nk], in0=I_sb[:, 0:nk], scalar1=mid,
                                                                     scalar2=0.0, op0=ALU.is_ge, op1=ALU.add, accum_out=cnt),
                             reads=[I_sb, sm], writes=[junk, sm])
                        S.op("dve", lambda e, k=k: e.scalar_tensor_tensor(out=wv, in0=cnt, scalar=256.0, in1=spw[:, k:k + 1],
                                                                          op0=ALU.is_ge, op1=ALU.mult),
                             reads=[sm, spw], writes=[sm])
                        S.op("dve", lambda e, k=k: e.scalar_tensor_tensor(out=mid, in0=mid, scalar=spw[:, KB + k:KB + k + 1],
                                                                          in1=wv, op0=ALU.subtract, op1=ALU.add),
                             reads=[sm, spw], writes=[sm])
                    thr = mid
                else:
                    thr = thr0
                S.op("dve", lambda e, nk=nk, thr=thr: e.tensor_scalar(out=junk[:, 0:nk], in0=I_sb[:, 0:nk], scalar1=thr,
                                                                      scalar2=None, op0=ALU.is_ge),
                     reads=[I_sb, sm], writes=[junk])
                nb = 2 * i + 2
                for g0 in range(0, nb, 8):
                    gn = min(8, nb - g0)
                    for j in range(gn):
                        S.op("pe", lambda e, j=j, g0=g0: e.transpose(out=PT[:, j * 128:(j + 1) * 128],
                                                                     in_=junk[:, (g0 + j) * 128:(g0 + j + 1) * 128],
                                                                     identity=ident_b[:]),
                             reads=[junk, ident_b], writes=[PT])
                    S.op("act", lambda e, i=i, g0=g0, gn=gn: e.activation(
                        out=maskT[:, moff[i] + g0:moff[i] + g0 + gn, :],
                        in_=PT[:, 0:gn * 128].rearrange("p (h t) -> p h t", h=gn), func=AF.Copy),
                        reads=[PT], writes=[maskT])
            S.barrier()
            S.emit()

        es_B2 = ExitStack()
        qT = sb(es_B2, "qT", [128, 8, 1024], BF16)
        with ExitStack() as p2:
            Wq_sb = sb(p2, "Wq_sb", [128, KC, 1024], BF16)
            xt = [sb(p2, "xt2_%d" % j, [128, KC, 128], BF16) for j in range(2)]
            q_b = sb(p2, "q_b", [128, 1024], BF16)
            tA2 = sb(p2, "tA2", [128, 4, 32], F32)
            tB2 = sb(p2, "tB2", [128, 4, 32], F32)
            load_wres(Wq_sb, Wq, 1024)
            zs_chunk(Wq_sb, 0, 512, ZOFF["q"])
            zs_chunk(Wq_sb, 512, 512, ZOFF["q"] + 512)
            for i in range(NO):
                x = xt[i % 2]
                load_xt(x, i)
                for c in range(2):
                    pz = PZ[c]
                    mm_x(pz, x, Wq_sb, c * 512, 512)
                    S.op("act", lambda e, pz=pz, c=c: e.activation(out=q_b[:, c * 512:(c + 1) * 512], in_=pz[:], func=AF.Copy,
                                                                   scale=QS), reads=[pz], writes=[q_b])
                    src3 = V3(lambda lo_, hi_, pz=pz: pz[:].rearrange("p (h d) -> p h d", h=4)[:, :, lo_:hi_], [pz])
                    dst3 = V3(lambda lo_, hi_, c=c: q_b[:, c * 512:(c + 1) * 512].rearrange("p (h d) -> p h d", h=4)[:, :, lo_:hi_],
                              [q_b])
                    rope(src3, dst3, 4, 16, ropeO_sb[:, i, 0:32], ropeO_sb[:, i, 32:64], tA2, tB2)
                    transpose4(lambda hd, c=c: q_b[:, c * 512 + hd * 128:c * 512 + (hd + 1) * 128], [q_b])
                    S.op("act", lambda e, c=c, i=i: e.activation(out=qT[:, c * 4:(c + 1) * 4, i * 128:(i + 1) * 128],
                                                                 in_=PT[:, 0:512].rearrange("p (h t) -> p h t", h=4), func=AF.Copy),
                         reads=[PT], writes=[qT])
            S.barrier()
            S.emit()

        a_out = TLV(R16, lambda t: t[:].rearrange("p (i c) -> p i c", i=NO))
        with ExitStack() as p3:
            Eb = [sb(p3, "Eb%d" % j, [128, 4, 128], BF16) for j in range(2)]
            Pb = [sb(p3, "Pb%d" % j, [128, 4, 128], BF16) for j in range(2)]
            rden = sb(p3, "rden", [128, 8], F32)
            PSb = [PG, PGL]
            PO = [PU[0], PU[1], PD]
            for i in range(NO):
                nb = 2 * i + 2
                groups = [(j, g) for j in range(nb) for g in range(2)]
                first = [True, True, True]

                def st_mm(n, i=i):
                    j, g = groups[n]
                    ps_ = PSb[n % 2]
                    for hh in range(4):
                        hd = g * 4 + hh
                        S.op("pe", lambda e, hh=hh, hd=hd, j=j, ps_=ps_: e.matmul(
                            ps_[:, hh * 128:(hh + 1) * 128], lhsT=KT[:, hd, j * 128:(j + 1) * 128],
                            rhs=qT[:, hd, i * 128:(i + 1) * 128], start=True, stop=True),
                            reads=[KT, qT], writes=[ps_])
                st_mm(0)
                for n in range(len(groups)):
                    j, g = groups[n]
                    if n + 1 < len(groups):
                        st_mm(n + 1)
                    ps_, eb, pb = PSb[n % 2], Eb[n % 2], Pb[n % 2]
                    S.op("act", lambda e, ps_=ps_, eb=eb: e.activation(out=eb[:].rearrange("p h t -> p (h t)"), in_=ps_[:],
                                                                       func=AF.Exp), reads=[ps_], writes=[eb])
                    S.op("pool", lambda e, eb=eb, pb=pb, i=i, j=j: e.tensor_tensor(
                        out=pb[:], in0=eb[:], in1=maskT[:, moff[i] + j, :].unsqueeze(1).to_broadcast([128, 4, 128]),
                        op=ALU.mult), reads=[eb, maskT], writes=[pb])
                    for hh in range(4):
                        hd = g * 4 + hh
                        bank = hd // 3
                        col = (hd % 3) * 130
                        st = first[bank]
                        first[bank] = False
                        S.op("pe", lambda e, hh=hh, hd=hd, j=j, pb=pb, bank=bank, col=col, st=st, nb=nb: e.matmul(
                            PO[bank][:, col:col + 130], lhsT=pb[:, hh, :], rhs=V[:, j, hd, :],
                            start=st, stop=(j == nb - 1), skip_group_check=True),
                            reads=[pb, V], writes=[PO[bank]], pe_acc=(not st))
                for bank in range(3):
                    nh = 3 if bank < 2 else 2
                    S.op("dve", lambda e, bank=bank, nh=nh: e.reciprocal(
                        out=rden[:, bank * 3:bank * 3 + nh],
                        in_=PO[bank][:, 0:nh * 130].rearrange("p (h c) -> p h c", c=130)[:, :, 128]),
                        reads=[PO[bank]], writes=[rden])
                    S.op("dve", lambda e, bank=bank, nh=nh, i=i: e.tensor_tensor(
                        out=a_out[:, i, bank * 384:bank * 384 + nh * 128].rearrange("p (h d) -> p h d", h=nh),
                        in0=PO[bank][:, 0:nh * 130].rearrange("p (h c) -> p h c", c=130)[:, :, 0:128],
                        in1=rden[:, bank * 3:bank * 3 + nh].unsqueeze(2).to_broadcast([128, nh, 128]), op=ALU.mult),
                        reads=[PO[bank], rden], writes=[a_out])
            S.barrier()
            S.emit()
        es_B2.close()
        es_B.close()
        es_A.close()
        if stage <= 4:
            return nc

        es_C = ExitStack()
        catT = sb(es_C, "catT", [128, KC, 1024], BF16)

        with ExitStack() as p4:
            Wg_sb = sb(p4, "Wg_sb", [128, KC, 1024], BF16)
            xt = [sb(p4, "xt4_%d" % j, [128, KC, 128], BF16) for j in range(2)]
            ga = [sb(p4, "ga%d" % j, [128, 512], F32) for j in range(2)]
            cab = [sb(p4, "cab%d" % j, [128, 512], BF16) for j in range(2)]
            load_wres(Wg_sb, Wg, 1024)
            zs_chunk(Wg_sb, 0, 512, ZOFF["g"])
            zs_chunk(Wg_sb, 512, 512, ZOFF["g"] + 512)
            for i in range(NO):
                x = xt[i % 2]
                load_xt(x, i)
                for c in range(2):
                    pz, g_, cb_ = PZ[c], ga[c], cab[c]
                    mm_x(pz, x, Wg_sb, c * 512, 512)
                    S.op("act", lambda e, pz=pz, g_=g_: e.activation(out=g_[:], in_=pz[:], func=AF.Silu), reads=[pz], writes=[g_])
                    S.op("dve", lambda e, g_=g_, cb_=cb_, i=i, c=c: e.tensor_tensor(
                        out=cb_[:], in0=a_out[:, i, c * 512:(c + 1) * 512], in1=g_[:], op=ALU.mult),
                        reads=[a_out, g_], writes=[cb_])
                    transpose4(lambda hd, cb_=cb_: cb_[:, hd * 128:(hd + 1) * 128], [cb_])
                    S.op("act", lambda e, c=c, i=i: e.activation(out=catT[:, c * 4:(c + 1) * 4, i * 128:(i + 1) * 128],
                                                                 in_=PT[:, 0:512].rearrange("p (h t) -> p h t", h=4), func=AF.Copy),
                         reads=[PT], writes=[catT])
            S.barrier()
            S.emit()

        with ExitStack() as p5:
            Wb_sb = sb(p5, "Wb_sb", [128, KC, 2048], BF16)
            xt = [sb(p5, "xt5_%d" % j, [128, KC, 128], BF16) for j in range(2)]
            ng_bc = sb(p5, "ng_bc", [128, 4, 128], F32)
            qs = sb(p5, "qs", [128, 512], F32)
            sg = sb(p5, "sg5", [128, 512], F32)
            gg = sb(p5, "gg5", [128, 512], F32)
            omf = sb(p5, "omf5", [128, 512], F32)
            Gs = sb(p5, "Gs5", [128, 512], F32)
            dlt = sb(p5, "dlt5", [128, 512], F32)
            eG = sb(p5, "eG", [128, 512], F32)
            enG = sb(p5, "enG", [128, 512], F32)
            gb = sb(p5, "gb5", [128, 512], F32)
            t1 = sb(p5, "t15", [128, 512], F32)
            kdec = sb(p5, "kdec5", [128, 512], BF16)
            vb = sb(p5, "vb5", [128, 512], BF16)
            kg = sb(p5, "kg", [128, 512], BF16)
            qg = sb(p5, "qg", [128, 512], BF16)
            qgT = sb(p5, "qgT", [128, 4, 128], BF16)
            qgT0 = sb(p5, "qgT0", [128, 4, 128], BF16)
            qgT1 = sb(p5, "qgT1", [128, 4, 128], BF16)
            kgT = sb(p5, "kgT", [128, 4, 128], BF16)
            ATb = sb(p5, "ATb", [128, 4, 128], BF16)
            S1b = sb(p5, "S1b", [128, 4, 128], BF16)
            cbb = sb(p5, "cbb", [128, 512], BF16)
            Dc0 = sb(p5, "Dc0", [128, 8], F32)
            ss = sb(p5, "ss5", [128, 4], F32)
            rstd = sb(p5, "rstd5", [128, 4], F32)
            jk5 = sb(p5, "jk5", [128, 128], F32)
            load_wres(Wb_sb, Wb, 2048)
            for o_ in range(4):
                zs_chunk(Wb_sb, o_ * 512, 512, ZOFF["b"] + o_ * 512)
            S.dma("sp", [lambda e, hd=hd: e.dma_start(out=ng_bc[:, hd, :], in_=ng[0:1, :].broadcast_to([128, 128]))
                         for hd in range(4)], writes=[ng_bc])
            S.op("pool", lambda e: e.memset(qgT0[:], 0.0), writes=[qgT0])
            S.op("pool", lambda e: e.memset(qgT1[:], 0.0), writes=[qgT1])
            for i in range(NO):
                x = xt[i % 2]
                load_xt(x, i)
                mm_x(PZ[0], x, Wb_sb, 0, 512)
                S.op("act", lambda e: e.activation(out=qs[:], in_=PZ[0][:], func=AF.Silu), reads=[PZ[0]], writes=[qs])
                mm_x(PZ[1], x, Wb_sb, 512, 512)
                S.op("act", lambda e: e.activation(out=sg[:], in_=PZ[1][:], func=AF.Sigmoid), reads=[PZ[1]], writes=[sg])
                mm_x(PZ[0], x, Wb_sb, 1024, 512)
                S.op("act", lambda e: e.activation(out=vb[:], in_=PZ[0][:], func=AF.Copy), reads=[PZ[0]], writes=[vb])
                mm_x(PZ[1], x, Wb_sb, 1536, 512)
                S.op("act", lambda e: e.activation(out=gb[:], in_=PZ[1][:], func=AF.Silu), reads=[PZ[1]], writes=[gb])
                S.op("dve", lambda e: e.tensor_tensor(out=sg[:], in0=sg[:], in1=oml_bc[:], op=ALU.mult),
                     reads=[sg, oml_bc], writes=[sg])
                S.op("dve", lambda e: e.tensor_tensor(out=sg[:], in0=sg[:], in1=lb_bc[:], op=ALU.add),
                     reads=[sg, lb_bc], writes=[sg])
                S.op("act", lambda e: e.activation(out=gg[:], in_=sg[:], func=AF.Ln), reads=[sg], writes=[gg])
                S.op("pool", lambda e: e.tensor_scalar(out=omf[:], in0=sg[:], scalar1=-1.0, scalar2=1.0,
                                                       op0=ALU.mult, op1=ALU.add), reads=[sg], writes=[omf])
                S.op("pe", lambda e: e.matmul(PG[:], lhsT=tri2[:], rhs=gg[:], start=True, stop=True),
                     reads=[tri2, gg], writes=[PG])
                S.op("pe", lambda e: e.matmul(PGL[:], lhsT=blk2[:], rhs=gg[:], start=True, stop=True),
                     reads=[blk2, gg], writes=[PGL])
                S.op("act", lambda e: e.activation(out=Gs[:], in_=PG[:], func=AF.Copy), reads=[PG], writes=[Gs])
                S.op("dve", lambda e: e.tensor_tensor(out=dlt[:], in0=PGL[:], in1=Gs[:], op=ALU.subtract),
                     reads=[PGL, Gs], writes=[dlt])
                S.op("act", lambda e: e.activation(out=dlt[:], in_=dlt[:], func=AF.Exp), reads=[dlt], writes=[dlt])
                S.op("dve", lambda e: e.tensor_tensor(out=kdec[:], in0=omf[:], in1=dlt[:], op=ALU.mult),
                     reads=[omf, dlt], writes=[kdec])
                S.op("act", lambda e: e.activation(out=eG[:], in_=Gs[:], func=AF.Exp), reads=[Gs], writes=[eG])
                S.op("act", lambda e: e.activation(out=enG[:], in_=Gs[:], func=AF.Exp, scale=-1.0), reads=[Gs], writes=[enG])
                S.op("dve", lambda e: e.tensor_tensor(out=qg[:], in0=qs[:], in1=eG[:], op=ALU.mult), reads=[qs, eG], writes=[qg])
                S.op("pool", lambda e: e.tensor_tensor(out=kg[:], in0=omf[:], in1=enG[:], op=ALU.mult),
                     reads=[omf, enG], writes=[kg])
                for hd in range(4):
                    S.op("pe", lambda e, hd=hd: e.matmul(PU[0][:, hd * 128:(hd + 1) * 128],
                                                         lhsT=kdec[0:64, hd * 128:(hd + 1) * 128],
                                                         rhs=vb[0:64, hd * 128:(hd + 1) * 128], start=True, stop=True),
                         reads=[kdec, vb], writes=[PU[0]])
                for hd in range(4):
                    S.op("pe", lambda e, hd=hd: e.matmul(PD[:, 2 * hd:2 * hd + 2], lhsT=gg[0:64, hd * 128:(hd + 1) * 128],
                                                         rhs=ones_f[0:64, :], start=True, stop=True),
                         reads=[gg, ones_f], writes=[PD])
                S.op("act", lambda e: e.activation(out=Dc0[:], in_=PD[:, 0:8], func=AF.Exp), reads=[PD], writes=[Dc0])
                for hd in range(4):
                    S.op("dve", lambda e, hd=hd, i=i: e.scalar_tensor_tensor(
                        out=S1b[:, hd, :], in0=Sown[:, i, hd * 128:(hd + 1) * 128], scalar=Dc0[:, 2 * hd:2 * hd + 1],
                        in1=PU[0][:, hd * 128:(hd + 1) * 128], op0=ALU.mult, op1=ALU.add),
                        reads=[Sown, Dc0, PU[0]], writes=[S1b])
                transpose4(lambda hd: qg[:, hd * 128:(hd + 1) * 128], [qg])
                transpose4(lambda hd: kg[:, hd * 128:(hd + 1) * 128], [kg], col0=512)
                ptq = lambda: PT[:, 0:512].rearrange("p (h t) -> p h t", h=4)
                S.op("act", lambda e: e.activation(out=qgT[:], in_=ptq(), func=AF.Copy), reads=[PT], writes=[qgT])
                S.op("dve", lambda e: e.tensor_copy(out=qgT0[:, :, 0:64], in_=ptq()[:, :, 0:64]), reads=[PT], writes=[qgT0])
                S.op("dve", lambda e: e.tensor_copy(out=qgT1[:, :, 64:128], in_=ptq()[:, :, 64:128]), reads=[PT], writes=[qgT1])
                S.op("act", lambda e: e.activation(out=kgT[:], in_=PT[:, 512:1024].rearrange("p (h t) -> p h t", h=4),
                                                   func=AF.Copy), reads=[PT], writes=[kgT])
                for hd in range(4):
                    S.op("pe", lambda e, hd=hd: e.matmul(PU[1][:, hd * 128:(hd + 1) * 128], lhsT=kgT[:, hd, :], rhs=qgT[:, hd, :],
                                                         start=True, stop=True), reads=[kgT, qgT], writes=[PU[1]])
                S.op("dve", lambda e: e.tensor_tensor(out=ATb[:], in0=PU[1][:].rearrange("p (h t) -> p h t", h=4),
                                                      in1=tri2[:].unsqueeze(1).to_broadcast([128, 4, 128]), op=ALU.mult),
                     reads=[PU[1], tri2], writes=[ATb])
                for hd in range(4):
                    cs_ = slice(hd * 128, (hd + 1) * 128)
                    S.op("pe", lambda e, hd=hd, cs_=cs_: e.matmul(PG[:, cs_], lhsT=ATb[:, hd, :], rhs=vb[:, cs_],
                                                                  start=True, stop=False), reads=[ATb, vb], writes=[PG])
                    S.op("pe", lambda e, hd=hd, cs_=cs_, i=i: e.matmul(PG[:, cs_], lhsT=qgT0[:, hd, :], rhs=Sown[:, i, cs_],
                                                                       start=False, stop=False),
                         reads=[qgT0, Sown], writes=[PG], pe_acc=True)
                    S.op("pe", lambda e, hd=hd, cs_=cs_: e.matmul(PG[:, cs_], lhsT=qgT1[:, hd, :], rhs=S1b[:, hd, :],
                                                                  start=False, stop=True),
                         reads=[qgT1, S1b], writes=[PG], pe_acc=True)
                for hd in range(4):
                    S.op("act", lambda e, hd=hd: e.activation(out=jk5[:], in_=PG[:, hd * 128:(hd + 1) * 128], func=AF.Square,
                                                              accum_out=ss[:, hd:hd + 1]), reads=[PG], writes=[jk5, ss])
                S.op("dve", lambda e: e.tensor_scalar(out=rstd[:], in0=ss[:], scalar1=1.0 / 128.0, scalar2=RMS_EPS,
                                                      op0=ALU.mult, op1=ALU.add), reads=[ss], writes=[rstd])
                S.op("act", lambda e: e.activation(out=rstd[:], in_=rstd[:], func=AF.Sqrt), reads=[rstd], writes=[rstd])
                S.op("dve", lambda e: e.reciprocal(out=rstd[:], in_=rstd[:]), reads=[rstd], writes=[rstd])
                S.op("dve", lambda e: e.tensor_tensor(out=t1[:].rearrange("p (h d) -> p h d", h=4),
                                                      in0=PG[:].rearrange("p (h d) -> p h d", h=4),
                                                      in1=rstd[:, 0:4].unsqueeze(2).to_broadcast([128, 4, 128]), op=ALU.mult),
                     reads=[PG, rstd], writes=[t1])
                S.op("pool", lambda e: e.tensor_tensor(out=t1[:], in0=t1[:], in1=ng_bc[:].rearrange("p h d -> p (h d)"),
                                                       op=ALU.mult), reads=[t1, ng_bc], writes=[t1])
                S.op("pool", lambda e: e.tensor_tensor(out=cbb[:], in0=t1[:], in1=gb[:], op=ALU.mult),
                     reads=[t1, gb], writes=[cbb])
                transpose4(lambda hd: cbb[:, hd * 128:(hd + 1) * 128], [cbb])
                S.op("act", lambda e, i=i: e.activation(out=catT[:, 8:12, i * 128:(i + 1) * 128],
                                                        in_=PT[:, 0:512].rearrange("p (h t) -> p h t", h=4), func=AF.Copy),
                     reads=[PT], writes=[catT])
            S.barrier()
            S.emit()

        with ExitStack() as p6:
            Wm_sb = sb(p6, "Wm_sb", [128, KC, 1024], BF16)
            xt = [sb(p6, "xt6_%d" % j, [128, KC, 128], BF16) for j in range(2)]
            mq_b = sb(p6, "mq_b", [128, 512], BF16)
            gm = sb(p6, "gm", [128, 512], F32)
            mqT = sb(p6, "mqT", [128, 4, 128], BF16)
            Em = [sb(p6, "Em%d" % j, [128, 4, 128], BF16) for j in range(2)]
            rdm = sb(p6, "rdm", [128, 4], F32)
            tm = sb(p6, "tm", [128, 512], F32)
            cmb = sb(p6, "cmb", [128, 512], BF16)
            POm = [PU[0], PU[1]]
            load_wres(Wm_sb, Wm, 1024)
            zs_chunk(Wm_sb, 0, 512, ZOFF["m"])
            zs_chunk(Wm_sb, 512, 512, ZOFF["m"] + 512)
            for i in range(NO):
                x = xt[i % 2]
                load_xt(x, i)
                mm_x(PZ[0], x, Wm_sb, 0, 512)
                S.op("act", lambda e: e.activation(out=mq_b[:], in_=PZ[0][:], func=AF.Copy, scale=QS), reads=[PZ[0]], writes=[mq_b])
                mm_x(PZ[1], x, Wm_sb, 512, 512)
                S.op("act", lambda e: e.activation(out=gm[:], in_=PZ[1][:], func=AF.Silu), reads=[PZ[1]], writes=[gm])
                transpose4(lambda hd: mq_b[:, hd * 128:(hd + 1) * 128], [mq_b])
                S.op("act", lambda e: e.activation(out=mqT[:], in_=PT[:, 0:512].rearrange("p (h t) -> p h t", h=4), func=AF.Copy),
                     reads=[PT], writes=[mqT])
                for nt in range(2):
                    ps_ = PA[nt]
                    for hd in range(4):
                        S.op("pe", lambda e, hd=hd, nt=nt, ps_=ps_: e.matmul(
                            ps_[:, hd * 128:(hd + 1) * 128], lhsT=mkT[:, hd, nt * 128:(nt + 1) * 128], rhs=mqT[:, hd, :],
                            start=True, stop=True), reads=[mkT, mqT], writes=[ps_])
                    S.op("act", lambda e, nt=nt, ps_=ps_: e.activation(out=Em[nt][:].rearrange("p h t -> p (h t)"), in_=ps_[:],
                                                                       func=AF.Exp), reads=[ps_], writes=[Em[nt]])
                for nt in range(2):
                    for hd in range(4):
                        bank = hd // 3
                        col = (hd % 3) * 130
                        st = (nt == 0 and hd % 3 == 0)
                        S.op("pe", lambda e, hd=hd, nt=nt, bank=bank, col=col, st=st: e.matmul(
                            POm[bank][:, col:col + 130], lhsT=Em[nt][:, hd, :], rhs=mv[:, nt, hd, :],
                            start=st, stop=(nt == 1), skip_group_check=True),
                            reads=[Em[nt], mv], writes=[POm[bank]], pe_acc=(not st))
                for bank in range(2):
                    nh = 3 if bank == 0 else 1
                    S.op("dve", lambda e, bank=bank, nh=nh: e.reciprocal(
                        out=rdm[:, bank * 3:bank * 3 + nh],
                        in_=POm[bank][:, 0:nh * 130].rearrange("p (h c) -> p h c", c=130)[:, :, 128]),
                        reads=[POm[bank]], writes=[rdm])
                    S.op("dve", lambda e, bank=bank, nh=nh: e.tensor_tensor(
                        out=tm[:, bank * 384:bank * 384 + nh * 128].rearrange("p (h d) -> p h d", h=nh),
                        in0=POm[bank][:, 0:nh * 130].rearrange("p (h c) -> p h c", c=130)[:, :, 0:128],
                        in1=rdm[:, bank * 3:bank * 3 + nh].unsqueeze(2).to_broadcast([128, nh, 128]), op=ALU.mult),
                        reads=[POm[bank], rdm], writes=[tm])
                S.op("pool", lambda e: e.tensor_tensor(out=cmb[:], in0=tm[:], in1=gm[:], op=ALU.mult), reads=[tm, gm], writes=[cmb])
                transpose4(lambda hd: cmb[:, hd * 128:(hd + 1) * 128], [cmb])
                S.op("act", lambda e, i=i: e.activation(out=catT[:, 12:16, i * 128:(i + 1) * 128],
                                                        in_=PT[:, 0:512].rearrange("p (h t) -> p h t", h=4), func=AF.Copy),
                     reads=[PT], writes=[catT])
            S.barrier()
            S.emit()

        print("ninstr at S start", S.ninstr, flush=True)
        es_S = ExitStack()
        cat_s = sb(es_S, "cat_s", [8, D], BF16)
        aq_sb = sb(es_S, "aq_sb", [8, 1024], BF16)
        ak_sb = sb(es_S, "ak_sb", [8, 1024], BF16)
        av_sb = sb(es_S, "av_sb", [8, 1024], BF16)
        ga_s = sb(es_S, "ga_s", [8, 1024], F32)
        iq_sb = sb(es_S, "iq_sb", [8, 16, 64], BF16)
        ik_sb = sb(es_S, "ik_sb", [8, 64], BF16)
        iw_s = sb(es_S, "iw_s", [8, 16], F32)
        ropeS_sb = sb(es_S, "ropeS_sb", [8, 160], F32)
        tri8 = sb(es_S, "tri8", [8, 8], F32)
        one8 = sb(es_S, "one8", [8, 8], F32)
        S.dma("sp", lambda e: e.dma_start(out=ropeS_sb[:], in_=ropeS[:, :]), writes=[ropeS_sb])
        S.dma("sp", lambda e: e.dma_start(out=tri8[:], in_=c_tri2[0:8, 0:8]), writes=[tri8])
        S.op("pool", lambda e: e.memset(one8[:], 1.0), writes=[one8])
        with ExitStack() as s1:
            zq = sb(s1, "zq", [8, 1024], F32)
            zk = sb(s1, "zk", [8, 1024], F32)
            zv = sb(s1, "zv", [8, 1024], F32)
            ziq = sb(s1, "ziq", [8, 1024], F32)
            zik = sb(s1, "zik", [8, 64], F32)
            zb = sb(s1, "zb", [8, 2048], F32)
            zm = sb(s1, "zm", [8, 1024], F32)
            tAs = sb(s1, "tAs", [8, 16, 32], F32)
            tBs = sb(s1, "tBs", [8, 16, 32], F32)
            for dst, key, n_ in ((zq, "q", 1024), (zk, "k", 1024), (zv, "v", 1024), (ga_s, "g", 1024), (ziq, "iq", 1024),
                                 (iw_s, "iw", 16), (zik, "ik", 64), (zb, "b", 2048), (zm, "m", 1024)):
                S.dma("sp", lambda e, dst=dst, key=key, n_=n_: e.dma_start(out=dst[:, 0:n_], in_=zs_d[:, ZOFF[key]:ZOFF[key] + n_]),
                      reads=[zsd_b], writes=[dst])
            CCk, SSk = ropeS_sb[:, 0:32], ropeS_sb[:, 32:64]
            CCi, SSi = ropeS_sb[:, 64:80], ropeS_sb[:, 80:96]
            CCq, SSq = ropeS_sb[:, 96:128], ropeS_sb[:, 128:160]
            v3 = lambda t, nh: V3(lambda lo_, hi_: t[:].rearrange("p (h d) -> p h d", h=nh)[:, :, lo_:hi_], [t])
            rope(v3(zk, 8), v3(zk, 8), 8, 16, CCk, SSk, tAs, tBs, np_=8)
            S.dma("sp", lambda e: e.dma_start(out=o_ks[:, :], in_=zk[:]), reads=[zk])
            S.dma("sp", lambda e: e.dma_start(out=o_vs[:, :], in_=zv[:]), reads=[zv])
            S.op("dve", lambda e: e.tensor_copy(out=ak_sb[:], in_=zk[:]), reads=[zk], writes=[ak_sb])
            S.op("dve", lambda e: e.tensor_copy(out=av_sb[:], in_=zv[:]), reads=[zv], writes=[av_sb])
            rope(v3(zik, 1), v3(zik, 1), 1, 8, CCi, SSi, tAs, tBs, np_=8)
            S.dma("sp", lambda e: e.dma_start(out=o_iks[:, :], in_=zik[:]), reads=[zik])
            S.op("dve", lambda e: e.tensor_copy(out=ik_sb[:], in_=zik[:]), reads=[zik], writes=[ik_sb])
            S.op("act", lambda e: e.activation(out=aq_sb[:], in_=zq[:], func=AF.Copy, scale=QS), reads=[zq], writes=[aq_sb])
            rope(v3(zq, 8), v3(aq_sb, 8), 8, 16, CCq, SSq, tAs, tBs, np_=8)
            S.op("act", lambda e: e.activation(out=iq_sb[:].rearrange("p h d -> p (h d)"), in_=ziq[:], func=AF.Copy),
                 reads=[ziq], writes=[iq_sb])
            rope(v3(ziq, 16), V3(lambda lo_, hi_: iq_sb[:, :, lo_:hi_], [iq_sb]), 16, 8, CCi, SSi, tAs, tBs, np_=8)
            S.op("act", lambda e: e.activation(out=ga_s[:], in_=ga_s[:], func=AF.Silu), reads=[ga_s], writes=[ga_s])

            S0f = sb(s1, "S0f", [128, 4, 128], F32)
            S0b = sb(s1, "S0b", [128, 4, 128], BF16)
            S1f = sb(s1, "S1f", [128, 4, 128], F32)
            hs = [sb(s1, "hs%d" % j, [8, 512], F32) for j in range(8)]
            qs_, sg_, gg_, omf_, Gs_, dl_, eG_, enG_ = hs
            gb_ = sb(s1, "gb_s", [8, 512], F32)
            t1_ = sb(s1, "t1_s", [8, 512], F32)
            kdec_ = sb(s1, "kdec_s", [8, 512], BF16)
            vb_ = sb(s1, "vb_s", [8, 512], BF16)
            kg_ = sb(s1, "kg_s", [8, 512], BF16)
            qg_ = sb(s1, "qg_s", [8, 512], BF16)
            qgT_ = sb(s1, "qgT_s", [128, 4, 8], BF16)
            kgT_ = sb(s1, "kgT_s", [128, 4, 8], BF16)
            ATb_ = sb(s1, "ATb_s", [8, 4, 8], BF16)
            Dcs = sb(s1, "Dcs", [128, 8], F32)
            ss_ = sb(s1, "ss_s", [8, 4], F32)
            rstd_ = sb(s1, "rstd_s", [8, 4], F32)
            jk_ = sb(s1, "jk_s", [8, 128], F32)
            ngs = sb(s1, "ngs", [8, 4, 128], F32)
            lb8 = sb(s1, "lb8", [8, 2, 512], F32)
            S.dma("sp", lambda e: e.dma_start(out=S0f[:], in_=st_h.rearrange("h k v -> k h v")), writes=[S0f])
            S.dma("sp", [lambda e, hd=hd: e.dma_start(out=ngs[:, hd, :], in_=ng[0:1, :].broadcast_to([8, 128]))
                         for hd in range(4)], writes=[ngs])
            S.op("dve", lambda e: e.tensor_copy(out=S0b[:], in_=S0f[:]), reads=[S0f], writes=[S0b])
            S.op("act", lambda e: e.activation(out=qs_[:], in_=zb[:, 0:512], func=AF.Silu), reads=[zb], writes=[qs_])
            S.op("act", lambda e: e.activation(out=sg_[:], in_=zb[:, 512:1024], func=AF.Sigmoid), reads=[zb], writes=[sg_])
            S.op("act", lambda e: e.activation(out=vb_[:], in_=zb[:, 1024:1536], func=AF.Copy), reads=[zb], writes=[vb_])
            S.op("act", lambda e: e.activation(out=gb_[:], in_=zb[:, 1536:2048], func=AF.Silu), reads=[zb], writes=[gb_])
            S.op("dve", lambda e: e.tensor_tensor(out=sg_[:], in0=sg_[:], in1=oml_bc[0:8, :], op=ALU.mult),
                 reads=[sg_, oml_bc], writes=[sg_])
            S.op("dve", lambda e: e.tensor_tensor(out=sg_[:], in0=sg_[:], in1=lb_bc[0:8, :], op=ALU.add),
                 reads=[sg_, lb_bc], writes=[sg_])
            S.op("act", lambda e: e.activation(out=gg_[:], in_=sg_[:], func=AF.Ln), reads=[sg_], writes=[gg_])
            S.op("dve", lambda e: e.tensor_scalar(out=omf_[:], in0=sg_[:], scalar1=-1.0, scalar2=1.0, op0=ALU.mult, op1=ALU.add),
                 reads=[sg_], writes=[omf_])
            S.op("pe", lambda e: e.matmul(PG[0:8, :], lhsT=tri8[:], rhs=gg_[:], start=True, stop=True),
                 reads=[tri8, gg_], writes=[PG])
            S.op("pe", lambda e: e.matmul(PGL[0:8, :], lhsT=one8[:], rhs=gg_[:], start=True, stop=True),
                 reads=[one8, gg_], writes=[PGL])
            S.op("act", lambda e: e.activation(out=Gs_[:], in_=PG[0:8, :], func=AF.Copy), reads=[PG], writes=[Gs_])
            S.op("dve", lambda e: e.tensor_tensor(out=dl_[:], in0=PGL[0:8, :], in1=Gs_[:], op=ALU.subtract),
                 reads=[PGL, Gs_], writes=[dl_])
            S.op("act", lambda e: e.activation(out=dl_[:], in_=dl_[:], func=AF.Exp), reads=[dl_], writes=[dl_])
            S.op("dve", lambda e: e.tensor_tensor(out=kdec_[:], in0=omf_[:], in1=dl_[:], op=ALU.mult),
                 reads=[omf_, dl_], writes=[kdec_])
            S.op("act", lambda e: e.activation(out=eG_[:], in_=Gs_[:], func=AF.Exp), reads=[Gs_], writes=[eG_])
            S.op("act", lambda e: e.activation(out=enG_[:], in_=Gs_[:], func=AF.Exp, scale=-1.0), reads=[Gs_], writes=[enG_])
            S.op("dve", lambda e: e.tensor_tensor(out=qg_[:], in0=qs_[:], in1=eG_[:], op=ALU.mult), reads=[qs_, eG_], writes=[qg_])
            S.op("dve", lambda e: e.tensor_tensor(out=kg_[:], in0=omf_[:], in1=enG_[:], op=ALU.mult),
                 reads=[omf_, enG_], writes=[kg_])
            for hd in range(4):
                cs_ = slice(hd * 128, (hd + 1) * 128)
                S.op("pe", lambda e, cs_=cs_: e.matmul(PU[0][:, cs_], lhsT=kdec_[:, cs_], rhs=vb_[:, cs_], start=True, stop=True),
                     reads=[kdec_, vb_], writes=[PU[0]])
            for hd in range(4):
                S.op("pe", lambda e, hd=hd: e.matmul(PD[:, 2 * hd:2 * hd + 2], lhsT=gg_[:, hd * 128:(hd + 1) * 128],
                                                     rhs=ones_f[0:8, :], start=True, stop=True),
                     reads=[gg_, ones_f], writes=[PD])
            S.op("act", lambda e: e.activation(out=Dcs[:], in_=PD[:, 0:8], func=AF.Exp), reads=[PD], writes=[Dcs])
            for hd in range(4):
                S.op("dve", lambda e, hd=hd: e.scalar_tensor_tensor(
                    out=S1f[:, hd, :], in0=S0f[:, hd, :], scalar=Dcs[:, 2 * hd:2 * hd + 1],
                    in1=PU[0][:, hd * 128:(hd + 1) * 128], op0=ALU.mult, op1=ALU.add),
                    reads=[S0f, Dcs, PU[0]], writes=[S1f])
            S.dma("sp", lambda e: e.dma_start(out=o_hgs[:, :, :], in_=S1f[:]), reads=[S1f])
            for hd in range(4):
                S.op("pe", lambda e, hd=hd: e.transpose(out=PT[:, hd * 8:(hd + 1) * 8], in_=qg_[:, hd * 128:(hd + 1) * 128],
                                                        identity=ident_b[0:8, 0:8]), reads=[qg_, ident_b], writes=[PT])
                S.op("pe", lambda e, hd=hd: e.transpose(out=PT[:, 32 + hd * 8:32 + (hd + 1) * 8], in_=kg_[:, hd * 128:(hd + 1) * 128],
                                                        identity=ident_b[0:8, 0:8]), reads=[kg_, ident_b], writes=[PT])
            S.op("act", lambda e: e.activation(out=qgT_[:].rearrange("p h t -> p (h t)"), in_=PT[:, 0:32], func=AF.Copy),
                 reads=[PT], writes=[qgT_])
            S.op("act", lambda e: e.activation(out=kgT_[:].rearrange("p h t -> p (h t)"), in_=PT[:, 32:64], func=AF.Copy),
                 reads=[PT], writes=[kgT_])
            for hd in range(4):
                S.op("pe", lambda e, hd=hd: e.matmul(PU[1][0:8, hd * 8:(hd + 1) * 8], lhsT=kgT_[:, hd, :], rhs=qgT_[:, hd, :],
                                                     start=True, stop=True), reads=[kgT_, qgT_], writes=[PU[1]])
            S.op("dve", lambda e: e.tensor_tensor(out=ATb_[:], in0=PU[1][0:8, 0:32].rearrange("p (h t) -> p h t", h=4),
                                                  in1=tri8[:].unsqueeze(1).to_broadcast([8, 4, 8]), op=ALU.mult),
                 reads=[PU[1], tri8], writes=[ATb_])
            for hd in range(4):
                cs_ = slice(hd * 128, (hd + 1) * 128)
                S.op("pe", lambda e, hd=hd, cs_=cs_: e.matmul(PG[0:8, cs_], lhsT=ATb_[:, hd, :], rhs=vb_[:, cs_],
                                                              start=True, stop=False), reads=[ATb_, vb_], writes=[PG])
                S.op("pe", lambda e, hd=hd, cs_=cs_: e.matmul(PG[0:8, cs_], lhsT=qgT_[:, hd, :], rhs=S0b[:, hd, :],
                                                              start=False, stop=True), reads=[qgT_, S0b], writes=[PG], pe_acc=True)
            for hd in range(4):
                S.op("act", lambda e, hd=hd: e.activation(out=jk_[:], in_=PG[0:8, hd * 128:(hd + 1) * 128], func=AF.Square,
                                                          accum_out=ss_[:, hd:hd + 1]), reads=[PG], writes=[jk_, ss_])
            S.op("dve", lambda e: e.tensor_scalar(out=rstd_[:], in0=ss_[:], scalar1=1.0 / 128.0, scalar2=RMS_EPS,
                                                  op0=ALU.mult, op1=ALU.add), reads=[ss_], writes=[rstd_])
            S.op("act", lambda e: e.activation(out=rstd_[:], in_=rstd_[:], func=AF.Sqrt), reads=[rstd_], writes=[rstd_])
            S.op("dve", lambda e: e.reciprocal(out=rstd_[:], in_=rstd_[:]), reads=[rstd_], writes=[rstd_])
            S.op("dve", lambda e: e.tensor_tensor(out=t1_[:].rearrange("p (h d) -> p h d", h=4),
                                                  in0=PG[0:8, :].rearrange("p (h d) -> p h d", h=4),
                                                  in1=rstd_[:, 0:4].unsqueeze(2).to_broadcast([8, 4, 128]), op=ALU.mult),
                 reads=[PG, rstd_], writes=[t1_])
            S.op("dve", lambda e: e.tensor_tensor(out=t1_[:], in0=t1_[:], in1=ngs[:].rearrange("p h d -> p (h d)"), op=ALU.mult),
                 reads=[t1_, ngs], writes=[t1_])
            S.op("dve", lambda e: e.tensor_tensor(out=cat_s[:, 1024:1536], in0=t1_[:], in1=gb_[:], op=ALU.mult),
                 reads=[t1_, gb_], writes=[cat_s])

            cmk = sb(s1, "cmk", [128, 2, 512], BF16)
            mvs = sb(s1, "mvs", [128, 2, 4, 130], BF16)
            mkTs = sb(s1, "mkTs", [128, 4, 256], BF16)
            mq_s = sb(s1, "mq_s", [8, 512], BF16)
            gm_s = sb(s1, "gm_s", [8, 512], F32)
            mqTs = sb(s1, "mqTs", [128, 4, 8], BF16)
            Ems = [sb(s1, "Ems%d" % j, [128, 4, 8], BF16) for j in range(2)]
            rdms = sb(s1, "rdms", [8, 4], F32)
            tms = sb(s1, "tms", [8, 512], F32)
            S.dma("pool", [lambda e, nt=nt: e.dma_start(out=cmk[:, nt, :], in_=cmk_d[nt * 128:(nt + 1) * 128, :])
                           for nt in range(2)], writes=[cmk])
            S.op("pool", lambda e: e.memset(mvs[:, :, :, 128:129], 1.0), writes=[mvs])
            S.dma("pool", [lambda e, nt=nt: e.dma_start(out=mvs[:, nt, :, 0:128],
                                                        in_=cmv_d[nt * 128:(nt + 1) * 128, :].rearrange("p (h d) -> p h d", h=4))
                           for nt in range(2)], writes=[mvs])
            for nt in range(2):
                transpose4(lambda hd, nt=nt: cmk[:, nt, hd * 128:(hd + 1) * 128], [cmk])
                S.op("act", lambda e, nt=nt: e.activation(out=mkTs[:, :, nt * 128:(nt + 1) * 128],
                                                          in_=PT[:, 0:512].rearrange("p (h t) -> p h t", h=4), func=AF.Copy),
                     reads=[PT], writes=[mkTs])
            S.op("act", lambda e: e.activation(out=mq_s[:], in_=zm[:, 0:512], func=AF.Copy, scale=QS), reads=[zm], writes=[mq_s])
            S.op("act", lambda e: e.activation(out=gm_s[:], in_=zm[:, 512:1024], func=AF.Silu), reads=[zm], writes=[gm_s])
            for hd in range(4):
                S.op("pe", lambda e, hd=hd: e.transpose(out=PT[:, hd * 8:(hd + 1) * 8], in_=mq_s[:, hd * 128:(hd + 1) * 128],
                                                        identity=ident_b[0:8, 0:8]), reads=[mq_s, ident_b], writes=[PT])
            S.op("act", lambda e: e.activation(out=mqTs[:].rearrange("p h t -> p (h t)"), in_=PT[:, 0:32], func=AF.Copy),
                 reads=[PT], writes=[mqTs])
            Es_ = sb(s1, "Es_s", [8, 2, 512], BF16)
            for nt in range(2):
                ps_ = PA[nt]
                for hd in range(4):
                    S.op("pe", lambda e, hd=hd, nt=nt, ps_=ps_: e.matmul(
                        ps_[0:8, hd * 128:(hd + 1) * 128], lhsT=mqTs[:, hd, :], rhs=mkTs[:, hd, nt * 128:(nt + 1) * 128],
                        start=True, stop=True), reads=[mkTs, mqTs], writes=[ps_])
                S.op("act", lambda e, nt=nt, ps_=ps_: e.activation(out=Es_[:, nt, :], in_=ps_[0:8, :], func=AF.Exp),
                     reads=[ps_], writes=[Es_])
                for hd in range(4):
                    S.op("pe", lambda e, hd=hd, nt=nt: e.transpose(out=PT[:, hd * 8:(hd + 1) * 8],
                                                                   in_=Es_[:, nt, hd * 128:(hd + 1) * 128],
                                                                   identity=ident_b[0:8, 0:8]), reads=[Es_, ident_b], writes=[PT])
                S.op("act", lambda e, nt=nt: e.activation(out=Ems[nt][:].rearrange("p h t -> p (h t)"), in_=PT[:, 0:32],
                                                          func=AF.Copy), reads=[PT], writes=[Ems[nt]])
            POs = [PU[0], PU[1]]
            for nt in range(2):
                for hd in range(4):
                    bank, col = hd // 3, (hd % 3) * 130
                    st = (nt == 0 and hd % 3 == 0)
                    S.op("pe", lambda e, hd=hd, nt=nt, bank=bank, col=col, st=st: e.matmul(
                        POs[bank][0:8, col:col + 130], lhsT=Ems[nt][:, hd, :], rhs=mvs[:, nt, hd, :],
                        start=st, stop=(nt == 1), skip_group_check=True),
                        reads=[Ems[nt], mvs], writes=[POs[bank]], pe_acc=(not st))
            for bank in range(2):
                nh = 3 if bank == 0 else 1
                S.op("dve", lambda e, bank=bank, nh=nh: e.reciprocal(
                    out=rdms[:, bank * 3:bank * 3 + nh],
                    in_=POs[bank][0:8, 0:nh * 130].rearrange("p (h c) -> p h c", c=130)[:, :, 128]),
                    reads=[POs[bank]], writes=[rdms])
                S.op("dve", lambda e, bank=bank, nh=nh: e.tensor_tensor(
                    out=tms[:, bank * 384:bank * 384 + nh * 128].rearrange("p (h d) -> p h d", h=nh),
                    in0=POs[bank][0:8, 0:nh * 130].rearrange("p (h c) -> p h c", c=130)[:, :, 0:128],
                    in1=rdms[:, bank * 3:bank * 3 + nh].unsqueeze(2).to_broadcast([8, nh, 128]), op=ALU.mult),
                    reads=[POs[bank], rdms], writes=[tms])
            S.op("dve", lambda e: e.tensor_tensor(out=cat_s[:, 1536:2048], in0=tms[:], in1=gm_s[:], op=ALU.mult),
                 reads=[tms, gm_s], writes=[cat_s])
            S.barrier()
            S.emit()
        print("ninstr at S1 end", S.ninstr, flush=True)
        KB2 = 30
        with ExitStack() as s3:
            ci = {}
            for nm, shp, dt_ in (("c_sel8", [8, 128], F32), ("c_hm", [128, 16], F32), ("c_bq8", [128, 8], F32),
                                 ("c_bq16", [128, 8], F32), ("c_BQ1", [128, 128], F32), ("c_BQm", [128, 128], F32),
                                 ("c_LT", [128, 128], F32), ("c_negm", [128, 8], F32), ("c_aiota", [128, 1024], F32),
                                 ("c_iota512", [128, 512], F32), ("c_esel", [8, 1024], F32), ("c_eq", [8, 64], F32),
                                 ("c_bd8", [8, 1024], F32), ("c_pow2", [128, 2 * KB2], F32), ("c_i32", [128, 768], I32)):
                t_ = sb(s3, "k" + nm, shp, dt_)
                S.dma("sp", lambda e, t_=t_, nm=nm: e.dma_start(out=t_[:], in_=cin[nm][:, :]), writes=[t_])
                ci[nm] = t_
            esel_b = sb(s3, "esel_b", [8, 8, 128], BF16)
            eq_b = sb(s3, "eq_b", [8, 8, 8], BF16)
            ones_b = sb(s3, "ones_b", [128, 2], BF16)
            S.op("dve", lambda e: e.tensor_copy(out=esel_b[:].rearrange("p a b -> p (a b)"), in_=ci["c_esel"][:]),
                 reads=[ci["c_esel"]], writes=[esel_b])
            S.op("dve", lambda e: e.tensor_copy(out=eq_b[:].rearrange("p a b -> p (a b)"), in_=ci["c_eq"][:]),
                 reads=[ci["c_eq"]], writes=[eq_b])
            S.op("pool", lambda e: e.memset(ones_b[:], 1.0), writes=[ones_b])
            I_s = sb(s3, "I_s", [128, 1032], F32)
            junk_s = sb(s3, "junk_s", [128, 1032], BF16)
            iqT_s = sb(s3, "iqT_s", [64, 128], BF16)
            ikTn = sb(s3, "ikTn", [64, 8], BF16)
            wtmp = sb(s3, "wtmp", [128, 16], F32)
            wcol = sb(s3, "wcol", [128, 1], F32)
            wdb = sb(s3, "wdb", [128, 8], F32)
            Wd_all = sb(s3, "Wd_all", [128, 16, 128], BF16)
            Tbs = [sb(s3, "Tbs%d" % j, [128, 512], BF16) for j in range(2)]
            Tn = sb(s3, "Tn", [128, 8], BF16)
            pt_i = sb(s3, "pt_i", [128, 1], I32)
            ptrow_i = sb(s3, "ptrow_i", [128, 128], I32)
            ptrow_f = sb(s3, "ptrow_f", [128, 128], F32)
            st2 = sb(s3, "st2", [128, 4], F32)
            sm2 = sb(s3, "sm2", [128, 8], F32)
            lo2, hi2, rng2, mid2, wv2 = [sm2[:, j:j + 1] for j in range(5)]
            cnt2 = sb(s3, "cnt2", [128, 2], F32)
            spw2 = sb(s3, "spw2", [128, 2 * KB2], F32)
            off_s = sb(s3, "off_s", [128, 1], F32)
            seln = sb(s3, "seln", [128, 8], F32)
            selnT = sb(s3, "selnT", [8, 128], F32)
            idx_i = sb(s3, "idx_i", [128, 2, 8], I32)
            vmask = sb(s3, "vmask", [128, 2, 8], F32)
            S.dma("sp", lambda e: e.dma_start(out=pt_i[:], in_=ptab.rearrange("o p -> p o")), writes=[pt_i])
            S.dma("sp", lambda e: e.dma_start(out=ptrow_i[:], in_=ptab[0:1, :].broadcast_to([128, 128])), writes=[ptrow_i])
            S.op("pool", lambda e: e.memset(cnt2[:], 0.0), writes=[cnt2])
            for h in range(16):
                S.op("pe", lambda e, h=h: e.transpose(out=PT[0:64, h * 8:(h + 1) * 8], in_=iq_sb[:, h, :],
                                                      identity=ident_b[0:8, 0:8]), reads=[iq_sb, ident_b], writes=[PT])
            S.op("pe", lambda e: e.transpose(out=PT[0:64, 128:136], in_=ik_sb[:], identity=ident_b[0:8, 0:8]),
                 reads=[ik_sb, ident_b], writes=[PT])
            S.op("act", lambda e: e.activation(out=iqT_s[:], in_=PT[0:64, 0:128], func=AF.Copy), reads=[PT], writes=[iqT_s])
            S.op("act", lambda e: e.activation(out=ikTn[:], in_=PT[0:64, 128:136], func=AF.Copy), reads=[PT], writes=[ikTn])
            S.op("pe", lambda e: e.matmul(PD[:, 0:16], lhsT=ci["c_sel8"][:], rhs=iw_s[:], start=True, stop=True),
                 reads=[ci["c_sel8"], iw_s], writes=[PD])
            S.op("dve", lambda e: e.tensor_tensor(out=wtmp[:], in0=PD[:, 0:16], in1=ci["c_hm"][:], op=ALU.mult),
                 reads=[PD, ci["c_hm"]], writes=[wtmp])
            S.op("dve", lambda e: e.tensor_reduce(out=wcol[:], in_=wtmp[:], axis=AX.X, op=ALU.add), reads=[wtmp], writes=[wcol])
            S.op("dve", lambda e: e.tensor_scalar(out=wcol[:], in0=wcol[:], scalar1=1.0 / 32.0, scalar2=None, op0=ALU.mult),
                 reads=[wcol], writes=[wcol])
            S.op("dve", lambda e: e.tensor_scalar(out=wdb[:], in0=ci["c_bq8"][:], scalar1=wcol[:, 0:1], scalar2=None, op0=ALU.mult),
                 reads=[ci["c_bq8"], wcol], writes=[wdb])
            S.op("pool", lambda e: e.memset(Wd_all[:], 0.0), writes=[Wd_all])
            for seg in range(16):
                S.op("dve", lambda e, seg=seg: e.tensor_copy(
                    out=Wd_all[:, seg, :].rearrange("p (q s) -> p q s", s=16)[:, :, seg], in_=wdb[:]),
                    reads=[wdb], writes=[Wd_all])
            with ExitStack() as sA:
                ikT_s = sb(sA, "ikT_s", [64, PAST], BF16)
                with ExitStack() as sA2:
                    ikp = sb(sA2, "ikp", [128, 8192], F32)
                    S.dma("pool", lambda e: e.indirect_dma_start(
                        out=ikp[:], out_offset=None, in_=cidx[:, :],
                        in_offset=bass.IndirectOffsetOnAxis(ap=pt_i[:, 0:1], axis=0)), reads=[pt_i], writes=[ikp])
                    for r in range(128):
                        pz = PZ[(r // 4) % 2]
                        S.op("pe", lambda e, r=r, pz=pz: e.transpose(out=pz[0:64, (r % 4) * 128:(r % 4 + 1) * 128],
                                                                     in_=ikp[:, r * 64:(r + 1) * 64], identity=ident_f[:]),
                             reads=[ikp, ident_f], writes=[pz])
                        if r % 4 == 3:
                            eng_ = "act" if (r // 4) % 2 == 0 else "dve"
                            if eng_ == "act":
                                S.op("act", lambda e, r=r, pz=pz: e.activation(out=ikT_s[:, (r - 3) * 128:(r + 1) * 128],
                                                                               in_=pz[0:64, :], func=AF.Copy),
                                     reads=[pz], writes=[ikT_s])
                            else:
                                S.op("dve", lambda e, r=r, pz=pz: e.tensor_copy(out=ikT_s[:, (r - 3) * 128:(r + 1) * 128],
                                                                                in_=pz[0:64, :]), reads=[pz], writes=[ikT_s])
                    S.barrier()
                    S.emit()
                def mm1s(c):
                    pa = PA[c % 2]
                    S.op("pe", lambda e, c=c, pa=pa: e.matmul(pa[:], lhsT=iqT_s[:], rhs=ikT_s[:, c * 512:(c + 1) * 512],
                                                              start=True, stop=True), reads=[iqT_s, ikT_s], writes=[pa])
                mm1s(0)
                for c in range(32):
                    if c + 1 < 32:
                        mm1s(c + 1)
                    pa, tb = PA[c % 2], Tbs[c % 2]
                    seg, half = c // 2, c % 2
                    S.op("act", lambda e, pa=pa, tb=tb: e.activation(out=tb[:], in_=pa[:], func=AF.Relu), reads=[pa], writes=[tb])
                    S.op("pe", lambda e, seg=seg, half=half, tb=tb: e.matmul(PI[half][:], lhsT=Wd_all[:, seg, :], rhs=tb[:],
                                                                            start=(seg == 0), stop=(seg == 15)),
                         reads=[Wd_all, tb], writes=[PI[half]], pe_acc=(seg > 0))
                for half in range(2):
                    S.op("dve", lambda e, half=half: e.tensor_copy(out=I_s[:, half * 512:(half + 1) * 512], in_=PI[half][:]),
                         reads=[PI[half]], writes=[I_s])
                S.op("pe", lambda e: e.matmul(PA[0][:, 0:8], lhsT=iqT_s[:], rhs=ikTn[:], start=True, stop=True),
                     reads=[iqT_s, ikTn], writes=[PA[0]])
                S.op("act", lambda e: e.activation(out=Tn[:], in_=PA[0][:, 0:8], func=AF.Relu), reads=[PA[0]], writes=[Tn])
                S.op("pe", lambda e: e.matmul(PD[:, 0:8], lhsT=Wd_all[:, 0, :], rhs=Tn[:], start=True, stop=True),
                     reads=[Wd_all, Tn], writes=[PD])
                S.op("dve", lambda e: e.tensor_reduce(out=st2[:, 2:3], in_=PD[:, 0:8], axis=AX.X, op=ALU.max,
                                                      apply_absolute_value=True), reads=[PD], writes=[st2])
                S.op("dve", lambda e: e.tensor_tensor(out=I_s[:, 1024:1032], in0=PD[:, 0:8], in1=ci["c_negm"][:], op=ALU.add),
                     reads=[PD, ci["c_negm"]], writes=[I_s])
                S.barrier()
                S.emit()
            S.op("dve", lambda e: e.tensor_reduce(out=st2[:, 0:1], in_=I_s[:, 0:1024], axis=AX.X, op=ALU.min),
                 reads=[I_s], writes=[st2])
            S.op("dve", lambda e: e.tensor_reduce(out=st2[:, 1:2], in_=I_s[:, 0:1024], axis=AX.X, op=ALU.max,
                                                  apply_absolute_value=True), reads=[I_s], writes=[st2])
            S.op("dve", lambda e: e.tensor_tensor(out=st2[:, 1:2], in0=st2[:, 1:2], in1=st2[:, 2:3], op=ALU.max),
                 reads=[st2], writes=[st2])
            S.op("pe", lambda e: e.matmul(PD[:, 16:18], lhsT=ci["c_BQm"][:], rhs=st2[:, 0:2], start=True, stop=True),
                 reads=[ci["c_BQm"], st2], writes=[PD])
            S.op("pe", lambda e: e.matmul(PD[:, 18:20], lhsT=ci["c_BQ1"][:], rhs=st2[:, 0:2], start=True, stop=True),
                 reads=[ci["c_BQ1"], st2], writes=[PD])
            S.op("dve", lambda e: e.tensor_copy(out=lo2, in_=PD[:, 16:17]), reads=[PD], writes=[sm2])
            S.op("dve", lambda e: e.tensor_copy(out=hi2, in_=PD[:, 19:20]), reads=[PD], writes=[sm2])
            S.op("dve", lambda e: e.tensor_tensor(out=rng2, in0=hi2, in1=lo2, op=ALU.subtract), reads=[sm2], writes=[sm2])
            S.op("dve", lambda e: e.tensor_scalar(out=spw2[:], in0=ci["c_pow2"][:], scalar1=rng2, scalar2=None, op0=ALU.mult),
                 reads=[ci["c_pow2"], sm2], writes=[spw2])
            S.op("dve", lambda e: e.tensor_tensor(out=mid2, in0=lo2, in1=spw2[:, 0:1], op=ALU.add), reads=[sm2, spw2], writes=[sm2])
            for k in range(KB2):
                S.op("dve", lambda e: e.tensor_scalar(out=junk_s[:], in0=I_s[:], scalar1=mid2, scalar2=0.0, op0=ALU.is_ge,
                                                      op1=ALU.add, accum_out=cnt2[:, 0:1]), reads=[I_s, sm2], writes=[junk_s, cnt2])
                S.op("pe", lambda e: e.matmul(PD[:, 32:34], lhsT=ci["c_BQ1"][:], rhs=cnt2[:], start=True, stop=True),
                     reads=[ci["c_BQ1"], cnt2], writes=[PD])
                S.op("dve", lambda e, k=k: e.scalar_tensor_tensor(out=wv2, in0=PD[:, 32:33], scalar=256.0, in1=spw2[:, k:k + 1],
                                                                  op0=ALU.is_ge, op1=ALU.mult), reads=[PD, spw2], writes=[sm2])
                S.op("dve", lambda e, k=k: e.scalar_tensor_tensor(out=mid2, in0=mid2, scalar=spw2[:, KB2 + k:KB2 + k + 1], in1=wv2,
                                                                  op0=ALU.subtract, op1=ALU.add), reads=[sm2, spw2], writes=[sm2])
            with ExitStack() as sB:
                selm = sb(sB, "selm", [128, 1024], F32)
                PH = sb(sB, "PH", [128, 8, 128], F32)
                kp = [sb(sB, "kp%d" % j, [128, 1024], F32) for j in range(2)]
                cand = sb(sB, "cand", [128, 256], F32)
                candi = sb(sB, "candi", [128, 256], I32)
                dgi = sb(sB, "dgi", [128, 256], I32)
                dgb = sb(sB, "dgb", [128, 3, 256], BF16)
                L_all = sb(sB, "L_all", [128, 256, 24], BF16)
                OH = sb(sB, "OH", [128, 512], BF16)
                Cs = sb(sB, "Cs", [24, 256], F32)
                dig_s = sb(sB, "dig_s", [128, 2, 24], F32)
                idxf = sb(sB, "idxf", [128, 2, 8], F32)
                S.op("dve", lambda e: e.tensor_scalar(out=selm[:], in0=I_s[:, 0:1024], scalar1=mid2, scalar2=0.0, op0=ALU.is_ge,
                                                      op1=ALU.add, accum_out=cnt2[:, 0:1]), reads=[I_s, sm2], writes=[selm, cnt2])
                S.op("pe", lambda e: e.matmul(PD[:, 34:36], lhsT=ci["c_LT"][:], rhs=cnt2[:], start=True, stop=True),
                     reads=[ci["c_LT"], cnt2], writes=[PD])
                S.op("dve", lambda e: e.tensor_copy(out=off_s[:], in_=PD[:, 34:35]), reads=[PD], writes=[off_s])
                S.op("dve", lambda e: e.tensor_scalar(out=seln[:], in0=I_s[:, 1024:1032], scalar1=mid2, scalar2=None, op0=ALU.is_ge),
                     reads=[I_s, sm2], writes=[seln])
                S.op("pe", lambda e: e.transpose(out=PZ[0][0:8, 0:128], in_=seln[:], identity=ident_f[:]),
                     reads=[seln, ident_f], writes=[PZ[0]])
                S.op("dve", lambda e: e.tensor_copy(out=selnT[:], in_=PZ[0][0:8, 0:128]), reads=[PZ[0]], writes=[selnT])
                S.op("dve", lambda e: e.tensor_copy(out=ptrow_f[:], in_=ptrow_i[:]), reads=[ptrow_i], writes=[ptrow_f])
                S.op("dve", lambda e: e.tensor_scalar(out=ptrow_f[:], in0=ptrow_f[:], scalar1=128.0, scalar2=None, op0=ALU.mult),
                     reads=[ptrow_f], writes=[ptrow_f])
                S.op("dve", lambda e: e.tensor_tensor(out=PH[:], in0=ci["c_aiota"][:].rearrange("p (a s) -> p a s", a=8),
                                                      in1=ptrow_f[:].unsqueeze(1).to_broadcast([128, 8, 128]), op=ALU.add),
                     reads=[ci["c_aiota"], ptrow_f], writes=[PH])
                S.op("dve", lambda e: e.tensor_tensor(out=kp[0][:], in0=selm[:], in1=PH[:].rearrange("p a s -> p (a s)"), op=ALU.mult),
                     reads=[selm, PH], writes=[kp[0]])
                for r in range(32):
                    a_, b_ = kp[r % 2], kp[(r + 1) % 2]
                    S.op("dve", lambda e, r=r, a_=a_: e.max(out=cand[:, r * 8:(r + 1) * 8], in_=a_[:]), reads=[a_], writes=[cand])
                    if r < 31:
                        S.op("dve", lambda e, r=r, a_=a_, b_=b_: e.match_replace(out=b_[:], in_to_replace=cand[:, r * 8:(r + 1) * 8],
                                                                                 in_values=a_[:], imm_value=0.0),
                             reads=[a_, cand], writes=[b_])
                S.op("dve", lambda e: e.tensor_copy(out=candi[:], in_=cand[:]), reads=[cand], writes=[candi])
                c255, c8, c16 = ci["c_i32"][:, 0:256], ci["c_i32"][:, 256:512], ci["c_i32"][:, 512:768]
                S.op("dve", lambda e: e.tensor_tensor(out=dgi[:], in0=candi[:], in1=c255, op=ALU.bitwise_and),
                     reads=[candi, ci["c_i32"]], writes=[dgi])
                S.op("dve", lambda e: e.tensor_copy(out=dgb[:, 0, :], in_=dgi[:]), reads=[dgi], writes=[dgb])
                S.op("dve", lambda e: e.tensor_tensor(out=dgi[:], in0=candi[:], in1=c8, op=ALU.logical_shift_right),
                     reads=[candi, ci["c_i32"]], writes=[dgi])
                S.op("dve", lambda e: e.tensor_tensor(out=dgi[:], in0=dgi[:], in1=c255, op=ALU.bitwise_and),
                     reads=[dgi, ci["c_i32"]], writes=[dgi])
                S.op("dve", lambda e: e.tensor_copy(out=dgb[:, 1, :], in_=dgi[:]), reads=[dgi], writes=[dgb])
                S.op("dve", lambda e: e.tensor_tensor(out=dgi[:], in0=candi[:], in1=c16, op=ALU.logical_shift_right),
                     reads=[candi, ci["c_i32"]], writes=[dgi])
                S.op("dve", lambda e: e.tensor_copy(out=dgb[:, 2, :], in_=dgi[:]), reads=[dgi], writes=[dgb])
                for dg in range(3):
                    S.op("dve", lambda e, dg=dg: e.tensor_tensor(
                        out=L_all[:, :, dg * 8:(dg + 1) * 8], in0=dgb[:, dg, :].unsqueeze(2).to_broadcast([128, 256, 8]),
                        in1=ci["c_bq16"][:].unsqueeze(1).to_broadcast([128, 256, 8]), op=ALU.mult),
                        reads=[dgb, ci["c_bq16"]], writes=[L_all])
                S.op("dve", lambda e: e.tensor_scalar(out=OH[:], in0=ci["c_iota512"][:], scalar1=off_s[:, 0:1], scalar2=None,
                                                      op0=ALU.is_equal), reads=[ci["c_iota512"], off_s], writes=[OH])
                for r in range(256):
                    S.op("pe", lambda e, r=r: e.matmul(PU[0][0:24, 0:256], lhsT=L_all[:, r, :], rhs=OH[:, 256 - r:512 - r],
                                                       start=(r == 0), stop=(r == 255)), reads=[L_all, OH], writes=[PU[0]],
                         pe_acc=(r > 0))
                S.op("dve", lambda e: e.tensor_copy(out=Cs[:], in_=PU[0][0:24, 0:256]), reads=[PU[0]], writes=[Cs])
                for half in range(2):
                    S.op("pe", lambda e, half=half: e.transpose(out=PZ[1][:, half * 24:(half + 1) * 24],
                                                                in_=Cs[:, half * 128:(half + 1) * 128], identity=ident_f[0:24, 0:24]),
                         reads=[Cs, ident_f], writes=[PZ[1]])
                S.op("dve", lambda e: e.tensor_copy(out=dig_s[:].rearrange("p a b -> p (a b)"), in_=PZ[1][:, 0:48]),
                     reads=[PZ[1]], writes=[dig_s])
                S.op("dve", lambda e: e.scalar_tensor_tensor(out=idxf[:], in0=dig_s[:, :, 16:24], scalar=256.0, in1=dig_s[:, :, 8:16],
                                                             op0=ALU.mult, op1=ALU.add), reads=[dig_s], writes=[idxf])
                S.op("dve", lambda e: e.scalar_tensor_tensor(out=idxf[:], in0=idxf[:], scalar=256.0, in1=dig_s[:, :, 0:8],
                                                             op0=ALU.mult, op1=ALU.add), reads=[idxf, dig_s], writes=[idxf])
                S.op("dve", lambda e: e.tensor_scalar(out=vmask[:], in0=idxf[:], scalar1=0.5, scalar2=None, op0=ALU.is_gt),
                     reads=[idxf], writes=[vmask])
                S.op("dve", lambda e: e.tensor_scalar(out=idxf[:], in0=idxf[:], scalar1=-1.0, scalar2=0.0, op0=ALU.add, op1=ALU.max),
                     reads=[idxf], writes=[idxf])
                S.op("dve", lambda e: e.tensor_copy(out=idx_i[:], in_=idxf[:]), reads=[idxf], writes=[idx_i])
                S.barrier()
                S.emit()
            with ExitStack() as sC:
                Ksel = sb(sC, "Ksel", [128, 16, 1024], BF16)
                Vsel = sb(sC, "Vsel", [128, 16, 1024], BF16)
                prod = sb(sC, "prod", [128, 1024], F32)
                sT = sb(sC, "sT", [128, 2, 8], F32)
                pT = sb(sC, "pT", [128, 2, 8], BF16)
                prodn = sb(sC, "prodn", [8, 1024], F32)
                sTn = sb(sC, "sTn", [8, 8], F32)
                pTn = sb(sC, "pTn", [8, 8], BF16)
                rden_s = sb(sC, "rden_s", [8, 1], F32)
                Mq = sb(sC, "Mq", [8, 1024], BF16)
                for q in range(8):
                    for half in range(2):
                        for dst_, src_ in ((Ksel, ck), (Vsel, cv)):
                            S.dma("pool", lambda e, q=q, half=half, dst_=dst_, src_=src_: e.indirect_dma_start(
                                out=dst_[:, q * 2 + half, :], out_offset=None, in_=src_[:, :],
                                in_offset=bass.IndirectOffsetOnAxis(ap=idx_i[:, half, q:q + 1], axis=0)),
                                reads=[idx_i], writes=[dst_])
                PAs = [PG, PGL]
                for q in range(8):
                    for cb in range(2):
                        S.op("pe", lambda e, q=q, cb=cb: e.matmul(PZ[cb][:], lhsT=esel_b[:, q, :], rhs=aq_sb[:, cb * 512:(cb + 1) * 512],
                                                                  start=True, stop=True), reads=[esel_b, aq_sb], writes=[PZ[cb]])
                    for half in range(2):
                        for cb in range(2):
                            S.op("dve", lambda e, q=q, half=half, cb=cb: e.tensor_tensor(
                                out=prod[:, cb * 512:(cb + 1) * 512], in0=Ksel[:, q * 2 + half, cb * 512:(cb + 1) * 512],
                                in1=PZ[cb][:], op=ALU.mult), reads=[Ksel, PZ[cb]], writes=[prod])
                        S.op("dve", lambda e, half=half: e.tensor_reduce(out=sT[:, half, :], in_=prod[:].rearrange("p (h d) -> p h d", h=8),
                                                                         axis=AX.X, op=ALU.add), reads=[prod], writes=[sT])
                    S.op("act", lambda e: e.activation(out=sT[:].rearrange("p a b -> p (a b)"), in_=sT[:].rearrange("p a b -> p (a b)"),
                                                       func=AF.Exp), reads=[sT], writes=[sT])
                    S.op("dve", lambda e, q=q: e.tensor_tensor(out=pT[:], in0=sT[:], in1=vmask[:, :, q:q + 1].to_broadcast([128, 2, 8]),
                                                               op=ALU.mult), reads=[sT, vmask], writes=[pT])
                    for cb in range(2):
                        S.op("dve", lambda e, cb=cb: e.tensor_tensor(out=prodn[:, cb * 512:(cb + 1) * 512],
                                                                     in0=ak_sb[:, cb * 512:(cb + 1) * 512], in1=PZ[cb][0:8, :],
                                                                     op=ALU.mult), reads=[ak_sb, PZ[cb]], writes=[prodn])
                    S.op("dve", lambda e: e.tensor_reduce(out=sTn[:], in_=prodn[:].rearrange("p (h d) -> p h d", h=8), axis=AX.X,
                                                          op=ALU.add), reads=[prodn], writes=[sTn])
                    S.op("act", lambda e: e.activation(out=sTn[:], in_=sTn[:], func=AF.Exp), reads=[sTn], writes=[sTn])
                    S.op("dve", lambda e, q=q: e.tensor_scalar(out=pTn[:], in0=sTn[:], scalar1=selnT[:, q * 16:q * 16 + 1], scalar2=None,
                                                               op0=ALU.mult), reads=[sTn, selnT], writes=[pTn])
                    for cb in range(2):
                        cs_ = slice(cb * 512, (cb + 1) * 512)
                        S.op("pe", lambda e, q=q, cb=cb, cs_=cs_: e.matmul(PU[cb][0:8, :], lhsT=pT[:, 0, :], rhs=Vsel[:, q * 2, cs_],
                                                                           start=True, stop=False), reads=[pT, Vsel], writes=[PU[cb]])
                        S.op("pe", lambda e, q=q, cb=cb, cs_=cs_: e.matmul(PU[cb][0:8, :], lhsT=pT[:, 1, :], rhs=Vsel[:, q * 2 + 1, cs_],
                                                                           start=False, stop=False), reads=[pT, Vsel], writes=[PU[cb]],
                             pe_acc=True)
                        S.op("pe", lambda e, cb=cb, cs_=cs_: e.matmul(PU[cb][0:8, :], lhsT=pTn[:], rhs=av_sb[:, cs_],
                                                                      start=False, stop=True), reads=[pTn, av_sb], writes=[PU[cb]],
                             pe_acc=True)
                    S.op("pe", lambda e: e.matmul(PD[0:8, 0:2], lhsT=pT[:, 0, :], rhs=ones_b[:], start=True, stop=False),
                         reads=[pT, ones_b], writes=[PD])
                    S.op("pe", lambda e: e.matmul(PD[0:8, 0:2], lhsT=pT[:, 1, :], rhs=ones_b[:], start=False, stop=False),
                         reads=[pT, ones_b], writes=[PD], pe_acc=True)
                    S.op("pe", lambda e: e.matmul(PD[0:8, 0:2], lhsT=pTn[:], rhs=ones_b[0:8, :], start=False, stop=True),
                         reads=[pTn, ones_b], writes=[PD], pe_acc=True)
                    S.op("dve", lambda e: e.reciprocal(out=rden_s[:], in_=PD[0:8, 0:1]), reads=[PD], writes=[rden_s])
                    for cb in range(2):
                        cs_ = slice(cb * 512, (cb + 1) * 512)
                        S.op("dve", lambda e, cb=cb, cs_=cs_: e.scalar_tensor_tensor(
                            out=Mq[:, cs_], in0=PU[cb][0:8, :], scalar=rden_s[:, 0:1], in1=ci["c_bd8"][:, cs_],
                            op0=ALU.mult, op1=ALU.mult), reads=[PU[cb], rden_s, ci["c_bd8"]], writes=[Mq])
                        S.op("pe", lambda e, q=q, cb=cb, cs_=cs_: e.matmul(PAs[cb][0:8, :], lhsT=eq_b[:, q, :], rhs=Mq[:, cs_],
                                                                           start=(q == 0), stop=(q == 7)),
                             reads=[eq_b, Mq], writes=[PAs[cb]], pe_acc=(q > 0))
                for cb in range(2):
                    cs_ = slice(cb * 512, (cb + 1) * 512)
                    S.op("dve", lambda e, cb=cb, cs_=cs_: e.tensor_tensor(out=cat_s[:, cs_], in0=PAs[cb][0:8, :], in1=ga_s[:, cs_],
                                                                          op=ALU.mult), reads=[PAs[cb], ga_s], writes=[cat_s])
                S.barrier()
                S.emit()
        if DBG:
            S.dma("sp", lambda e: e.dma_start(out=o_cat[:, :], in_=catT[:].rearrange("p k t -> p (k t)")), reads=[catT])
        with ExitStack() as p7:
            wo_sb = sb(p7, "wo_sb", [128, KC, 2048], BF16)
            xr = [sb(p7, "xr%d" % j, [128, 2048], F32) for j in range(2)]
            rr = sb(p7, "rr", [128, 2048], F32)
            yo = [sb(p7, "yo%d" % j, [128, 2048], F32) for j in range(2)]
            lng_bc = sb(p7, "lng_bc", [128, 2048], F32)
            lnb_bc = sb(p7, "lnb_bc", [128, 2048], F32)
            stats = sb(p7, "stats", [128, 4, 6], F32)
            mv2 = sb(p7, "mv2", [128, 2], F32)
            rs2 = sb(p7, "rs2", [128, 2], F32)
            load_wres(wo_sb, Wo, 2048)
            S.dma("sp", lambda e: e.dma_start(out=lng_bc[:], in_=lng[0:1, :].broadcast_to([128, 2048])), writes=[lng_bc])
            S.dma("sp", lambda e: e.dma_start(out=lnb_bc[:], in_=lnb[0:1, :].broadcast_to([128, 2048])), writes=[lnb_bc])
            csT = sb(p7, "csT", [128, KC, 8], BF16)
            for kc in range(KC):
                S.op("pe", lambda e, kc=kc: e.transpose(out=PT[:, kc * 8:(kc + 1) * 8], in_=cat_s[:, kc * 128:(kc + 1) * 128],
                                                        identity=ident_b[0:8, 0:8]), reads=[cat_s, ident_b], writes=[PT])
            S.op("act", lambda e: e.activation(out=csT[:].rearrange("p k t -> p (k t)"), in_=PT[:, 0:128], func=AF.Copy),
                 reads=[PT], writes=[csT])

            def merge_rows(np_, j, lhs_fn, lhs_deps, x_src, o_dst):
                x_, y_ = xr[j % 2], yo[j % 2]
                S.dma("sp", lambda e: e.dma_start(out=x_[0:np_, :], in_=x_src), writes=[x_])
                for c in range(4):
                    pz = PZ[c % 2]
                    for k in range(KC):
                        S.op("pe", lambda e, k=k, c=c, pz=pz: e.matmul(
                            pz[0:np_, :], lhsT=lhs_fn(k), rhs=wo_sb[:, k, c * 512:(c + 1) * 512],
                            start=(k == 0), stop=(k == KC - 1)), reads=lhs_deps + [wo_sb], writes=[pz], pe_acc=(k > 0))
                    S.op("dve", lambda e, c=c, pz=pz: e.scalar_tensor_tensor(
                        out=rr[0:np_, c * 512:(c + 1) * 512], in0=x_[0:np_, c * 512:(c + 1) * 512], scalar=ALPHA, in1=pz[0:np_, :],
                        op0=ALU.mult, op1=ALU.add), reads=[x_, pz], writes=[rr])
                    S.op("dve", lambda e, c=c: e.bn_stats(out=stats[0:np_, c, :], in_=rr[0:np_, c * 512:(c + 1) * 512]),
                         reads=[rr], writes=[stats])
                S.op("dve", lambda e: e.bn_aggr(out=mv2[0:np_, :], in_=stats[0:np_].rearrange("p a b -> p (a b)")),
                     reads=[stats], writes=[mv2])
                S.op("dve", lambda e: e.tensor_scalar(out=rs2[0:np_, 0:1], in0=mv2[0:np_, 1:2], scalar1=LN_EPS, scalar2=None,
                                                      op0=ALU.add), reads=[mv2], writes=[rs2])
                S.op("act", lambda e: e.activation(out=rs2[0:np_, 0:1], in_=rs2[0:np_, 0:1], func=AF.Sqrt), reads=[rs2], writes=[rs2])
                S.op("dve", lambda e: e.reciprocal(out=rs2[0:np_, 0:1], in_=rs2[0:np_, 0:1]), reads=[rs2], writes=[rs2])
                S.op("dve", lambda e: e.scalar_tensor_tensor(out=rs2[0:np_, 1:2], in0=mv2[0:np_, 0:1], scalar=-1.0,
                                                             in1=rs2[0:np_, 0:1], op0=ALU.mult, op1=ALU.mult),
                     reads=[mv2, rs2], writes=[rs2])
                S.op("act", lambda e: e.activation(out=y_[0:np_, :], in_=rr[0:np_, :], func=AF.Identity, scale=rs2[0:np_, 0:1],
                                                   bias=rs2[0:np_, 1:2]), reads=[rr, rs2], writes=[y_])
                S.op("pool", lambda e: e.tensor_tensor(out=y_[0:np_, :], in0=y_[0:np_, :], in1=lng_bc[0:np_, :], op=ALU.mult),
                     reads=[y_, lng_bc], writes=[y_])
                S.op("pool", lambda e: e.tensor_tensor(out=y_[0:np_, :], in0=y_[0:np_, :], in1=lnb_bc[0:np_, :], op=ALU.add),
                     reads=[y_, lnb_bc], writes=[y_])
                S.dma("sp", lambda e: e.dma_start(out=o_dst, in_=y_[0:np_, :]), reads=[y_])
            for i in range(NO):
                merge_rows(128, i, lambda k, i=i: catT[:, k, i * 128:(i + 1) * 128], [catT],
                           xo[i * 128:(i + 1) * 128, :], o_y[i * 128:(i + 1) * 128, :])
            merge_rows(8, NO, lambda k: csT[:, k, :], [csT], xs[:, :], o_ys[:, :])
            S.barrier()
            S.emit()
        es_S.close()
        es_C.close()
    return nc


def _rope_tab(pos, half):
    inv = (np.float32(ROPE_THETA) ** (-np.arange(half, dtype=np.float32) / np.float32(half))).astype(np.float32)
    ang = pos.astype(np.float32)[:, None] * inv[None, :]
    c = np.cos(ang).astype(np.float32)
    s = np.sin(ang).astype(np.float32)
    return np.concatenate([c, c], 1), np.concatenate([-s, s], 1)


def _consts():
    s = np.arange(128)
    same = (s[:, None] // 64) == (s[None, :] // 64)
    tri2 = (same & (s[:, None] <= s[None, :])).astype(np.float32)
    blk2 = same.astype(np.float32)
    return dict(c_tri2=tri2, c_blk2=blk2, c_ident=np.eye(128, dtype=np.float32))


_NC_CACHE = {}


def kernel(x_prompt, x_sample, mem_prompt, cache_k, cache_v, cache_idx_k, state_hgrn,
           cache_mem_k, cache_mem_v, page_table, w_in, lb_logits, hgrn_norm_g,
           w_mem_k, w_mem_v, w_out, ln_g, ln_b):
    f32 = np.float32
    x_prompt = np.asarray(x_prompt, f32)
    w = np.asarray(w_in, f32)[0]
    ca = np.ascontiguousarray
    Wk = ca(w[:, O_AK:O_AV])
    Wv = ca(w[:, O_AV:O_AG])
    W3 = ca(np.concatenate([w[:, O_BF:O_BI], w[:, O_BI:O_BG], w[:, O_IK:O_IW]], axis=1))
    pos = np.arange(SEQ)
    cc128, ss128 = _rope_tab(pos, 16)
    cc64, ss64 = _rope_tab(pos, 8)
    ropeN = ca(np.concatenate([cc128, ss128, cc64, ss64], 1).astype(f32))
    consts = _consts()
    consts["c_j"] = np.tile(np.arange(256, dtype=f32)[None, :], (128, 1))
    pw = np.zeros(48, f32)
    for k in range(24):
        pw[k] = 2.0 ** -(k + 1)
        pw[24 + k] = 2.0 ** -(k + 2) if k < 23 else 2.0 ** -24
    consts["c_pow"] = np.tile(pw[None, :], (128, 1))
    Wi_ = ca(np.concatenate([w[:, O_IQ:O_IK], w[:, O_IW:O_BQ]], axis=1))
    Wb_ = ca(np.concatenate([w[:, O_BQ:O_BF], w[:, O_BF:O_BI], w[:, O_BI:O_BG], w[:, O_BG:O_MQ]], axis=1))
    consts.update(Wi=Wi_, Wq=ca(w[:, O_AQ:O_AK]), Wg=ca(w[:, O_AG:O_IQ]), Wb=Wb_, Wm=ca(w[:, O_MQ:O_END]),
                  Wo=ca(np.asarray(w_out, f32)[0]), ng=ca(np.asarray(hgrn_norm_g, f32).reshape(1, 128)),
                  lng=ca(np.asarray(ln_g, f32).reshape(1, D)), lnb=ca(np.asarray(ln_b, f32).reshape(1, D)))
    qs_ = np.float32(128.0 ** -0.5)
    P = np.arange(128)
    hq_h, hq_q = P // 8, P % 8
    qs_q, qs_s = P // 16, P % 16
    consts["c_sel8"] = (np.arange(8)[:, None] == hq_q[None, :]).astype(f32)
    consts["c_hm"] = (hq_h[:, None] == np.arange(16)[None, :]).astype(f32)
    consts["c_bq8"] = (hq_q[:, None] == np.arange(8)[None, :]).astype(f32)
    consts["c_bq16"] = (qs_q[:, None] == np.arange(8)[None, :]).astype(f32)
    sameq = (qs_q[:, None] == qs_q[None, :])
    consts["c_BQ1"] = sameq.astype(f32)
    consts["c_BQm"] = (sameq / 16.0).astype(f32)
    consts["c_LT"] = (sameq & (qs_s[:, None] < qs_s[None, :])).astype(f32)
    consts["c_negm"] = np.where((qs_s[:, None] == 0) & (np.arange(8)[None, :] <= qs_q[:, None]), 0.0, NEG).astype(f32)
    consts["c_aiota"] = (qs_s[:, None] * 8 + (np.arange(1024)[None, :] // 128) + 1).astype(f32)
    consts["c_iota512"] = np.tile((np.arange(512) - 256).astype(f32)[None, :], (128, 1))
    esel = np.zeros((8, 8, 128), f32)
    eq = np.zeros((8, 8, 8), f32)
    for q_ in range(8):
        esel[q_, q_, :] = 1.0
        eq[:, q_, q_] = 1.0
    consts["c_esel"] = esel.reshape(8, 1024)
    consts["c_eq"] = eq.reshape(8, 64)
    consts["c_bd8"] = np.repeat((np.arange(8)[:, None] == np.arange(8)[None, :]).astype(f32), 128, axis=1)
    pw2 = np.zeros(60, f32)
    for k in range(30):
        pw2[k] = 2.0 ** -(k + 1)
        pw2[30 + k] = 2.0 ** -(k + 2) if k < 29 else 2.0 ** -30
    consts["c_pow2"] = np.tile(pw2[None, :], (128, 1))
    consts["c_i32"] = np.concatenate([np.full((128, 256), 255), np.full((128, 256), 8), np.full((128, 256), 16)], 1).astype(np.int32)
    ck_flat = np.asarray(cache_k, f32)[0].reshape(-1, 1024)
    cv_flat = np.asarray(cache_v, f32)[0].reshape(-1, 1024)
    cidx_flat = np.asarray(cache_idx_k, f32)[0].reshape(1280, 8192)
    consts.update(ck=ck_flat, cv=cv_flat, cidx=cidx_flat)
    shared = dict(Wk=Wk, Wv=Wv, W3=W3, Wmk=ca(np.asarray(w_mem_k, f32)[0]), Wmv=ca(np.asarray(w_mem_v, f32)[0]),
                  lbl=ca(np.asarray(lb_logits, f32)), ropeN=ropeN, **consts)
    pos_s = PAST + np.arange(8)
    c128s, s128s = _rope_tab(pos_s, 16)
    c64s, s64s = _rope_tab(pos_s, 8)
    ropeS = ca(np.concatenate([c128s, s128s, c64s, s64s, c128s * qs_, s128s * qs_], 1).astype(f32))
    x_sample = np.asarray(x_sample, f32)
    in_maps = []
    for c in range(8):
        b, h = c // 2, c % 2
        posq = np.zeros((128, NO + 1), f32)
        for i in range(NO):
            posq[:, i] = (2 * i + h) * 128 + np.arange(128)
        posq[:, NO] = h
        m = dict(shared)
        own = np.concatenate([(2 * i + h) * 128 + np.arange(128) for i in range(NO)])
        ropeO = ca(np.concatenate([cc128[own] * qs_, ss128[own] * qs_, cc64[own], ss64[own]], 1).astype(f32))
        m.update(xTn=ca(x_prompt[b].T), memT=ca(np.asarray(mem_prompt, f32)[b].T), posq=posq,
                 xTo=ca(x_prompt[b][own].T), xo=ca(x_prompt[b][own]), ropeO=ropeO,
                 xsT=ca(x_sample[c].T), xs=ca(x_sample[c]), ropeS=ropeS,
                 st_h=ca(np.asarray(state_hgrn, f32)[0, c]),
                 ptab=ca(np.asarray(page_table)[c].astype(np.int32).reshape(1, 128)),
                 cmk_d=ca(np.asarray(cache_mem_k, f32)[0, c].reshape(256, 512)),
                 cmv_d=ca(np.asarray(cache_mem_v, f32)[0, c].reshape(256, 512)))
        in_maps.append(m)
    import os
    stage = int(os.environ.get("KSTAGE", "99"))
    Sched.LIMIT = int(os.environ.get("KLIMIT", str(10 ** 9)))
    ncores = int(os.environ.get("KCORES", "8"))
    if "nc" not in _NC_CACHE:
        _NC_CACHE["nc"] = build_program(stage)
    nc = _NC_CACHE["nc"]
    res = run_bass_kernel_spmd(nc, in_maps[:ncores], core_ids=list(range(ncores)))
    if ncores < 8:
        res.results.extend([res.results[0]] * (8 - ncores))
    R = res.results
    _NC_CACHE["R"] = R
    B = 4
    y_p = np.zeros((B, SEQ, D), f32)
    if "o_y" in R[0]:
        for c in range(8):
            b, h = c // 2, c % 2
            oy = R[c]["o_y"]
            for i in range(NO):
                n = 2 * i + h
                y_p[b, n * 128:(n + 1) * 128] = oy[i * 128:(i + 1) * 128]
    y_s = np.zeros((8, 8, D), f32)
    k_p = np.stack([R[2 * b]["o_k"].reshape(SEQ, 8, 128) for b in range(B)])[None]
    v_p = np.stack([R[2 * b]["o_v"].reshape(SEQ, 8, 128) for b in range(B)])[None]
    ik_p = np.stack([R[2 * b]["o_ik"] for b in range(B)])[None]
    hg_p = np.stack([R[2 * b]["o_hg"].transpose(1, 0, 2) for b in range(B)])[None]
    mk_p = np.stack([R[2 * b]["o_mk"].reshape(256, 4, 128) for b in range(B)])[None]
    mv_p = np.stack([R[2 * b]["o_mv"].reshape(256, 4, 128) for b in range(B)])[None]
    k_s = np.stack([R[c]["o_ks"].reshape(8, 8, 128) for c in range(8)])[None].astype(f32)
    v_s = np.stack([R[c]["o_vs"].reshape(8, 8, 128) for c in range(8)])[None].astype(f32)
    ik_s = np.stack([R[c]["o_iks"] for c in range(8)])[None].astype(f32)
    hg_s = np.stack([R[c]["o_hgs"].transpose(1, 0, 2) for c in range(8)])[None].astype(f32)
    y_s = np.stack([R[c]["o_ys"] for c in range(8)]).astype(f32)
    return (y_p, y_s, k_p.astype(f32), v_p.astype(f32), ik_p.astype(f32), hg_p.astype(f32), mk_p.astype(f32),
            mv_p.astype(f32), k_s, v_s, ik_s, hg_s)
```

```python
from contextlib import ExitStack
import numpy as np
import concourse.bass as bass
import concourse.mybir as mybir
from concourse.bass_utils import run_bass_kernel_spmd

F32 = mybir.dt.float32
BF16 = mybir.dt.bfloat16
I32 = mybir.dt.int32
AF = mybir.ActivationFunctionType
ALU = mybir.AluOpType
AX = mybir.AxisListType

D = 2048
KC = 16
SEQ = 2048
NT = 16
NO = 8
ROPE_THETA = 500000.0
PAST = 16384
ALPHA = 2.0 ** 0.25
LN_EPS = 1e-5
RMS_EPS = 1e-6
NEG = -1.0e30

O_AQ, O_AK, O_AV, O_AG, O_IQ, O_IK, O_IW, O_BQ, O_BF, O_BI, O_BG, O_MQ, O_MG, O_END = (
    0, 1024, 2048, 3072, 4096, 5120, 5184, 5200, 5712, 6224, 6736, 7248, 7760, 8272)


class Buf:
    __slots__ = ("name", "w", "r", "dsem", "dval", "excl")

    def __init__(self, name):
        self.name = name
        self.excl = False
        self.w = None
        self.r = {}
        self.dsem = None
        self.dval = 0


class TL:
    def __init__(self, t, name):
        self.t = t
        self.b = Buf(name)

    def __getitem__(self, k):
        return self.t[k]


class TLV(TL):
    def __init__(self, base, fn):
        self.t = None
        self.base = base
        self.fn = fn
        self.b = base.b

    def __getitem__(self, k):
        return self.fn(self.base.t)[k]


class Sched:
    ENG = ("pe", "act", "dve", "pool", "sp")

    def __init__(self, nc, es):
        self.nc = nc
        self.es = es
        self.q = {e: [] for e in self.ENG}
        self.cnt = {e: 0 for e in self.ENG}
        self.known = {e: {} for e in self.ENG}
        self.sems = {}
        for e in self.ENG:
            self.sems["e:" + e] = es.enter_context(nc.semaphore("s_" + e))
        self.ndsem = 0
        self.ninstr = 0
        self.dvals = {}

    def _dsem(self, buf):
        if buf.dsem is None:
            key = "d:%d" % self.ndsem
            self.ndsem += 1
            self.sems[key] = self.es.enter_context(self.nc.semaphore("sd%d" % self.ndsem))
            buf.dsem = key
        return buf.dsem

    def _deps(self, eng, reads, writes, skip_self_pe=False):
        deps = {}

        def add(k, v):
            if deps.get(k, 0) < v:
                deps[k] = v
        for b in reads:
            if b.w is not None:
                add(*b.w)
            if b.excl:
                for k, v in b.r.items():
                    if k != "e:" + eng:
                        add(k, v)
        for b in writes:
            if b.w is not None:
                add(*b.w)
            for k, v in b.r.items():
                add(k, v)
        waits = []
        for k, v in deps.items():
            if skip_self_pe and k == "e:pe":
                continue
            if self.known[eng].get(k, 0) < v:
                self.known[eng][k] = v
                waits.append((k, v))
        return waits

    def _mark(self, ev, reads, writes):
        for b in reads:
            if b.r.get(ev[0], 0) < ev[1]:
                b.r[ev[0]] = ev[1]
        for b in writes:
            b.w = ev
            b.r = {}

    LIMIT = 10 ** 9

    def op(self, eng, fn, reads=(), writes=(), pe_acc=False):
        if self.ninstr >= Sched.LIMIT:
            return None
        reads = [x.b if isinstance(x, TL) else x for x in reads]
        writes = [x.b if isinstance(x, TL) else x for x in writes]
        waits = self._deps(eng, reads, writes, skip_self_pe=(eng == "pe" and pe_acc))
        self.cnt[eng] += 1
        ev = ("e:" + eng, self.cnt[eng])
        self.q[eng].append((waits, fn, ev[0], 1))
        self._mark(ev, reads, writes)
        self.ninstr += 1
        return ev

    def dma(self, queue, fns, reads=(), writes=(), owner=None):
        if self.ninstr >= Sched.LIMIT:
            return None
        reads = [x.b if isinstance(x, TL) else x for x in reads]
        writes = [x.b if isinstance(x, TL) else x for x in writes]
        if not isinstance(fns, (list, tuple)):
            fns = [fns]
        if owner is None:
            owner = (list(writes) + list(reads))[0]
        elif isinstance(owner, TL):
            owner = owner.b
        key = self._dsem(owner)
        waits = self._deps(queue, reads, writes)
        for i, fn in enumerate(fns):
            owner.dval += 16
            self.q[queue].append((waits if i == 0 else [], fn, key, 16))
            self.ninstr += 1
        ev = (key, owner.dval)
        self.dvals[key] = owner.dval
        self._mark(ev, reads, writes)
        return ev

    def barrier(self):
        tgt = {"e:" + e: self.cnt[e] for e in self.ENG if self.cnt[e] > 0}
        for k, v in self.dvals.items():
            tgt[k] = v
        for e in self.ENG:
            waits = []
            for k, v in tgt.items():
                if k == "e:" + e:
                    continue
                if self.known[e].get(k, 0) < v:
                    self.known[e][k] = v
                    waits.append((k, v))
            self.q[e].append((waits, None, None, 0))

    def finish_wait(self, bufs, eng="sp"):
        deps = {}
        for b in bufs:
            b = b.b if isinstance(b, TL) else b
            for k, v in ([b.w] if b.w else []) + list(b.r.items()):
                if deps.get(k, 0) < v:
                    deps[k] = v
        self.q[eng].append((list(deps.items()), None, None, 0))

    def emit(self):
        nc = self.nc
        if not hasattr(self, "hw"):
            self.hw = {e: 0 for e in self.ENG}
            self.emitted = {e: 0 for e in self.ENG}
            self.vmap = {e: {} for e in self.ENG}
        ref = {e: set() for e in self.ENG}
        for e in self.ENG:
            for waits, fn, semkey, inc in self.q[e]:
                for k, v in waits:
                    if k.startswith("e:"):
                        ref[k[2:]].add(v)
        plan = {}
        for e in self.ENG:
            key = "e:" + e
            idx = self.emitted[e]
            keep = []
            for waits, fn, semkey, inc in self.q[e]:
                if semkey == key:
                    idx += 1
                    if idx in ref[e]:
                        self.hw[e] += 1
                        self.vmap[e][idx] = self.hw[e]
                        keep.append(True)
                    else:
                        keep.append(False)
                else:
                    keep.append(semkey is not None)
            self.emitted[e] = idx
            plan[e] = keep
        engobj = {"pe": "tensor", "act": "scalar", "dve": "vector", "pool": "gpsimd", "sp": "sync"}
        with nc.Block() as block:
            for e in self.ENG:
                items = self.q[e]
                if not items:
                    continue

                def body(eng, items=items, keep=plan[e]):
                    for (waits, fn, semkey, inc), kp in zip(items, keep):
                        for k, v in waits:
                            if k.startswith("e:"):
                                v = self.vmap[k[2:]][v]
                            eng.wait_ge(self.sems[k], v)
                        if fn is not None:
                            ins = fn(eng)
                            if kp:
                                ins.then_inc(self.sems[semkey], inc)
                getattr(block, engobj[e])(body)
        self.q = {e: [] for e in self.ENG}


def build_program(stage=99):
    nc = bass.Bass("TRN2", target_bir_lowering=False)

    def din(name, shape, dt=F32):
        return nc.dram_tensor(name, list(shape), dt, kind="ExternalInput").ap()

    def dout(name, shape, dt=F32):
        return nc.dram_tensor(name, list(shape), dt, kind="ExternalOutput").ap()

    xTn = din("xTn", [D, SEQ])
    memT = din("memT", [D, 256])
    Wk = din("Wk", [D, 1024])
    Wv = din("Wv", [D, 1024])
    W3 = din("W3", [D, 1088])
    Wmk = din("Wmk", [D, 512])
    Wmv = din("Wmv", [D, 512])
    lbl = din("lbl", [2, 512])
    ropeN = din("ropeN", [SEQ, 96])
    c_tri2 = din("c_tri2", [128, 128])
    c_blk2 = din("c_blk2", [128, 128])
    c_ident = din("c_ident", [128, 128])
    posq = din("posq", [128, NO + 1])
    xTo = din("xTo", [D, 1024])
    xo = din("xo", [1024, D])
    Wi = din("Wi", [D, 1040])
    Wq = din("Wq", [D, 1024])
    Wg = din("Wg", [D, 1024])
    Wb = din("Wb", [D, 2048])
    Wm = din("Wm", [D, 1024])
    Wo = din("Wo", [D, D])
    ropeO = din("ropeO", [1024, 96])
    c_j = din("c_j", [128, 256])
    c_pow = din("c_pow", [128, 48])
    ng = din("ng", [1, 128])
    lng = din("lng", [1, D])
    lnb = din("lnb", [1, D])
    o_k = dout("o_k", [SEQ, 1024])
    o_v = dout("o_v", [SEQ, 1024])
    o_ik = dout("o_ik", [SEQ, 64])
    o_hg = dout("o_hg", [128, 4, 128])
    o_mk = dout("o_mk", [256, 512])
    o_mv = dout("o_mv", [256, 512])
    o_y = dout("o_y", [1024, D])
    xsT = din("xsT", [D, 8])
    xs = din("xs", [8, D])
    ropeS = din("ropeS", [8, 160])
    st_h = din("st_h", [4, 128, 128])
    cmk_d = din("cmk_d", [256, 512])
    cmv_d = din("cmv_d", [256, 512])
    ptab = din("ptab", [1, 128], I32)
    cidx = din("cidx", [1280, 8192])
    ck = din("ck", [163840, 1024])
    cv = din("cv", [163840, 1024])
    cin = {}
    for nm_, shp_, dt__ in (("c_sel8", [8, 128], F32), ("c_hm", [128, 16], F32), ("c_bq8", [128, 8], F32),
                            ("c_bq16", [128, 8], F32), ("c_BQ1", [128, 128], F32), ("c_BQm", [128, 128], F32),
                            ("c_LT", [128, 128], F32), ("c_negm", [128, 8], F32), ("c_aiota", [128, 1024], F32),
                            ("c_iota512", [128, 512], F32), ("c_esel", [8, 1024], F32), ("c_eq", [8, 64], F32),
                            ("c_bd8", [8, 1024], F32), ("c_pow2", [128, 60], F32), ("c_i32", [128, 768], I32)):
        cin[nm_] = din(nm_, shp_, dt__)
    o_ks = dout("o_ks", [8, 1024])
    o_vs = dout("o_vs", [8, 1024])
    o_iks = dout("o_iks", [8, 64])
    o_hgs = dout("o_hgs", [128, 4, 128])
    o_ys = dout("o_ys", [8, D])
    import os
    DBG = os.environ.get("KDBG", "0") == "1"
    if DBG:
        o_cat = dout("o_cat", [128, KC * 1024], BF16)

    with ExitStack() as es:
        S = Sched(nc, es)
        outbufs = []

        def sb(st, name, shape, dt):
            return TL(st.enter_context(nc.sbuf_tensor(name, list(shape), dt)), name)

        def ps(st, name, shape, dt):
            t = TL(st.enter_context(nc.psum_tensor(name, list(shape), dt)), name)
            t.b.excl = True
            return t

        ident_f = sb(es, "ident_f", [128, 128], F32)
        ident_b = sb(es, "ident_b", [128, 128], BF16)
        tri2 = sb(es, "tri2", [128, 128], F32)
        blk2 = sb(es, "blk2", [128, 128], F32)
        ones_f = sb(es, "ones_f", [128, 2], F32)
        lb_bc = sb(es, "lb_bc", [128, 512], F32)
        oml_bc = sb(es, "oml_bc", [128, 512], F32)
        posq_sb = sb(es, "posq_sb", [128, NO + 1], F32)
        hflag = sb(es, "hflag", [128, 1], F32)
        mkT = sb(es, "mkT", [128, 4, 256], BF16)
        mv = sb(es, "mv", [128, 2, 4, 130], BF16)
        Sown = sb(es, "Sown", [128, NO, 512], BF16)
        xsT_b = sb(es, "xsT_b", [128, KC, 8], BF16)
        zst = sb(es, "zst", [8, 512], F32)
        R16 = sb(es, "R16", [128, 8192], BF16)
        es_A = ExitStack()
        KT = sb(es_A, "KT", [128, 8, SEQ], BF16)
        V = sb(es_A, "V", [128, NT, 8, 130], BF16)
        ikT = sb(es_A, "ikT", [64, SEQ], BF16)

        PZ = [ps(es, "PZ%d" % i, [128, 512], F32) for i in range(2)]
        PT = ps(es, "PT", [128, 1024], BF16)
        PG = ps(es, "PG", [128, 512], F32)
        PGL = ps(es, "PGL", [128, 512], F32)
        PU = [ps(es, "PU%d" % i, [128, 512], F32) for i in range(2)]
        PD = ps(es, "PD", [128, 512], F32)

        zs_d = nc.dram_tensor("zs_d", [8, 8448], F32).ap()
        zsd_b = Buf("zs_d")
        S.dma("pool", lambda e: e.dma_start(out=xsT_b[:], in_=xsT.rearrange("(k p) t -> p k t", p=128)), writes=[xsT_b])
        ZOFF = dict(q=0, k=1024, v=2048, g=3072, iq=4096, iw=5120, ik=5136, b=5200, m=7248)

        def zs_chunk(wsb, c0, ncol, dcol):
            for k in range(KC):
                S.op("pe", lambda e, k=k: e.matmul(PD[0:8, 0:ncol], lhsT=xsT_b[:, k, :], rhs=wsb[:, k, c0:c0 + ncol],
                                                   start=(k == 0), stop=(k == KC - 1)),
                     reads=[xsT_b, wsb], writes=[PD], pe_acc=(k > 0))
            S.op("act", lambda e: e.activation(out=zst[:, 0:ncol], in_=PD[0:8, 0:ncol], func=AF.Copy), reads=[PD], writes=[zst])
            S.dma("sp", lambda e: e.dma_start(out=zs_d[:, dcol:dcol + ncol], in_=zst[:, 0:ncol]), reads=[zst], writes=[zsd_b],
                  owner=zst)

        S.dma("sp", lambda e: e.dma_start(out=ident_f[:], in_=c_ident[:, :]), writes=[ident_f])
        S.dma("sp", lambda e: e.dma_start(out=tri2[:], in_=c_tri2[:, :]), writes=[tri2])
        S.dma("sp", lambda e: e.dma_start(out=blk2[:], in_=c_blk2[:, :]), writes=[blk2])
        S.dma("sp", lambda e: e.dma_start(out=posq_sb[:], in_=posq[:, :]), writes=[posq_sb])
        tmp_es = ExitStack()
        lb2 = sb(tmp_es, "lb2", [128, 2, 512], F32)
        S.dma("sp", [lambda e, l=l: e.dma_start(out=lb2[:, l, :], in_=lbl[l:l + 1, :].broadcast_to([128, 512]))
                     for l in range(2)], writes=[lb2])
        S.op("pool", lambda e: e.memset(ones_f[:], 1.0), writes=[ones_f])
        S.op("dve", lambda e: e.tensor_copy(out=ident_b[:], in_=ident_f[:]), reads=[ident_f], writes=[ident_b])
        S.op("dve", lambda e: e.tensor_tensor(out=oml_bc[:], in0=lb2[:, 0, :], in1=lb2[:, 1, :], op=ALU.subtract),
             reads=[lb2], writes=[oml_bc])
        S.op("act", lambda e: e.activation(out=lb_bc[:], in_=oml_bc[:], func=AF.Sigmoid), reads=[oml_bc], writes=[lb_bc])
        S.op("dve", lambda e: e.tensor_scalar(out=oml_bc[:], in0=lb_bc[:], scalar1=-1.0, scalar2=1.0,
                                              op0=ALU.mult, op1=ALU.add), reads=[lb_bc], writes=[oml_bc])
        S.op("dve", lambda e: e.tensor_copy(out=hflag[:], in_=posq_sb[:, NO:NO + 1]), reads=[posq_sb], writes=[hflag])
        S.op("pool", lambda e: e.memset(V[:, :, :, 128:129], 1.0), writes=[V])
        S.op("pool", lambda e: e.memset(mv[:, :, :, 128:129], 1.0), writes=[mv])
        S.barrier()
        S.emit()
        tmp_es.close()

        def rope(src3, dst3, nh, half, CC, SS, tA, tB, np_=128):
            r = 2 * half
            ccb = CC.unsqueeze(1).to_broadcast([np_, nh, r])
            s1b = SS[:, 0:half].unsqueeze(1).to_broadcast([np_, nh, half])
            s2b = SS[:, half:r].unsqueeze(1).to_broadcast([np_, nh, half])
            S.op("dve", lambda e: e.tensor_tensor(out=tA[0:np_, 0:nh, 0:r], in0=src3(0, r), in1=ccb, op=ALU.mult),
                 reads=src3.deps, writes=[tA])
            S.op("dve", lambda e: e.tensor_tensor(out=tB[0:np_, 0:nh, 0:half], in0=src3(half, r), in1=s1b, op=ALU.mult),
                 reads=src3.deps, writes=[tB])
            S.op("dve", lambda e: e.tensor_tensor(out=tB[0:np_, 0:nh, half:r], in0=src3(0, half), in1=s2b, op=ALU.mult),
                 reads=src3.deps + [tB], writes=[tB])
            S.op("dve", lambda e: e.tensor_tensor(out=dst3(0, r), in0=tA[0:np_, 0:nh, 0:r], in1=tB[0:np_, 0:nh, 0:r],
                                                  op=ALU.add), reads=[tA, tB], writes=dst3.deps)

        class V3:
            def __init__(self, fn, deps):
                self.fn = fn
                self.deps = deps

            def __call__(self, lo, hi):
                return self.fn(lo, hi)

        with ExitStack() as pm:
            memTb = sb(pm, "memTb", [128, KC, 256], BF16)
            wmk = sb(pm, "wmk", [128, KC, 512], BF16)
            wmv = sb(pm, "wmv", [128, KC, 512], BF16)
            mf = [sb(pm, "mf%d" % i, [128, 512], F32) for i in range(2)]
            mb = sb(pm, "mb", [128, 512], BF16)
            S.dma("pool", [lambda e, k=k: e.dma_start(out=memTb[:, k, :], in_=memT[k * 128:(k + 1) * 128, :])
                           for k in range(KC)], writes=[memTb])
            S.dma("pool", [lambda e, k=k: e.dma_start(out=wmk[:, k, :], in_=Wmk[k * 128:(k + 1) * 128, :])
                           for k in range(KC)], writes=[wmk])
            S.dma("pool", [lambda e, k=k: e.dma_start(out=wmv[:, k, :], in_=Wmv[k * 128:(k + 1) * 128, :])
                           for k in range(KC)], writes=[wmv])
            cnt = 0
            for nt in range(2):
                for which in range(2):
                    wsb = wmk if which == 0 else wmv
                    pz = PZ[cnt % 2]
                    f = mf[cnt % 2]
                    cnt += 1
                    for k in range(KC):
                        S.op("pe", lambda e, k=k, pz=pz, wsb=wsb, nt=nt: e.matmul(
                            pz[:], lhsT=memTb[:, k, nt * 128:(nt + 1) * 128], rhs=wsb[:, k, :],
                            start=(k == 0), stop=(k == KC - 1)), reads=[memTb, wsb], writes=[pz], pe_acc=(k > 0))
                    S.op("act", lambda e, pz=pz, f=f: e.activation(out=f[:], in_=pz[:], func=AF.Copy),
                         reads=[pz], writes=[f])
                    dst = o_mk if which == 0 else o_mv
                    S.dma("sp", lambda e, f=f, dst=dst, nt=nt: e.dma_start(out=dst[nt * 128:(nt + 1) * 128, :], in_=f[:]),
                          reads=[f])
                    outbufs.append(f)
                    if which == 0:
                        S.op("dve", lambda e, pz=pz: e.tensor_copy(out=mb[:], in_=pz[:]), reads=[pz], writes=[mb])
                        for hd in range(4):
                            S.op("pe", lambda e, hd=hd: e.transpose(out=PT[:, hd * 128:(hd + 1) * 128],
                                                                    in_=mb[:, hd * 128:(hd + 1) * 128],
                                                                    identity=ident_b[:]),
                                 reads=[mb, ident_b], writes=[PT])
                        S.op("act", lambda e, nt=nt: e.activation(
                            out=mkT[:, :, nt * 128:(nt + 1) * 128],
                            in_=PT[:, 0:512].rearrange("p (h t) -> p h t", h=4), func=AF.Copy),
                            reads=[PT], writes=[mkT])
                    else:
                        S.op("dve", lambda e, pz=pz, nt=nt: e.tensor_copy(
                            out=mv[:, nt, :, 0:128], in_=pz[:].rearrange("p (h d) -> p h d", h=4)),
                            reads=[pz], writes=[mv])
            S.barrier()
            S.emit()
        if stage == 1:
            return nc

        with ExitStack() as pn:
            xh = sb(pn, "xh", [128, KC, 1024], BF16)
            W0 = sb(pn, "W0", [128, KC, 1088], BF16)
            W1 = TLV(R16, lambda t: t[:].rearrange("p (k c) -> p k c", k=KC))
            ropeN_sb = sb(pn, "ropeN_sb", [128, NT, 96], F32)
            Kf = [sb(pn, "Kf%d" % i, [128, 512], F32) for i in range(2)]
            Kb = [sb(pn, "Kb%d" % i, [128, 512], BF16) for i in range(2)]
            tA = sb(pn, "tA", [128, 4, 32], F32)
            tB = sb(pn, "tB", [128, 4, 32], F32)
            sg = sb(pn, "sg", [128, 512], F32)
            ff = sg
            gg = sb(pn, "gg", [128, 512], F32)
            omf = sb(pn, "omf", [128, 512], F32)
            Gs = sb(pn, "Gs", [128, 512], F32)
            dlt = sb(pn, "dlt", [128, 512], F32)
            EE = dlt
            kdec = sb(pn, "kdec", [128, 512], BF16)
            vb = sb(pn, "vb", [128, 512], BF16)
            ikf = sb(pn, "ikf", [128, 64], F32)
            ikb = sb(pn, "ikb", [128, 64], BF16)
            Dc = sb(pn, "Dc", [128, 16], F32)
            St = sb(pn, "St", [128, 4, 128], F32)
            Sev = sb(pn, "Sev", [128, 512], F32)
            Stmp = sb(pn, "Stmp", [128, 512], F32)

            S.dma("sp", lambda e: e.dma_start(out=ropeN_sb[:], in_=ropeN.rearrange("(n p) c -> p n c", p=128)),
                  writes=[ropeN_sb])
            S.op("pool", lambda e: e.memset(St[:], 0.0), writes=[St])

            def load_w(dst, src, c0, ncol):
                S.dma("pool", [lambda e, k=k: e.dma_start(out=dst[:, k, 0:ncol], in_=src[k * 128:(k + 1) * 128, c0:c0 + ncol])
                               for k in range(KC)], writes=[dst])

            def mm_chunk(pz, tl, wsb, c0, ncol):
                for k in range(KC):
                    S.op("pe", lambda e, k=k: e.matmul(pz[:, 0:ncol], lhsT=xh[:, k, tl * 128:(tl + 1) * 128],
                                                       rhs=wsb[:, k, c0:c0 + ncol], start=(k == 0), stop=(k == KC - 1)),
                         reads=[xh, wsb], writes=[pz], pe_acc=(k > 0))

            ctr = [0]
            for half in range(2 if stage != 2 else 1):
                S.dma("pool", [lambda e, k=k, half=half: e.dma_start(out=xh[:, k, :],
                                                          in_=xTn[k * 128:(k + 1) * 128, half * 1024:(half + 1) * 1024])
                               for k in range(KC)], writes=[xh])
                for blk in range(2):
                    wsb = W0 if blk == 0 else W1
                    load_w(wsb, Wk, blk * 512, 512)
                    if half == 0:
                        zs_chunk(wsb, 0, 512, ZOFF["k"] + blk * 512)
                    for tl in range(8):
                        n = half * 8 + tl
                        i = ctr[0] % 2
                        ctr[0] += 1
                        pz, kf, kb = PZ[i], Kf[i], Kb[i]
                        mm_chunk(pz, tl, wsb, 0, 512)
                        S.op("act", lambda e, pz=pz, kf=kf: e.activation(out=kf[:], in_=pz[:], func=AF.Copy),
                             reads=[pz], writes=[kf])
                        src3 = V3(lambda lo, hi, pz=pz: pz[:].rearrange("p (h d) -> p h d", h=4)[:, :, lo:hi], [pz])
                        dst3 = V3(lambda lo, hi, kf=kf: kf[:].rearrange("p (h d) -> p h d", h=4)[:, :, lo:hi], [kf])
                        rope(src3, dst3, 4, 16, ropeN_sb[:, n, 0:32], ropeN_sb[:, n, 32:64], tA, tB)
                        S.dma("sp", lambda e, kf=kf, n=n, blk=blk: e.dma_start(
                            out=o_k[n * 128:(n + 1) * 128, blk * 512:(blk + 1) * 512], in_=kf[:]), reads=[kf])
                        S.op("pool", lambda e, kf=kf, kb=kb: e.tensor_copy(out=kb[:], in_=kf[:]), reads=[kf], writes=[kb])
                        for hd in range(4):
                            S.op("pe", lambda e, hd=hd, kb=kb: e.transpose(out=PT[:, hd * 128:(hd + 1) * 128],
                                                                           in_=kb[:, hd * 128:(hd + 1) * 128],
                                                                           identity=ident_b[:]),
                                 reads=[kb, ident_b], writes=[PT])
                        S.op("act", lambda e, n=n, blk=blk: e.activation(
                            out=KT[:, blk * 4:(blk + 1) * 4, n * 128:(n + 1) * 128],
                            in_=PT[:, 0:512].rearrange("p (h t) -> p h t", h=4), func=AF.Copy),
                            reads=[PT], writes=[KT])
                for blk in range(2):
                    wsb = W0 if blk == 0 else W1
                    load_w(wsb, Wv, blk * 512, 512)
                    if half == 0:
                        zs_chunk(wsb, 0, 512, ZOFF["v"] + blk * 512)
                    for tl in range(8):
                        n = half * 8 + tl
                        i = ctr[0] % 2
                        ctr[0] += 1
                        pz, kf = PZ[i], Kf[i]
                        mm_chunk(pz, tl, wsb, 0, 512)
                        S.op("act", lambda e, pz=pz, kf=kf: e.activation(out=kf[:], in_=pz[:], func=AF.Copy),
                             reads=[pz], writes=[kf])
                        S.dma("sp", lambda e, kf=kf, n=n, blk=blk: e.dma_start(
                            out=o_v[n * 128:(n + 1) * 128, blk * 512:(blk + 1) * 512], in_=kf[:]), reads=[kf])
                        S.op("dve", lambda e, pz=pz, n=n, blk=blk: e.tensor_copy(
                            out=V[:, n, blk * 4:(blk + 1) * 4, 0:128], in_=pz[:].rearrange("p (h d) -> p h d", h=4)),
                            reads=[pz], writes=[V])
                load_w(W0, W3, 0, 1088)
                if half == 0:
                    zs_chunk(W0, 1024, 64, ZOFF["ik"])
                for tl in range(8):
                    n = half * 8 + tl
                    pza, pzb = PZ[0], PZ[1]
                    mm_chunk(pza, tl, W0, 0, 512)
                    S.op("act", lambda e: e.activation(out=sg[:], in_=pza[:], func=AF.Sigmoid), reads=[pza], writes=[sg])
                    mm_chunk(pzb, tl, W0, 512, 512)
                    S.op("act", lambda e: e.activation(out=vb[:], in_=pzb[:], func=AF.Copy), reads=[pzb], writes=[vb])
                    mm_chunk(pza, tl, W0, 1024, 64)
                    S.op("act", lambda e: e.activation(out=ikf[:], in_=pza[:, 0:64], func=AF.Copy), reads=[pza], writes=[ikf])
                    src3 = V3(lambda lo, hi: pza[:, 0:64].rearrange("p (h d) -> p h d", h=1)[:, :, lo:hi], [pza])
                    dst3 = V3(lambda lo, hi: ikf[:].rearrange("p (h d) -> p h d", h=1)[:, :, lo:hi], [ikf])
                    rope(src3, dst3, 1, 8, ropeN_sb[:, n, 64:80], ropeN_sb[:, n, 80:96], tA, tB)
                    S.dma("sp", lambda e, n=n: e.dma_start(out=o_ik[n * 128:(n + 1) * 128, :], in_=ikf[:]), reads=[ikf])
                    S.op("pool", lambda e: e.tensor_copy(out=ikb[:], in_=ikf[:]), reads=[ikf], writes=[ikb])
                    S.op("pe", lambda e: e.transpose(out=PT[0:64, 512:640], in_=ikb[:], identity=ident_b[:]),
                         reads=[ikb, ident_b], writes=[PT])
                    S.op("act", lambda e, n=n: e.activation(out=ikT[:, n * 128:(n + 1) * 128], in_=PT[0:64, 512:640],
                                                            func=AF.Copy), reads=[PT], writes=[ikT])
                    S.op("dve", lambda e: e.tensor_tensor(out=ff[:], in0=sg[:], in1=oml_bc[:], op=ALU.mult),
                         reads=[sg, oml_bc], writes=[sg])
                    S.op("dve", lambda e: e.tensor_tensor(out=ff[:], in0=ff[:], in1=lb_bc[:], op=ALU.add),
                         reads=[ff, lb_bc], writes=[ff])
                    S.op("act", lambda e: e.activation(out=gg[:], in_=ff[:], func=AF.Ln), reads=[ff], writes=[gg])
                    S.op("pool", lambda e: e.tensor_scalar(out=omf[:], in0=ff[:], scalar1=-1.0, scalar2=1.0,
                                                           op0=ALU.mult, op1=ALU.add), reads=[ff], writes=[omf])
                    S.op("pe", lambda e: e.matmul(PG[:], lhsT=tri2[:], rhs=gg[:], start=True, stop=True),
                         reads=[tri2, gg], writes=[PG])
                    S.op("pe", lambda e: e.matmul(PGL[:], lhsT=blk2[:], rhs=gg[:], start=True, stop=True),
                         reads=[blk2, gg], writes=[PGL])
                    S.op("act", lambda e: e.activation(out=Gs[:], in_=PG[:], func=AF.Copy), reads=[PG], writes=[Gs])
                    S.op("dve", lambda e: e.tensor_tensor(out=dlt[:], in0=PGL[:], in1=Gs[:], op=ALU.subtract),
                         reads=[PGL, Gs], writes=[dlt])
                    S.op("act", lambda e: e.activation(out=EE[:], in_=dlt[:], func=AF.Exp), reads=[dlt], writes=[EE])
                    S.op("dve", lambda e: e.tensor_tensor(out=kdec[:], in0=omf[:], in1=EE[:], op=ALU.mult),
                         reads=[omf, EE], writes=[kdec])
                    for c in range(2):
                        for hd in range(4):
                            j = c * 4 + hd
                            S.op("pe", lambda e, c=c, hd=hd, j=j: e.matmul(
                                PD[:, 2 * j:2 * j + 2], lhsT=gg[c * 64:(c + 1) * 64, hd * 128:(hd + 1) * 128],
                                rhs=ones_f[c * 64:(c + 1) * 64, :], start=True, stop=True),
                                reads=[gg, ones_f], writes=[PD])
                    S.op("act", lambda e: e.activation(out=Dc[:], in_=PD[:, 0:16], func=AF.Exp), reads=[PD], writes=[Dc])
                    if n % 2 == 0:
                        S.op("pool", lambda e: e.tensor_copy(out=Sev[:], in_=St[:].rearrange("p h d -> p (h d)")),
                             reads=[St], writes=[Sev])
                    else:
                        S.op("dve", lambda e: e.tensor_tensor(out=Stmp[:], in0=St[:].rearrange("p h d -> p (h d)"),
                                                              in1=Sev[:], op=ALU.subtract), reads=[St, Sev], writes=[Stmp])
                        S.op("dve", lambda e, n=n: e.scalar_tensor_tensor(
                            out=Sown[:, n // 2, :], in0=Stmp[:], scalar=hflag[:, 0:1], in1=Sev[:],
                            op0=ALU.mult, op1=ALU.add), reads=[Stmp, hflag, Sev], writes=[Sown])
                    for c in range(2):
                        pu = PU[c]
                        for hd in range(4):
                            S.op("pe", lambda e, c=c, hd=hd, pu=pu: e.matmul(
                                pu[:, hd * 128:(hd + 1) * 128], lhsT=kdec[c * 64:(c + 1) * 64, hd * 128:(hd + 1) * 128],
                                rhs=vb[c * 64:(c + 1) * 64, hd * 128:(hd + 1) * 128], start=True, stop=True),
                                reads=[kdec, vb], writes=[pu])
                        for hd in range(4):
                            j = c * 4 + hd
                            S.op("dve", lambda e, hd=hd, j=j, pu=pu: e.scalar_tensor_tensor(
                                out=St[:, hd, :], in0=St[:, hd, :], scalar=Dc[:, 2 * j:2 * j + 1],
                                in1=pu[:, hd * 128:(hd + 1) * 128], op0=ALU.mult, op1=ALU.add),
                                reads=[St, Dc, pu], writes=[St])
            S.dma("sp", lambda e: e.dma_start(out=o_hg[:, :, :], in_=St[:]), reads=[St])
            S.barrier()
            S.emit()
        if stage <= 3:
            return nc
        QS = 128.0 ** -0.5
        KB = 24
        PA = [PG, PGL]
        PI = PU
        moff = [sum(2 * j + 2 for j in range(i)) for i in range(NO)]

        def load_wres(dst, src, ncol):
            S.dma("pool", [lambda e, k=k: e.dma_start(out=dst[:, k, 0:ncol], in_=src[k * 128:(k + 1) * 128, 0:ncol])
                           for k in range(KC)], writes=[dst])

        def load_xt(dst, i):
            S.dma("pool", lambda e: e.dma_start(out=dst[:], in_=xTo[:, i * 128:(i + 1) * 128].rearrange("(k p) t -> p k t", p=128)),
                  writes=[dst])

        def mm_x(pz, xt_, wsb, c0, ncol):
            for k in range(KC):
                S.op("pe", lambda e, k=k: e.matmul(pz[:, 0:ncol], lhsT=xt_[:, k, :], rhs=wsb[:, k, c0:c0 + ncol],
                                                   start=(k == 0), stop=(k == KC - 1)),
                     reads=[xt_, wsb], writes=[pz], pe_acc=(k > 0))

        def transpose4(src_fn, srcdeps, col0=0, n=4):
            for hd in range(n):
                S.op("pe", lambda e, hd=hd: e.transpose(out=PT[:, col0 + hd * 128:col0 + (hd + 1) * 128], in_=src_fn(hd),
                                                        identity=ident_b[:]), reads=srcdeps + [ident_b], writes=[PT])

        es_B = ExitStack()
        maskT = sb(es_B, "maskT", [128, 72, 128], BF16)
        ropeO_sb = sb(es_B, "ropeO_sb", [128, NO, 96], F32)
        S.dma("sp", lambda e: e.dma_start(out=ropeO_sb[:], in_=ropeO.rearrange("(n p) c -> p n c", p=128)), writes=[ropeO_sb])

        with ExitStack() as p1:
            Wi_sb = sb(p1, "Wi_sb", [128, KC, 1040], BF16)
            xt = [sb(p1, "xt1_%d" % j, [128, KC, 128], BF16) for j in range(2)]
            cb = sb(p1, "cb", [128, 256], F32)
            cj = sb(p1, "cj", [128, 256], F32)
            pw = sb(p1, "pw", [128, 2 * KB], F32)
            spw = sb(p1, "spw", [128, 2 * KB], F32)
            iq_b = sb(p1, "iq_b", [128, 16, 64], BF16)
            w_s = sb(p1, "w_s", [128, 16], F32)
            iqT = sb(p1, "iqT", [64, 16, 128], BF16)
            Dg = sb(p1, "Dg", [128, 16, 128], BF16)
            Tb = [sb(p1, "Tb%d" % j, [128, 512], BF16) for j in range(2)]
            I_sbs = [sb(p1, "I_sb%d" % j, [128, 2048], F32) for j in range(2)]
            cbias = sb(p1, "cbias", [128, 256], BF16)
            junk = sb(p1, "junk", [128, 2048], BF16)
            tA1 = sb(p1, "tA1", [128, 8, 16], F32)
            tB1 = sb(p1, "tB1", [128, 8, 16], F32)
            sm = sb(p1, "sm", [128, 8], F32)
            lo, hi, rng, mid, cnt, wv, thr0 = [sm[:, j:j + 1] for j in range(7)]
            load_wres(Wi_sb, Wi, 1040)
            zs_chunk(Wi_sb, 0, 512, ZOFF["iq"])
            zs_chunk(Wi_sb, 512, 512, ZOFF["iq"] + 512)
            zs_chunk(Wi_sb, 1024, 16, ZOFF["iw"])
            S.dma("sp", lambda e: e.dma_start(out=cj[:], in_=c_j[:, :]), writes=[cj])
            S.dma("sp", lambda e: e.dma_start(out=pw[:], in_=c_pow[:, :]), writes=[pw])
            S.op("dve", lambda e: e.tensor_scalar(out=cb[:], in0=cj[:], scalar1=posq_sb[:, 0:1], scalar2=NEG,
                                                  op0=ALU.is_gt, op1=ALU.mult), reads=[cj, posq_sb], writes=[cb])
            S.op("pool", lambda e: e.memset(sm[:, 6:7], -1.0e29), writes=[sm])
            S.op("dve", lambda e: e.tensor_copy(out=cbias[:], in_=cb[:]), reads=[cb], writes=[cbias])
            def stage_a(i):
                nk = (2 * i + 2) * 128
                I_sb = I_sbs[i % 2]
                x = xt[i % 2]
                load_xt(x, i)
                for c in range(2):
                    pz = PZ[c]
                    mm_x(pz, x, Wi_sb, c * 512, 512)
                    S.op("act", lambda e, pz=pz, c=c: e.activation(
                        out=iq_b[:, c * 8:(c + 1) * 8, :], in_=pz[:].rearrange("p (h d) -> p h d", h=8), func=AF.Copy),
                        reads=[pz], writes=[iq_b])
                    src3 = V3(lambda lo_, hi_, pz=pz: pz[:].rearrange("p (h d) -> p h d", h=8)[:, :, lo_:hi_], [pz])
                    dst3 = V3(lambda lo_, hi_, c=c: iq_b[:, c * 8:(c + 1) * 8, lo_:hi_], [iq_b])
                    rope(src3, dst3, 8, 8, ropeO_sb[:, i, 64:80], ropeO_sb[:, i, 80:96], tA1, tB1)
                pz = PZ[0]
                mm_x(pz, x, Wi_sb, 1024, 16)
                S.op("act", lambda e, pz=pz: e.activation(out=w_s[:], in_=pz[:, 0:16], func=AF.Copy, scale=1.0 / 32.0),
                     reads=[pz], writes=[w_s])
                for r in range(2):
                    for hh in range(8):
                        S.op("pe", lambda e, r=r, hh=hh: e.transpose(out=PT[0:64, hh * 128:(hh + 1) * 128],
                                                                     in_=iq_b[:, r * 8 + hh, :], identity=ident_b[:]),
                             reads=[iq_b, ident_b], writes=[PT])
                    S.op("act", lambda e, r=r: e.activation(out=iqT[:, r * 8:(r + 1) * 8, :],
                                                            in_=PT[0:64, :].rearrange("p (h t) -> p h t", h=8), func=AF.Copy),
                         reads=[PT], writes=[iqT])
                for h in range(16):
                    S.op("pool", lambda e, h=h: e.tensor_scalar(out=Dg[:, h, :], in0=ident_b[:], scalar1=w_s[:, h:h + 1],
                                                                scalar2=None, op0=ALU.mult),
                         reads=[ident_b, w_s], writes=[Dg])
                nch = (nk + 511) // 512
                lastlo = nk - 256
                for c in range(nch):
                    kw = min(512, nk - 512 * c)
                    pi = PI[c % 2]

                    def mm1(h, c=c, kw=kw):
                        pa = PA[h % 2]
                        S.op("pe", lambda e, h=h, pa=pa: e.matmul(pa[:, 0:kw], lhsT=iqT[:, h, :],
                                                                  rhs=ikT[:, c * 512:c * 512 + kw], start=True, stop=True),
                             reads=[iqT, ikT], writes=[pa])
                    mm1(0)
                    for h in range(16):
                        if h + 1 < 16:
                            mm1(h + 1)
                        pa = PA[h % 2]
                        tb = Tb[h % 2]
                        S.op("act", lambda e, pa=pa, tb=tb, kw=kw: e.activation(out=tb[:, 0:kw], in_=pa[:, 0:kw], func=AF.Relu),
                             reads=[pa], writes=[tb])
                        hasb = (512 * c + kw > lastlo)
                        S.op("pe", lambda e, h=h, tb=tb, pi=pi, kw=kw, hasb=hasb: e.matmul(
                            pi[:, 0:kw], lhsT=Dg[:, h, :], rhs=tb[:, 0:kw], start=(h == 0), stop=(h == 15 and not hasb)),
                            reads=[Dg, tb], writes=[pi], pe_acc=(h > 0))
                        if h == 15 and hasb:
                            b0 = max(lastlo - 512 * c, 0)
                            S.op("pe", lambda e, pi=pi, kw=kw, b0=b0: e.matmul(pi[:, b0:kw], lhsT=ident_b[:], rhs=cbias[:, 256 - (kw - b0):256],
                                                                               start=False, stop=True),
                                 reads=[ident_b, cbias], writes=[pi], pe_acc=True)
                    a0 = 512 * c
                    a1 = a0 + kw
                    S.op("act", lambda e, pi=pi, a0=a0, a1=a1, kw=kw: e.activation(out=I_sb[:, a0:a1], in_=pi[:, 0:kw], func=AF.Copy),
                         reads=[pi], writes=[I_sb])
            def stage_b(i):
                nk = (2 * i + 2) * 128
                I_sb = I_sbs[i % 2]
                if i >= 1:
                    S.op("dve", lambda e, nk=nk: e.tensor_reduce(out=lo, in_=I_sb[:, 0:nk - 256], axis=AX.X, op=ALU.min),
                         reads=[I_sb], writes=[sm])
                    S.op("dve", lambda e, nk=nk: e.tensor_reduce(out=hi, in_=I_sb[:, 0:nk], axis=AX.X, op=ALU.max),
                         reads=[I_sb], writes=[sm])
                    S.op("dve", lambda e: e.tensor_tensor(out=rng, in0=hi, in1=lo, op=ALU.subtract), reads=[sm], writes=[sm])
                    S.op("dve", lambda e: e.tensor_scalar(out=spw[:], in0=pw[:], scalar1=rng, scalar2=None, op0=ALU.mult),
                         reads=[pw, sm], writes=[spw])
                    S.op("dve", lambda e: e.tensor_tensor(out=mid, in0=lo, in1=spw[:, 0:1], op=ALU.add),
                         reads=[sm, spw], writes=[sm])
                    for k in range(KB):
                        S.op("dve", lambda e, nk=nk: e.tensor_scalar(out=junk[:, 0:nk], in0=I_sb[:, 0:nk], scalar1=mid,
                                                                     scalar2=0.0, op0=ALU.is_ge, op1=ALU.add, accum_out=cnt),
                             reads=[I_sb, sm], writes=[junk, sm])
                        S.op("dve", lambda e, k=k: e.scalar_tensor_tensor(out=wv, in0=cnt, scalar=256.0, in1=spw[:, k:k + 1],
                                                                          op0=ALU.is_ge, op1=ALU.mult),
                             reads=[sm, spw], writes=[sm])
                        S.op("dve", lambda e, k=k: e.scalar_tensor_tensor(out=mid, in0=mid, scalar=spw[:, KB + k:KB + k + 1],
                                                                          in1=wv, op0=ALU.subtract, op1=ALU.add),
                             reads=[sm, spw], writes=[sm])
                    thr = mid
                else:
                    thr = thr0
                S.op("dve", lambda e, nk=nk, thr=thr: e.tensor_scalar(out=junk[:, 0:nk], in0=I_sb[:, 0:nk], scalar1=thr,
                                                                      scalar2=None, op0=ALU.is_ge),
                     reads=[I_sb, sm], writes=[junk])
                return thr
            def stage_c(i):
                nk = (2 * i + 2) * 128
                nb = 2 * i + 2
                for g0 in range(0, nb, 8):
                    gn = min(8, nb - g0)
                    for j in range(gn):
                        S.op("pe", lambda e, j=j, g0=g0: e.transpose(out=PT[:, j * 128:(j + 1) * 128],
                                                                     in_=junk[:, (g0 + j) * 128:(g0 + j + 1) * 128],
                                                                     identity=ident_b[:]),
                             reads=[junk, ident_b], writes=[PT])
                    S.op("act", lambda e, i=i, g0=g0, gn=gn: e.activation(
                        out=maskT[:, moff[i] + g0:moff[i] + g0 + gn, :],
                        in_=PT[:, 0:gn * 128].rearrange("p (h t) -> p h t", h=gn), func=AF.Copy),
                        reads=[PT], writes=[maskT])
            stage_a(0)
            for i in range(NO):
                if i + 1 < NO:
                    stage_a(i + 1)
                stage_b(i)
                stage_c(i)
            S.barrier()
            S.emit()

        es_B2 = ExitStack()
        qT = sb(es_B2, "qT", [128, 8, 1024], BF16)
        with ExitStack() as p2:
            Wq_sb = sb(p2, "Wq_sb", [128, KC, 1024], BF16)
            xt = [sb(p2, "xt2_%d" % j, [128, KC, 128], BF16) for j in range(2)]
            q_b = sb(p2, "q_b", [128, 1024], BF16)
            tA2 = sb(p2, "tA2", [128, 4, 32], F32)
            tB2 = sb(p2, "tB2", [128, 4, 32], F32)
            load_wres(Wq_sb, Wq, 1024)
            zs_chunk(Wq_sb, 0, 512, ZOFF["q"])
            zs_chunk(Wq_sb, 512, 512, ZOFF["q"] + 512)
            for i in range(NO):
                x = xt[i % 2]
                load_xt(x, i)
                for c in range(2):
                    pz = PZ[c]
                    mm_x(pz, x, Wq_sb, c * 512, 512)
                    S.op("act", lambda e, pz=pz, c=c: e.activation(out=q_b[:, c * 512:(c + 1) * 512], in_=pz[:], func=AF.Copy,
                                                                   scale=QS), reads=[pz], writes=[q_b])
                    src3 = V3(lambda lo_, hi_, pz=pz: pz[:].rearrange("p (h d) -> p h d", h=4)[:, :, lo_:hi_], [pz])
                    dst3 = V3(lambda lo_, hi_, c=c: q_b[:, c * 512:(c + 1) * 512].rearrange("p (h d) -> p h d", h=4)[:, :, lo_:hi_],
                              [q_b])
                    rope(src3, dst3, 4, 16, ropeO_sb[:, i, 0:32], ropeO_sb[:, i, 32:64], tA2, tB2)
                    transpose4(lambda hd, c=c: q_b[:, c * 512 + hd * 128:c * 512 + (hd + 1) * 128], [q_b])
                    S.op("act", lambda e, c=c, i=i: e.activation(out=qT[:, c * 4:(c + 1) * 4, i * 128:(i + 1) * 128],
                                                                 in_=PT[:, 0:512].rearrange("p (h t) -> p h t", h=4), func=AF.Copy),
                         reads=[PT], writes=[qT])
            S.barrier()
            S.emit()

        a_out = TLV(R16, lambda t: t[:].rearrange("p (i c) -> p i c", i=NO))
        with ExitStack() as p3:
            Eb = [sb(p3, "Eb%d" % j, [128, 4, 128], BF16) for j in range(2)]
            Pb = [sb(p3, "Pb%d" % j, [128, 4, 128], BF16) for j in range(2)]
            rden = sb(p3, "rden", [128, 8], F32)
            PSb = [PG, PGL]
            PO = [PU[0], PU[1], PD]
            for i in range(NO):
                nb = 2 * i + 2
                groups = [(j, g) for j in range(nb) for g in range(2)]
                first = [True, True, True]

                def st_mm(n, i=i):
                    j, g = groups[n]
                    ps_ = PSb[n % 2]
                    for hh in range(4):
                        hd = g * 4 + hh
                        S.op("pe", lambda e, hh=hh, hd=hd, j=j, ps_=ps_: e.matmul(
                            ps_[:, hh * 128:(hh + 1) * 128], lhsT=KT[:, hd, j * 128:(j + 1) * 128],
                            rhs=qT[:, hd, i * 128:(i + 1) * 128], start=True, stop=True),
                            reads=[KT, qT], writes=[ps_])
                st_mm(0)
                for n in range(len(groups)):
                    j, g = groups[n]
                    if n + 1 < len(groups):
                        st_mm(n + 1)
                    ps_, eb, pb = PSb[n % 2], Eb[n % 2], Pb[n % 2]
                    S.op("act", lambda e, ps_=ps_, eb=eb: e.activation(out=eb[:].rearrange("p h t -> p (h t)"), in_=ps_[:],
                                                                       func=AF.Exp), reads=[ps_], writes=[eb])
                    S.op("pool", lambda e, eb=eb, pb=pb, i=i, j=j: e.tensor_tensor(
                        out=pb[:], in0=eb[:], in1=maskT[:, moff[i] + j, :].unsqueeze(1).to_broadcast([128, 4, 128]),
                        op=ALU.mult), reads=[eb, maskT], writes=[pb])
                    for hh in range(4):
                        hd = g * 4 + hh
                        bank = hd // 3
                        col = (hd % 3) * 130
                        st = first[bank]
                        first[bank] = False
                        S.op("pe", lambda e, hh=hh, hd=hd, j=j, pb=pb, bank=bank, col=col, st=st, nb=nb: e.matmul(
                            PO[bank][:, col:col + 130], lhsT=pb[:, hh, :], rhs=V[:, j, hd, :],
                            start=st, stop=(j == nb - 1), skip_group_check=True),
                            reads=[pb, V], writes=[PO[bank]], pe_acc=(not st))
                for bank in range(3):
                    nh = 3 if bank < 2 else 2
                    S.op("dve", lambda e, bank=bank, nh=nh: e.reciprocal(
                        out=rden[:, bank * 3:bank * 3 + nh],
                        in_=PO[bank][:, 0:nh * 130].rearrange("p (h c) -> p h c", c=130)[:, :, 128]),
                        reads=[PO[bank]], writes=[rden])
                    S.op("dve", lambda e, bank=bank, nh=nh, i=i: e.tensor_tensor(
                        out=a_out[:, i, bank * 384:bank * 384 + nh * 128].rearrange("p (h d) -> p h d", h=nh),
                        in0=PO[bank][:, 0:nh * 130].rearrange("p (h c) -> p h c", c=130)[:, :, 0:128],
                        in1=rden[:, bank * 3:bank * 3 + nh].unsqueeze(2).to_broadcast([128, nh, 128]), op=ALU.mult),
                        reads=[PO[bank], rden], writes=[a_out])
            S.barrier()
            S.emit()
        es_B2.close()
        es_B.close()
        es_A.close()
        if stage <= 4:
            return nc

        es_C = ExitStack()
        catT = sb(es_C, "catT", [128, KC, 1024], BF16)

        with ExitStack() as p4:
            Wg_sb = sb(p4, "Wg_sb", [128, KC, 1024], BF16)
            xt = [sb(p4, "xt4_%d" % j, [128, KC, 128], BF16) for j in range(2)]
            ga = [sb(p4, "ga%d" % j, [128, 512], F32) for j in range(2)]
            cab = [sb(p4, "cab%d" % j, [128, 512], BF16) for j in range(2)]
            load_wres(Wg_sb, Wg, 1024)
            zs_chunk(Wg_sb, 0, 512, ZOFF["g"])
            zs_chunk(Wg_sb, 512, 512, ZOFF["g"] + 512)
            for i in range(NO):
                x = xt[i % 2]
                load_xt(x, i)
                for c in range(2):
                    pz, g_, cb_ = PZ[c], ga[c], cab[c]
                    mm_x(pz, x, Wg_sb, c * 512, 512)
                    S.op("act", lambda e, pz=pz, g_=g_: e.activation(out=g_[:], in_=pz[:], func=AF.Silu), reads=[pz], writes=[g_])
                    S.op("dve", lambda e, g_=g_, cb_=cb_, i=i, c=c: e.tensor_tensor(
                        out=cb_[:], in0=a_out[:, i, c * 512:(c + 1) * 512], in1=g_[:], op=ALU.mult),
                        reads=[a_out, g_], writes=[cb_])
                    transpose4(lambda hd, cb_=cb_: cb_[:, hd * 128:(hd + 1) * 128], [cb_])
                    S.op("act", lambda e, c=c, i=i: e.activation(out=catT[:, c * 4:(c + 1) * 4, i * 128:(i + 1) * 128],
                                                                 in_=PT[:, 0:512].rearrange("p (h t) -> p h t", h=4), func=AF.Copy),
                         reads=[PT], writes=[catT])
            S.barrier()
            S.emit()

        with ExitStack() as p5:
            Wb_sb = sb(p5, "Wb_sb", [128, KC, 2048], BF16)
            xt = [sb(p5, "xt5_%d" % j, [128, KC, 128], BF16) for j in range(2)]
            ng_bc = sb(p5, "ng_bc", [128, 4, 128], F32)
            qs = sb(p5, "qs", [128, 512], F32)
            sg = sb(p5, "sg5", [128, 512], F32)
            gg = sb(p5, "gg5", [128, 512], F32)
            omf = sb(p5, "omf5", [128, 512], F32)
            Gs = sb(p5, "Gs5", [128, 512], F32)
            dlt = sb(p5, "dlt5", [128, 512], F32)
            eG = sb(p5, "eG", [128, 512], F32)
            enG = sb(p5, "enG", [128, 512], F32)
            gb = sb(p5, "gb5", [128, 512], F32)
            t1 = sb(p5, "t15", [128, 512], F32)
            kdec = sb(p5, "kdec5", [128, 512], BF16)
            vb = sb(p5, "vb5", [128, 512], BF16)
            kg = sb(p5, "kg", [128, 512], BF16)
            qg = sb(p5, "qg", [128, 512], BF16)
            qgT = sb(p5, "qgT", [128, 4, 128], BF16)
            qgT0 = sb(p5, "qgT0", [128, 4, 128], BF16)
            qgT1 = sb(p5, "qgT1", [128, 4, 128], BF16)
            kgT = sb(p5, "kgT", [128, 4, 128], BF16)
            ATb = sb(p5, "ATb", [128, 4, 128], BF16)
            S1b = sb(p5, "S1b", [128, 4, 128], BF16)
            cbb = sb(p5, "cbb", [128, 512], BF16)
            Dc0 = sb(p5, "Dc0", [128, 8], F32)
            ss = sb(p5, "ss5", [128, 4], F32)
            rstd = sb(p5, "rstd5", [128, 4], F32)
            jk5 = sb(p5, "jk5", [128, 128], F32)
            load_wres(Wb_sb, Wb, 2048)
            for o_ in range(4):
                zs_chunk(Wb_sb, o_ * 512, 512, ZOFF["b"] + o_ * 512)
            S.dma("sp", [lambda e, hd=hd: e.dma_start(out=ng_bc[:, hd, :], in_=ng[0:1, :].broadcast_to([128, 128]))
                         for hd in range(4)], writes=[ng_bc])
            S.op("pool", lambda e: e.memset(qgT0[:], 0.0), writes=[qgT0])
            S.op("pool", lambda e: e.memset(qgT1[:], 0.0), writes=[qgT1])
            for i in range(NO):
                x = xt[i % 2]
                load_xt(x, i)
                mm_x(PZ[0], x, Wb_sb, 0, 512)
                S.op("act", lambda e: e.activation(out=qs[:], in_=PZ[0][:], func=AF.Silu), reads=[PZ[0]], writes=[qs])
                mm_x(PZ[1], x, Wb_sb, 512, 512)
                S.op("act", lambda e: e.activation(out=sg[:], in_=PZ[1][:], func=AF.Sigmoid), reads=[PZ[1]], writes=[sg])
                mm_x(PZ[0], x, Wb_sb, 1024, 512)
                S.op("act", lambda e: e.activation(out=vb[:], in_=PZ[0][:], func=AF.Copy), reads=[PZ[0]], writes=[vb])
                mm_x(PZ[1], x, Wb_sb, 1536, 512)
                S.op("act", lambda e: e.activation(out=gb[:], in_=PZ[1][:], func=AF.Silu), reads=[PZ[1]], writes=[gb])
                S.op("dve", lambda e: e.tensor_tensor(out=sg[:], in0=sg[:], in1=oml_bc[:], op=ALU.mult),
                     reads=[sg, oml_bc], writes=[sg])
                S.op("dve", lambda e: e.tensor_tensor(out=sg[:], in0=sg[:], in1=lb_bc[:], op=ALU.add),
                     reads=[sg, lb_bc], writes=[sg])
                S.op("act", lambda e: e.activation(out=gg[:], in_=sg[:], func=AF.Ln), reads=[sg], writes=[gg])
                S.op("pool", lambda e: e.tensor_scalar(out=omf[:], in0=sg[:], scalar1=-1.0, scalar2=1.0,
                                                       op0=ALU.mult, op1=ALU.add), reads=[sg], writes=[omf])
                S.op("pe", lambda e: e.matmul(PG[:], lhsT=tri2[:], rhs=gg[:], start=True, stop=True),
                     reads=[tri2, gg], writes=[PG])
                S.op("pe", lambda e: e.matmul(PGL[:], lhsT=blk2[:], rhs=gg[:], start=True, stop=True),
                     reads=[blk2, gg], writes=[PGL])
                S.op("act", lambda e: e.activation(out=Gs[:], in_=PG[:], func=AF.Copy), reads=[PG], writes=[Gs])
                S.op("dve", lambda e: e.tensor_tensor(out=dlt[:], in0=PGL[:], in1=Gs[:], op=ALU.subtract),
                     reads=[PGL, Gs], writes=[dlt])
                S.op("act", lambda e: e.activation(out=dlt[:], in_=dlt[:], func=AF.Exp), reads=[dlt], writes=[dlt])
                S.op("dve", lambda e: e.tensor_tensor(out=kdec[:], in0=omf[:], in1=dlt[:], op=ALU.mult),
                     reads=[omf, dlt], writes=[kdec])
                S.op("act", lambda e: e.activation(out=eG[:], in_=Gs[:], func=AF.Exp), reads=[Gs], writes=[eG])
                S.op("act", lambda e: e.activation(out=enG[:], in_=Gs[:], func=AF.Exp, scale=-1.0), reads=[Gs], writes=[enG])
                S.op("dve", lambda e: e.tensor_tensor(out=qg[:], in0=qs[:], in1=eG[:], op=ALU.mult), reads=[qs, eG], writes=[qg])
                S.op("pool", lambda e: e.tensor_tensor(out=kg[:], in0=omf[:], in1=enG[:], op=ALU.mult),
                     reads=[omf, enG], writes=[kg])
                for hd in range(4):
                    S.op("pe", lambda e, hd=hd: e.matmul(PU[0][:, hd * 128:(hd + 1) * 128],
                                                         lhsT=kdec[0:64, hd * 128:(hd + 1) * 128],
                                                         rhs=vb[0:64, hd * 128:(hd + 1) * 128], start=True, stop=True),
                         reads=[kdec, vb], writes=[PU[0]])
                for hd in range(4):
                    S.op("pe", lambda e, hd=hd: e.matmul(PD[:, 2 * hd:2 * hd + 2], lhsT=gg[0:64, hd * 128:(hd + 1) * 128],
                                                         rhs=ones_f[0:64, :], start=True, stop=True),
                         reads=[gg, ones_f], writes=[PD])
                S.op("act", lambda e: e.activation(out=Dc0[:], in_=PD[:, 0:8], func=AF.Exp), reads=[PD], writes=[Dc0])
                for hd in range(4):
                    S.op("dve", lambda e, hd=hd, i=i: e.scalar_tensor_tensor(
                        out=S1b[:, hd, :], in0=Sown[:, i, hd * 128:(hd + 1) * 128], scalar=Dc0[:, 2 * hd:2 * hd + 1],
                        in1=PU[0][:, hd * 128:(hd + 1) * 128], op0=ALU.mult, op1=ALU.add),
                        reads=[Sown, Dc0, PU[0]], writes=[S1b])
                transpose4(lambda hd: qg[:, hd * 128:(hd + 1) * 128], [qg])
                transpose4(lambda hd: kg[:, hd * 128:(hd + 1) * 128], [kg], col0=512)
                ptq = lambda: PT[:, 0:512].rearrange("p (h t) -> p h t", h=4)
                S.op("act", lambda e: e.activation(out=qgT[:], in_=ptq(), func=AF.Copy), reads=[PT], writes=[qgT])
                S.op("dve", lambda e: e.tensor_copy(out=qgT0[:, :, 0:64], in_=ptq()[:, :, 0:64]), reads=[PT], writes=[qgT0])
                S.op("dve", lambda e: e.tensor_copy(out=qgT1[:, :, 64:128], in_=ptq()[:, :, 64:128]), reads=[PT], writes=[qgT1])
                S.op("act", lambda e: e.activation(out=kgT[:], in_=PT[:, 512:1024].rearrange("p (h t) -> p h t", h=4),
                                                   func=AF.Copy), reads=[PT], writes=[kgT])
                for hd in range(4):
                    S.op("pe", lambda e, hd=hd: e.matmul(PU[1][:, hd * 128:(hd + 1) * 128], lhsT=kgT[:, hd, :], rhs=qgT[:, hd, :],
                                                         start=True, stop=True), reads=[kgT, qgT], writes=[PU[1]])
                S.op("dve", lambda e: e.tensor_tensor(out=ATb[:], in0=PU[1][:].rearrange("p (h t) -> p h t", h=4),
                                                      in1=tri2[:].unsqueeze(1).to_broadcast([128, 4, 128]), op=ALU.mult),
                     reads=[PU[1], tri2], writes=[ATb])
                for hd in range(4):
                    cs_ = slice(hd * 128, (hd + 1) * 128)
                    S.op("pe", lambda e, hd=hd, cs_=cs_: e.matmul(PG[:, cs_], lhsT=ATb[:, hd, :], rhs=vb[:, cs_],
                                                                  start=True, stop=False), reads=[ATb, vb], writes=[PG])
                    S.op("pe", lambda e, hd=hd, cs_=cs_, i=i: e.matmul(PG[:, cs_], lhsT=qgT0[:, hd, :], rhs=Sown[:, i, cs_],
                                                                       start=False, stop=False),
                         reads=[qgT0, Sown], writes=[PG], pe_acc=True)
                    S.op("pe", lambda e, hd=hd, cs_=cs_: e.matmul(PG[:, cs_], lhsT=qgT1[:, hd, :], rhs=S1b[:, hd, :],
                                                                  start=False, stop=True),
                         reads=[qgT1, S1b], writes=[PG], pe_acc=True)
                for hd in range(4):
                    S.op("act", lambda e, hd=hd: e.activation(out=jk5[:], in_=PG[:, hd * 128:(hd + 1) * 128], func=AF.Square,
                                                              accum_out=ss[:, hd:hd + 1]), reads=[PG], writes=[jk5, ss])
                S.op("dve", lambda e: e.tensor_scalar(out=rstd[:], in0=ss[:], scalar1=1.0 / 128.0, scalar2=RMS_EPS,
                                                      op0=ALU.mult, op1=ALU.add), reads=[ss], writes=[rstd])
                S.op("act", lambda e: e.activation(out=rstd[:], in_=rstd[:], func=AF.Sqrt), reads=[rstd], writes=[rstd])
                S.op("dve", lambda e: e.reciprocal(out=rstd[:], in_=rstd[:]), reads=[rstd], writes=[rstd])
                S.op("dve", lambda e: e.tensor_tensor(out=t1[:].rearrange("p (h d) -> p h d", h=4),
                                                      in0=PG[:].rearrange("p (h d) -> p h d", h=4),
                                                      in1=rstd[:, 0:4].unsqueeze(2).to_broadcast([128, 4, 128]), op=ALU.mult),
                     reads=[PG, rstd], writes=[t1])
                S.op("pool", lambda e: e.tensor_tensor(out=t1[:], in0=t1[:], in1=ng_bc[:].rearrange("p h d -> p (h d)"),
                                                       op=ALU.mult), reads=[t1, ng_bc], writes=[t1])
                S.op("pool", lambda e: e.tensor_tensor(out=cbb[:], in0=t1[:], in1=gb[:], op=ALU.mult),
                     reads=[t1, gb], writes=[cbb])
                transpose4(lambda hd: cbb[:, hd * 128:(hd + 1) * 128], [cbb])
                S.op("act", lambda e, i=i: e.activation(out=catT[:, 8:12, i * 128:(i + 1) * 128],
                                                        in_=PT[:, 0:512].rearrange("p (h t) -> p h t", h=4), func=AF.Copy),
                     reads=[PT], writes=[catT])
            S.barrier()
            S.emit()

        with ExitStack() as p6:
            Wm_sb = sb(p6, "Wm_sb", [128, KC, 1024], BF16)
            xt = [sb(p6, "xt6_%d" % j, [128, KC, 128], BF16) for j in range(2)]
            mq_b = sb(p6, "mq_b", [128, 512], BF16)
            gm = sb(p6, "gm", [128, 512], F32)
            mqT = sb(p6, "mqT", [128, 4, 128], BF16)
            Em = [sb(p6, "Em%d" % j, [128, 4, 128], BF16) for j in range(2)]
            rdm = sb(p6, "rdm", [128, 4], F32)
            tm = sb(p6, "tm", [128, 512], F32)
            cmb = sb(p6, "cmb", [128, 512], BF16)
            POm = [PU[0], PU[1]]
            load_wres(Wm_sb, Wm, 1024)
            zs_chunk(Wm_sb, 0, 512, ZOFF["m"])
            zs_chunk(Wm_sb, 512, 512, ZOFF["m"] + 512)
            for i in range(NO):
                x = xt[i % 2]
                load_xt(x, i)
                mm_x(PZ[0], x, Wm_sb, 0, 512)
                S.op("act", lambda e: e.activation(out=mq_b[:], in_=PZ[0][:], func=AF.Copy, scale=QS), reads=[PZ[0]], writes=[mq_b])
                mm_x(PZ[1], x, Wm_sb, 512, 512)
                S.op("act", lambda e: e.activation(out=gm[:], in_=PZ[1][:], func=AF.Silu), reads=[PZ[1]], writes=[gm])
                transpose4(lambda hd: mq_b[:, hd * 128:(hd + 1) * 128], [mq_b])
                S.op("act", lambda e: e.activation(out=mqT[:], in_=PT[:, 0:512].rearrange("p (h t) -> p h t", h=4), func=AF.Copy),
                     reads=[PT], writes=[mqT])
                for nt in range(2):
                    ps_ = PA[nt]
                    for hd in range(4):
                        S.op("pe", lambda e, hd=hd, nt=nt, ps_=ps_: e.matmul(
                            ps_[:, hd * 128:(hd + 1) * 128], lhsT=mkT[:, hd, nt * 128:(nt + 1) * 128], rhs=mqT[:, hd, :],
                            start=True, stop=True), reads=[mkT, mqT], writes=[ps_])
                    S.op("act", lambda e, nt=nt, ps_=ps_: e.activation(out=Em[nt][:].rearrange("p h t -> p (h t)"), in_=ps_[:],
                                                                       func=AF.Exp), reads=[ps_], writes=[Em[nt]])
                for nt in range(2):
                    for hd in range(4):
                        bank = hd // 3
                        col = (hd % 3) * 130
                        st = (nt == 0 and hd % 3 == 0)
                        S.op("pe", lambda e, hd=hd, nt=nt, bank=bank, col=col, st=st: e.matmul(
                            POm[bank][:, col:col + 130], lhsT=Em[nt][:, hd, :], rhs=mv[:, nt, hd, :],
                            start=st, stop=(nt == 1), skip_group_check=True),
                            reads=[Em[nt], mv], writes=[POm[bank]], pe_acc=(not st))
                for bank in range(2):
                    nh = 3 if bank == 0 else 1
                    S.op("dve", lambda e, bank=bank, nh=nh: e.reciprocal(
                        out=rdm[:, bank * 3:bank * 3 + nh],
                        in_=POm[bank][:, 0:nh * 130].rearrange("p (h c) -> p h c", c=130)[:, :, 128]),
                        reads=[POm[bank]], writes=[rdm])
                    S.op("dve", lambda e, bank=bank, nh=nh: e.tensor_tensor(
                        out=tm[:, bank * 384:bank * 384 + nh * 128].rearrange("p (h d) -> p h d", h=nh),
                        in0=POm[bank][:, 0:nh * 130].rearrange("p (h c) -> p h c", c=130)[:, :, 0:128],
                        in1=rdm[:, bank * 3:bank * 3 + nh].unsqueeze(2).to_broadcast([128, nh, 128]), op=ALU.mult),
                        reads=[POm[bank], rdm], writes=[tm])
                S.op("pool", lambda e: e.tensor_tensor(out=cmb[:], in0=tm[:], in1=gm[:], op=ALU.mult), reads=[tm, gm], writes=[cmb])
                transpose4(lambda hd: cmb[:, hd * 128:(hd + 1) * 128], [cmb])
                S.op("act", lambda e, i=i: e.activation(out=catT[:, 12:16, i * 128:(i + 1) * 128],
                                                        in_=PT[:, 0:512].rearrange("p (h t) -> p h t", h=4), func=AF.Copy),
                     reads=[PT], writes=[catT])
            S.barrier()
            S.emit()

        print("ninstr at S start", S.ninstr, flush=True)
        es_S = ExitStack()
        cat_s = sb(es_S, "cat_s", [8, D], BF16)
        aq_sb = sb(es_S, "aq_sb", [8, 1024], BF16)
        ak_sb = sb(es_S, "ak_sb", [8, 1024], BF16)
        av_sb = sb(es_S, "av_sb", [8, 1024], BF16)
        ga_s = sb(es_S, "ga_s", [8, 1024], F32)
        iq_sb = sb(es_S, "iq_sb", [8, 16, 64], BF16)
        ik_sb = sb(es_S, "ik_sb", [8, 64], BF16)
        iw_s = sb(es_S, "iw_s", [8, 16], F32)
        ropeS_sb = sb(es_S, "ropeS_sb", [8, 160], F32)
        tri8 = sb(es_S, "tri8", [8, 8], F32)
        one8 = sb(es_S, "one8", [8, 8], F32)
        S.dma("sp", lambda e: e.dma_start(out=ropeS_sb[:], in_=ropeS[:, :]), writes=[ropeS_sb])
        S.dma("sp", lambda e: e.dma_start(out=tri8[:], in_=c_tri2[0:8, 0:8]), writes=[tri8])
        S.op("pool", lambda e: e.memset(one8[:], 1.0), writes=[one8])
        with ExitStack() as s1:
            zq = sb(s1, "zq", [8, 1024], F32)
            zk = sb(s1, "zk", [8, 1024], F32)
            zv = sb(s1, "zv", [8, 1024], F32)
            ziq = sb(s1, "ziq", [8, 1024], F32)
            zik = sb(s1, "zik", [8, 64], F32)
            zb = sb(s1, "zb", [8, 2048], F32)
            zm = sb(s1, "zm", [8, 1024], F32)
            tAs = sb(s1, "tAs", [8, 16, 32], F32)
            tBs = sb(s1, "tBs", [8, 16, 32], F32)
            for dst, key, n_ in ((zq, "q", 1024), (zk, "k", 1024), (zv, "v", 1024), (ga_s, "g", 1024), (ziq, "iq", 1024),
                                 (iw_s, "iw", 16), (zik, "ik", 64), (zb, "b", 2048), (zm, "m", 1024)):
                S.dma("sp", lambda e, dst=dst, key=key, n_=n_: e.dma_start(out=dst[:, 0:n_], in_=zs_d[:, ZOFF[key]:ZOFF[key] + n_]),
                      reads=[zsd_b], writes=[dst])
            CCk, SSk = ropeS_sb[:, 0:32], ropeS_sb[:, 32:64]
            CCi, SSi = ropeS_sb[:, 64:80], ropeS_sb[:, 80:96]
            CCq, SSq = ropeS_sb[:, 96:128], ropeS_sb[:, 128:160]
            v3 = lambda t, nh: V3(lambda lo_, hi_: t[:].rearrange("p (h d) -> p h d", h=nh)[:, :, lo_:hi_], [t])
            rope(v3(zk, 8), v3(zk, 8), 8, 16, CCk, SSk, tAs, tBs, np_=8)
            S.dma("sp", lambda e: e.dma_start(out=o_ks[:, :], in_=zk[:]), reads=[zk])
            S.dma("sp", lambda e: e.dma_start(out=o_vs[:, :], in_=zv[:]), reads=[zv])
            S.op("dve", lambda e: e.tensor_copy(out=ak_sb[:], in_=zk[:]), reads=[zk], writes=[ak_sb])
            S.op("dve", lambda e: e.tensor_copy(out=av_sb[:], in_=zv[:]), reads=[zv], writes=[av_sb])
            rope(v3(zik, 1), v3(zik, 1), 1, 8, CCi, SSi, tAs, tBs, np_=8)
            S.dma("sp", lambda e: e.dma_start(out=o_iks[:, :], in_=zik[:]), reads=[zik])
            S.op("dve", lambda e: e.tensor_copy(out=ik_sb[:], in_=zik[:]), reads=[zik], writes=[ik_sb])
            S.op("act", lambda e: e.activation(out=aq_sb[:], in_=zq[:], func=AF.Copy, scale=QS), reads=[zq], writes=[aq_sb])
            rope(v3(zq, 8), v3(aq_sb, 8), 8, 16, CCq, SSq, tAs, tBs, np_=8)
            S.op("act", lambda e: e.activation(out=iq_sb[:].rearrange("p h d -> p (h d)"), in_=ziq[:], func=AF.Copy),
                 reads=[ziq], writes=[iq_sb])
            rope(v3(ziq, 16), V3(lambda lo_, hi_: iq_sb[:, :, lo_:hi_], [iq_sb]), 16, 8, CCi, SSi, tAs, tBs, np_=8)
            S.op("act", lambda e: e.activation(out=ga_s[:], in_=ga_s[:], func=AF.Silu), reads=[ga_s], writes=[ga_s])

            S0f = sb(s1, "S0f", [128, 4, 128], F32)
            S0b = sb(s1, "S0b", [128, 4, 128], BF16)
            S1f = sb(s1, "S1f", [128, 4, 128], F32)
            hs = [sb(s1, "hs%d" % j, [8, 512], F32) for j in range(8)]
            qs_, sg_, gg_, omf_, Gs_, dl_, eG_, enG_ = hs
            gb_ = sb(s1, "gb_s", [8, 512], F32)
            t1_ = sb(s1, "t1_s", [8, 512], F32)
            kdec_ = sb(s1, "kdec_s", [8, 512], BF16)
            vb_ = sb(s1, "vb_s", [8, 512], BF16)
            kg_ = sb(s1, "kg_s", [8, 512], BF16)
            qg_ = sb(s1, "qg_s", [8, 512], BF16)
            qgT_ = sb(s1, "qgT_s", [128, 4, 8], BF16)
            kgT_ = sb(s1, "kgT_s", [128, 4, 8], BF16)
            ATb_ = sb(s1, "ATb_s", [8, 4, 8], BF16)
            Dcs = sb(s1, "Dcs", [128, 8], F32)
            ss_ = sb(s1, "ss_s", [8, 4], F32)
            rstd_ = sb(s1, "rstd_s", [8, 4], F32)
            jk_ = sb(s1, "jk_s", [8, 128], F32)
            ngs = sb(s1, "ngs", [8, 4, 128], F32)
            lb8 = sb(s1, "lb8", [8, 2, 512], F32)
            S.dma("sp", lambda e: e.dma_start(out=S0f[:], in_=st_h.rearrange("h k v -> k h v")), writes=[S0f])
            S.dma("sp", [lambda e, hd=hd: e.dma_start(out=ngs[:, hd, :], in_=ng[0:1, :].broadcast_to([8, 128]))
                         for hd in range(4)], writes=[ngs])
            S.op("dve", lambda e: e.tensor_copy(out=S0b[:], in_=S0f[:]), reads=[S0f], writes=[S0b])
            S.op("act", lambda e: e.activation(out=qs_[:], in_=zb[:, 0:512], func=AF.Silu), reads=[zb], writes=[qs_])
            S.op("act", lambda e: e.activation(out=sg_[:], in_=zb[:, 512:1024], func=AF.Sigmoid), reads=[zb], writes=[sg_])
            S.op("act", lambda e: e.activation(out=vb_[:], in_=zb[:, 1024:1536], func=AF.Copy), reads=[zb], writes=[vb_])
            S.op("act", lambda e: e.activation(out=gb_[:], in_=zb[:, 1536:2048], func=AF.Silu), reads=[zb], writes=[gb_])
            S.op("dve", lambda e: e.tensor_tensor(out=sg_[:], in0=sg_[:], in1=oml_bc[0:8, :], op=ALU.mult),
                 reads=[sg_, oml_bc], writes=[sg_])
            S.op("dve", lambda e: e.tensor_tensor(out=sg_[:], in0=sg_[:], in1=lb_bc[0:8, :], op=ALU.add),
                 reads=[sg_, lb_bc], writes=[sg_])
            S.op("act", lambda e: e.activation(out=gg_[:], in_=sg_[:], func=AF.Ln), reads=[sg_], writes=[gg_])
            S.op("dve", lambda e: e.tensor_scalar(out=omf_[:], in0=sg_[:], scalar1=-1.0, scalar2=1.0, op0=ALU.mult, op1=ALU.add),
                 reads=[sg_], writes=[omf_])
            S.op("pe", lambda e: e.matmul(PG[0:8, :], lhsT=tri8[:], rhs=gg_[:], start=True, stop=True),
                 reads=[tri8, gg_], writes=[PG])
            S.op("pe", lambda e: e.matmul(PGL[0:8, :], lhsT=one8[:], rhs=gg_[:], start=True, stop=True),
                 reads=[one8, gg_], writes=[PGL])
            S.op("act", lambda e: e.activation(out=Gs_[:], in_=PG[0:8, :], func=AF.Copy), reads=[PG], writes=[Gs_])
            S.op("dve", lambda e: e.tensor_tensor(out=dl_[:], in0=PGL[0:8, :], in1=Gs_[:], op=ALU.subtract),
                 reads=[PGL, Gs_], writes=[dl_])
            S.op("act", lambda e: e.activation(out=dl_[:], in_=dl_[:], func=AF.Exp), reads=[dl_], writes=[dl_])
            S.op("dve", lambda e: e.tensor_tensor(out=kdec_[:], in0=omf_[:], in1=dl_[:], op=ALU.mult),
                 reads=[omf_, dl_], writes=[kdec_])
            S.op("act", lambda e: e.activation(out=eG_[:], in_=Gs_[:], func=AF.Exp), reads=[Gs_], writes=[eG_])
            S.op("act", lambda e: e.activation(out=enG_[:], in_=Gs_[:], func=AF.Exp, scale=-1.0), reads=[Gs_], writes=[enG_])
            S.op("dve", lambda e: e.tensor_tensor(out=qg_[:], in0=qs_[:], in1=eG_[:], op=ALU.mult), reads=[qs_, eG_], writes=[qg_])
            S.op("dve", lambda e: e.tensor_tensor(out=kg_[:], in0=omf_[:], in1=enG_[:], op=ALU.mult),
                 reads=[omf_, enG_], writes=[kg_])
            for hd in range(4):
                cs_ = slice(hd * 128, (hd + 1) * 128)
                S.op("pe", lambda e, cs_=cs_: e.matmul(PU[0][:, cs_], lhsT=kdec_[:, cs_], rhs=vb_[:, cs_], start=True, stop=True),
                     reads=[kdec_, vb_], writes=[PU[0]])
            for hd in range(4):
                S.op("pe", lambda e, hd=hd: e.matmul(PD[:, 2 * hd:2 * hd + 2], lhsT=gg_[:, hd * 128:(hd + 1) * 128],
                                                     rhs=ones_f[0:8, :], start=True, stop=True),
                     reads=[gg_, ones_f], writes=[PD])
            S.op("act", lambda e: e.activation(out=Dcs[:], in_=PD[:, 0:8], func=AF.Exp), reads=[PD], writes=[Dcs])
            for hd in range(4):
                S.op("dve", lambda e, hd=hd: e.scalar_tensor_tensor(
                    out=S1f[:, hd, :], in0=S0f[:, hd, :], scalar=Dcs[:, 2 * hd:2 * hd + 1],
                    in1=PU[0][:, hd * 128:(hd + 1) * 128], op0=ALU.mult, op1=ALU.add),
                    reads=[S0f, Dcs, PU[0]], writes=[S1f])
            S.dma("sp", lambda e: e.dma_start(out=o_hgs[:, :, :], in_=S1f[:]), reads=[S1f])
            for hd in range(4):
                S.op("pe", lambda e, hd=hd: e.transpose(out=PT[:, hd * 8:(hd + 1) * 8], in_=qg_[:, hd * 128:(hd + 1) * 128],
                                                        identity=ident_b[0:8, 0:8]), reads=[qg_, ident_b], writes=[PT])
                S.op("pe", lambda e, hd=hd: e.transpose(out=PT[:, 32 + hd * 8:32 + (hd + 1) * 8], in_=kg_[:, hd * 128:(hd + 1) * 128],
                                                        identity=ident_b[0:8, 0:8]), reads=[kg_, ident_b], writes=[PT])
            S.op("act", lambda e: e.activation(out=qgT_[:].rearrange("p h t -> p (h t)"), in_=PT[:, 0:32], func=AF.Copy),
                 reads=[PT], writes=[qgT_])
            S.op("act", lambda e: e.activation(out=kgT_[:].rearrange("p h t -> p (h t)"), in_=PT[:, 32:64], func=AF.Copy),
                 reads=[PT], writes=[kgT_])
            for hd in range(4):
                S.op("pe", lambda e, hd=hd: e.matmul(PU[1][0:8, hd * 8:(hd + 1) * 8], lhsT=kgT_[:, hd, :], rhs=qgT_[:, hd, :],
                                                     start=True, stop=True), reads=[kgT_, qgT_], writes=[PU[1]])
            S.op("dve", lambda e: e.tensor_tensor(out=ATb_[:], in0=PU[1][0:8, 0:32].rearrange("p (h t) -> p h t", h=4),
                                                  in1=tri8[:].unsqueeze(1).to_broadcast([8, 4, 8]), op=ALU.mult),
                 reads=[PU[1], tri8], writes=[ATb_])
            for hd in range(4):
                cs_ = slice(hd * 128, (hd + 1) * 128)
                S.op("pe", lambda e, hd=hd, cs_=cs_: e.matmul(PG[0:8, cs_], lhsT=ATb_[:, hd, :], rhs=vb_[:, cs_],
                                                              start=True, stop=False), reads=[ATb_, vb_], writes=[PG])
                S.op("pe", lambda e, hd=hd, cs_=cs_: e.matmul(PG[0:8, cs_], lhsT=qgT_[:, hd, :], rhs=S0b[:, hd, :],
                                                              start=False, stop=True), reads=[qgT_, S0b], writes=[PG], pe_acc=True)
            for hd in range(4):
                S.op("act", lambda e, hd=hd: e.activation(out=jk_[:], in_=PG[0:8, hd * 128:(hd + 1) * 128], func=AF.Square,
                                                          accum_out=ss_[:, hd:hd + 1]), reads=[PG], writes=[jk_, ss_])
            S.op("dve", lambda e: e.tensor_scalar(out=rstd_[:], in0=ss_[:], scalar1=1.0 / 128.0, scalar2=RMS_EPS,
                                                  op0=ALU.mult, op1=ALU.add), reads=[ss_], writes=[rstd_])
            S.op("act", lambda e: e.activation(out=rstd_[:], in_=rstd_[:], func=AF.Sqrt), reads=[rstd_], writes=[rstd_])
            S.op("dve", lambda e: e.reciprocal(out=rstd_[:], in_=rstd_[:]), reads=[rstd_], writes=[rstd_])
            S.op("dve", lambda e: e.tensor_tensor(out=t1_[:].rearrange("p (h d) -> p h d", h=4),
                                                  in0=PG[0:8, :].rearrange("p (h d) -> p h d", h=4),
                                                  in1=rstd_[:, 0:4].unsqueeze(2).to_broadcast([8, 4, 128]), op=ALU.mult),
                 reads=[PG, rstd_], writes=[t1_])
            S.op("dve", lambda e: e.tensor_tensor(out=t1_[:], in0=t1_[:], in1=ngs[:].rearrange("p h d -> p (h d)"), op=ALU.mult),
                 reads=[t1_, ngs], writes=[t1_])
            S.op("dve", lambda e: e.tensor_tensor(out=cat_s[:, 1024:1536], in0=t1_[:], in1=gb_[:], op=ALU.mult),
                 reads=[t1_, gb_], writes=[cat_s])

            cmk = sb(s1, "cmk", [128, 2, 512], BF16)
            mvs = sb(s1, "mvs", [128, 2, 4, 130], BF16)
            mkTs = sb(s1, "mkTs", [128, 4, 256], BF16)
            mq_s = sb(s1, "mq_s", [8, 512], BF16)
            gm_s = sb(s1, "gm_s", [8, 512], F32)
            mqTs = sb(s1, "mqTs", [128, 4, 8], BF16)
            Ems = [sb(s1, "Ems%d" % j, [128, 4, 8], BF16) for j in range(2)]
            rdms = sb(s1, "rdms", [8, 4], F32)
            tms = sb(s1, "tms", [8, 512], F32)
            S.dma("pool", [lambda e, nt=nt: e.dma_start(out=cmk[:, nt, :], in_=cmk_d[nt * 128:(nt + 1) * 128, :])
                           for nt in range(2)], writes=[cmk])
            S.op("pool", lambda e: e.memset(mvs[:, :, :, 128:129], 1.0), writes=[mvs])
            S.dma("pool", [lambda e, nt=nt: e.dma_start(out=mvs[:, nt, :, 0:128],
                                                        in_=cmv_d[nt * 128:(nt + 1) * 128, :].rearrange("p (h d) -> p h d", h=4))
                           for nt in range(2)], writes=[mvs])
            for nt in range(2):
                transpose4(lambda hd, nt=nt: cmk[:, nt, hd * 128:(hd + 1) * 128], [cmk])
                S.op("act", lambda e, nt=nt: e.activation(out=mkTs[:, :, nt * 128:(nt + 1) * 128],
                                                          in_=PT[:, 0:512].rearrange("p (h t) -> p h t", h=4), func=AF.Copy),
                     reads=[PT], writes=[mkTs])
            S.op("act", lambda e: e.activation(out=mq_s[:], in_=zm[:, 0:512], func=AF.Copy, scale=QS), reads=[zm], writes=[mq_s])
            S.op("act", lambda e: e.activation(out=gm_s[:], in_=zm[:, 512:1024], func=AF.Silu), reads=[zm], writes=[gm_s])
            for hd in range(4):
                S.op("pe", lambda e, hd=hd: e.transpose(out=PT[:, hd * 8:(hd + 1) * 8], in_=mq_s[:, hd * 128:(hd + 1) * 128],
                                                        identity=ident_b[0:8, 0:8]), reads=[mq_s, ident_b], writes=[PT])
            S.op("act", lambda e: e.activation(out=mqTs[:].rearrange("p h t -> p (h t)"), in_=PT[:, 0:32], func=AF.Copy),
                 reads=[PT], writes=[mqTs])
            Es_ = sb(s1, "Es_s", [8, 2, 512], BF16)
            for nt in range(2):
                ps_ = PA[nt]
                for hd in range(4):
                    S.op("pe", lambda e, hd=hd, nt=nt, ps_=ps_: e.matmul(
                        ps_[0:8, hd * 128:(hd + 1) * 128], lhsT=mqTs[:, hd, :], rhs=mkTs[:, hd, nt * 128:(nt + 1) * 128],
                        start=True, stop=True), reads=[mkTs, mqTs], writes=[ps_])
                S.op("act", lambda e, nt=nt, ps_=ps_: e.activation(out=Es_[:, nt, :], in_=ps_[0:8, :], func=AF.Exp),
                     reads=[ps_], writes=[Es_])
                for hd in range(4):
                    S.op("pe", lambda e, hd=hd, nt=nt: e.transpose(out=PT[:, hd * 8:(hd + 1) * 8],
                                                                   in_=Es_[:, nt, hd * 128:(hd + 1) * 128],
                                                                   identity=ident_b[0:8, 0:8]), reads=[Es_, ident_b], writes=[PT])
                S.op("act", lambda e, nt=nt: e.activation(out=Ems[nt][:].rearrange("p h t -> p (h t)"), in_=PT[:, 0:32],
                                                          func=AF.Copy), reads=[PT], writes=[Ems[nt]])
            POs = [PU[0], PU[1]]
            for nt in range(2):
                for hd in range(4):
                    bank, col = hd // 3, (hd % 3) * 130
                    st = (nt == 0 and hd % 3 == 0)
                    S.op("pe", lambda e, hd=hd, nt=nt, bank=bank, col=col, st=st: e.matmul(
                        POs[bank][0:8, col:col + 130], lhsT=Ems[nt][:, hd, :], rhs=mvs[:, nt, hd, :],
                        start=st, stop=(nt == 1), skip_group_check=True),
                        reads=[Ems[nt], mvs], writes=[POs[bank]], pe_acc=(not st))
            for bank in range(2):
                nh = 3 if bank == 0 else 1
                S.op("dve", lambda e, bank=bank, nh=nh: e.reciprocal(
                    out=rdms[:, bank * 3:bank * 3 + nh],
                    in_=POs[bank][0:8, 0:nh * 130].rearrange("p (h c) -> p h c", c=130)[:, :, 128]),
                    reads=[POs[bank]], writes=[rdms])
                S.op("dve", lambda e, bank=bank, nh=nh: e.tensor_tensor(
                    out=tms[:, bank * 384:bank * 384 + nh * 128].rearrange("p (h d) -> p h d", h=nh),
                    in0=POs[bank][0:8, 0:nh * 130].rearrange("p (h c) -> p h c", c=130)[:, :, 0:128],
                    in1=rdms[:, bank * 3:bank * 3 + nh].unsqueeze(2).to_broadcast([8, nh, 128]), op=ALU.mult),
                    reads=[POs[bank], rdms], writes=[tms])
            S.op("dve", lambda e: e.tensor_tensor(out=cat_s[:, 1536:2048], in0=tms[:], in1=gm_s[:], op=ALU.mult),
                 reads=[tms, gm_s], writes=[cat_s])
            S.barrier()
            S.emit()
        print("ninstr at S1 end", S.ninstr, flush=True)
        KB2 = 30
        with ExitStack() as s3:
            ci = {}
            for nm, shp, dt_ in (("c_sel8", [8, 128], F32), ("c_hm", [128, 16], F32), ("c_bq8", [128, 8], F32),
                                 ("c_bq16", [128, 8], F32), ("c_BQ1", [128, 128], F32), ("c_BQm", [128, 128], F32),
                                 ("c_LT", [128, 128], F32), ("c_negm", [128, 8], F32), ("c_aiota", [128, 1024], F32),
                                 ("c_iota512", [128, 512], F32), ("c_esel", [8, 1024], F32), ("c_eq", [8, 64], F32),
                                 ("c_bd8", [8, 1024], F32), ("c_pow2", [128, 2 * KB2], F32), ("c_i32", [128, 768], I32)):
                t_ = sb(s3, "k" + nm, shp, dt_)
                S.dma("sp", lambda e, t_=t_, nm=nm: e.dma_start(out=t_[:], in_=cin[nm][:, :]), writes=[t_])
                ci[nm] = t_
            esel_b = sb(s3, "esel_b", [8, 8, 128], BF16)
            eq_b = sb(s3, "eq_b", [8, 8, 8], BF16)
            ones_b = sb(s3, "ones_b", [128, 2], BF16)
            S.op("dve", lambda e: e.tensor_copy(out=esel_b[:].rearrange("p a b -> p (a b)"), in_=ci["c_esel"][:]),
                 reads=[ci["c_esel"]], writes=[esel_b])
            S.op("dve", lambda e: e.tensor_copy(out=eq_b[:].rearrange("p a b -> p (a b)"), in_=ci["c_eq"][:]),
                 reads=[ci["c_eq"]], writes=[eq_b])
            S.op("pool", lambda e: e.memset(ones_b[:], 1.0), writes=[ones_b])
            I_s = sb(s3, "I_s", [128, 1032], F32)
            junk_s = sb(s3, "junk_s", [128, 1032], BF16)
            iqT_s = sb(s3, "iqT_s", [64, 128], BF16)
            ikTn = sb(s3, "ikTn", [64, 8], BF16)
            wtmp = sb(s3, "wtmp", [128, 16], F32)
            wcol = sb(s3, "wcol", [128, 1], F32)
            wdb = sb(s3, "wdb", [128, 8], F32)
            Wd_all = sb(s3, "Wd_all", [128, 16, 128], BF16)
            Tbs = [sb(s3, "Tbs%d" % j, [128, 512], BF16) for j in range(2)]
            Tn = sb(s3, "Tn", [128, 8], BF16)
            pt_i = sb(s3, "pt_i", [128, 1], I32)
            ptrow_i = sb(s3, "ptrow_i", [128, 128], I32)
            ptrow_f = sb(s3, "ptrow_f", [128, 128], F32)
            st2 = sb(s3, "st2", [128, 4], F32)
            sm2 = sb(s3, "sm2", [128, 8], F32)
            lo2, hi2, rng2, mid2, wv2 = [sm2[:, j:j + 1] for j in range(5)]
            cnt2 = sb(s3, "cnt2", [128, 2], F32)
            spw2 = sb(s3, "spw2", [128, 2 * KB2], F32)
            off_s = sb(s3, "off_s", [128, 1], F32)
            seln = sb(s3, "seln", [128, 8], F32)
            selnT = sb(s3, "selnT", [8, 128], F32)
            idx_i = sb(s3, "idx_i", [128, 2, 8], I32)
            vmask = sb(s3, "vmask", [128, 2, 8], F32)
            S.dma("sp", lambda e: e.dma_start(out=pt_i[:], in_=ptab.rearrange("o p -> p o")), writes=[pt_i])
            S.dma("sp", lambda e: e.dma_start(out=ptrow_i[:], in_=ptab[0:1, :].broadcast_to([128, 128])), writes=[ptrow_i])
            S.op("pool", lambda e: e.memset(cnt2[:], 0.0), writes=[cnt2])
            for h in range(16):
                S.op("pe", lambda e, h=h: e.transpose(out=PT[0:64, h * 8:(h + 1) * 8], in_=iq_sb[:, h, :],
                                                      identity=ident_b[0:8, 0:8]), reads=[iq_sb, ident_b], writes=[PT])
            S.op("pe", lambda e: e.transpose(out=PT[0:64, 128:136], in_=ik_sb[:], identity=ident_b[0:8, 0:8]),
                 reads=[ik_sb, ident_b], writes=[PT])
            S.op("act", lambda e: e.activation(out=iqT_s[:], in_=PT[0:64, 0:128], func=AF.Copy), reads=[PT], writes=[iqT_s])
            S.op("act", lambda e: e.activation(out=ikTn[:], in_=PT[0:64, 128:136], func=AF.Copy), reads=[PT], writes=[ikTn])
            S.op("pe", lambda e: e.matmul(PD[:, 0:16], lhsT=ci["c_sel8"][:], rhs=iw_s[:], start=True, stop=True),
                 reads=[ci["c_sel8"], iw_s], writes=[PD])
            S.op("dve", lambda e: e.tensor_tensor(out=wtmp[:], in0=PD[:, 0:16], in1=ci["c_hm"][:], op=ALU.mult),
                 reads=[PD, ci["c_hm"]], writes=[wtmp])
            S.op("dve", lambda e: e.tensor_reduce(out=wcol[:], in_=wtmp[:], axis=AX.X, op=ALU.add), reads=[wtmp], writes=[wcol])
            S.op("dve", lambda e: e.tensor_scalar(out=wcol[:], in0=wcol[:], scalar1=1.0 / 32.0, scalar2=None, op0=ALU.mult),
                 reads=[wcol], writes=[wcol])
            S.op("dve", lambda e: e.tensor_scalar(out=wdb[:], in0=ci["c_bq8"][:], scalar1=wcol[:, 0:1], scalar2=None, op0=ALU.mult),
                 reads=[ci["c_bq8"], wcol], writes=[wdb])
            S.op("pool", lambda e: e.memset(Wd_all[:], 0.0), writes=[Wd_all])
            for seg in range(16):
                S.op("dve", lambda e, seg=seg: e.tensor_copy(
                    out=Wd_all[:, seg, :].rearrange("p (q s) -> p q s", s=16)[:, :, seg], in_=wdb[:]),
                    reads=[wdb], writes=[Wd_all])
            with ExitStack() as sA:
                ikT_s = sb(sA, "ikT_s", [64, PAST], BF16)
                with ExitStack() as sA2:
                    ikp = sb(sA2, "ikp", [128, 8192], F32)
                    S.dma("pool", lambda e: e.indirect_dma_start(
                        out=ikp[:], out_offset=None, in_=cidx[:, :],
                        in_offset=bass.IndirectOffsetOnAxis(ap=pt_i[:, 0:1], axis=0)), reads=[pt_i], writes=[ikp])
                    for r in range(128):
                        pz = PZ[(r // 4) % 2]
                        S.op("pe", lambda e, r=r, pz=pz: e.transpose(out=pz[0:64, (r % 4) * 128:(r % 4 + 1) * 128],
                                                                     in_=ikp[:, r * 64:(r + 1) * 64], identity=ident_f[:]),
                             reads=[ikp, ident_f], writes=[pz])
                        if r % 4 == 3:
                            eng_ = "act" if (r // 4) % 2 == 0 else "dve"
                            if eng_ == "act":
                                S.op("act", lambda e, r=r, pz=pz: e.activation(out=ikT_s[:, (r - 3) * 128:(r + 1) * 128],
                                                                               in_=pz[0:64, :], func=AF.Copy),
                                     reads=[pz], writes=[ikT_s])
                            else:
                                S.op("dve", lambda e, r=r, pz=pz: e.tensor_copy(out=ikT_s[:, (r - 3) * 128:(r + 1) * 128],
                                                                                in_=pz[0:64, :]), reads=[pz], writes=[ikT_s])
                    S.barrier()
                    S.emit()
                def mm1s(c):
                    pa = PA[c % 2]
                    S.op("pe", lambda e, c=c, pa=pa: e.matmul(pa[:], lhsT=iqT_s[:], rhs=ikT_s[:, c * 512:(c + 1) * 512],
                                                              start=True, stop=True), reads=[iqT_s, ikT_s], writes=[pa])
                mm1s(0)
                for c in range(32):
                    if c + 1 < 32:
                        mm1s(c + 1)
                    pa, tb = PA[c % 2], Tbs[c % 2]
                    seg, half = c // 2, c % 2
                    S.op("act", lambda e, pa=pa, tb=tb: e.activation(out=tb[:], in_=pa[:], func=AF.Relu), reads=[pa], writes=[tb])
                    S.op("pe", lambda e, seg=seg, half=half, tb=tb: e.matmul(PI[half][:], lhsT=Wd_all[:, seg, :], rhs=tb[:],
                                                                            start=(seg == 0), stop=(seg == 15)),
                         reads=[Wd_all, tb], writes=[PI[half]], pe_acc=(seg > 0))
                for half in range(2):
                    S.op("dve", lambda e, half=half: e.tensor_copy(out=I_s[:, half * 512:(half + 1) * 512], in_=PI[half][:]),
                         reads=[PI[half]], writes=[I_s])
                S.op("pe", lambda e: e.matmul(PA[0][:, 0:8], lhsT=iqT_s[:], rhs=ikTn[:], start=True, stop=True),
                     reads=[iqT_s, ikTn], writes=[PA[0]])
                S.op("act", lambda e: e.activation(out=Tn[:], in_=PA[0][:, 0:8], func=AF.Relu), reads=[PA[0]], writes=[Tn])
                S.op("pe", lambda e: e.matmul(PD[:, 0:8], lhsT=Wd_all[:, 0, :], rhs=Tn[:], start=True, stop=True),
                     reads=[Wd_all, Tn], writes=[PD])
                S.op("dve", lambda e: e.tensor_reduce(out=st2[:, 2:3], in_=PD[:, 0:8], axis=AX.X, op=ALU.max,
                                                      apply_absolute_value=True), reads=[PD], writes=[st2])
                S.op("dve", lambda e: e.tensor_tensor(out=I_s[:, 1024:1032], in0=PD[:, 0:8], in1=ci["c_negm"][:], op=ALU.add),
                     reads=[PD, ci["c_negm"]], writes=[I_s])
                S.barrier()
                S.emit()
            S.op("dve", lambda e: e.tensor_reduce(out=st2[:, 0:1], in_=I_s[:, 0:1024], axis=AX.X, op=ALU.min),
                 reads=[I_s], writes=[st2])
            S.op("dve", lambda e: e.tensor_reduce(out=st2[:, 1:2], in_=I_s[:, 0:1024], axis=AX.X, op=ALU.max,
                                                  apply_absolute_value=True), reads=[I_s], writes=[st2])
            S.op("dve", lambda e: e.tensor_tensor(out=st2[:, 1:2], in0=st2[:, 1:2], in1=st2[:, 2:3], op=ALU.max),
                 reads=[st2], writes=[st2])
            S.op("pe", lambda e: e.matmul(PD[:, 16:18], lhsT=ci["c_BQm"][:], rhs=st2[:, 0:2], start=True, stop=True),
                 reads=[ci["c_BQm"], st2], writes=[PD])
            S.op("pe", lambda e: e.matmul(PD[:, 18:20], lhsT=ci["c_BQ1"][:], rhs=st2[:, 0:2], start=True, stop=True),
                 reads=[ci["c_BQ1"], st2], writes=[PD])
            S.op("dve", lambda e: e.tensor_copy(out=lo2, in_=PD[:, 16:17]), reads=[PD], writes=[sm2])
            S.op("dve", lambda e: e.tensor_copy(out=hi2, in_=PD[:, 19:20]), reads=[PD], writes=[sm2])
            S.op("dve", lambda e: e.tensor_tensor(out=rng2, in0=hi2, in1=lo2, op=ALU.subtract), reads=[sm2], writes=[sm2])
            S.op("dve", lambda e: e.tensor_scalar(out=spw2[:], in0=ci["c_pow2"][:], scalar1=rng2, scalar2=None, op0=ALU.mult),
                 reads=[ci["c_pow2"], sm2], writes=[spw2])
            S.op("dve", lambda e: e.tensor_tensor(out=mid2, in0=lo2, in1=spw2[:, 0:1], op=ALU.add), reads=[sm2, spw2], writes=[sm2])
            for k in range(KB2):
                S.op("dve", lambda e: e.tensor_scalar(out=junk_s[:], in0=I_s[:], scalar1=mid2, scalar2=0.0, op0=ALU.is_ge,
                                                      op1=ALU.add, accum_out=cnt2[:, 0:1]), reads=[I_s, sm2], writes=[junk_s, cnt2])
                S.op("pe", lambda e: e.matmul(PD[:, 32:34], lhsT=ci["c_BQ1"][:], rhs=cnt2[:], start=True, stop=True),
                     reads=[ci["c_BQ1"], cnt2], writes=[PD])
                S.op("dve", lambda e, k=k: e.scalar_tensor_tensor(out=wv2, in0=PD[:, 32:33], scalar=256.0, in1=spw2[:, k:k + 1],
                                                                  op0=ALU.is_ge, op1=ALU.mult), reads=[PD, spw2], writes=[sm2])
                S.op("dve", lambda e, k=k: e.scalar_tensor_tensor(out=mid2, in0=mid2, scalar=spw2[:, KB2 + k:KB2 + k + 1], in1=wv2,
                                                                  op0=ALU.subtract, op1=ALU.add), reads=[sm2, spw2], writes=[sm2])
            with ExitStack() as sB:
                selm = sb(sB, "selm", [128, 1024], F32)
                PH = sb(sB, "PH", [128, 8, 128], F32)
                kp = [sb(sB, "kp%d" % j, [128, 1024], F32) for j in range(2)]
                cand = sb(sB, "cand", [128, 256], F32)
                candi = sb(sB, "candi", [128, 256], I32)
                dgi = sb(sB, "dgi", [128, 256], I32)
                dgb = sb(sB, "dgb", [128, 3, 256], BF16)
                L_all = sb(sB, "L_all", [128, 256, 24], BF16)
                OH = sb(sB, "OH", [128, 512], BF16)
                Cs = sb(sB, "Cs", [24, 256], F32)
                dig_s = sb(sB, "dig_s", [128, 2, 24], F32)
                idxf = sb(sB, "idxf", [128, 2, 8], F32)
                S.op("dve", lambda e: e.tensor_scalar(out=selm[:], in0=I_s[:, 0:1024], scalar1=mid2, scalar2=0.0, op0=ALU.is_ge,
                                                      op1=ALU.add, accum_out=cnt2[:, 0:1]), reads=[I_s, sm2], writes=[selm, cnt2])
                S.op("pe", lambda e: e.matmul(PD[:, 34:36], lhsT=ci["c_LT"][:], rhs=cnt2[:], start=True, stop=True),
                     reads=[ci["c_LT"], cnt2], writes=[PD])
                S.op("dve", lambda e: e.tensor_copy(out=off_s[:], in_=PD[:, 34:35]), reads=[PD], writes=[off_s])
                S.op("dve", lambda e: e.tensor_scalar(out=seln[:], in0=I_s[:, 1024:1032], scalar1=mid2, scalar2=None, op0=ALU.is_ge),
                     reads=[I_s, sm2], writes=[seln])
                S.op("pe", lambda e: e.transpose(out=PZ[0][0:8, 0:128], in_=seln[:], identity=ident_f[:]),
                     reads=[seln, ident_f], writes=[PZ[0]])
                S.op("dve", lambda e: e.tensor_copy(out=selnT[:], in_=PZ[0][0:8, 0:128]), reads=[PZ[0]], writes=[selnT])
                S.op("dve", lambda e: e.tensor_copy(out=ptrow_f[:], in_=ptrow_i[:]), reads=[ptrow_i], writes=[ptrow_f])
                S.op("dve", lambda e: e.tensor_scalar(out=ptrow_f[:], in0=ptrow_f[:], scalar1=128.0, scalar2=None, op0=ALU.mult),
                     reads=[ptrow_f], writes=[ptrow_f])
                S.op("dve", lambda e: e.tensor_tensor(out=PH[:], in0=ci["c_aiota"][:].rearrange("p (a s) -> p a s", a=8),
                                                      in1=ptrow_f[:].unsqueeze(1).to_broadcast([128, 8, 128]), op=ALU.add),
                     reads=[ci["c_aiota"], ptrow_f], writes=[PH])
                S.op("dve", lambda e: e.tensor_tensor(out=kp[0][:], in0=selm[:], in1=PH[:].rearrange("p a s -> p (a s)"), op=ALU.mult),
                     reads=[selm, PH], writes=[kp[0]])
                for r in range(32):
                    a_, b_ = kp[r % 2], kp[(r + 1) % 2]
                    S.op("dve", lambda e, r=r, a_=a_: e.max(out=cand[:, r * 8:(r + 1) * 8], in_=a_[:]), reads=[a_], writes=[cand])
                    if r < 31:
                        S.op("dve", lambda e, r=r, a_=a_, b_=b_: e.match_replace(out=b_[:], in_to_replace=cand[:, r * 8:(r + 1) * 8],
                                                                                 in_values=a_[:], imm_value=0.0),
                             reads=[a_, cand], writes=[b_])
                S.op("dve", lambda e: e.tensor_copy(out=candi[:], in_=cand[:]), reads=[cand], writes=[candi])
                c255, c8, c16 = ci["c_i32"][:, 0:256], ci["c_i32"][:, 256:512], ci["c_i32"][:, 512:768]
                S.op("dve", lambda e: e.tensor_tensor(out=dgi[:], in0=candi[:], in1=c255, op=ALU.bitwise_and),
                     reads=[candi, ci["c_i32"]], writes=[dgi])
                S.op("dve", lambda e: e.tensor_copy(out=dgb[:, 0, :], in_=dgi[:]), reads=[dgi], writes=[dgb])
                S.op("dve", lambda e: e.tensor_tensor(out=dgi[:], in0=candi[:], in1=c8, op=ALU.logical_shift_right),
                     reads=[candi, ci["c_i32"]], writes=[dgi])
                S.op("dve", lambda e: e.tensor_tensor(out=dgi[:], in0=dgi[:], in1=c255, op=ALU.bitwise_and),
                     reads=[dgi, ci["c_i32"]], writes=[dgi])
                S.op("dve", lambda e: e.tensor_copy(out=dgb[:, 1, :], in_=dgi[:]), reads=[dgi], writes=[dgb])
                S.op("dve", lambda e: e.tensor_tensor(out=dgi[:], in0=candi[:], in1=c16, op=ALU.logical_shift_right),
                     reads=[candi, ci["c_i32"]], writes=[dgi])
                S.op("dve", lambda e: e.tensor_copy(out=dgb[:, 2, :], in_=dgi[:]), reads=[dgi], writes=[dgb])
                for dg in range(3):
                    S.op("dve", lambda e, dg=dg: e.tensor_tensor(
                        out=L_all[:, :, dg * 8:(dg + 1) * 8], in0=dgb[:, dg, :].unsqueeze(2).to_broadcast([128, 256, 8]),
                        in1=ci["c_bq16"][:].unsqueeze(1).to_broadcast([128, 256, 8]), op=ALU.mult),
                        reads=[dgb, ci["c_bq16"]], writes=[L_all])
                S.op("dve", lambda e: e.tensor_scalar(out=OH[:], in0=ci["c_iota512"][:], scalar1=off_s[:, 0:1], scalar2=None,
                                                      op0=ALU.is_equal), reads=[ci["c_iota512"], off_s], writes=[OH])
                for r in range(256):
                    S.op("pe", lambda e, r=r: e.matmul(PU[0][0:24, 0:256], lhsT=L_all[:, r, :], rhs=OH[:, 256 - r:512 - r],
                                                       start=(r == 0), stop=(r == 255)), reads=[L_all, OH], writes=[PU[0]],
                         pe_acc=(r > 0))
                S.op("dve", lambda e: e.tensor_copy(out=Cs[:], in_=PU[0][0:24, 0:256]), reads=[PU[0]], writes=[Cs])
                for half in range(2):
                    S.op("pe", lambda e, half=half: e.transpose(out=PZ[1][:, half * 24:(half + 1) * 24],
                                                                in_=Cs[:, half * 128:(half + 1) * 128], identity=ident_f[0:24, 0:24]),
                         reads=[Cs, ident_f], writes=[PZ[1]])
                S.op("dve", lambda e: e.tensor_copy(out=dig_s[:].rearrange("p a b -> p (a b)"), in_=PZ[1][:, 0:48]),
                     reads=[PZ[1]], writes=[dig_s])
                S.op("dve", lambda e: e.scalar_tensor_tensor(out=idxf[:], in0=dig_s[:, :, 16:24], scalar=256.0, in1=dig_s[:, :, 8:16],
                                                             op0=ALU.mult, op1=ALU.add), reads=[dig_s], writes=[idxf])
                S.op("dve", lambda e: e.scalar_tensor_tensor(out=idxf[:], in0=idxf[:], scalar=256.0, in1=dig_s[:, :, 0:8],
                                                             op0=ALU.mult, op1=ALU.add), reads=[idxf, dig_s], writes=[idxf])
                S.op("dve", lambda e: e.tensor_scalar(out=vmask[:], in0=idxf[:], scalar1=0.5, scalar2=None, op0=ALU.is_gt),
                     reads=[idxf], writes=[vmask])
                S.op("dve", lambda e: e.tensor_scalar(out=idxf[:], in0=idxf[:], scalar1=-1.0, scalar2=0.0, op0=ALU.add, op1=ALU.max),
                     reads=[idxf], writes=[idxf])
                S.op("dve", lambda e: e.tensor_copy(out=idx_i[:], in_=idxf[:]), reads=[idxf], writes=[idx_i])
                S.barrier()
                S.emit()
            with ExitStack() as sC:
                Ksel = sb(sC, "Ksel", [128, 16, 1024], BF16)
                Vsel = sb(sC, "Vsel", [128, 16, 1024], BF16)
                prod = sb(sC, "prod", [128, 1024], F32)
                sT = sb(sC, "sT", [128, 2, 8], F32)
                pT = sb(sC, "pT", [128, 2, 8], BF16)
                prodn = sb(sC, "prodn", [8, 1024], F32)
                sTn = sb(sC, "sTn", [8, 8], F32)
                pTn = sb(sC, "pTn", [8, 8], BF16)
                rden_s = sb(sC, "rden_s", [8, 1], F32)
                Mq = sb(sC, "Mq", [8, 1024], BF16)
                for q in range(8):
                    for half in range(2):
                        for dst_, src_ in ((Ksel, ck), (Vsel, cv)):
                            S.dma("pool", lambda e, q=q, half=half, dst_=dst_, src_=src_: e.indirect_dma_start(
                                out=dst_[:, q * 2 + half, :], out_offset=None, in_=src_[:, :],
                                in_offset=bass.IndirectOffsetOnAxis(ap=idx_i[:, half, q:q + 1], axis=0)),
                                reads=[idx_i], writes=[dst_])
                PAs = [PG, PGL]
                for q in range(8):
                    for cb in range(2):
                        S.op("pe", lambda e, q=q, cb=cb: e.matmul(PZ[cb][:], lhsT=esel_b[:, q, :], rhs=aq_sb[:, cb * 512:(cb + 1) * 512],
                                                                  start=True, stop=True), reads=[esel_b, aq_sb], writes=[PZ[cb]])
                    for half in range(2):
                        for cb in range(2):
                            S.op("dve", lambda e, q=q, half=half, cb=cb: e.tensor_tensor(
                                out=prod[:, cb * 512:(cb + 1) * 512], in0=Ksel[:, q * 2 + half, cb * 512:(cb + 1) * 512],
                                in1=PZ[cb][:], op=ALU.mult), reads=[Ksel, PZ[cb]], writes=[prod])
                        S.op("dve", lambda e, half=half: e.tensor_reduce(out=sT[:, half, :], in_=prod[:].rearrange("p (h d) -> p h d", h=8),
                                                                         axis=AX.X, op=ALU.add), reads=[prod], writes=[sT])
                    S.op("act", lambda e: e.activation(out=sT[:].rearrange("p a b -> p (a b)"), in_=sT[:].rearrange("p a b -> p (a b)"),
                                                       func=AF.Exp), reads=[sT], writes=[sT])
                    S.op("dve", lambda e, q=q: e.tensor_tensor(out=pT[:], in0=sT[:], in1=vmask[:, :, q:q + 1].to_broadcast([128, 2, 8]),
                                                               op=ALU.mult), reads=[sT, vmask], writes=[pT])
                    for cb in range(2):
                        S.op("dve", lambda e, cb=cb: e.tensor_tensor(out=prodn[:, cb * 512:(cb + 1) * 512],
                                                                     in0=ak_sb[:, cb * 512:(cb + 1) * 512], in1=PZ[cb][0:8, :],
                                                                     op=ALU.mult), reads=[ak_sb, PZ[cb]], writes=[prodn])
                    S.op("dve", lambda e: e.tensor_reduce(out=sTn[:], in_=prodn[:].rearrange("p (h d) -> p h d", h=8), axis=AX.X,
                                                          op=ALU.add), reads=[prodn], writes=[sTn])
                    S.op("act", lambda e: e.activation(out=sTn[:], in_=sTn[:], func=AF.Exp), reads=[sTn], writes=[sTn])
                    S.op("dve", lambda e, q=q: e.tensor_scalar(out=pTn[:], in0=sTn[:], scalar1=selnT[:, q * 16:q * 16 + 1], scalar2=None,
                                                               op0=ALU.mult), reads=[sTn, selnT], writes=[pTn])
                    for cb in range(2):
                        cs_ = slice(cb * 512, (cb + 1) * 512)
                        S.op("pe", lambda e, q=q, cb=cb, cs_=cs_: e.matmul(PU[cb][0:8, :], lhsT=pT[:, 0, :], rhs=Vsel[:, q * 2, cs_],
                                                                           start=True, stop=False), reads=[pT, Vsel], writes=[PU[cb]])
                        S.op("pe", lambda e, q=q, cb=cb, cs_=cs_: e.matmul(PU[cb][0:8, :], lhsT=pT[:, 1, :], rhs=Vsel[:, q * 2 + 1, cs_],
                                                                           start=False, stop=False), reads=[pT, Vsel], writes=[PU[cb]],
                             pe_acc=True)
                        S.op("pe", lambda e, cb=cb, cs_=cs_: e.matmul(PU[cb][0:8, :], lhsT=pTn[:], rhs=av_sb[:, cs_],
                                                                      start=False, stop=True), reads=[pTn, av_sb], writes=[PU[cb]],
                             pe_acc=True)
                    S.op("pe", lambda e: e.matmul(PD[0:8, 0:2], lhsT=pT[:, 0, :], rhs=ones_b[:], start=True, stop=False),
                         reads=[pT, ones_b], writes=[PD])
                    S.op("pe", lambda e: e.matmul(PD[0:8, 0:2], lhsT=pT[:, 1, :], rhs=ones_b[:], start=False, stop=False),
                         reads=[pT, ones_b], writes=[PD], pe_acc=True)
                    S.op("pe", lambda e: e.matmul(PD[0:8, 0:2], lhsT=pTn[:], rhs=ones_b[0:8, :], start=False, stop=True),
                         reads=[pTn, ones_b], writes=[PD], pe_acc=True)
                    S.op("dve", lambda e: e.reciprocal(out=rden_s[:], in_=PD[0:8, 0:1]), reads=[PD], writes=[rden_s])
                    for cb in range(2):
                        cs_ = slice(cb * 512, (cb + 1) * 512)
                        S.op("dve", lambda e, cb=cb, cs_=cs_: e.scalar_tensor_tensor(
                            out=Mq[:, cs_], in0=PU[cb][0:8, :], scalar=rden_s[:, 0:1], in1=ci["c_bd8"][:, cs_],
                            op0=ALU.mult, op1=ALU.mult), reads=[PU[cb], rden_s, ci["c_bd8"]], writes=[Mq])
                        S.op("pe", lambda e, q=q, cb=cb, cs_=cs_: e.matmul(PAs[cb][0:8, :], lhsT=eq_b[:, q, :], rhs=Mq[:, cs_],
                                                                           start=(q == 0), stop=(q == 7)),
                             reads=[eq_b, Mq], writes=[PAs[cb]], pe_acc=(q > 0))
                for cb in range(2):
                    cs_ = slice(cb * 512, (cb + 1) * 512)
                    S.op("dve", lambda e, cb=cb, cs_=cs_: e.tensor_tensor(out=cat_s[:, cs_], in0=PAs[cb][0:8, :], in1=ga_s[:, cs_],
                                                                          op=ALU.mult), reads=[PAs[cb], ga_s], writes=[cat_s])
                S.barrier()
                S.emit()
        if DBG:
            S.dma("sp", lambda e: e.dma_start(out=o_cat[:, :], in_=catT[:].rearrange("p k t -> p (k t)")), reads=[catT])
        with ExitStack() as p7:
            wo_sb = sb(p7, "wo_sb", [128, KC, 2048], BF16)
            xr = [sb(p7, "xr%d" % j, [128, 2048], F32) for j in range(2)]
            rr = sb(p7, "rr", [128, 2048], F32)
            yo = [sb(p7, "yo%d" % j, [128, 2048], F32) for j in range(2)]
            lng_bc = sb(p7, "lng_bc", [128, 2048], F32)
            lnb_bc = sb(p7, "lnb_bc", [128, 2048], F32)
            stats = sb(p7, "stats", [128, 4, 6], F32)
            mv2 = sb(p7, "mv2", [128, 2], F32)
            rs2 = sb(p7, "rs2", [128, 2], F32)
            load_wres(wo_sb, Wo, 2048)
            S.dma("sp", lambda e: e.dma_start(out=lng_bc[:], in_=lng[0:1, :].broadcast_to([128, 2048])), writes=[lng_bc])
            S.dma("sp", lambda e: e.dma_start(out=lnb_bc[:], in_=lnb[0:1, :].broadcast_to([128, 2048])), writes=[lnb_bc])
            csT = sb(p7, "csT", [128, KC, 8], BF16)
            for kc in range(KC):
                S.op("pe", lambda e, kc=kc: e.transpose(out=PT[:, kc * 8:(kc + 1) * 8], in_=cat_s[:, kc * 128:(kc + 1) * 128],
                                                        identity=ident_b[0:8, 0:8]), reads=[cat_s, ident_b], writes=[PT])
            S.op("act", lambda e: e.activation(out=csT[:].rearrange("p k t -> p (k t)"), in_=PT[:, 0:128], func=AF.Copy),
                 reads=[PT], writes=[csT])

            def merge_rows(np_, j, lhs_fn, lhs_deps, x_src, o_dst):
                x_, y_ = xr[j % 2], yo[j % 2]
                S.dma("sp", lambda e: e.dma_start(out=x_[0:np_, :], in_=x_src), writes=[x_])
                for c in range(4):
                    pz = PZ[c % 2]
                    for k in range(KC):
                        S.op("pe", lambda e, k=k, c=c, pz=pz: e.matmul(
                            pz[0:np_, :], lhsT=lhs_fn(k), rhs=wo_sb[:, k, c * 512:(c + 1) * 512],
                            start=(k == 0), stop=(k == KC - 1)), reads=lhs_deps + [wo_sb], writes=[pz], pe_acc=(k > 0))
                    S.op("dve", lambda e, c=c, pz=pz: e.scalar_tensor_tensor(
                        out=rr[0:np_, c * 512:(c + 1) * 512], in0=x_[0:np_, c * 512:(c + 1) * 512], scalar=ALPHA, in1=pz[0:np_, :],
                        op0=ALU.mult, op1=ALU.add), reads=[x_, pz], writes=[rr])
                    S.op("dve", lambda e, c=c: e.bn_stats(out=stats[0:np_, c, :], in_=rr[0:np_, c * 512:(c + 1) * 512]),
                         reads=[rr], writes=[stats])
                S.op("dve", lambda e: e.bn_aggr(out=mv2[0:np_, :], in_=stats[0:np_].rearrange("p a b -> p (a b)")),
                     reads=[stats], writes=[mv2])
                S.op("dve", lambda e: e.tensor_scalar(out=rs2[0:np_, 0:1], in0=mv2[0:np_, 1:2], scalar1=LN_EPS, scalar2=None,
                                                      op0=ALU.add), reads=[mv2], writes=[rs2])
                S.op("act", lambda e: e.activation(out=rs2[0:np_, 0:1], in_=rs2[0:np_, 0:1], func=AF.Sqrt), reads=[rs2], writes=[rs2])
                S.op("dve", lambda e: e.reciprocal(out=rs2[0:np_, 0:1], in_=rs2[0:np_, 0:1]), reads=[rs2], writes=[rs2])
                S.op("dve", lambda e: e.scalar_tensor_tensor(out=rs2[0:np_, 1:2], in0=mv2[0:np_, 0:1], scalar=-1.0,
                                                             in1=rs2[0:np_, 0:1], op0=ALU.mult, op1=ALU.mult),
                     reads=[mv2, rs2], writes=[rs2])
                S.op("act", lambda e: e.activation(out=y_[0:np_, :], in_=rr[0:np_, :], func=AF.Identity, scale=rs2[0:np_, 0:1],
                                                   bias=rs2[0:np_, 1:2]), reads=[rr, rs2], writes=[y_])
                S.op("pool", lambda e: e.tensor_tensor(out=y_[0:np_, :], in0=y_[0:np_, :], in1=lng_bc[0:np_, :], op=ALU.mult),
                     reads=[y_, lng_bc], writes=[y_])
                S.op("pool", lambda e: e.tensor_tensor(out=y_[0:np_, :], in0=y_[0:np_, :], in1=lnb_bc[0:np_, :], op=ALU.add),
                     reads=[y_, lnb_bc], writes=[y_])
                S.dma("sp", lambda e: e.dma_start(out=o_dst, in_=y_[0:np_, :]), reads=[y_])
            for i in range(NO):
                merge_rows(128, i, lambda k, i=i: catT[:, k, i * 128:(i + 1) * 128], [catT],
                           xo[i * 128:(i + 1) * 128, :], o_y[i * 128:(i + 1) * 128, :])
            merge_rows(8, NO, lambda k: csT[:, k, :], [csT], xs[:, :], o_ys[:, :])
            S.barrier()
            S.emit()
        es_S.close()
        es_C.close()
    return nc


def _rope_tab(pos, half):
    inv = (np.float32(ROPE_THETA) ** (-np.arange(half, dtype=np.float32) / np.float32(half))).astype(np.float32)
    ang = pos.astype(np.float32)[:, None] * inv[None, :]
    c = np.cos(ang).astype(np.float32)
    s = np.sin(ang).astype(np.float32)
    return np.concatenate([c, c], 1), np.concatenate([-s, s], 1)


def _consts():
    s = np.arange(128)
    same = (s[:, None] // 64) == (s[None, :] // 64)
    tri2 = (same & (s[:, None] <= s[None, :])).astype(np.float32)
    blk2 = same.astype(np.float32)
    return dict(c_tri2=tri2, c_blk2=blk2, c_ident=np.eye(128, dtype=np.float32))


_NC_CACHE = {}


def kernel(x_prompt, x_sample, mem_prompt, cache_k, cache_v, cache_idx_k, state_hgrn,
           cache_mem_k, cache_mem_v, page_table, w_in, lb_logits, hgrn_norm_g,
           w_mem_k, w_mem_v, w_out, ln_g, ln_b):
    f32 = np.float32
    x_prompt = np.asarray(x_prompt, f32)
    w = np.asarray(w_in, f32)[0]
    ca = np.ascontiguousarray
    Wk = ca(w[:, O_AK:O_AV])
    Wv = ca(w[:, O_AV:O_AG])
    W3 = ca(np.concatenate([w[:, O_BF:O_BI], w[:, O_BI:O_BG], w[:, O_IK:O_IW]], axis=1))
    pos = np.arange(SEQ)
    cc128, ss128 = _rope_tab(pos, 16)
    cc64, ss64 = _rope_tab(pos, 8)
    ropeN = ca(np.concatenate([cc128, ss128, cc64, ss64], 1).astype(f32))
    consts = _consts()
    consts["c_j"] = np.tile(np.arange(256, dtype=f32)[None, :], (128, 1))
    pw = np.zeros(48, f32)
    for k in range(24):
        pw[k] = 2.0 ** -(k + 1)
        pw[24 + k] = 2.0 ** -(k + 2) if k < 23 else 2.0 ** -24
    consts["c_pow"] = np.tile(pw[None, :], (128, 1))
    Wi_ = ca(np.concatenate([w[:, O_IQ:O_IK], w[:, O_IW:O_BQ]], axis=1))
    Wb_ = ca(np.concatenate([w[:, O_BQ:O_BF], w[:, O_BF:O_BI], w[:, O_BI:O_BG], w[:, O_BG:O_MQ]], axis=1))
    consts.update(Wi=Wi_, Wq=ca(w[:, O_AQ:O_AK]), Wg=ca(w[:, O_AG:O_IQ]), Wb=Wb_, Wm=ca(w[:, O_MQ:O_END]),
                  Wo=ca(np.asarray(w_out, f32)[0]), ng=ca(np.asarray(hgrn_norm_g, f32).reshape(1, 128)),
                  lng=ca(np.asarray(ln_g, f32).reshape(1, D)), lnb=ca(np.asarray(ln_b, f32).reshape(1, D)))
    qs_ = np.float32(128.0 ** -0.5)
    P = np.arange(128)
    hq_h, hq_q = P // 8, P % 8
    qs_q, qs_s = P // 16, P % 16
    consts["c_sel8"] = (np.arange(8)[:, None] == hq_q[None, :]).astype(f32)
    consts["c_hm"] = (hq_h[:, None] == np.arange(16)[None, :]).astype(f32)
    consts["c_bq8"] = (hq_q[:, None] == np.arange(8)[None, :]).astype(f32)
    consts["c_bq16"] = (qs_q[:, None] == np.arange(8)[None, :]).astype(f32)
    sameq = (qs_q[:, None] == qs_q[None, :])
    consts["c_BQ1"] = sameq.astype(f32)
    consts["c_BQm"] = (sameq / 16.0).astype(f32)
    consts["c_LT"] = (sameq & (qs_s[:, None] < qs_s[None, :])).astype(f32)
    consts["c_negm"] = np.where((qs_s[:, None] == 0) & (np.arange(8)[None, :] <= qs_q[:, None]), 0.0, NEG).astype(f32)
    consts["c_aiota"] = (qs_s[:, None] * 8 + (np.arange(1024)[None, :] // 128) + 1).astype(f32)
    consts["c_iota512"] = np.tile((np.arange(512) - 256).astype(f32)[None, :], (128, 1))
    esel = np.zeros((8, 8, 128), f32)
    eq = np.zeros((8, 8, 8), f32)
    for q_ in range(8):
        esel[q_, q_, :] = 1.0
        eq[:, q_, q_] = 1.0
    consts["c_esel"] = esel.reshape(8, 1024)
    consts["c_eq"] = eq.reshape(8, 64)
    consts["c_bd8"] = np.repeat((np.arange(8)[:, None] == np.arange(8)[None, :]).astype(f32), 128, axis=1)
    pw2 = np.zeros(60, f32)
    for k in range(30):
        pw2[k] = 2.0 ** -(k + 1)
        pw2[30 + k] = 2.0 ** -(k + 2) if k < 29 else 2.0 ** -30
    consts["c_pow2"] = np.tile(pw2[None, :], (128, 1))
    consts["c_i32"] = np.concatenate([np.full((128, 256), 255), np.full((128, 256), 8), np.full((128, 256), 16)], 1).astype(np.int32)
    ck_flat = np.asarray(cache_k, f32)[0].reshape(-1, 1024)
    cv_flat = np.asarray(cache_v, f32)[0].reshape(-1, 1024)
    cidx_flat = np.asarray(cache_idx_k, f32)[0].reshape(1280, 8192)
    consts.update(ck=ck_flat, cv=cv_flat, cidx=cidx_flat)
    shared = dict(Wk=Wk, Wv=Wv, W3=W3, Wmk=ca(np.asarray(w_mem_k, f32)[0]), Wmv=ca(np.asarray(w_mem_v, f32)[0]),
                  lbl=ca(np.asarray(lb_logits, f32)), ropeN=ropeN, **consts)
    pos_s = PAST + np.arange(8)
    c128s, s128s = _rope_tab(pos_s, 16)
    c64s, s64s = _rope_tab(pos_s, 8)
    ropeS = ca(np.concatenate([c128s, s128s, c64s, s64s, c128s * qs_, s128s * qs_], 1).astype(f32))
    x_sample = np.asarray(x_sample, f32)
    in_maps = []
    for c in range(8):
        b, h = c // 2, c % 2
        posq = np.zeros((128, NO + 1), f32)
        for i in range(NO):
            posq[:, i] = (2 * i + h) * 128 + np.arange(128)
        posq[:, NO] = h
        m = dict(shared)
        own = np.concatenate([(2 * i + h) * 128 + np.arange(128) for i in range(NO)])
        ropeO = ca(np.concatenate([cc128[own] * qs_, ss128[own] * qs_, cc64[own], ss64[own]], 1).astype(f32))
        m.update(xTn=ca(x_prompt[b].T), memT=ca(np.asarray(mem_prompt, f32)[b].T), posq=posq,
                 xTo=ca(x_prompt[b][own].T), xo=ca(x_prompt[b][own]), ropeO=ropeO,
                 xsT=ca(x_sample[c].T), xs=ca(x_sample[c]), ropeS=ropeS,
                 st_h=ca(np.asarray(state_hgrn, f32)[0, c]),
                 ptab=ca(np.asarray(page_table)[c].astype(np.int32).reshape(1, 128)),
                 cmk_d=ca(np.asarray(cache_mem_k, f32)[0, c].reshape(256, 512)),
                 cmv_d=ca(np.asarray(cache_mem_v, f32)[0, c].reshape(256, 512)))
        in_maps.append(m)
    import os
    stage = int(os.environ.get("KSTAGE", "99"))
    Sched.LIMIT = int(os.environ.get("KLIMIT", str(10 ** 9)))
    ncores = int(os.environ.get("KCORES", "8"))
    if "nc" not in _NC_CACHE:
        _NC_CACHE["nc"] = build_program(stage)
    nc = _NC_CACHE["nc"]
    res = run_bass_kernel_spmd(nc, in_maps[:ncores], core_ids=list(range(ncores)))
    if ncores < 8:
        res.results.extend([res.results[0]] * (8 - ncores))
    R = res.results
    _NC_CACHE["R"] = R
    B = 4
    y_p = np.zeros((B, SEQ, D), f32)
    if "o_y" in R[0]:
        for c in range(8):
            b, h = c // 2, c % 2
            oy = R[c]["o_y"]
            for i in range(NO):
                n = 2 * i + h
                y_p[b, n * 128:(n + 1) * 128] = oy[i * 128:(i + 1) * 128]
    y_s = np.zeros((8, 8, D), f32)
    k_p = np.stack([R[2 * b]["o_k"].reshape(SEQ, 8, 128) for b in range(B)])[None]
    v_p = np.stack([R[2 * b]["o_v"].reshape(SEQ, 8, 128) for b in range(B)])[None]
    ik_p = np.stack([R[2 * b]["o_ik"] for b in range(B)])[None]
    hg_p = np.stack([R[2 * b]["o_hg"].transpose(1, 0, 2) for b in range(B)])[None]
    mk_p = np.stack([R[2 * b]["o_mk"].reshape(256, 4, 128) for b in range(B)])[None]
    mv_p = np.stack([R[2 * b]["o_mv"].reshape(256, 4, 128) for b in range(B)])[None]
    k_s = np.stack([R[c]["o_ks"].reshape(8, 8, 128) for c in range(8)])[None].astype(f32)
    v_s = np.stack([R[c]["o_vs"].reshape(8, 8, 128) for c in range(8)])[None].astype(f32)
    ik_s = np.stack([R[c]["o_iks"] for c in range(8)])[None].astype(f32)
    hg_s = np.stack([R[c]["o_hgs"].transpose(1, 0, 2) for c in range(8)])[None].astype(f32)
    y_s = np.stack([R[c]["o_ys"] for c in range(8)]).astype(f32)
    return (y_p, y_s, k_p.astype(f32), v_p.astype(f32), ik_p.astype(f32), hg_p.astype(f32), mk_p.astype(f32),
            mv_p.astype(f32), k_s, v_s, ik_s, hg_s)
```

```python
from contextlib import ExitStack
import numpy as np
import concourse.bass as bass
import concourse.mybir as mybir
from concourse.bass_utils import run_bass_kernel_spmd

F32 = mybir.dt.float32
BF16 = mybir.dt.bfloat16
I32 = mybir.dt.int32
AF = mybir.ActivationFunctionType
ALU = mybir.AluOpType
AX = mybir.AxisListType

D = 2048
KC = 16
SEQ = 2048
NT = 16
NO = 8
ROPE_THETA = 500000.0
PAST = 16384
ALPHA = 2.0 ** 0.25
LN_EPS = 1e-5
RMS_EPS = 1e-6
NEG = -1.0e30

O_AQ, O_AK, O_AV, O_AG, O_IQ, O_IK, O_IW, O_BQ, O_BF, O_BI, O_BG, O_MQ, O_MG, O_END = (
    0, 1024, 2048, 3072, 4096, 5120, 5184, 5200, 5712, 6224, 6736, 7248, 7760, 8272)


class Buf:
    __slots__ = ("name", "w", "r", "dsem", "dval", "excl")

    def __init__(self, name):
        self.name = name
        self.excl = False
        self.w = None
        self.r = {}
        self.dsem = None
        self.dval = 0


class TL:
    def __init__(self, t, name):
        self.t = t
        self.b = Buf(name)

    def __getitem__(self, k):
        return self.t[k]


class TLV(TL):
    def __init__(self, base, fn):
        self.t = None
        self.base = base
        self.fn = fn
        self.b = base.b

    def __getitem__(self, k):
        return self.fn(self.base.t)[k]


class Sched:
    ENG = ("pe", "act", "dve", "pool", "sp")

    def __init__(self, nc, es):
        self.nc = nc
        self.es = es
        self.q = {e: [] for e in self.ENG}
        self.cnt = {e: 0 for e in self.ENG}
        self.known = {e: {} for e in self.ENG}
        self.sems = {}
        for e in self.ENG:
            self.sems["e:" + e] = es.enter_context(nc.semaphore("s_" + e))
        self.ndsem = 0
        self.ninstr = 0
        self.dvals = {}

    def _dsem(self, buf):
        if buf.dsem is None:
            key = "d:%d" % self.ndsem
            self.ndsem += 1
            self.sems[key] = self.es.enter_context(self.nc.semaphore("sd%d" % self.ndsem))
            buf.dsem = key
        return buf.dsem

    def _deps(self, eng, reads, writes, skip_self_pe=False):
        deps = {}

        def add(k, v):
            if deps.get(k, 0) < v:
                deps[k] = v
        for b in reads:
            if b.w is not None:
                add(*b.w)
            if b.excl:
                for k, v in b.r.items():
                    if k != "e:" + eng:
                        add(k, v)
        for b in writes:
            if b.w is not None:
                add(*b.w)
            for k, v in b.r.items():
                add(k, v)
        waits = []
        for k, v in deps.items():
            if skip_self_pe and k == "e:pe":
                continue
            if self.known[eng].get(k, 0) < v:
                self.known[eng][k] = v
                waits.append((k, v))
        return waits

    def _mark(self, ev, reads, writes):
        for b in reads:
            if b.r.get(ev[0], 0) < ev[1]:
                b.r[ev[0]] = ev[1]
        for b in writes:
            b.w = ev
            b.r = {}

    LIMIT = 10 ** 9

    def op(self, eng, fn, reads=(), writes=(), pe_acc=False):
        if self.ninstr >= Sched.LIMIT:
            return None
        reads = [x.b if isinstance(x, TL) else x for x in reads]
        writes = [x.b if isinstance(x, TL) else x for x in writes]
        waits = self._deps(eng, reads, writes, skip_self_pe=(eng == "pe" and pe_acc))
        self.cnt[eng] += 1
        ev = ("e:" + eng, self.cnt[eng])
        self.q[eng].append((waits, fn, ev[0], 1))
        self._mark(ev, reads, writes)
        self.ninstr += 1
        return ev

    def dma(self, queue, fns, reads=(), writes=(), owner=None):
        if self.ninstr >= Sched.LIMIT:
            return None
        reads = [x.b if isinstance(x, TL) else x for x in reads]
        writes = [x.b if isinstance(x, TL) else x for x in writes]
        if not isinstance(fns, (list, tuple)):
            fns = [fns]
        if owner is None:
            owner = (list(writes) + list(reads))[0]
        elif isinstance(owner, TL):
            owner = owner.b
        key = self._dsem(owner)
        waits = self._deps(queue, reads, writes)
        for i, fn in enumerate(fns):
            owner.dval += 16
            self.q[queue].append((waits if i == 0 else [], fn, key, 16))
            self.ninstr += 1
        ev = (key, owner.dval)
        self.dvals[key] = owner.dval
        self._mark(ev, reads, writes)
        return ev

    def barrier(self):
        tgt = {"e:" + e: self.cnt[e] for e in self.ENG if self.cnt[e] > 0}
        for k, v in self.dvals.items():
            tgt[k] = v
        for e in self.ENG:
            waits = []
            for k, v in tgt.items():
                if k == "e:" + e:
                    continue
                if self.known[e].get(k, 0) < v:
                    self.known[e][k] = v
                    waits.append((k, v))
            self.q[e].append((waits, None, None, 0))

    def finish_wait(self, bufs, eng="sp"):
        deps = {}
        for b in bufs:
            b = b.b if isinstance(b, TL) else b
            for k, v in ([b.w] if b.w else []) + list(b.r.items()):
                if deps.get(k, 0) < v:
                    deps[k] = v
        self.q[eng].append((list(deps.items()), None, None, 0))

    def emit(self):
        nc = self.nc
        if not hasattr(self, "hw"):
            self.hw = {e: 0 for e in self.ENG}
            self.emitted = {e: 0 for e in self.ENG}
            self.vmap = {e: {} for e in self.ENG}
        ref = {e: set() for e in self.ENG}
        for e in self.ENG:
            for waits, fn, semkey, inc in self.q[e]:
                for k, v in waits:
                    if k.startswith("e:"):
                        ref[k[2:]].add(v)
        plan = {}
        for e in self.ENG:
            key = "e:" + e
            idx = self.emitted[e]
            keep = []
            for waits, fn, semkey, inc in self.q[e]:
                if semkey == key:
                    idx += 1
                    if idx in ref[e]:
                        self.hw[e] += 1
                        self.vmap[e][idx] = self.hw[e]
                        keep.append(True)
                    else:
                        keep.append(False)
                else:
                    keep.append(semkey is not None)
            self.emitted[e] = idx
            plan[e] = keep
        engobj = {"pe": "tensor", "act": "scalar", "dve": "vector", "pool": "gpsimd", "sp": "sync"}
        with nc.Block() as block:
            for e in self.ENG:
                items = self.q[e]
                if not items:
                    continue

                def body(eng, items=items, keep=plan[e]):
                    for (waits, fn, semkey, inc), kp in zip(items, keep):
                        for k, v in waits:
                            if k.startswith("e:"):
                                v = self.vmap[k[2:]][v]
                            eng.wait_ge(self.sems[k], v)
                        if fn is not None:
                            ins = fn(eng)
                            if kp:
                                ins.then_inc(self.sems[semkey], inc)
                getattr(block, engobj[e])(body)
        self.q = {e: [] for e in self.ENG}


def build_program(stage=99):
    nc = bass.Bass("TRN2", target_bir_lowering=False)

    def din(name, shape, dt=F32):
        return nc.dram_tensor(name, list(shape), dt, kind="ExternalInput").ap()

    def dout(name, shape, dt=F32):
        return nc.dram_tensor(name, list(shape), dt, kind="ExternalOutput").ap()

    xTn = din("xTn", [D, SEQ])
    memT = din("memT", [D, 256])
    Wk = din("Wk", [D, 1024])
    Wv = din("Wv", [D, 1024])
    W3 = din("W3", [D, 1088])
    Wmk = din("Wmk", [D, 512])
    Wmv = din("Wmv", [D, 512])
    lbl = din("lbl", [2, 512])
    ropeN = din("ropeN", [SEQ, 96])
    c_tri2 = din("c_tri2", [128, 128])
    c_blk2 = din("c_blk2", [128, 128])
    c_ident = din("c_ident", [128, 128])
    posq = din("posq", [128, NO + 1])
    xTo = din("xTo", [D, 1024])
    xo = din("xo", [1024, D])
    Wi = din("Wi", [D, 1040])
    Wq = din("Wq", [D, 1024])
    Wg = din("Wg", [D, 1024])
    Wb = din("Wb", [D, 2048])
    Wm = din("Wm", [D, 1024])
    Wo = din("Wo", [D, D])
    ropeO = din("ropeO", [1024, 96])
    c_j = din("c_j", [128, 256])
    c_pow = din("c_pow", [128, 48])
    ng = din("ng", [1, 128])
    lng = din("lng", [1, D])
    lnb = din("lnb", [1, D])
    o_k = dout("o_k", [SEQ, 1024])
    o_v = dout("o_v", [SEQ, 1024])
    o_ik = dout("o_ik", [SEQ, 64])
    o_hg = dout("o_hg", [128, 4, 128])
    o_mk = dout("o_mk", [256, 512])
    o_mv = dout("o_mv", [256, 512])
    o_y = dout("o_y", [1024, D])
    xsT = din("xsT", [D, 8])
    xs = din("xs", [8, D])
    ropeS = din("ropeS", [8, 160])
    st_h = din("st_h", [4, 128, 128])
    cmk_d = din("cmk_d", [256, 512])
    cmv_d = din("cmv_d", [256, 512])
    ptab = din("ptab", [1, 128], I32)
    cidx = din("cidx", [1280, 8192])
    ck = din("ck", [163840, 1024])
    cv = din("cv", [163840, 1024])
    cin = {}
    for nm_, shp_, dt__ in (("c_sel8", [8, 128], F32), ("c_hm", [128, 16], F32), ("c_bq8", [128, 8], F32),
                            ("c_bq16", [128, 8], F32), ("c_BQ1", [128, 128], F32), ("c_BQm", [128, 128], F32),
                            ("c_LT", [128, 128], F32), ("c_negm", [128, 8], F32), ("c_aiota", [128, 1024], F32),
                            ("c_iota512", [128, 512], F32), ("c_esel", [8, 1024], F32), ("c_eq", [8, 64], F32),
                            ("c_bd8", [8, 1024], F32), ("c_pow2", [128, 60], F32), ("c_i32", [128, 768], I32)):
        cin[nm_] = din(nm_, shp_, dt__)
    o_ks = dout("o_ks", [8, 1024])
    o_vs = dout("o_vs", [8, 1024])
    o_iks = dout("o_iks", [8, 64])
    o_hgs = dout("o_hgs", [128, 4, 128])
    o_ys = dout("o_ys", [8, D])
    import os
    DBG = os.environ.get("KDBG", "0") == "1"
    if DBG:
        o_cat = dout("o_cat", [128, KC * 1024], BF16)

    with ExitStack() as es:
        S = Sched(nc, es)
        outbufs = []

        def sb(st, name, shape, dt):
            return TL(st.enter_context(nc.sbuf_tensor(name, list(shape), dt)), name)

        def ps(st, name, shape, dt):
            t = TL(st.enter_context(nc.psum_tensor(name, list(shape), dt)), name)
            t.b.excl = True
            return t

        ident_f = sb(es, "ident_f", [128, 128], F32)
        ident_b = sb(es, "ident_b", [128, 128], BF16)
        tri2 = sb(es, "tri2", [128, 128], F32)
        blk2 = sb(es, "blk2", [128, 128], F32)
        ones_f = sb(es, "ones_f", [128, 2], F32)
        lb_bc = sb(es, "lb_bc", [128, 512], F32)
        oml_bc = sb(es, "oml_bc", [128, 512], F32)
        posq_sb = sb(es, "posq_sb", [128, NO + 1], F32)
        hflag = sb(es, "hflag", [128, 1], F32)
        mkT = sb(es, "mkT", [128, 4, 256], BF16)
        mv = sb(es, "mv", [128, 2, 4, 130], BF16)
        Sown = sb(es, "Sown", [128, NO, 512], BF16)
        xsT_b = sb(es, "xsT_b", [128, KC, 8], BF16)
        zst = sb(es, "zst", [8, 512], F32)
        R16 = sb(es, "R16", [128, 8192], BF16)
        es_A = ExitStack()
        KT = sb(es_A, "KT", [128, 8, SEQ], BF16)
        V = sb(es_A, "V", [128, NT, 8, 130], BF16)
        ikT = sb(es_A, "ikT", [64, SEQ], BF16)

        PZ = [ps(es, "PZ%d" % i, [128, 512], F32) for i in range(2)]
        PT = ps(es, "PT", [128, 1024], BF16)
        PG = ps(es, "PG", [128, 512], F32)
        PGL = ps(es, "PGL", [128, 512], F32)
        PU = [ps(es, "PU%d" % i, [128, 512], F32) for i in range(2)]
        PD = ps(es, "PD", [128, 512], F32)

        zs_d = nc.dram_tensor("zs_d", [8, 8448], F32).ap()
        zsd_b = Buf("zs_d")
        S.dma("pool", lambda e: e.dma_start(out=xsT_b[:], in_=xsT.rearrange("(k p) t -> p k t", p=128)), writes=[xsT_b])
        ZOFF = dict(q=0, k=1024, v=2048, g=3072, iq=4096, iw=5120, ik=5136, b=5200, m=7248)

        def zs_chunk(wsb, c0, ncol, dcol):
            for k in range(KC):
                S.op("pe", lambda e, k=k: e.matmul(PD[0:8, 0:ncol], lhsT=xsT_b[:, k, :], rhs=wsb[:, k, c0:c0 + ncol],
                                                   start=(k == 0), stop=(k == KC - 1)),
                     reads=[xsT_b, wsb], writes=[PD], pe_acc=(k > 0))
            S.op("act", lambda e: e.activation(out=zst[:, 0:ncol], in_=PD[0:8, 0:ncol], func=AF.Copy), reads=[PD], writes=[zst])
            S.dma("sp", lambda e: e.dma_start(out=zs_d[:, dcol:dcol + ncol], in_=zst[:, 0:ncol]), reads=[zst], writes=[zsd_b],
                  owner=zst)

        S.dma("sp", lambda e: e.dma_start(out=ident_f[:], in_=c_ident[:, :]), writes=[ident_f])
        S.dma("sp", lambda e: e.dma_start(out=tri2[:], in_=c_tri2[:, :]), writes=[tri2])
        S.dma("sp", lambda e: e.dma_start(out=blk2[:], in_=c_blk2[:, :]), writes=[blk2])
        S.dma("sp", lambda e: e.dma_start(out=posq_sb[:], in_=posq[:, :]), writes=[posq_sb])
        tmp_es = ExitStack()
        lb2 = sb(tmp_es, "lb2", [128, 2, 512], F32)
        S.dma("sp", [lambda e, l=l: e.dma_start(out=lb2[:, l, :], in_=lbl[l:l + 1, :].broadcast_to([128, 512]))
                     for l in range(2)], writes=[lb2])
        S.op("pool", lambda e: e.memset(ones_f[:], 1.0), writes=[ones_f])
        S.op("dve", lambda e: e.tensor_copy(out=ident_b[:], in_=ident_f[:]), reads=[ident_f], writes=[ident_b])
        S.op("dve", lambda e: e.tensor_tensor(out=oml_bc[:], in0=lb2[:, 0, :], in1=lb2[:, 1, :], op=ALU.subtract),
             reads=[lb2], writes=[oml_bc])
        S.op("act", lambda e: e.activation(out=lb_bc[:], in_=oml_bc[:], func=AF.Sigmoid), reads=[oml_bc], writes=[lb_bc])
        S.op("dve", lambda e: e.tensor_scalar(out=oml_bc[:], in0=lb_bc[:], scalar1=-1.0, scalar2=1.0,
                                              op0=ALU.mult, op1=ALU.add), reads=[lb_bc], writes=[oml_bc])
        S.op("dve", lambda e: e.tensor_copy(out=hflag[:], in_=posq_sb[:, NO:NO + 1]), reads=[posq_sb], writes=[hflag])
        S.op("pool", lambda e: e.memset(V[:, :, :, 128:129], 1.0), writes=[V])
        S.op("pool", lambda e: e.memset(mv[:, :, :, 128:129], 1.0), writes=[mv])
        S.barrier()
        S.emit()
        tmp_es.close()

        def rope(src3, dst3, nh, half, CC, SS, tA, tB, np_=128):
            r = 2 * half
            ccb = CC.unsqueeze(1).to_broadcast([np_, nh, r])
            s1b = SS[:, 0:half].unsqueeze(1).to_broadcast([np_, nh, half])
            s2b = SS[:, half:r].unsqueeze(1).to_broadcast([np_, nh, half])
            S.op("dve", lambda e: e.tensor_tensor(out=tA[0:np_, 0:nh, 0:r], in0=src3(0, r), in1=ccb, op=ALU.mult),
                 reads=src3.deps, writes=[tA])
            S.op("dve", lambda e: e.tensor_tensor(out=tB[0:np_, 0:nh, 0:half], in0=src3(half, r), in1=s1b, op=ALU.mult),
                 reads=src3.deps, writes=[tB])
            S.op("dve", lambda e: e.tensor_tensor(out=tB[0:np_, 0:nh, half:r], in0=src3(0, half), in1=s2b, op=ALU.mult),
                 reads=src3.deps + [tB], writes=[tB])
            S.op("dve", lambda e: e.tensor_tensor(out=dst3(0, r), in0=tA[0:np_, 0:nh, 0:r], in1=tB[0:np_, 0:nh, 0:r],
                                                  op=ALU.add), reads=[tA, tB], writes=dst3.deps)

        class V3:
            def __init__(self, fn, deps):
                self.fn = fn
                self.deps = deps

            def __call__(self, lo, hi):
                return self.fn(lo, hi)

        with ExitStack() as pm:
            memTb = sb(pm, "memTb", [128, KC, 256], BF16)
            wmk = sb(pm, "wmk", [128, KC, 512], BF16)
            wmv = sb(pm, "wmv", [128, KC, 512], BF16)
            mf = [sb(pm, "mf%d" % i, [128, 512], F32) for i in range(2)]
            mb = sb(pm, "mb", [128, 512], BF16)
            S.dma("pool", [lambda e, k=k: e.dma_start(out=memTb[:, k, :], in_=memT[k * 128:(k + 1) * 128, :])
                           for k in range(KC)], writes=[memTb])
            S.dma("pool", [lambda e, k=k: e.dma_start(out=wmk[:, k, :], in_=Wmk[k * 128:(k + 1) * 128, :])
                           for k in range(KC)], writes=[wmk])
            S.dma("pool", [lambda e, k=k: e.dma_start(out=wmv[:, k, :], in_=Wmv[k * 128:(k + 1) * 128, :])
                           for k in range(KC)], writes=[wmv])
            cnt = 0
            for nt in range(2):
                for which in range(2):
                    wsb = wmk if which == 0 else wmv
                    pz = PZ[cnt % 2]
                    f = mf[cnt % 2]
                    cnt += 1
                    for k in range(KC):
                        S.op("pe", lambda e, k=k, pz=pz, wsb=wsb, nt=nt: e.matmul(
                            pz[:], lhsT=memTb[:, k, nt * 128:(nt + 1) * 128], rhs=wsb[:, k, :],
                            start=(k == 0), stop=(k == KC - 1)), reads=[memTb, wsb], writes=[pz], pe_acc=(k > 0))
                    S.op("act", lambda e, pz=pz, f=f: e.activation(out=f[:], in_=pz[:], func=AF.Copy),
                         reads=[pz], writes=[f])
                    dst = o_mk if which == 0 else o_mv
                    S.dma("sp", lambda e, f=f, dst=dst, nt=nt: e.dma_start(out=dst[nt * 128:(nt + 1) * 128, :], in_=f[:]),
                          reads=[f])
                    outbufs.append(f)
                    if which == 0:
                        S.op("dve", lambda e, pz=pz: e.tensor_copy(out=mb[:], in_=pz[:]), reads=[pz], writes=[mb])
                        for hd in range(4):
                            S.op("pe", lambda e, hd=hd: e.transpose(out=PT[:, hd * 128:(hd + 1) * 128],
                                                                    in_=mb[:, hd * 128:(hd + 1) * 128],
                                                                    identity=ident_b[:]),
                                 reads=[mb, ident_b], writes=[PT])
                        S.op("act", lambda e, nt=nt: e.activation(
                            out=mkT[:, :, nt * 128:(nt + 1) * 128],
                            in_=PT[:, 0:512].rearrange("p (h t) -> p h t", h=4), func=AF.Copy),
                            reads=[PT], writes=[mkT])
                    else:
                        S.op("dve", lambda e, pz=pz, nt=nt: e.tensor_copy(
                            out=mv[:, nt, :, 0:128], in_=pz[:].rearrange("p (h d) -> p h d", h=4)),
                            reads=[pz], writes=[mv])
            S.barrier()
            S.emit()
        if stage == 1:
            return nc

        with ExitStack() as pn:
            xh = sb(pn, "xh", [128, KC, 1024], BF16)
            W0 = sb(pn, "W0", [128, KC, 1088], BF16)
            W1 = TLV(R16, lambda t: t[:].rearrange("p (k c) -> p k c", k=KC))
            ropeN_sb = sb(pn, "ropeN_sb", [128, NT, 96], F32)
            Kf = [sb(pn, "Kf%d" % i, [128, 512], F32) for i in range(2)]
            Kb = [sb(pn, "Kb%d" % i, [128, 512], BF16) for i in range(2)]
            tA = sb(pn, "tA", [128, 4, 32], F32)
            tB = sb(pn, "tB", [128, 4, 32], F32)
            sg = sb(pn, "sg", [128, 512], F32)
            ff = sg
            gg = sb(pn, "gg", [128, 512], F32)
            omf = sb(pn, "omf", [128, 512], F32)
            Gs = sb(pn, "Gs", [128, 512], F32)
            dlt = sb(pn, "dlt", [128, 512], F32)
            EE = dlt
            kdec = sb(pn, "kdec", [128, 512], BF16)
            vb = sb(pn, "vb", [128, 512], BF16)
            ikf = sb(pn, "ikf", [128, 64], F32)
            ikb = sb(pn, "ikb", [128, 64], BF16)
            Dc = sb(pn, "Dc", [128, 16], F32)
            St = sb(pn, "St", [128, 4, 128], F32)
            Sev = sb(pn, "Sev", [128, 512], F32)
            Stmp = sb(pn, "Stmp", [128, 512], F32)

            S.dma("sp", lambda e: e.dma_start(out=ropeN_sb[:], in_=ropeN.rearrange("(n p) c -> p n c", p=128)),
                  writes=[ropeN_sb])
            S.op("pool", lambda e: e.memset(St[:], 0.0), writes=[St])

            def load_w(dst, src, c0, ncol):
                S.dma("pool", [lambda e, k=k: e.dma_start(out=dst[:, k, 0:ncol], in_=src[k * 128:(k + 1) * 128, c0:c0 + ncol])
                               for k in range(KC)], writes=[dst])

            def mm_chunk(pz, tl, wsb, c0, ncol):
                for k in range(KC):
                    S.op("pe", lambda e, k=k: e.matmul(pz[:, 0:ncol], lhsT=xh[:, k, tl * 128:(tl + 1) * 128],
                                                       rhs=wsb[:, k, c0:c0 + ncol], start=(k == 0), stop=(k == KC - 1)),
                         reads=[xh, wsb], writes=[pz], pe_acc=(k > 0))

            ctr = [0]
            for half in range(2 if stage != 2 else 1):
                S.dma("pool", [lambda e, k=k, half=half: e.dma_start(out=xh[:, k, :],
                                                          in_=xTn[k * 128:(k + 1) * 128, half * 1024:(half + 1) * 1024])
                               for k in range(KC)], writes=[xh])
                for blk in range(2):
                    wsb = W0 if blk == 0 else W1
                    load_w(wsb, Wk, blk * 512, 512)
                    if half == 0:
                        zs_chunk(wsb, 0, 512, ZOFF["k"] + blk * 512)
                    for tl in range(8):
                        n = half * 8 + tl
                        i = ctr[0] % 2
                        ctr[0] += 1
                        pz, kf, kb = PZ[i], Kf[i], Kb[i]
                        mm_chunk(pz, tl, wsb, 0, 512)
                        S.op("act", lambda e, pz=pz, kf=kf: e.activation(out=kf[:], in_=pz[:], func=AF.Copy),
                             reads=[pz], writes=[kf])
                        src3 = V3(lambda lo, hi, pz=pz: pz[:].rearrange("p (h d) -> p h d", h=4)[:, :, lo:hi], [pz])
                        dst3 = V3(lambda lo, hi, kf=kf: kf[:].rearrange("p (h d) -> p h d", h=4)[:, :, lo:hi], [kf])
                        rope(src3, dst3, 4, 16, ropeN_sb[:, n, 0:32], ropeN_sb[:, n, 32:64], tA, tB)
                        S.dma("sp", lambda e, kf=kf, n=n, blk=blk: e.dma_start(
                            out=o_k[n * 128:(n + 1) * 128, blk * 512:(blk + 1) * 512], in_=kf[:]), reads=[kf])
                        S.op("pool", lambda e, kf=kf, kb=kb: e.tensor_copy(out=kb[:], in_=kf[:]), reads=[kf], writes=[kb])
                        for hd in range(4):
                            S.op("pe", lambda e, hd=hd, kb=kb: e.transpose(out=PT[:, hd * 128:(hd + 1) * 128],
                                                                           in_=kb[:, hd * 128:(hd + 1) * 128],
                                                                           identity=ident_b[:]),
                                 reads=[kb, ident_b], writes=[PT])
                        S.op("act", lambda e, n=n, blk=blk: e.activation(
                            out=KT[:, blk * 4:(blk + 1) * 4, n * 128:(n + 1) * 128],
                            in_=PT[:, 0:512].rearrange("p (h t) -> p h t", h=4), func=AF.Copy),
                            reads=[PT], writes=[KT])
                for blk in range(2):
                    wsb = W0 if blk == 0 else W1
                    load_w(wsb, Wv, blk * 512, 512)
                    if half == 0:
                        zs_chunk(wsb, 0, 512, ZOFF["v"] + blk * 512)
                    for tl in range(8):
                        n = half * 8 + tl
                        i = ctr[0] % 2
                        ctr[0] += 1
                        pz, kf = PZ[i], Kf[i]
                        mm_chunk(pz, tl, wsb, 0, 512)
                        S.op("act", lambda e, pz=pz, kf=kf: e.activation(out=kf[:], in_=pz[:], func=AF.Copy),
                             reads=[pz], writes=[kf])
                        S.dma("sp", lambda e, kf=kf, n=n, blk=blk: e.dma_start(
                            out=o_v[n * 128:(n + 1) * 128, blk * 512:(blk + 1) * 512], in_=kf[:]), reads=[kf])
                        S.op("dve", lambda e, pz=pz, n=n, blk=blk: e.tensor_copy(
                            out=V[:, n, blk * 4:(blk + 1) * 4, 0:128], in_=pz[:].rearrange("p (h d) -> p h d", h=4)),
                            reads=[pz], writes=[V])
                load_w(W0, W3, 0, 1088)
                if half == 0:
                    zs_chunk(W0, 1024, 64, ZOFF["ik"])
                for tl in range(8):
                    n = half * 8 + tl
                    pza, pzb = PZ[0], PZ[1]
                    mm_chunk(pza, tl, W0, 0, 512)
                    S.op("act", lambda e: e.activation(out=sg[:], in_=pza[:], func=AF.Sigmoid), reads=[pza], writes=[sg])
                    mm_chunk(pzb, tl, W0, 512, 512)
                    S.op("act", lambda e: e.activation(out=vb[:], in_=pzb[:], func=AF.Copy), reads=[pzb], writes=[vb])
                    mm_chunk(pza, tl, W0, 1024, 64)
                    S.op("act", lambda e: e.activation(out=ikf[:], in_=pza[:, 0:64], func=AF.Copy), reads=[pza], writes=[ikf])
                    src3 = V3(lambda lo, hi: pza[:, 0:64].rearrange("p (h d) -> p h d", h=1)[:, :, lo:hi], [pza])
                    dst3 = V3(lambda lo, hi: ikf[:].rearrange("p (h d) -> p h d", h=1)[:, :, lo:hi], [ikf])
                    rope(src3, dst3, 1, 8, ropeN_sb[:, n, 64:80], ropeN_sb[:, n, 80:96], tA, tB)
                    S.dma("sp", lambda e, n=n: e.dma_start(out=o_ik[n * 128:(n + 1) * 128, :], in_=ikf[:]), reads=[ikf])
                    S.op("pool", lambda e: e.tensor_copy(out=ikb[:], in_=ikf[:]), reads=[ikf], writes=[ikb])
                    S.op("pe", lambda e: e.transpose(out=PT[0:64, 512:640], in_=ikb[:], identity=ident_b[:]),
                         reads=[ikb, ident_b], writes=[PT])
                    S.op("act", lambda e, n=n: e.activation(out=ikT[:, n * 128:(n + 1) * 128], in_=PT[0:64, 512:640],
                                                            func=AF.Copy), reads=[PT], writes=[ikT])
                    S.op("dve", lambda e: e.tensor_tensor(out=ff[:], in0=sg[:], in1=oml_bc[:], op=ALU.mult),
                         reads=[sg, oml_bc], writes=[sg])
                    S.op("dve", lambda e: e.tensor_tensor(out=ff[:], in0=ff[:], in1=lb_bc[:], op=ALU.add),
                         reads=[ff, lb_bc], writes=[ff])
                    S.op("act", lambda e: e.activation(out=gg[:], in_=ff[:], func=AF.Ln), reads=[ff], writes=[gg])
                    S.op("pool", lambda e: e.tensor_scalar(out=omf[:], in0=ff[:], scalar1=-1.0, scalar2=1.0,
                                                           op0=ALU.mult, op1=ALU.add), reads=[ff], writes=[omf])
                    S.op("pe", lambda e: e.matmul(PG[:], lhsT=tri2[:], rhs=gg[:], start=True, stop=True),
                         reads=[tri2, gg], writes=[PG])
                    S.op("pe", lambda e: e.matmul(PGL[:], lhsT=blk2[:], rhs=gg[:], start=True, stop=True),
                         reads=[blk2, gg], writes=[PGL])
                    S.op("act", lambda e: e.activation(out=Gs[:], in_=PG[:], func=AF.Copy), reads=[PG], writes=[Gs])
                    S.op("dve", lambda e: e.tensor_tensor(out=dlt[:], in0=PGL[:], in1=Gs[:], op=ALU.subtract),
                         reads=[PGL, Gs], writes=[dlt])
                    S.op("act", lambda e: e.activation(out=EE[:], in_=dlt[:], func=AF.Exp), reads=[dlt], writes=[EE])
                    S.op("dve", lambda e: e.tensor_tensor(out=kdec[:], in0=omf[:], in1=EE[:], op=ALU.mult),
                         reads=[omf, EE], writes=[kdec])
                    for c in range(2):
                        for hd in range(4):
                            j = c * 4 + hd
                            S.op("pe", lambda e, c=c, hd=hd, j=j: e.matmul(
                                PD[:, 2 * j:2 * j + 2], lhsT=gg[c * 64:(c + 1) * 64, hd * 128:(hd + 1) * 128],
                                rhs=ones_f[c * 64:(c + 1) * 64, :], start=True, stop=True),
                                reads=[gg, ones_f], writes=[PD])
                    S.op("act", lambda e: e.activation(out=Dc[:], in_=PD[:, 0:16], func=AF.Exp), reads=[PD], writes=[Dc])
                    if n % 2 == 0:
                        S.op("pool", lambda e: e.tensor_copy(out=Sev[:], in_=St[:].rearrange("p h d -> p (h d)")),
                             reads=[St], writes=[Sev])
                    else:
                        S.op("dve", lambda e: e.tensor_tensor(out=Stmp[:], in0=St[:].rearrange("p h d -> p (h d)"),
                                                              in1=Sev[:], op=ALU.subtract), reads=[St, Sev], writes=[Stmp])
                        S.op("dve", lambda e, n=n: e.scalar_tensor_tensor(
                            out=Sown[:, n // 2, :], in0=Stmp[:], scalar=hflag[:, 0:1], in1=Sev[:],
                            op0=ALU.mult, op1=ALU.add), reads=[Stmp, hflag, Sev], writes=[Sown])
                    for c in range(2):
                        pu = PU[c]
                        for hd in range(4):
                            S.op("pe", lambda e, c=c, hd=hd, pu=pu: e.matmul(
                                pu[:, hd * 128:(hd + 1) * 128], lhsT=kdec[c * 64:(c + 1) * 64, hd * 128:(hd + 1) * 128],
                                rhs=vb[c * 64:(c + 1) * 64, hd * 128:(hd + 1) * 128], start=True, stop=True),
                                reads=[kdec, vb], writes=[pu])
                        for hd in range(4):
                            j = c * 4 + hd
                            S.op("dve", lambda e, hd=hd, j=j, pu=pu: e.scalar_tensor_tensor(
                                out=St[:, hd, :], in0=St[:, hd, :], scalar=Dc[:, 2 * j:2 * j + 1],
                                in1=pu[:, hd * 128:(hd + 1) * 128], op0=ALU.mult, op1=ALU.add),
                                reads=[St, Dc, pu], writes=[St])
            S.dma("sp", lambda e: e.dma_start(out=o_hg[:, :, :], in_=St[:]), reads=[St])
            S.barrier()
            S.emit()
        if stage <= 3:
            return nc
        QS = 128.0 ** -0.5
        KB = 24
        PA = [PG, PGL]
        PI = PU
        moff = [sum(2 * j + 2 for j in range(i)) for i in range(NO)]

        def load_wres(dst, src, ncol):
            S.dma("pool", [lambda e, k=k: e.dma_start(out=dst[:, k, 0:ncol], in_=src[k * 128:(k + 1) * 128, 0:ncol])
                           for k in range(KC)], writes=[dst])

        def load_xt(dst, i):
            S.dma("pool", lambda e: e.dma_start(out=dst[:], in_=xTo[:, i * 128:(i + 1) * 128].rearrange("(k p) t -> p k t", p=128)),
                  writes=[dst])

        def mm_x(pz, xt_, wsb, c0, ncol):
            for k in range(KC):
                S.op("pe", lambda e, k=k: e.matmul(pz[:, 0:ncol], lhsT=xt_[:, k, :], rhs=wsb[:, k, c0:c0 + ncol],
                                                   start=(k == 0), stop=(k == KC - 1)),
                     reads=[xt_, wsb], writes=[pz], pe_acc=(k > 0))

        def transpose4(src_fn, srcdeps, col0=0, n=4):
            for hd in range(n):
                S.op("pe", lambda e, hd=hd: e.transpose(out=PT[:, col0 + hd * 128:col0 + (hd + 1) * 128], in_=src_fn(hd),
                                                        identity=ident_b[:]), reads=srcdeps + [ident_b], writes=[PT])

        es_B = ExitStack()
        maskT = sb(es_B, "maskT", [128, 72, 128], BF16)
        ropeO_sb = sb(es_B, "ropeO_sb", [128, NO, 96], F32)
        S.dma("sp", lambda e: e.dma_start(out=ropeO_sb[:], in_=ropeO.rearrange("(n p) c -> p n c", p=128)), writes=[ropeO_sb])

        with ExitStack() as p1:
            Wi_sb = sb(p1, "Wi_sb", [128, KC, 1040], BF16)
            xt = [sb(p1, "xt1_%d" % j, [128, KC, 128], BF16) for j in range(2)]
            cb = sb(p1, "cb", [128, 256], F32)
            cj = sb(p1, "cj", [128, 256], F32)
            pw = sb(p1, "pw", [128, 2 * KB], F32)
            spw = sb(p1, "spw", [128, 2 * KB], F32)
            iq_b = sb(p1, "iq_b", [128, 16, 64], BF16)
            w_s = sb(p1, "w_s", [128, 16], F32)
            iqT = sb(p1, "iqT", [64, 16, 128], BF16)
            Dg = sb(p1, "Dg", [128, 16, 128], BF16)
            Tb = [sb(p1, "Tb%d" % j, [128, 512], BF16) for j in range(2)]
            I_sbs = [sb(p1, "I_sb%d" % j, [128, 2048], F32) for j in range(2)]
            cbias = sb(p1, "cbias", [128, 256], BF16)
            junk = sb(p1, "junk", [128, 2048], BF16)
            tA1 = sb(p1, "tA1", [128, 8, 16], F32)
            tB1 = sb(p1, "tB1", [128, 8, 16], F32)
            sm = sb(p1, "sm", [128, 8], F32)
            lo, hi, rng, mid, cnt, wv, thr0 = [sm[:, j:j + 1] for j in range(7)]
            load_wres(Wi_sb, Wi, 1040)
            zs_chunk(Wi_sb, 0, 512, ZOFF["iq"])
            zs_chunk(Wi_sb, 512, 512, ZOFF["iq"] + 512)
            zs_chunk(Wi_sb, 1024, 16, ZOFF["iw"])
            S.dma("sp", lambda e: e.dma_start(out=cj[:], in_=c_j[:, :]), writes=[cj])
            S.dma("sp", lambda e: e.dma_start(out=pw[:], in_=c_pow[:, :]), writes=[pw])
            S.op("dve", lambda e: e.tensor_scalar(out=cb[:], in0=cj[:], scalar1=posq_sb[:, 0:1], scalar2=NEG,
                                                  op0=ALU.is_gt, op1=ALU.mult), reads=[cj, posq_sb], writes=[cb])
            S.op("pool", lambda e: e.memset(sm[:, 6:7], -1.0e29), writes=[sm])
            S.op("dve", lambda e: e.tensor_copy(out=cbias[:], in_=cb[:]), reads=[cb], writes=[cbias])
            def stage_a(i):
                nk = (2 * i + 2) * 128
                I_sb = I_sbs[i % 2]
                x = xt[i % 2]
                load_xt(x, i)
                for c in range(2):
                    pz = PZ[c]
                    mm_x(pz, x, Wi_sb, c * 512, 512)
                    S.op("act", lambda e, pz=pz, c=c: e.activation(
                        out=iq_b[:, c * 8:(c + 1) * 8, :], in_=pz[:].rearrange("p (h d) -> p h d", h=8), func=AF.Copy),
                        reads=[pz], writes=[iq_b])
                    src3 = V3(lambda lo_, hi_, pz=pz: pz[:].rearrange("p (h d) -> p h d", h=8)[:, :, lo_:hi_], [pz])
                    dst3 = V3(lambda lo_, hi_, c=c: iq_b[:, c * 8:(c + 1) * 8, lo_:hi_], [iq_b])
                    rope(src3, dst3, 8, 8, ropeO_sb[:, i, 64:80], ropeO_sb[:, i, 80:96], tA1, tB1)
                pz = PZ[0]
                mm_x(pz, x, Wi_sb, 1024, 16)
                S.op("act", lambda e, pz=pz: e.activation(out=w_s[:], in_=pz[:, 0:16], func=AF.Copy, scale=1.0 / 32.0),
                     reads=[pz], writes=[w_s])
                for r in range(2):
                    for hh in range(8):
                        S.op("pe", lambda e, r=r, hh=hh: e.transpose(out=PT[0:64, hh * 128:(hh + 1) * 128],
                                                                     in_=iq_b[:, r * 8 + hh, :], identity=ident_b[:]),
                             reads=[iq_b, ident_b], writes=[PT])
                    S.op("act", lambda e, r=r: e.activation(out=iqT[:, r * 8:(r + 1) * 8, :],
                                                            in_=PT[0:64, :].rearrange("p (h t) -> p h t", h=8), func=AF.Copy),
                         reads=[PT], writes=[iqT])
                for h in range(16):
                    S.op("pool", lambda e, h=h: e.tensor_scalar(out=Dg[:, h, :], in0=ident_b[:], scalar1=w_s[:, h:h + 1],
                                                                scalar2=None, op0=ALU.mult),
                         reads=[ident_b, w_s], writes=[Dg])
                nch = (nk + 511) // 512
                lastlo = nk - 256
                for c in range(nch):
                    kw = min(512, nk - 512 * c)
                    pi = PI[c % 2]

                    def mm1(h, c=c, kw=kw):
                        pa = PA[h % 2]
                        S.op("pe", lambda e, h=h, pa=pa: e.matmul(pa[:, 0:kw], lhsT=iqT[:, h, :],
                                                                  rhs=ikT[:, c * 512:c * 512 + kw], start=True, stop=True),
                             reads=[iqT, ikT], writes=[pa])
                    mm1(0)
                    for h in range(16):
                        if h + 1 < 16:
                            mm1(h + 1)
                        pa = PA[h % 2]
                        tb = Tb[h % 2]
                        S.op("act", lambda e, pa=pa, tb=tb, kw=kw: e.activation(out=tb[:, 0:kw], in_=pa[:, 0:kw], func=AF.Relu),
                             reads=[pa], writes=[tb])
                        hasb = (512 * c + kw > lastlo)
                        S.op("pe", lambda e, h=h, tb=tb, pi=pi, kw=kw, hasb=hasb: e.matmul(
                            pi[:, 0:kw], lhsT=Dg[:, h, :], rhs=tb[:, 0:kw], start=(h == 0), stop=(h == 15 and not hasb)),
                            reads=[Dg, tb], writes=[pi], pe_acc=(h > 0))
                        if h == 15 and hasb:
                            b0 = max(lastlo - 512 * c, 0)
                            S.op("pe", lambda e, pi=pi, kw=kw, b0=b0: e.matmul(pi[:, b0:kw], lhsT=ident_b[:], rhs=cbias[:, 256 - (kw - b0):256],
                                                                               start=False, stop=True),
                                 reads=[ident_b, cbias], writes=[pi], pe_acc=True)
                    a0 = 512 * c
                    a1 = a0 + kw
                    S.op("act", lambda e, pi=pi, a0=a0, a1=a1, kw=kw: e.activation(out=I_sb[:, a0:a1], in_=pi[:, 0:kw], func=AF.Copy),
                         reads=[pi], writes=[I_sb])
            def stage_b(i):
                nk = (2 * i + 2) * 128
                I_sb = I_sbs[i % 2]
                if i >= 1:
                    S.op("dve", lambda e, nk=nk: e.tensor_reduce(out=lo, in_=I_sb[:, 0:nk - 256], axis=AX.X, op=ALU.min),
                         reads=[I_sb], writes=[sm])
                    S.op("dve", lambda e, nk=nk: e.tensor_reduce(out=hi, in_=I_sb[:, 0:nk], axis=AX.X, op=ALU.max),
                         reads=[I_sb], writes=[sm])
                    S.op("dve", lambda e: e.tensor_tensor(out=rng, in0=hi, in1=lo, op=ALU.subtract), reads=[sm], writes=[sm])
                    S.op("dve", lambda e: e.tensor_scalar(out=spw[:], in0=pw[:], scalar1=rng, scalar2=None, op0=ALU.mult),
                         reads=[pw, sm], writes=[spw])
                    S.op("dve", lambda e: e.tensor_tensor(out=mid, in0=lo, in1=spw[:, 0:1], op=ALU.add),
                         reads=[sm, spw], writes=[sm])
                    for k in range(KB):
                        S.op("dve", lambda e, nk=nk: e.tensor_scalar(out=junk[:, 0:nk], in0=I_sb[:, 0:nk], scalar1=mid,
                                                                     scalar2=0.0, op0=ALU.is_ge, op1=ALU.add, accum_out=cnt),
                             reads=[I_sb, sm], writes=[junk, sm])
                        S.op("dve", lambda e, k=k: e.scalar_tensor_tensor(out=wv, in0=cnt, scalar=256.0, in1=spw[:, k:k + 1],
                                                                          op0=ALU.is_ge, op1=ALU.mult),
                             reads=[sm, spw], writes=[sm])
                        S.op("dve", lambda e, k=k: e.scalar_tensor_tensor(out=mid, in0=mid, scalar=spw[:, KB + k:KB + k + 1],
                                                                          in1=wv, op0=ALU.subtract, op1=ALU.add),
                             reads=[sm, spw], writes=[sm])
                    thr = mid
                else:
                    thr = thr0
                S.op("dve", lambda e, nk=nk, thr=thr: e.tensor_scalar(out=junk[:, 0:nk], in0=I_sb[:, 0:nk], scalar1=thr,
                                                                      scalar2=None, op0=ALU.is_ge),
                     reads=[I_sb, sm], writes=[junk])
                return thr
            def stage_c(i):
                nk = (2 * i + 2) * 128
                nb = 2 * i + 2
                for g0 in range(0, nb, 8):
                    gn = min(8, nb - g0)
                    for j in range(gn):
                        S.op("pe", lambda e, j=j, g0=g0: e.transpose(out=PT[:, j * 128:(j + 1) * 128],
                                                                     in_=junk[:, (g0 + j) * 128:(g0 + j + 1) * 128],
                                                                     identity=ident_b[:]),
                             reads=[junk, ident_b], writes=[PT])
                    S.op("act", lambda e, i=i, g0=g0, gn=gn: e.activation(
                        out=maskT[:, moff[i] + g0:moff[i] + g0 + gn, :],
                        in_=PT[:, 0:gn * 128].rearrange("p (h t) -> p h t", h=gn), func=AF.Copy),
                        reads=[PT], writes=[maskT])
            stage_a(0)
            for i in range(NO):
                if i + 1 < NO:
                    stage_a(i + 1)
                stage_b(i)
                stage_c(i)
            S.barrier()
            S.emit()

        es_B2 = ExitStack()
        qT = sb(es_B2, "qT", [128, 8, 1024], BF16)
        with ExitStack() as p2:
            Wq_sb = sb(p2, "Wq_sb", [128, KC, 1024], BF16)
            xt = [sb(p2, "xt2_%d" % j, [128, KC, 128], BF16) for j in range(2)]
            q_b = sb(p2, "q_b", [128, 1024], BF16)
            tA2 = sb(p2, "tA2", [128, 4, 32], F32)
            tB2 = sb(p2, "tB2", [128, 4, 32], F32)
            load_wres(Wq_sb, Wq, 1024)
            zs_chunk(Wq_sb, 0, 512, ZOFF["q"])
            zs_chunk(Wq_sb, 512, 512, ZOFF["q"] + 512)
            for i in range(NO):
                x = xt[i % 2]
                load_xt(x, i)
                for c in range(2):
                    pz = PZ[c]
                    mm_x(pz, x, Wq_sb, c * 512, 512)
                    S.op("act", lambda e, pz=pz, c=c: e.activation(out=q_b[:, c * 512:(c + 1) * 512], in_=pz[:], func=AF.Copy,
                                                                   scale=QS), reads=[pz], writes=[q_b])
                    src3 = V3(lambda lo_, hi_, pz=pz: pz[:].rearrange("p (h d) -> p h d", h=4)[:, :, lo_:hi_], [pz])
                    dst3 = V3(lambda lo_, hi_, c=c: q_b[:, c * 512:(c + 1) * 512].rearrange("p (h d) -> p h d", h=4)[:, :, lo_:hi_],
                              [q_b])
                    rope(src3, dst3, 4, 16, ropeO_sb[:, i, 0:32], ropeO_sb[:, i, 32:64], tA2, tB2)
                    transpose4(lambda hd, c=c: q_b[:, c * 512 + hd * 128:c * 512 + (hd + 1) * 128], [q_b])
                    S.op("act", lambda e, c=c, i=i: e.activation(out=qT[:, c * 4:(c + 1) * 4, i * 128:(i + 1) * 128],
                                                                 in_=PT[:, 0:512].rearrange("p (h t) -> p h t", h=4), func=AF.Copy),
                         reads=[PT], writes=[qT])
            S.barrier()
            S.emit()

        a_out = TLV(R16, lambda t: t[:].rearrange("p (i c) -> p i c", i=NO))
        with ExitStack() as p3:
            NBUF = 4
            Eb = [sb(p3, "Eb%d" % j, [128, 4, 128], BF16) for j in range(NBUF)]
            Pb = [sb(p3, "Pb%d" % j, [128, 4, 128], BF16) for j in range(NBUF)]
            rden = sb(p3, "rden", [128, 8], F32)
            PSb = [PG, PGL, PZ[0], PZ[1]]
            PO = [PU[0], PU[1], PD]
            for i in range(NO):
                nb = 2 * i + 2
                groups = [(j, g) for j in range(nb) for g in range(2)]
                first = [True, True, True]

                def st_mm(n, i=i):
                    j, g = groups[n]
                    ps_ = PSb[n % NBUF]
                    for hh in range(4):
                        hd = g * 4 + hh
                        S.op("pe", lambda e, hh=hh, hd=hd, j=j, ps_=ps_: e.matmul(
                            ps_[:, hh * 128:(hh + 1) * 128], lhsT=KT[:, hd, j * 128:(j + 1) * 128],
                            rhs=qT[:, hd, i * 128:(i + 1) * 128], start=True, stop=True),
                            reads=[KT, qT], writes=[ps_])
                for n0 in range(min(NBUF - 1, len(groups))):
                    st_mm(n0)
                for n in range(len(groups)):
                    j, g = groups[n]
                    if n + NBUF - 1 < len(groups):
                        st_mm(n + NBUF - 1)
                    ps_, eb, pb = PSb[n % NBUF], Eb[n % NBUF], Pb[n % NBUF]
                    S.op("act", lambda e, ps_=ps_, eb=eb: e.activation(out=eb[:].rearrange("p h t -> p (h t)"), in_=ps_[:],
                                                                       func=AF.Exp), reads=[ps_], writes=[eb])
                    S.op("dve", lambda e, eb=eb, pb=pb, i=i, j=j: e.tensor_tensor(
                        out=pb[:], in0=eb[:], in1=maskT[:, moff[i] + j, :].unsqueeze(1).to_broadcast([128, 4, 128]),
                        op=ALU.mult), reads=[eb, maskT], writes=[pb])
                    for hh in range(4):
                        hd = g * 4 + hh
                        bank = hd // 3
                        col = (hd % 3) * 130
                        st = first[bank]
                        first[bank] = False
                        S.op("pe", lambda e, hh=hh, hd=hd, j=j, pb=pb, bank=bank, col=col, st=st, nb=nb: e.matmul(
                            PO[bank][:, col:col + 130], lhsT=pb[:, hh, :], rhs=V[:, j, hd, :],
                            start=st, stop=(j == nb - 1), skip_group_check=True),
                            reads=[pb, V], writes=[PO[bank]], pe_acc=(not st))
                for bank in range(3):
                    nh = 3 if bank < 2 else 2
                    S.op("dve", lambda e, bank=bank, nh=nh: e.reciprocal(
                        out=rden[:, bank * 3:bank * 3 + nh],
                        in_=PO[bank][:, 0:nh * 130].rearrange("p (h c) -> p h c", c=130)[:, :, 128]),
                        reads=[PO[bank]], writes=[rden])
                    S.op("dve", lambda e, bank=bank, nh=nh, i=i: e.tensor_tensor(
                        out=a_out[:, i, bank * 384:bank * 384 + nh * 128].rearrange("p (h d) -> p h d", h=nh),
                        in0=PO[bank][:, 0:nh * 130].rearrange("p (h c) -> p h c", c=130)[:, :, 0:128],
                        in1=rden[:, bank * 3:bank * 3 + nh].unsqueeze(2).to_broadcast([128, nh, 128]), op=ALU.mult),
                        reads=[PO[bank], rden], writes=[a_out])
            S.barrier()
            S.emit()
        es_B2.close()
        es_B.close()
        es_A.close()
        if stage <= 4:
            return nc

        es_C = ExitStack()
        catT = sb(es_C, "catT", [128, KC, 1024], BF16)

        with ExitStack() as p4:
            Wg_sb = sb(p4, "Wg_sb", [128, KC, 1024], BF16)
            xt = [sb(p4, "xt4_%d" % j, [128, KC, 128], BF16) for j in range(2)]
            ga = [sb(p4, "ga%d" % j, [128, 512], F32) for j in range(2)]
            cab = [sb(p4, "cab%d" % j, [128, 512], BF16) for j in range(2)]
            load_wres(Wg_sb, Wg, 1024)
            zs_chunk(Wg_sb, 0, 512, ZOFF["g"])
            zs_chunk(Wg_sb, 512, 512, ZOFF["g"] + 512)
            for i in range(NO):
                x = xt[i % 2]
                load_xt(x, i)
                for c in range(2):
                    pz, g_, cb_ = PZ[c], ga[c], cab[c]
                    mm_x(pz, x, Wg_sb, c * 512, 512)
                    S.op("act", lambda e, pz=pz, g_=g_: e.activation(out=g_[:], in_=pz[:], func=AF.Silu), reads=[pz], writes=[g_])
                    S.op("dve", lambda e, g_=g_, cb_=cb_, i=i, c=c: e.tensor_tensor(
                        out=cb_[:], in0=a_out[:, i, c * 512:(c + 1) * 512], in1=g_[:], op=ALU.mult),
                        reads=[a_out, g_], writes=[cb_])
                    transpose4(lambda hd, cb_=cb_: cb_[:, hd * 128:(hd + 1) * 128], [cb_])
                    S.op("act", lambda e, c=c, i=i: e.activation(out=catT[:, c * 4:(c + 1) * 4, i * 128:(i + 1) * 128],
                                                                 in_=PT[:, 0:512].rearrange("p (h t) -> p h t", h=4), func=AF.Copy),
                         reads=[PT], writes=[catT])
            S.barrier()
            S.emit()

        with ExitStack() as p5:
            Wb_sb = sb(p5, "Wb_sb", [128, KC, 2048], BF16)
            xt = [sb(p5, "xt5_%d" % j, [128, KC, 128], BF16) for j in range(2)]
            ng_bc = sb(p5, "ng_bc", [128, 4, 128], F32)
            qs = sb(p5, "qs", [128, 512], F32)
            sg = sb(p5, "sg5", [128, 512], F32)
            gg = sb(p5, "gg5", [128, 512], F32)
            omf = sb(p5, "omf5", [128, 512], F32)
            Gs = sb(p5, "Gs5", [128, 512], F32)
            dlt = sb(p5, "dlt5", [128, 512], F32)
            eG = sb(p5, "eG", [128, 512], F32)
            enG = sb(p5, "enG", [128, 512], F32)
            gb = sb(p5, "gb5", [128, 512], F32)
            t1 = sb(p5, "t15", [128, 512], F32)
            kdec = sb(p5, "kdec5", [128, 512], BF16)
            vb = sb(p5, "vb5", [128, 512], BF16)
            kg = sb(p5, "kg", [128, 512], BF16)
            qg = sb(p5, "qg", [128, 512], BF16)
            qgT = sb(p5, "qgT", [128, 4, 128], BF16)
            qgT0 = sb(p5, "qgT0", [128, 4, 128], BF16)
            qgT1 = sb(p5, "qgT1", [128, 4, 128], BF16)
            kgT = sb(p5, "kgT", [128, 4, 128], BF16)
            ATb = sb(p5, "ATb", [128, 4, 128], BF16)
            S1b = sb(p5, "S1b", [128, 4, 128], BF16)
            cbb = sb(p5, "cbb", [128, 512], BF16)
            Dc0 = sb(p5, "Dc0", [128, 8], F32)
            ss = sb(p5, "ss5", [128, 4], F32)
            rstd = sb(p5, "rstd5", [128, 4], F32)
            jk5 = sb(p5, "jk5", [128, 128], F32)
            load_wres(Wb_sb, Wb, 2048)
            for o_ in range(4):
                zs_chunk(Wb_sb, o_ * 512, 512, ZOFF["b"] + o_ * 512)
            S.dma("sp", [lambda e, hd=hd: e.dma_start(out=ng_bc[:, hd, :], in_=ng[0:1, :].broadcast_to([128, 128]))
                         for hd in range(4)], writes=[ng_bc])
            S.op("pool", lambda e: e.memset(qgT0[:], 0.0), writes=[qgT0])
            S.op("pool", lambda e: e.memset(qgT1[:], 0.0), writes=[qgT1])
            for i in range(NO):
                x = xt[i % 2]
                load_xt(x, i)
                mm_x(PZ[0], x, Wb_sb, 0, 512)
                S.op("act", lambda e: e.activation(out=qs[:], in_=PZ[0][:], func=AF.Silu), reads=[PZ[0]], writes=[qs])
                mm_x(PZ[1], x, Wb_sb, 512, 512)
                S.op("act", lambda e: e.activation(out=sg[:], in_=PZ[1][:], func=AF.Sigmoid), reads=[PZ[1]], writes=[sg])
                mm_x(PZ[0], x, Wb_sb, 1024, 512)
                S.op("act", lambda e: e.activation(out=vb[:], in_=PZ[0][:], func=AF.Copy), reads=[PZ[0]], writes=[vb])
                mm_x(PZ[1], x, Wb_sb, 1536, 512)
                S.op("act", lambda e: e.activation(out=gb[:], in_=PZ[1][:], func=AF.Silu), reads=[PZ[1]], writes=[gb])
                S.op("dve", lambda e: e.tensor_tensor(out=sg[:], in0=sg[:], in1=oml_bc[:], op=ALU.mult),
                     reads=[sg, oml_bc], writes=[sg])
                S.op("dve", lambda e: e.tensor_tensor(out=sg[:], in0=sg[:], in1=lb_bc[:], op=ALU.add),
                     reads=[sg, lb_bc], writes=[sg])
                S.op("act", lambda e: e.activation(out=gg[:], in_=sg[:], func=AF.Ln), reads=[sg], writes=[gg])
                S.op("pool", lambda e: e.tensor_scalar(out=omf[:], in0=sg[:], scalar1=-1.0, scalar2=1.0,
                                                       op0=ALU.mult, op1=ALU.add), reads=[sg], writes=[omf])
                S.op("pe", lambda e: e.matmul(PG[:], lhsT=tri2[:], rhs=gg[:], start=True, stop=True),
                     reads=[tri2, gg], writes=[PG])
                S.op("pe", lambda e: e.matmul(PGL[:], lhsT=blk2[:], rhs=gg[:], start=True, stop=True),
                     reads=[blk2, gg], writes=[PGL])
                S.op("act", lambda e: e.activation(out=Gs[:], in_=PG[:], func=AF.Copy), reads=[PG], writes=[Gs])
                S.op("dve", lambda e: e.tensor_tensor(out=dlt[:], in0=PGL[:], in1=Gs[:], op=ALU.subtract),
                     reads=[PGL, Gs], writes=[dlt])
                S.op("act", lambda e: e.activation(out=dlt[:], in_=dlt[:], func=AF.Exp), reads=[dlt], writes=[dlt])
                S.op("dve", lambda e: e.tensor_tensor(out=kdec[:], in0=omf[:], in1=dlt[:], op=ALU.mult),
                     reads=[omf, dlt], writes=[kdec])
                S.op("act", lambda e: e.activation(out=eG[:], in_=Gs[:], func=AF.Exp), reads=[Gs], writes=[eG])
                S.op("act", lambda e: e.activation(out=enG[:], in_=Gs[:], func=AF.Exp, scale=-1.0), reads=[Gs], writes=[enG])
                S.op("dve", lambda e: e.tensor_tensor(out=qg[:], in0=qs[:], in1=eG[:], op=ALU.mult), reads=[qs, eG], writes=[qg])
                S.op("pool", lambda e: e.tensor_tensor(out=kg[:], in0=omf[:], in1=enG[:], op=ALU.mult),
                     reads=[omf, enG], writes=[kg])
                for hd in range(4):
                    S.op("pe", lambda e, hd=hd: e.matmul(PU[0][:, hd * 128:(hd + 1) * 128],
                                                         lhsT=kdec[0:64, hd * 128:(hd + 1) * 128],
                                                         rhs=vb[0:64, hd * 128:(hd + 1) * 128], start=True, stop=True),
                         reads=[kdec, vb], writes=[PU[0]])
                for hd in range(4):
                    S.op("pe", lambda e, hd=hd: e.matmul(PD[:, 2 * hd:2 * hd + 2], lhsT=gg[0:64, hd * 128:(hd + 1) * 128],
                                                         rhs=ones_f[0:64, :], start=True, stop=True),
                         reads=[gg, ones_f], writes=[PD])
                S.op("act", lambda e: e.activation(out=Dc0[:], in_=PD[:, 0:8], func=AF.Exp), reads=[PD], writes=[Dc0])
                for hd in range(4):
                    S.op("dve", lambda e, hd=hd, i=i: e.scalar_tensor_tensor(
                        out=S1b[:, hd, :], in0=Sown[:, i, hd * 128:(hd + 1) * 128], scalar=Dc0[:, 2 * hd:2 * hd + 1],
                        in1=PU[0][:, hd * 128:(hd + 1) * 128], op0=ALU.mult, op1=ALU.add),
                        reads=[Sown, Dc0, PU[0]], writes=[S1b])
                transpose4(lambda hd: qg[:, hd * 128:(hd + 1) * 128], [qg])
                transpose4(lambda hd: kg[:, hd * 128:(hd + 1) * 128], [kg], col0=512)
                ptq = lambda: PT[:, 0:512].rearrange("p (h t) -> p h t", h=4)
                S.op("act", lambda e: e.activation(out=qgT[:], in_=ptq(), func=AF.Copy), reads=[PT], writes=[qgT])
                S.op("dve", lambda e: e.tensor_copy(out=qgT0[:, :, 0:64], in_=ptq()[:, :, 0:64]), reads=[PT], writes=[qgT0])
                S.op("dve", lambda e: e.tensor_copy(out=qgT1[:, :, 64:128], in_=ptq()[:, :, 64:128]), reads=[PT], writes=[qgT1])
                S.op("act", lambda e: e.activation(out=kgT[:], in_=PT[:, 512:1024].rearrange("p (h t) -> p h t", h=4),
                                                   func=AF.Copy), reads=[PT], writes=[kgT])
                for hd in range(4):
                    S.op("pe", lambda e, hd=hd: e.matmul(PU[1][:, hd * 128:(hd + 1) * 128], lhsT=kgT[:, hd, :], rhs=qgT[:, hd, :],
                                                         start=True, stop=True), reads=[kgT, qgT], writes=[PU[1]])
                S.op("dve", lambda e: e.tensor_tensor(out=ATb[:], in0=PU[1][:].rearrange("p (h t) -> p h t", h=4),
                                                      in1=tri2[:].unsqueeze(1).to_broadcast([128, 4, 128]), op=ALU.mult),
                     reads=[PU[1], tri2], writes=[ATb])
                for hd in range(4):
                    cs_ = slice(hd * 128, (hd + 1) * 128)
                    S.op("pe", lambda e, hd=hd, cs_=cs_: e.matmul(PG[:, cs_], lhsT=ATb[:, hd, :], rhs=vb[:, cs_],
                                                                  start=True, stop=False), reads=[ATb, vb], writes=[PG])
                    S.op("pe", lambda e, hd=hd, cs_=cs_, i=i: e.matmul(PG[:, cs_], lhsT=qgT0[:, hd, :], rhs=Sown[:, i, cs_],
                                                                       start=False, stop=False),
                         reads=[qgT0, Sown], writes=[PG], pe_acc=True)
                    S.op("pe", lambda e, hd=hd, cs_=cs_: e.matmul(PG[:, cs_], lhsT=qgT1[:, hd, :], rhs=S1b[:, hd, :],
                                                                  start=False, stop=True),
                         reads=[qgT1, S1b], writes=[PG], pe_acc=True)
                for hd in range(4):
                    S.op("act", lambda e, hd=hd: e.activation(out=jk5[:], in_=PG[:, hd * 128:(hd + 1) * 128], func=AF.Square,
                                                              accum_out=ss[:, hd:hd + 1]), reads=[PG], writes=[jk5, ss])
                S.op("dve", lambda e: e.tensor_scalar(out=rstd[:], in0=ss[:], scalar1=1.0 / 128.0, scalar2=RMS_EPS,
                                                      op0=ALU.mult, op1=ALU.add), reads=[ss], writes=[rstd])
                S.op("act", lambda e: e.activation(out=rstd[:], in_=rstd[:], func=AF.Sqrt), reads=[rstd], writes=[rstd])
                S.op("dve", lambda e: e.reciprocal(out=rstd[:], in_=rstd[:]), reads=[rstd], writes=[rstd])
                S.op("dve", lambda e: e.tensor_tensor(out=t1[:].rearrange("p (h d) -> p h d", h=4),
                                                      in0=PG[:].rearrange("p (h d) -> p h d", h=4),
                                                      in1=rstd[:, 0:4].unsqueeze(2).to_broadcast([128, 4, 128]), op=ALU.mult),
                     reads=[PG, rstd], writes=[t1])
                S.op("pool", lambda e: e.tensor_tensor(out=t1[:], in0=t1[:], in1=ng_bc[:].rearrange("p h d -> p (h d)"),
                                                       op=ALU.mult), reads=[t1, ng_bc], writes=[t1])
                S.op("pool", lambda e: e.tensor_tensor(out=cbb[:], in0=t1[:], in1=gb[:], op=ALU.mult),
                     reads=[t1, gb], writes=[cbb])
                transpose4(lambda hd: cbb[:, hd * 128:(hd + 1) * 128], [cbb])
                S.op("act", lambda e, i=i: e.activation(out=catT[:, 8:12, i * 128:(i + 1) * 128],
                                                        in_=PT[:, 0:512].rearrange("p (h t) -> p h t", h=4), func=AF.Copy),
                     reads=[PT], writes=[catT])
            S.barrier()
            S.emit()

        with ExitStack() as p6:
            Wm_sb = sb(p6, "Wm_sb", [128, KC, 1024], BF16)
            xt = [sb(p6, "xt6_%d" % j, [128, KC, 128], BF16) for j in range(2)]
            mq_b = sb(p6, "mq_b", [128, 512], BF16)
            gm = sb(p6, "gm", [128, 512], F32)
            mqT = sb(p6, "mqT", [128, 4, 128], BF16)
            Em = [sb(p6, "Em%d" % j, [128, 4, 128], BF16) for j in range(2)]
            rdm = sb(p6, "rdm", [128, 4], F32)
            tm = sb(p6, "tm", [128, 512], F32)
            cmb = sb(p6, "cmb", [128, 512], BF16)
            POm = [PU[0], PU[1]]
            load_wres(Wm_sb, Wm, 1024)
            zs_chunk(Wm_sb, 0, 512, ZOFF["m"])
            zs_chunk(Wm_sb, 512, 512, ZOFF["m"] + 512)
            for i in range(NO):
                x = xt[i % 2]
                load_xt(x, i)
                mm_x(PZ[0], x, Wm_sb, 0, 512)
                S.op("act", lambda e: e.activation(out=mq_b[:], in_=PZ[0][:], func=AF.Copy, scale=QS), reads=[PZ[0]], writes=[mq_b])
                mm_x(PZ[1], x, Wm_sb, 512, 512)
                S.op("act", lambda e: e.activation(out=gm[:], in_=PZ[1][:], func=AF.Silu), reads=[PZ[1]], writes=[gm])
                transpose4(lambda hd: mq_b[:, hd * 128:(hd + 1) * 128], [mq_b])
                S.op("act", lambda e: e.activation(out=mqT[:], in_=PT[:, 0:512].rearrange("p (h t) -> p h t", h=4), func=AF.Copy),
                     reads=[PT], writes=[mqT])
                for nt in range(2):
                    ps_ = PA[nt]
                    for hd in range(4):
                        S.op("pe", lambda e, hd=hd, nt=nt, ps_=ps_: e.matmul(
                            ps_[:, hd * 128:(hd + 1) * 128], lhsT=mkT[:, hd, nt * 128:(nt + 1) * 128], rhs=mqT[:, hd, :],
                            start=True, stop=True), reads=[mkT, mqT], writes=[ps_])
                    S.op("act", lambda e, nt=nt, ps_=ps_: e.activation(out=Em[nt][:].rearrange("p h t -> p (h t)"), in_=ps_[:],
                                                                       func=AF.Exp), reads=[ps_], writes=[Em[nt]])
                for nt in range(2):
                    for hd in range(4):
                        bank = hd // 3
                        col = (hd % 3) * 130
                        st = (nt == 0 and hd % 3 == 0)
                        S.op("pe", lambda e, hd=hd, nt=nt, bank=bank, col=col, st=st: e.matmul(
                            POm[bank][:, col:col + 130], lhsT=Em[nt][:, hd, :], rhs=mv[:, nt, hd, :],
                            start=st, stop=(nt == 1), skip_group_check=True),
                            reads=[Em[nt], mv], writes=[POm[bank]], pe_acc=(not st))
                for bank in range(2):
                    nh = 3 if bank == 0 else 1
                    S.op("dve", lambda e, bank=bank, nh=nh: e.reciprocal(
                        out=rdm[:, bank * 3:bank * 3 + nh],
                        in_=POm[bank][:, 0:nh * 130].rearrange("p (h c) -> p h c", c=130)[:, :, 128]),
                        reads=[POm[bank]], writes=[rdm])
                    S.op("dve", lambda e, bank=bank, nh=nh: e.tensor_tensor(
                        out=tm[:, bank * 384:bank * 384 + nh * 128].rearrange("p (h d) -> p h d", h=nh),
                        in0=POm[bank][:, 0:nh * 130].rearrange("p (h c) -> p h c", c=130)[:, :, 0:128],
                        in1=rdm[:, bank * 3:bank * 3 + nh].unsqueeze(2).to_broadcast([128, nh, 128]), op=ALU.mult),
                        reads=[POm[bank], rdm], writes=[tm])
                S.op("pool", lambda e: e.tensor_tensor(out=cmb[:], in0=tm[:], in1=gm[:], op=ALU.mult), reads=[tm, gm], writes=[cmb])
                transpose4(lambda hd: cmb[:, hd * 128:(hd + 1) * 128], [cmb])
                S.op("act", lambda e, i=i: e.activation(out=catT[:, 12:16, i * 128:(i + 1) * 128],
                                                        in_=PT[:, 0:512].rearrange("p (h t) -> p h t", h=4), func=AF.Copy),
                     reads=[PT], writes=[catT])
            S.barrier()
            S.emit()

        print("ninstr at S start", S.ninstr, flush=True)
        es_S = ExitStack()
        cat_s = sb(es_S, "cat_s", [8, D], BF16)
        aq_sb = sb(es_S, "aq_sb", [8, 1024], BF16)
        ak_sb = sb(es_S, "ak_sb", [8, 1024], BF16)
        av_sb = sb(es_S, "av_sb", [8, 1024], BF16)
        ga_s = sb(es_S, "ga_s", [8, 1024], F32)
        iq_sb = sb(es_S, "iq_sb", [8, 16, 64], BF16)
        ik_sb = sb(es_S, "ik_sb", [8, 64], BF16)
        iw_s = sb(es_S, "iw_s", [8, 16], F32)
        ropeS_sb = sb(es_S, "ropeS_sb", [8, 160], F32)
        tri8 = sb(es_S, "tri8", [8, 8], F32)
        one8 = sb(es_S, "one8", [8, 8], F32)
        S.dma("sp", lambda e: e.dma_start(out=ropeS_sb[:], in_=ropeS[:, :]), writes=[ropeS_sb])
        S.dma("sp", lambda e: e.dma_start(out=tri8[:], in_=c_tri2[0:8, 0:8]), writes=[tri8])
        S.op("pool", lambda e: e.memset(one8[:], 1.0), writes=[one8])
        with ExitStack() as s1:
            zq = sb(s1, "zq", [8, 1024], F32)
            zk = sb(s1, "zk", [8, 1024], F32)
            zv = sb(s1, "zv", [8, 1024], F32)
            ziq = sb(s1, "ziq", [8, 1024], F32)
            zik = sb(s1, "zik", [8, 64], F32)
            zb = sb(s1, "zb", [8, 2048], F32)
            zm = sb(s1, "zm", [8, 1024], F32)
            tAs = sb(s1, "tAs", [8, 16, 32], F32)
            tBs = sb(s1, "tBs", [8, 16, 32], F32)
            for dst, key, n_ in ((zq, "q", 1024), (zk, "k", 1024), (zv, "v", 1024), (ga_s, "g", 1024), (ziq, "iq", 1024),
                                 (iw_s, "iw", 16), (zik, "ik", 64), (zb, "b", 2048), (zm, "m", 1024)):
                S.dma("sp", lambda e, dst=dst, key=key, n_=n_: e.dma_start(out=dst[:, 0:n_], in_=zs_d[:, ZOFF[key]:ZOFF[key] + n_]),
                      reads=[zsd_b], writes=[dst])
            CCk, SSk = ropeS_sb[:, 0:32], ropeS_sb[:, 32:64]
            CCi, SSi = ropeS_sb[:, 64:80], ropeS_sb[:, 80:96]
            CCq, SSq = ropeS_sb[:, 96:128], ropeS_sb[:, 128:160]
            v3 = lambda t, nh: V3(lambda lo_, hi_: t[:].rearrange("p (h d) -> p h d", h=nh)[:, :, lo_:hi_], [t])
            rope(v3(zk, 8), v3(zk, 8), 8, 16, CCk, SSk, tAs, tBs, np_=8)
            S.dma("sp", lambda e: e.dma_start(out=o_ks[:, :], in_=zk[:]), reads=[zk])
            S.dma("sp", lambda e: e.dma_start(out=o_vs[:, :], in_=zv[:]), reads=[zv])
            S.op("dve", lambda e: e.tensor_copy(out=ak_sb[:], in_=zk[:]), reads=[zk], writes=[ak_sb])
            S.op("dve", lambda e: e.tensor_copy(out=av_sb[:], in_=zv[:]), reads=[zv], writes=[av_sb])
            rope(v3(zik, 1), v3(zik, 1), 1, 8, CCi, SSi, tAs, tBs, np_=8)
            S.dma("sp", lambda e: e.dma_start(out=o_iks[:, :], in_=zik[:]), reads=[zik])
            S.op("dve", lambda e: e.tensor_copy(out=ik_sb[:], in_=zik[:]), reads=[zik], writes=[ik_sb])
            S.op("act", lambda e: e.activation(out=aq_sb[:], in_=zq[:], func=AF.Copy, scale=QS), reads=[zq], writes=[aq_sb])
            rope(v3(zq, 8), v3(aq_sb, 8), 8, 16, CCq, SSq, tAs, tBs, np_=8)
            S.op("act", lambda e: e.activation(out=iq_sb[:].rearrange("p h d -> p (h d)"), in_=ziq[:], func=AF.Copy),
                 reads=[ziq], writes=[iq_sb])
            rope(v3(ziq, 16), V3(lambda lo_, hi_: iq_sb[:, :, lo_:hi_], [iq_sb]), 16, 8, CCi, SSi, tAs, tBs, np_=8)
            S.op("act", lambda e: e.activation(out=ga_s[:], in_=ga_s[:], func=AF.Silu), reads=[ga_s], writes=[ga_s])

            S0f = sb(s1, "S0f", [128, 4, 128], F32)
            S0b = sb(s1, "S0b", [128, 4, 128], BF16)
            S1f = sb(s1, "S1f", [128, 4, 128], F32)
            hs = [sb(s1, "hs%d" % j, [8, 512], F32) for j in range(8)]
            qs_, sg_, gg_, omf_, Gs_, dl_, eG_, enG_ = hs
            gb_ = sb(s1, "gb_s", [8, 512], F32)
            t1_ = sb(s1, "t1_s", [8, 512], F32)
            kdec_ = sb(s1, "kdec_s", [8, 512], BF16)
            vb_ = sb(s1, "vb_s", [8, 512], BF16)
            kg_ = sb(s1, "kg_s", [8, 512], BF16)
            qg_ = sb(s1, "qg_s", [8, 512], BF16)
            qgT_ = sb(s1, "qgT_s", [128, 4, 8], BF16)
            kgT_ = sb(s1, "kgT_s", [128, 4, 8], BF16)
            ATb_ = sb(s1, "ATb_s", [8, 4, 8], BF16)
            Dcs = sb(s1, "Dcs", [128, 8], F32)
            ss_ = sb(s1, "ss_s", [8, 4], F32)
            rstd_ = sb(s1, "rstd_s", [8, 4], F32)
            jk_ = sb(s1, "jk_s", [8, 128], F32)
            ngs = sb(s1, "ngs", [8, 4, 128], F32)
            lb8 = sb(s1, "lb8", [8, 2, 512], F32)
            S.dma("sp", lambda e: e.dma_start(out=S0f[:], in_=st_h.rearrange("h k v -> k h v")), writes=[S0f])
            S.dma("sp", [lambda e, hd=hd: e.dma_start(out=ngs[:, hd, :], in_=ng[0:1, :].broadcast_to([8, 128]))
                         for hd in range(4)], writes=[ngs])
            S.op("dve", lambda e: e.tensor_copy(out=S0b[:], in_=S0f[:]), reads=[S0f], writes=[S0b])
            S.op("act", lambda e: e.activation(out=qs_[:], in_=zb[:, 0:512], func=AF.Silu), reads=[zb], writes=[qs_])
            S.op("act", lambda e: e.activation(out=sg_[:], in_=zb[:, 512:1024], func=AF.Sigmoid), reads=[zb], writes=[sg_])
            S.op("act", lambda e: e.activation(out=vb_[:], in_=zb[:, 1024:1536], func=AF.Copy), reads=[zb], writes=[vb_])
            S.op("act", lambda e: e.activation(out=gb_[:], in_=zb[:, 1536:2048], func=AF.Silu), reads=[zb], writes=[gb_])
            S.op("dve", lambda e: e.tensor_tensor(out=sg_[:], in0=sg_[:], in1=oml_bc[0:8, :], op=ALU.mult),
                 reads=[sg_, oml_bc], writes=[sg_])
            S.op("dve", lambda e: e.tensor_tensor(out=sg_[:], in0=sg_[:], in1=lb_bc[0:8, :], op=ALU.add),
                 reads=[sg_, lb_bc], writes=[sg_])
            S.op("act", lambda e: e.activation(out=gg_[:], in_=sg_[:], func=AF.Ln), reads=[sg_], writes=[gg_])
            S.op("dve", lambda e: e.tensor_scalar(out=omf_[:], in0=sg_[:], scalar1=-1.0, scalar2=1.0, op0=ALU.mult, op1=ALU.add),
                 reads=[sg_], writes=[omf_])
            S.op("pe", lambda e: e.matmul(PG[0:8, :], lhsT=tri8[:], rhs=gg_[:], start=True, stop=True),
                 reads=[tri8, gg_], writes=[PG])
            S.op("pe", lambda e: e.matmul(PGL[0:8, :], lhsT=one8[:], rhs=gg_[:], start=True, stop=True),
                 reads=[one8, gg_], writes=[PGL])
            S.op("act", lambda e: e.activation(out=Gs_[:], in_=PG[0:8, :], func=AF.Copy), reads=[PG], writes=[Gs_])
            S.op("dve", lambda e: e.tensor_tensor(out=dl_[:], in0=PGL[0:8, :], in1=Gs_[:], op=ALU.subtract),
                 reads=[PGL, Gs_], writes=[dl_])
            S.op("act", lambda e: e.activation(out=dl_[:], in_=dl_[:], func=AF.Exp), reads=[dl_], writes=[dl_])
            S.op("dve", lambda e: e.tensor_tensor(out=kdec_[:], in0=omf_[:], in1=dl_[:], op=ALU.mult),
                 reads=[omf_, dl_], writes=[kdec_])
            S.op("act", lambda e: e.activation(out=eG_[:], in_=Gs_[:], func=AF.Exp), reads=[Gs_], writes=[eG_])
            S.op("act", lambda e: e.activation(out=enG_[:], in_=Gs_[:], func=AF.Exp, scale=-1.0), reads=[Gs_], writes=[enG_])
            S.op("dve", lambda e: e.tensor_tensor(out=qg_[:], in0=qs_[:], in1=eG_[:], op=ALU.mult), reads=[qs_, eG_], writes=[qg_])
            S.op("dve", lambda e: e.tensor_tensor(out=kg_[:], in0=omf_[:], in1=enG_[:], op=ALU.mult),
                 reads=[omf_, enG_], writes=[kg_])
            for hd in range(4):
                cs_ = slice(hd * 128, (hd + 1) * 128)
                S.op("pe", lambda e, cs_=cs_: e.matmul(PU[0][:, cs_], lhsT=kdec_[:, cs_], rhs=vb_[:, cs_], start=True, stop=True),
                     reads=[kdec_, vb_], writes=[PU[0]])
            for hd in range(4):
                S.op("pe", lambda e, hd=hd: e.matmul(PD[:, 2 * hd:2 * hd + 2], lhsT=gg_[:, hd * 128:(hd + 1) * 128],
                                                     rhs=ones_f[0:8, :], start=True, stop=True),
                     reads=[gg_, ones_f], writes=[PD])
            S.op("act", lambda e: e.activation(out=Dcs[:], in_=PD[:, 0:8], func=AF.Exp), reads=[PD], writes=[Dcs])
            for hd in range(4):
                S.op("dve", lambda e, hd=hd: e.scalar_tensor_tensor(
                    out=S1f[:, hd, :], in0=S0f[:, hd, :], scalar=Dcs[:, 2 * hd:2 * hd + 1],
                    in1=PU[0][:, hd * 128:(hd + 1) * 128], op0=ALU.mult, op1=ALU.add),
                    reads=[S0f, Dcs, PU[0]], writes=[S1f])
            S.dma("sp", lambda e: e.dma_start(out=o_hgs[:, :, :], in_=S1f[:]), reads=[S1f])
            for hd in range(4):
                S.op("pe", lambda e, hd=hd: e.transpose(out=PT[:, hd * 8:(hd + 1) * 8], in_=qg_[:, hd * 128:(hd + 1) * 128],
                                                        identity=ident_b[0:8, 0:8]), reads=[qg_, ident_b], writes=[PT])
                S.op("pe", lambda e, hd=hd: e.transpose(out=PT[:, 32 + hd * 8:32 + (hd + 1) * 8], in_=kg_[:, hd * 128:(hd + 1) * 128],
                                                        identity=ident_b[0:8, 0:8]), reads=[kg_, ident_b], writes=[PT])
            S.op("act", lambda e: e.activation(out=qgT_[:].rearrange("p h t -> p (h t)"), in_=PT[:, 0:32], func=AF.Copy),
                 reads=[PT], writes=[qgT_])
            S.op("act", lambda e: e.activation(out=kgT_[:].rearrange("p h t -> p (h t)"), in_=PT[:, 32:64], func=AF.Copy),
                 reads=[PT], writes=[kgT_])
            for hd in range(4):
                S.op("pe", lambda e, hd=hd: e.matmul(PU[1][0:8, hd * 8:(hd + 1) * 8], lhsT=kgT_[:, hd, :], rhs=qgT_[:, hd, :],
                                                     start=True, stop=True), reads=[kgT_, qgT_], writes=[PU[1]])
            S.op("dve", lambda e: e.tensor_tensor(out=ATb_[:], in0=PU[1][0:8, 0:32].rearrange("p (h t) -> p h t", h=4),
                                                  in1=tri8[:].unsqueeze(1).to_broadcast([8, 4, 8]), op=ALU.mult),
                 reads=[PU[1], tri8], writes=[ATb_])
            for hd in range(4):
                cs_ = slice(hd * 128, (hd + 1) * 128)
                S.op("pe", lambda e, hd=hd, cs_=cs_: e.matmul(PG[0:8, cs_], lhsT=ATb_[:, hd, :], rhs=vb_[:, cs_],
                                                              start=True, stop=False), reads=[ATb_, vb_], writes=[PG])
                S.op("pe", lambda e, hd=hd, cs_=cs_: e.matmul(PG[0:8, cs_], lhsT=qgT_[:, hd, :], rhs=S0b[:, hd, :],
                                                              start=False, stop=True), reads=[qgT_, S0b], writes=[PG], pe_acc=True)
            for hd in range(4):
                S.op("act", lambda e, hd=hd: e.activation(out=jk_[:], in_=PG[0:8, hd * 128:(hd + 1) * 128], func=AF.Square,
                                                          accum_out=ss_[:, hd:hd + 1]), reads=[PG], writes=[jk_, ss_])
            S.op("dve", lambda e: e.tensor_scalar(out=rstd_[:], in0=ss_[:], scalar1=1.0 / 128.0, scalar2=RMS_EPS,
                                                  op0=ALU.mult, op1=ALU.add), reads=[ss_], writes=[rstd_])
            S.op("act", lambda e: e.activation(out=rstd_[:], in_=rstd_[:], func=AF.Sqrt), reads=[rstd_], writes=[rstd_])
            S.op("dve", lambda e: e.reciprocal(out=rstd_[:], in_=rstd_[:]), reads=[rstd_], writes=[rstd_])
            S.op("dve", lambda e: e.tensor_tensor(out=t1_[:].rearrange("p (h d) -> p h d", h=4),
                                                  in0=PG[0:8, :].rearrange("p (h d) -> p h d", h=4),
                                                  in1=rstd_[:, 0:4].unsqueeze(2).to_broadcast([8, 4, 128]), op=ALU.mult),
                 reads=[PG, rstd_], writes=[t1_])
            S.op("dve", lambda e: e.tensor_tensor(out=t1_[:], in0=t1_[:], in1=ngs[:].rearrange("p h d -> p (h d)"), op=ALU.mult),
                 reads=[t1_, ngs], writes=[t1_])
            S.op("dve", lambda e: e.tensor_tensor(out=cat_s[:, 1024:1536], in0=t1_[:], in1=gb_[:], op=ALU.mult),
                 reads=[t1_, gb_], writes=[cat_s])

            cmk = sb(s1, "cmk", [128, 2, 512], BF16)
            mvs = sb(s1, "mvs", [128, 2, 4, 130], BF16)
            mkTs = sb(s1, "mkTs", [128, 4, 256], BF16)
            mq_s = sb(s1, "mq_s", [8, 512], BF16)
            gm_s = sb(s1, "gm_s", [8, 512], F32)
            mqTs = sb(s1, "mqTs", [128, 4, 8], BF16)
            Ems = [sb(s1, "Ems%d" % j, [128, 4, 8], BF16) for j in range(2)]
            rdms = sb(s1, "rdms", [8, 4], F32)
            tms = sb(s1, "tms", [8, 512], F32)
            S.dma("pool", [lambda e, nt=nt: e.dma_start(out=cmk[:, nt, :], in_=cmk_d[nt * 128:(nt + 1) * 128, :])
                           for nt in range(2)], writes=[cmk])
            S.op("pool", lambda e: e.memset(mvs[:, :, :, 128:129], 1.0), writes=[mvs])
            S.dma("pool", [lambda e, nt=nt: e.dma_start(out=mvs[:, nt, :, 0:128],
                                                        in_=cmv_d[nt * 128:(nt + 1) * 128, :].rearrange("p (h d) -> p h d", h=4))
                           for nt in range(2)], writes=[mvs])
            for nt in range(2):
                transpose4(lambda hd, nt=nt: cmk[:, nt, hd * 128:(hd + 1) * 128], [cmk])
                S.op("act", lambda e, nt=nt: e.activation(out=mkTs[:, :, nt * 128:(nt + 1) * 128],
                                                          in_=PT[:, 0:512].rearrange("p (h t) -> p h t", h=4), func=AF.Copy),
                     reads=[PT], writes=[mkTs])
            S.op("act", lambda e: e.activation(out=mq_s[:], in_=zm[:, 0:512], func=AF.Copy, scale=QS), reads=[zm], writes=[mq_s])
            S.op("act", lambda e: e.activation(out=gm_s[:], in_=zm[:, 512:1024], func=AF.Silu), reads=[zm], writes=[gm_s])
            for hd in range(4):
                S.op("pe", lambda e, hd=hd: e.transpose(out=PT[:, hd * 8:(hd + 1) * 8], in_=mq_s[:, hd * 128:(hd + 1) * 128],
                                                        identity=ident_b[0:8, 0:8]), reads=[mq_s, ident_b], writes=[PT])
            S.op("act", lambda e: e.activation(out=mqTs[:].rearrange("p h t -> p (h t)"), in_=PT[:, 0:32], func=AF.Copy),
                 reads=[PT], writes=[mqTs])
            Es_ = sb(s1, "Es_s", [8, 2, 512], BF16)
            for nt in range(2):
                ps_ = PA[nt]
                for hd in range(4):
                    S.op("pe", lambda e, hd=hd, nt=nt, ps_=ps_: e.matmul(
                        ps_[0:8, hd * 128:(hd + 1) * 128], lhsT=mqTs[:, hd, :], rhs=mkTs[:, hd, nt * 128:(nt + 1) * 128],
                        start=True, stop=True), reads=[mkTs, mqTs], writes=[ps_])
                S.op("act", lambda e, nt=nt, ps_=ps_: e.activation(out=Es_[:, nt, :], in_=ps_[0:8, :], func=AF.Exp),
                     reads=[ps_], writes=[Es_])
                for hd in range(4):
                    S.op("pe", lambda e, hd=hd, nt=nt: e.transpose(out=PT[:, hd * 8:(hd + 1) * 8],
                                                                   in_=Es_[:, nt, hd * 128:(hd + 1) * 128],
                                                                   identity=ident_b[0:8, 0:8]), reads=[Es_, ident_b], writes=[PT])
                S.op("act", lambda e, nt=nt: e.activation(out=Ems[nt][:].rearrange("p h t -> p (h t)"), in_=PT[:, 0:32],
                                                          func=AF.Copy), reads=[PT], writes=[Ems[nt]])
            POs = [PU[0], PU[1]]
            for nt in range(2):
                for hd in range(4):
                    bank, col = hd // 3, (hd % 3) * 130
                    st = (nt == 0 and hd % 3 == 0)
                    S.op("pe", lambda e, hd=hd, nt=nt, bank=bank, col=col, st=st: e.matmul(
                        POs[bank][0:8, col:col + 130], lhsT=Ems[nt][:, hd, :], rhs=mvs[:, nt, hd, :],
                        start=st, stop=(nt == 1), skip_group_check=True),
                        reads=[Ems[nt], mvs], writes=[POs[bank]], pe_acc=(not st))
            for bank in range(2):
                nh = 3 if bank == 0 else 1
                S.op("dve", lambda e, bank=bank, nh=nh: e.reciprocal(
                    out=rdms[:, bank * 3:bank * 3 + nh],
                    in_=POs[bank][0:8, 0:nh * 130].rearrange("p (h c) -> p h c", c=130)[:, :, 128]),
                    reads=[POs[bank]], writes=[rdms])
                S.op("dve", lambda e, bank=bank, nh=nh: e.tensor_tensor(
                    out=tms[:, bank * 384:bank * 384 + nh * 128].rearrange("p (h d) -> p h d", h=nh),
                    in0=POs[bank][0:8, 0:nh * 130].rearrange("p (h c) -> p h c", c=130)[:, :, 0:128],
                    in1=rdms[:, bank * 3:bank * 3 + nh].unsqueeze(2).to_broadcast([8, nh, 128]), op=ALU.mult),
                    reads=[POs[bank], rdms], writes=[tms])
            S.op("dve", lambda e: e.tensor_tensor(out=cat_s[:, 1536:2048], in0=tms[:], in1=gm_s[:], op=ALU.mult),
                 reads=[tms, gm_s], writes=[cat_s])
            S.barrier()
            S.emit()
        print("ninstr at S1 end", S.ninstr, flush=True)
        KB2 = 30
        with ExitStack() as s3:
            ci = {}
            for nm, shp, dt_ in (("c_sel8", [8, 128], F32), ("c_hm", [128, 16], F32), ("c_bq8", [128, 8], F32),
                                 ("c_bq16", [128, 8], F32), ("c_BQ1", [128, 128], F32), ("c_BQm", [128, 128], F32),
                                 ("c_LT", [128, 128], F32), ("c_negm", [128, 8], F32), ("c_aiota", [128, 1024], F32),
                                 ("c_iota512", [128, 512], F32), ("c_esel", [8, 1024], F32), ("c_eq", [8, 64], F32),
                                 ("c_bd8", [8, 1024], F32), ("c_pow2", [128, 2 * KB2], F32), ("c_i32", [128, 768], I32)):
                t_ = sb(s3, "k" + nm, shp, dt_)
                S.dma("sp", lambda e, t_=t_, nm=nm: e.dma_start(out=t_[:], in_=cin[nm][:, :]), writes=[t_])
                ci[nm] = t_
            esel_b = sb(s3, "esel_b", [8, 8, 128], BF16)
            eq_b = sb(s3, "eq_b", [8, 8, 8], BF16)
            ones_b = sb(s3, "ones_b", [128, 2], BF16)
            S.op("dve", lambda e: e.tensor_copy(out=esel_b[:].rearrange("p a b -> p (a b)"), in_=ci["c_esel"][:]),
                 reads=[ci["c_esel"]], writes=[esel_b])
            S.op("dve", lambda e: e.tensor_copy(out=eq_b[:].rearrange("p a b -> p (a b)"), in_=ci["c_eq"][:]),
                 reads=[ci["c_eq"]], writes=[eq_b])
            S.op("pool", lambda e: e.memset(ones_b[:], 1.0), writes=[ones_b])
            I_s = sb(s3, "I_s", [128, 1032], F32)
            junk_s = sb(s3, "junk_s", [128, 1032], BF16)
            iqT_s = sb(s3, "iqT_s", [64, 128], BF16)
            ikTn = sb(s3, "ikTn", [64, 8], BF16)
            wtmp = sb(s3, "wtmp", [128, 16], F32)
            wcol = sb(s3, "wcol", [128, 1], F32)
            wdb = sb(s3, "wdb", [128, 8], F32)
            Wd_all = sb(s3, "Wd_all", [128, 16, 128], BF16)
            Tbs = [sb(s3, "Tbs%d" % j, [128, 512], BF16) for j in range(2)]
            Tn = sb(s3, "Tn", [128, 8], BF16)
            pt_i = sb(s3, "pt_i", [128, 1], I32)
            ptrow_i = sb(s3, "ptrow_i", [128, 128], I32)
            ptrow_f = sb(s3, "ptrow_f", [128, 128], F32)
            st2 = sb(s3, "st2", [128, 4], F32)
            sm2 = sb(s3, "sm2", [128, 8], F32)
            lo2, hi2, rng2, mid2, wv2 = [sm2[:, j:j + 1] for j in range(5)]
            cnt2 = sb(s3, "cnt2", [128, 2], F32)
            spw2 = sb(s3, "spw2", [128, 2 * KB2], F32)
            off_s = sb(s3, "off_s", [128, 1], F32)
            seln = sb(s3, "seln", [128, 8], F32)
            selnT = sb(s3, "selnT", [8, 128], F32)
            idx_i = sb(s3, "idx_i", [128, 2, 8], I32)
            vmask = sb(s3, "vmask", [128, 2, 8], F32)
            S.dma("sp", lambda e: e.dma_start(out=pt_i[:], in_=ptab.rearrange("o p -> p o")), writes=[pt_i])
            S.dma("sp", lambda e: e.dma_start(out=ptrow_i[:], in_=ptab[0:1, :].broadcast_to([128, 128])), writes=[ptrow_i])
            S.op("pool", lambda e: e.memset(cnt2[:], 0.0), writes=[cnt2])
            for h in range(16):
                S.op("pe", lambda e, h=h: e.transpose(out=PT[0:64, h * 8:(h + 1) * 8], in_=iq_sb[:, h, :],
                                                      identity=ident_b[0:8, 0:8]), reads=[iq_sb, ident_b], writes=[PT])
            S.op("pe", lambda e: e.transpose(out=PT[0:64, 128:136], in_=ik_sb[:], identity=ident_b[0:8, 0:8]),
                 reads=[ik_sb, ident_b], writes=[PT])
            S.op("act", lambda e: e.activation(out=iqT_s[:], in_=PT[0:64, 0:128], func=AF.Copy), reads=[PT], writes=[iqT_s])
            S.op("act", lambda e: e.activation(out=ikTn[:], in_=PT[0:64, 128:136], func=AF.Copy), reads=[PT], writes=[ikTn])
            S.op("pe", lambda e: e.matmul(PD[:, 0:16], lhsT=ci["c_sel8"][:], rhs=iw_s[:], start=True, stop=True),
                 reads=[ci["c_sel8"], iw_s], writes=[PD])
            S.op("dve", lambda e: e.tensor_tensor(out=wtmp[:], in0=PD[:, 0:16], in1=ci["c_hm"][:], op=ALU.mult),
                 reads=[PD, ci["c_hm"]], writes=[wtmp])
            S.op("dve", lambda e: e.tensor_reduce(out=wcol[:], in_=wtmp[:], axis=AX.X, op=ALU.add), reads=[wtmp], writes=[wcol])
            S.op("dve", lambda e: e.tensor_scalar(out=wcol[:], in0=wcol[:], scalar1=1.0 / 32.0, scalar2=None, op0=ALU.mult),
                 reads=[wcol], writes=[wcol])
            S.op("dve", lambda e: e.tensor_scalar(out=wdb[:], in0=ci["c_bq8"][:], scalar1=wcol[:, 0:1], scalar2=None, op0=ALU.mult),
                 reads=[ci["c_bq8"], wcol], writes=[wdb])
            S.op("pool", lambda e: e.memset(Wd_all[:], 0.0), writes=[Wd_all])
            for seg in range(16):
                S.op("dve", lambda e, seg=seg: e.tensor_copy(
                    out=Wd_all[:, seg, :].rearrange("p (q s) -> p q s", s=16)[:, :, seg], in_=wdb[:]),
                    reads=[wdb], writes=[Wd_all])
            with ExitStack() as sA:
                ikT_s = sb(sA, "ikT_s", [64, PAST], BF16)
                with ExitStack() as sA2:
                    ikp = sb(sA2, "ikp", [128, 8192], F32)
                    S.dma("pool", lambda e: e.indirect_dma_start(
                        out=ikp[:], out_offset=None, in_=cidx[:, :],
                        in_offset=bass.IndirectOffsetOnAxis(ap=pt_i[:, 0:1], axis=0)), reads=[pt_i], writes=[ikp])
                    for r in range(128):
                        pz = PZ[(r // 4) % 2]
                        S.op("pe", lambda e, r=r, pz=pz: e.transpose(out=pz[0:64, (r % 4) * 128:(r % 4 + 1) * 128],
                                                                     in_=ikp[:, r * 64:(r + 1) * 64], identity=ident_f[:]),
                             reads=[ikp, ident_f], writes=[pz])
                        if r % 4 == 3:
                            eng_ = "act" if (r // 4) % 2 == 0 else "dve"
                            if eng_ == "act":
                                S.op("act", lambda e, r=r, pz=pz: e.activation(out=ikT_s[:, (r - 3) * 128:(r + 1) * 128],
                                                                               in_=pz[0:64, :], func=AF.Copy),
                                     reads=[pz], writes=[ikT_s])
                            else:
                                S.op("dve", lambda e, r=r, pz=pz: e.tensor_copy(out=ikT_s[:, (r - 3) * 128:(r + 1) * 128],
                                                                                in_=pz[0:64, :]), reads=[pz], writes=[ikT_s])
                    S.barrier()
                    S.emit()
                def mm1s(c):
                    pa = PA[c % 2]
                    S.op("pe", lambda e, c=c, pa=pa: e.matmul(pa[:], lhsT=iqT_s[:], rhs=ikT_s[:, c * 512:(c + 1) * 512],
                                                              start=True, stop=True), reads=[iqT_s, ikT_s], writes=[pa])
                mm1s(0)
                for c in range(32):
                    if c + 1 < 32:
                        mm1s(c + 1)
                    pa, tb = PA[c % 2], Tbs[c % 2]
                    seg, half = c // 2, c % 2
                    S.op("act", lambda e, pa=pa, tb=tb: e.activation(out=tb[:], in_=pa[:], func=AF.Relu), reads=[pa], writes=[tb])
                    S.op("pe", lambda e, seg=seg, half=half, tb=tb: e.matmul(PI[half][:], lhsT=Wd_all[:, seg, :], rhs=tb[:],
                                                                            start=(seg == 0), stop=(seg == 15)),
                         reads=[Wd_all, tb], writes=[PI[half]], pe_acc=(seg > 0))
                for half in range(2):
                    S.op("dve", lambda e, half=half: e.tensor_copy(out=I_s[:, half * 512:(half + 1) * 512], in_=PI[half][:]),
                         reads=[PI[half]], writes=[I_s])
                S.op("pe", lambda e: e.matmul(PA[0][:, 0:8], lhsT=iqT_s[:], rhs=ikTn[:], start=True, stop=True),
                     reads=[iqT_s, ikTn], writes=[PA[0]])
                S.op("act", lambda e: e.activation(out=Tn[:], in_=PA[0][:, 0:8], func=AF.Relu), reads=[PA[0]], writes=[Tn])
                S.op("pe", lambda e: e.matmul(PD[:, 0:8], lhsT=Wd_all[:, 0, :], rhs=Tn[:], start=True, stop=True),
                     reads=[Wd_all, Tn], writes=[PD])
                S.op("dve", lambda e: e.tensor_reduce(out=st2[:, 2:3], in_=PD[:, 0:8], axis=AX.X, op=ALU.max,
                                                      apply_absolute_value=True), reads=[PD], writes=[st2])
                S.op("dve", lambda e: e.tensor_tensor(out=I_s[:, 1024:1032], in0=PD[:, 0:8], in1=ci["c_negm"][:], op=ALU.add),
                     reads=[PD, ci["c_negm"]], writes=[I_s])
                S.barrier()
                S.emit()
            S.op("dve", lambda e: e.tensor_reduce(out=st2[:, 0:1], in_=I_s[:, 0:1024], axis=AX.X, op=ALU.min),
                 reads=[I_s], writes=[st2])
            S.op("dve", lambda e: e.tensor_reduce(out=st2[:, 1:2], in_=I_s[:, 0:1024], axis=AX.X, op=ALU.max,
                                                  apply_absolute_value=True), reads=[I_s], writes=[st2])
            S.op("dve", lambda e: e.tensor_tensor(out=st2[:, 1:2], in0=st2[:, 1:2], in1=st2[:, 2:3], op=ALU.max),
                 reads=[st2], writes=[st2])
            S.op("pe", lambda e: e.matmul(PD[:, 16:18], lhsT=ci["c_BQm"][:], rhs=st2[:, 0:2], start=True, stop=True),
                 reads=[ci["c_BQm"], st2], writes=[PD])
            S.op("pe", lambda e: e.matmul(PD[:, 18:20], lhsT=ci["c_BQ1"][:], rhs=st2[:, 0:2], start=True, stop=True),
                 reads=[ci["c_BQ1"], st2], writes=[PD])
            S.op("dve", lambda e: e.tensor_copy(out=lo2, in_=PD[:, 16:17]), reads=[PD], writes=[sm2])
            S.op("dve", lambda e: e.tensor_copy(out=hi2, in_=PD[:, 19:20]), reads=[PD], writes=[sm2])
            S.op("dve", lambda e: e.tensor_tensor(out=rng2, in0=hi2, in1=lo2, op=ALU.subtract), reads=[sm2], writes=[sm2])
            S.op("dve", lambda e: e.tensor_scalar(out=spw2[:], in0=ci["c_pow2"][:], scalar1=rng2, scalar2=None, op0=ALU.mult),
                 reads=[ci["c_pow2"], sm2], writes=[spw2])
            S.op("dve", lambda e: e.tensor_tensor(out=mid2, in0=lo2, in1=spw2[:, 0:1], op=ALU.add), reads=[sm2, spw2], writes=[sm2])
            for k in range(KB2):
                S.op("dve", lambda e: e.tensor_scalar(out=junk_s[:], in0=I_s[:], scalar1=mid2, scalar2=0.0, op0=ALU.is_ge,
                                                      op1=ALU.add, accum_out=cnt2[:, 0:1]), reads=[I_s, sm2], writes=[junk_s, cnt2])
                S.op("pe", lambda e: e.matmul(PD[:, 32:34], lhsT=ci["c_BQ1"][:], rhs=cnt2[:], start=True, stop=True),
                     reads=[ci["c_BQ1"], cnt2], writes=[PD])
                S.op("dve", lambda e, k=k: e.scalar_tensor_tensor(out=wv2, in0=PD[:, 32:33], scalar=256.0, in1=spw2[:, k:k + 1],
                                                                  op0=ALU.is_ge, op1=ALU.mult), reads=[PD, spw2], writes=[sm2])
                S.op("dve", lambda e, k=k: e.scalar_tensor_tensor(out=mid2, in0=mid2, scalar=spw2[:, KB2 + k:KB2 + k + 1], in1=wv2,
                                                                  op0=ALU.subtract, op1=ALU.add), reads=[sm2, spw2], writes=[sm2])
            with ExitStack() as sB:
                selm = sb(sB, "selm", [128, 1024], F32)
                PH = sb(sB, "PH", [128, 8, 128], F32)
                kp = [sb(sB, "kp%d" % j, [128, 1024], F32) for j in range(2)]
                cand = sb(sB, "cand", [128, 256], F32)
                candi = sb(sB, "candi", [128, 256], I32)
                dgi = sb(sB, "dgi", [128, 256], I32)
                dgb = sb(sB, "dgb", [128, 3, 256], BF16)
                L_all = sb(sB, "L_all", [128, 256, 24], BF16)
                OH = sb(sB, "OH", [128, 512], BF16)
                Cs = sb(sB, "Cs", [24, 256], F32)
                dig_s = sb(sB, "dig_s", [128, 2, 24], F32)
                idxf = sb(sB, "idxf", [128, 2, 8], F32)
                S.op("dve", lambda e: e.tensor_scalar(out=selm[:], in0=I_s[:, 0:1024], scalar1=mid2, scalar2=0.0, op0=ALU.is_ge,
                                                      op1=ALU.add, accum_out=cnt2[:, 0:1]), reads=[I_s, sm2], writes=[selm, cnt2])
                S.op("pe", lambda e: e.matmul(PD[:, 34:36], lhsT=ci["c_LT"][:], rhs=cnt2[:], start=True, stop=True),
                     reads=[ci["c_LT"], cnt2], writes=[PD])
                S.op("dve", lambda e: e.tensor_copy(out=off_s[:], in_=PD[:, 34:35]), reads=[PD], writes=[off_s])
                S.op("dve", lambda e: e.tensor_scalar(out=seln[:], in0=I_s[:, 1024:1032], scalar1=mid2, scalar2=None, op0=ALU.is_ge),
                     reads=[I_s, sm2], writes=[seln])
                S.op("pe", lambda e: e.transpose(out=PZ[0][0:8, 0:128], in_=seln[:], identity=ident_f[:]),
                     reads=[seln, ident_f], writes=[PZ[0]])
                S.op("dve", lambda e: e.tensor_copy(out=selnT[:], in_=PZ[0][0:8, 0:128]), reads=[PZ[0]], writes=[selnT])
                S.op("dve", lambda e: e.tensor_copy(out=ptrow_f[:], in_=ptrow_i[:]), reads=[ptrow_i], writes=[ptrow_f])
                S.op("dve", lambda e: e.tensor_scalar(out=ptrow_f[:], in0=ptrow_f[:], scalar1=128.0, scalar2=None, op0=ALU.mult),
                     reads=[ptrow_f], writes=[ptrow_f])
                S.op("dve", lambda e: e.tensor_tensor(out=PH[:], in0=ci["c_aiota"][:].rearrange("p (a s) -> p a s", a=8),
                                                      in1=ptrow_f[:].unsqueeze(1).to_broadcast([128, 8, 128]), op=ALU.add),
                     reads=[ci["c_aiota"], ptrow_f], writes=[PH])
                S.op("dve", lambda e: e.tensor_tensor(out=kp[0][:], in0=selm[:], in1=PH[:].rearrange("p a s -> p (a s)"), op=ALU.mult),
                     reads=[selm, PH], writes=[kp[0]])
                for r in range(32):
                    a_, b_ = kp[r % 2], kp[(r + 1) % 2]
                    S.op("dve", lambda e, r=r, a_=a_: e.max(out=cand[:, r * 8:(r + 1) * 8], in_=a_[:]), reads=[a_], writes=[cand])
                    if r < 31:
                        S.op("dve", lambda e, r=r, a_=a_, b_=b_: e.match_replace(out=b_[:], in_to_replace=cand[:, r * 8:(r + 1) * 8],
                                                                                 in_values=a_[:], imm_value=0.0),
                             reads=[a_, cand], writes=[b_])
                S.op("dve", lambda e: e.tensor_copy(out=candi[:], in_=cand[:]), reads=[cand], writes=[candi])
                c255, c8, c16 = ci["c_i32"][:, 0:256], ci["c_i32"][:, 256:512], ci["c_i32"][:, 512:768]
                S.op("dve", lambda e: e.tensor_tensor(out=dgi[:], in0=candi[:], in1=c255, op=ALU.bitwise_and),
                     reads=[candi, ci["c_i32"]], writes=[dgi])
                S.op("dve", lambda e: e.tensor_copy(out=dgb[:, 0, :], in_=dgi[:]), reads=[dgi], writes=[dgb])
                S.op("dve", lambda e: e.tensor_tensor(out=dgi[:], in0=candi[:], in1=c8, op=ALU.logical_shift_right),
                     reads=[candi, ci["c_i32"]], writes=[dgi])
                S.op("dve", lambda e: e.tensor_tensor(out=dgi[:], in0=dgi[:], in1=c255, op=ALU.bitwise_and),
                     reads=[dgi, ci["c_i32"]], writes=[dgi])
                S.op("dve", lambda e: e.tensor_copy(out=dgb[:, 1, :], in_=dgi[:]), reads=[dgi], writes=[dgb])
                S.op("dve", lambda e: e.tensor_tensor(out=dgi[:], in0=candi[:], in1=c16, op=ALU.logical_shift_right),
                     reads=[candi, ci["c_i32"]], writes=[dgi])
                S.op("dve", lambda e: e.tensor_copy(out=dgb[:, 2, :], in_=dgi[:]), reads=[dgi], writes=[dgb])
                for dg in range(3):
                    S.op("dve", lambda e, dg=dg: e.tensor_tensor(
                        out=L_all[:, :, dg * 8:(dg + 1) * 8], in0=dgb[:, dg, :].unsqueeze(2).to_broadcast([128, 256, 8]),
                        in1=ci["c_bq16"][:].unsqueeze(1).to_broadcast([128, 256, 8]), op=ALU.mult),
                        reads=[dgb, ci["c_bq16"]], writes=[L_all])
                S.op("dve", lambda e: e.tensor_scalar(out=OH[:], in0=ci["c_iota512"][:], scalar1=off_s[:, 0:1], scalar2=None,
                                                      op0=ALU.is_equal), reads=[ci["c_iota512"], off_s], writes=[OH])
                for r in range(256):
                    S.op("pe", lambda e, r=r: e.matmul(PU[0][0:24, 0:256], lhsT=L_all[:, r, :], rhs=OH[:, 256 - r:512 - r],
                                                       start=(r == 0), stop=(r == 255)), reads=[L_all, OH], writes=[PU[0]],
                         pe_acc=(r > 0))
                S.op("dve", lambda e: e.tensor_copy(out=Cs[:], in_=PU[0][0:24, 0:256]), reads=[PU[0]], writes=[Cs])
                for half in range(2):
                    S.op("pe", lambda e, half=half: e.transpose(out=PZ[1][:, half * 24:(half + 1) * 24],
                                                                in_=Cs[:, half * 128:(half + 1) * 128], identity=ident_f[0:24, 0:24]),
                         reads=[Cs, ident_f], writes=[PZ[1]])
                S.op("dve", lambda e: e.tensor_copy(out=dig_s[:].rearrange("p a b -> p (a b)"), in_=PZ[1][:, 0:48]),
                     reads=[PZ[1]], writes=[dig_s])
                S.op("dve", lambda e: e.scalar_tensor_tensor(out=idxf[:], in0=dig_s[:, :, 16:24], scalar=256.0, in1=dig_s[:, :, 8:16],
                                                             op0=ALU.mult, op1=ALU.add), reads=[dig_s], writes=[idxf])
                S.op("dve", lambda e: e.scalar_tensor_tensor(out=idxf[:], in0=idxf[:], scalar=256.0, in1=dig_s[:, :, 0:8],
                                                             op0=ALU.mult, op1=ALU.add), reads=[idxf, dig_s], writes=[idxf])
                S.op("dve", lambda e: e.tensor_scalar(out=vmask[:], in0=idxf[:], scalar1=0.5, scalar2=None, op0=ALU.is_gt),
                     reads=[idxf], writes=[vmask])
                S.op("dve", lambda e: e.tensor_scalar(out=idxf[:], in0=idxf[:], scalar1=-1.0, scalar2=0.0, op0=ALU.add, op1=ALU.max),
                     reads=[idxf], writes=[idxf])
                S.op("dve", lambda e: e.tensor_copy(out=idx_i[:], in_=idxf[:]), reads=[idxf], writes=[idx_i])
                S.barrier()
                S.emit()
            with ExitStack() as sC:
                Ksel = sb(sC, "Ksel", [128, 16, 1024], BF16)
                Vsel = sb(sC, "Vsel", [128, 16, 1024], BF16)
                prod = sb(sC, "prod", [128, 1024], F32)
                sT = sb(sC, "sT", [128, 2, 8], F32)
                pT = sb(sC, "pT", [128, 2, 8], BF16)
                prodn = sb(sC, "prodn", [8, 1024], F32)
                sTn = sb(sC, "sTn", [8, 8], F32)
                pTn = sb(sC, "pTn", [8, 8], BF16)
                rden_s = sb(sC, "rden_s", [8, 1], F32)
                Mq = sb(sC, "Mq", [8, 1024], BF16)
                for q in range(8):
                    for half in range(2):
                        for dst_, src_ in ((Ksel, ck), (Vsel, cv)):
                            S.dma("pool", lambda e, q=q, half=half, dst_=dst_, src_=src_: e.indirect_dma_start(
                                out=dst_[:, q * 2 + half, :], out_offset=None, in_=src_[:, :],
                                in_offset=bass.IndirectOffsetOnAxis(ap=idx_i[:, half, q:q + 1], axis=0)),
                                reads=[idx_i], writes=[dst_])
                PAs = [PG, PGL]
                for q in range(8):
                    for cb in range(2):
                        S.op("pe", lambda e, q=q, cb=cb: e.matmul(PZ[cb][:], lhsT=esel_b[:, q, :], rhs=aq_sb[:, cb * 512:(cb + 1) * 512],
                                                                  start=True, stop=True), reads=[esel_b, aq_sb], writes=[PZ[cb]])
                    for half in range(2):
                        for cb in range(2):
                            S.op("dve", lambda e, q=q, half=half, cb=cb: e.tensor_tensor(
                                out=prod[:, cb * 512:(cb + 1) * 512], in0=Ksel[:, q * 2 + half, cb * 512:(cb + 1) * 512],
                                in1=PZ[cb][:], op=ALU.mult), reads=[Ksel, PZ[cb]], writes=[prod])
                        S.op("dve", lambda e, half=half: e.tensor_reduce(out=sT[:, half, :], in_=prod[:].rearrange("p (h d) -> p h d", h=8),
                                                                         axis=AX.X, op=ALU.add), reads=[prod], writes=[sT])
                    S.op("act", lambda e: e.activation(out=sT[:].rearrange("p a b -> p (a b)"), in_=sT[:].rearrange("p a b -> p (a b)"),
                                                       func=AF.Exp), reads=[sT], writes=[sT])
                    S.op("dve", lambda e, q=q: e.tensor_tensor(out=pT[:], in0=sT[:], in1=vmask[:, :, q:q + 1].to_broadcast([128, 2, 8]),
                                                               op=ALU.mult), reads=[sT, vmask], writes=[pT])
                    for cb in range(2):
                        S.op("dve", lambda e, cb=cb: e.tensor_tensor(out=prodn[:, cb * 512:(cb + 1) * 512],
                                                                     in0=ak_sb[:, cb * 512:(cb + 1) * 512], in1=PZ[cb][0:8, :],
                                                                     op=ALU.mult), reads=[ak_sb, PZ[cb]], writes=[prodn])
                    S.op("dve", lambda e: e.tensor_reduce(out=sTn[:], in_=prodn[:].rearrange("p (h d) -> p h d", h=8), axis=AX.X,
                                                          op=ALU.add), reads=[prodn], writes=[sTn])
                    S.op("act", lambda e: e.activation(out=sTn[:], in_=sTn[:], func=AF.Exp), reads=[sTn], writes=[sTn])
                    S.op("dve", lambda e, q=q: e.tensor_scalar(out=pTn[:], in0=sTn[:], scalar1=selnT[:, q * 16:q * 16 + 1], scalar2=None,
                                                               op0=ALU.mult), reads=[sTn, selnT], writes=[pTn])
                    for cb in range(2):
                        cs_ = slice(cb * 512, (cb + 1) * 512)
                        S.op("pe", lambda e, q=q, cb=cb, cs_=cs_: e.matmul(PU[cb][0:8, :], lhsT=pT[:, 0, :], rhs=Vsel[:, q * 2, cs_],
                                                                           start=True, stop=False), reads=[pT, Vsel], writes=[PU[cb]])
                        S.op("pe", lambda e, q=q, cb=cb, cs_=cs_: e.matmul(PU[cb][0:8, :], lhsT=pT[:, 1, :], rhs=Vsel[:, q * 2 + 1, cs_],
                                                                           start=False, stop=False), reads=[pT, Vsel], writes=[PU[cb]],
                             pe_acc=True)
                        S.op("pe", lambda e, cb=cb, cs_=cs_: e.matmul(PU[cb][0:8, :], lhsT=pTn[:], rhs=av_sb[:, cs_],
                                                                      start=False, stop=True), reads=[pTn, av_sb], writes=[PU[cb]],
                             pe_acc=True)
                    S.op("pe", lambda e: e.matmul(PD[0:8, 0:2], lhsT=pT[:, 0, :], rhs=ones_b[:], start=True, stop=False),
                         reads=[pT, ones_b], writes=[PD])
                    S.op("pe", lambda e: e.matmul(PD[0:8, 0:2], lhsT=pT[:, 1, :], rhs=ones_b[:], start=False, stop=False),
                         reads=[pT, ones_b], writes=[PD], pe_acc=True)
                    S.op("pe", lambda e: e.matmul(PD[0:8, 0:2], lhsT=pTn[:], rhs=ones_b[0:8, :], start=False, stop=True),
                         reads=[pTn, ones_b], writes=[PD], pe_acc=True)
                    S.op("dve", lambda e: e.reciprocal(out=rden_s[:], in_=PD[0:8, 0:1]), reads=[PD], writes=[rden_s])
                    for cb in range(2):
                        cs_ = slice(cb * 512, (cb + 1) * 512)
                        S.op("dve", lambda e, cb=cb, cs_=cs_: e.scalar_tensor_tensor(
                            out=Mq[:, cs_], in0=PU[cb][0:8, :], scalar=rden_s[:, 0:1], in1=ci["c_bd8"][:, cs_],
                            op0=ALU.mult, op1=ALU.mult), reads=[PU[cb], rden_s, ci["c_bd8"]], writes=[Mq])
                        S.op("pe", lambda e, q=q, cb=cb, cs_=cs_: e.matmul(PAs[cb][0:8, :], lhsT=eq_b[:, q, :], rhs=Mq[:, cs_],
                                                                           start=(q == 0), stop=(q == 7)),
                             reads=[eq_b, Mq], writes=[PAs[cb]], pe_acc=(q > 0))
                for cb in range(2):
                    cs_ = slice(cb * 512, (cb + 1) * 512)
                    S.op("dve", lambda e, cb=cb, cs_=cs_: e.tensor_tensor(out=cat_s[:, cs_], in0=PAs[cb][0:8, :], in1=ga_s[:, cs_],
                                                                          op=ALU.mult), reads=[PAs[cb], ga_s], writes=[cat_s])
                S.barrier()
                S.emit()
        if DBG:
            S.dma("sp", lambda e: e.dma_start(out=o_cat[:, :], in_=catT[:].rearrange("p k t -> p (k t)")), reads=[catT])
        with ExitStack() as p7:
            wo_sb = sb(p7, "wo_sb", [128, KC, 2048], BF16)
            xr = [sb(p7, "xr%d" % j, [128, 2048], F32) for j in range(2)]
            rr = sb(p7, "rr", [128, 2048], F32)
            yo = [sb(p7, "yo%d" % j, [128, 2048], F32) for j in range(2)]
            lng_bc = sb(p7, "lng_bc", [128, 2048], F32)
            lnb_bc = sb(p7, "lnb_bc", [128, 2048], F32)
            stats = sb(p7, "stats", [128, 4, 6], F32)
            mv2 = sb(p7, "mv2", [128, 2], F32)
            rs2 = sb(p7, "rs2", [128, 2], F32)
            load_wres(wo_sb, Wo, 2048)
            S.dma("sp", lambda e: e.dma_start(out=lng_bc[:], in_=lng[0:1, :].broadcast_to([128, 2048])), writes=[lng_bc])
            S.dma("sp", lambda e: e.dma_start(out=lnb_bc[:], in_=lnb[0:1, :].broadcast_to([128, 2048])), writes=[lnb_bc])
            csT = sb(p7, "csT", [128, KC, 8], BF16)
            for kc in range(KC):
                S.op("pe", lambda e, kc=kc: e.transpose(out=PT[:, kc * 8:(kc + 1) * 8], in_=cat_s[:, kc * 128:(kc + 1) * 128],
                                                        identity=ident_b[0:8, 0:8]), reads=[cat_s, ident_b], writes=[PT])
            S.op("act", lambda e: e.activation(out=csT[:].rearrange("p k t -> p (k t)"), in_=PT[:, 0:128], func=AF.Copy),
                 reads=[PT], writes=[csT])

            def merge_rows(np_, j, lhs_fn, lhs_deps, x_src, o_dst):
                x_, y_ = xr[j % 2], yo[j % 2]
                S.dma("sp", lambda e: e.dma_start(out=x_[0:np_, :], in_=x_src), writes=[x_])
                for c in range(4):
                    pz = PZ[c % 2]
                    for k in range(KC):
                        S.op("pe", lambda e, k=k, c=c, pz=pz: e.matmul(
                            pz[0:np_, :], lhsT=lhs_fn(k), rhs=wo_sb[:, k, c * 512:(c + 1) * 512],
                            start=(k == 0), stop=(k == KC - 1)), reads=lhs_deps + [wo_sb], writes=[pz], pe_acc=(k > 0))
                    S.op("dve", lambda e, c=c, pz=pz: e.scalar_tensor_tensor(
                        out=rr[0:np_, c * 512:(c + 1) * 512], in0=x_[0:np_, c * 512:(c + 1) * 512], scalar=ALPHA, in1=pz[0:np_, :],
                        op0=ALU.mult, op1=ALU.add), reads=[x_, pz], writes=[rr])
                    S.op("dve", lambda e, c=c: e.bn_stats(out=stats[0:np_, c, :], in_=rr[0:np_, c * 512:(c + 1) * 512]),
                         reads=[rr], writes=[stats])
                S.op("dve", lambda e: e.bn_aggr(out=mv2[0:np_, :], in_=stats[0:np_].rearrange("p a b -> p (a b)")),
                     reads=[stats], writes=[mv2])
                S.op("dve", lambda e: e.tensor_scalar(out=rs2[0:np_, 0:1], in0=mv2[0:np_, 1:2], scalar1=LN_EPS, scalar2=None,
                                                      op0=ALU.add), reads=[mv2], writes=[rs2])
                S.op("act", lambda e: e.activation(out=rs2[0:np_, 0:1], in_=rs2[0:np_, 0:1], func=AF.Sqrt), reads=[rs2], writes=[rs2])
                S.op("dve", lambda e: e.reciprocal(out=rs2[0:np_, 0:1], in_=rs2[0:np_, 0:1]), reads=[rs2], writes=[rs2])
                S.op("dve", lambda e: e.scalar_tensor_tensor(out=rs2[0:np_, 1:2], in0=mv2[0:np_, 0:1], scalar=-1.0,
                                                             in1=rs2[0:np_, 0:1], op0=ALU.mult, op1=ALU.mult),
                     reads=[mv2, rs2], writes=[rs2])
                S.op("act", lambda e: e.activation(out=y_[0:np_, :], in_=rr[0:np_, :], func=AF.Identity, scale=rs2[0:np_, 0:1],
                                                   bias=rs2[0:np_, 1:2]), reads=[rr, rs2], writes=[y_])
                S.op("dve", lambda e: e.tensor_tensor(out=y_[0:np_, :], in0=y_[0:np_, :], in1=lng_bc[0:np_, :], op=ALU.mult),
                     reads=[y_, lng_bc], writes=[y_])
                S.op("pool", lambda e: e.tensor_tensor(out=y_[0:np_, :], in0=y_[0:np_, :], in1=lnb_bc[0:np_, :], op=ALU.add),
                     reads=[y_, lnb_bc], writes=[y_])
                S.dma("sp", lambda e: e.dma_start(out=o_dst, in_=y_[0:np_, :]), reads=[y_])
            for i in range(NO):
                merge_rows(128, i, lambda k, i=i: catT[:, k, i * 128:(i + 1) * 128], [catT],
                           xo[i * 128:(i + 1) * 128, :], o_y[i * 128:(i + 1) * 128, :])
            merge_rows(8, NO, lambda k: csT[:, k, :], [csT], xs[:, :], o_ys[:, :])
            S.barrier()
            S.emit()
        es_S.close()
        es_C.close()
    return nc


def _rope_tab(pos, half):
    inv = (np.float32(ROPE_THETA) ** (-np.arange(half, dtype=np.float32) / np.float32(half))).astype(np.float32)
    ang = pos.astype(np.float32)[:, None] * inv[None, :]
    c = np.cos(ang).astype(np.float32)
    s = np.sin(ang).astype(np.float32)
    return np.concatenate([c, c], 1), np.concatenate([-s, s], 1)


def _consts():
    s = np.arange(128)
    same = (s[:, None] // 64) == (s[None, :] // 64)
    tri2 = (same & (s[:, None] <= s[None, :])).astype(np.float32)
    blk2 = same.astype(np.float32)
    return dict(c_tri2=tri2, c_blk2=blk2, c_ident=np.eye(128, dtype=np.float32))


_NC_CACHE = {}


def kernel(x_prompt, x_sample, mem_prompt, cache_k, cache_v, cache_idx_k, state_hgrn,
           cache_mem_k, cache_mem_v, page_table, w_in, lb_logits, hgrn_norm_g,
           w_mem_k, w_mem_v, w_out, ln_g, ln_b):
    f32 = np.float32
    x_prompt = np.asarray(x_prompt, f32)
    w = np.asarray(w_in, f32)[0]
    ca = np.ascontiguousarray
    Wk = ca(w[:, O_AK:O_AV])
    Wv = ca(w[:, O_AV:O_AG])
    W3 = ca(np.concatenate([w[:, O_BF:O_BI], w[:, O_BI:O_BG], w[:, O_IK:O_IW]], axis=1))
    pos = np.arange(SEQ)
    cc128, ss128 = _rope_tab(pos, 16)
    cc64, ss64 = _rope_tab(pos, 8)
    ropeN = ca(np.concatenate([cc128, ss128, cc64, ss64], 1).astype(f32))
    consts = _consts()
    consts["c_j"] = np.tile(np.arange(256, dtype=f32)[None, :], (128, 1))
    pw = np.zeros(48, f32)
    for k in range(24):
        pw[k] = 2.0 ** -(k + 1)
        pw[24 + k] = 2.0 ** -(k + 2) if k < 23 else 2.0 ** -24
    consts["c_pow"] = np.tile(pw[None, :], (128, 1))
    Wi_ = ca(np.concatenate([w[:, O_IQ:O_IK], w[:, O_IW:O_BQ]], axis=1))
    Wb_ = ca(np.concatenate([w[:, O_BQ:O_BF], w[:, O_BF:O_BI], w[:, O_BI:O_BG], w[:, O_BG:O_MQ]], axis=1))
    consts.update(Wi=Wi_, Wq=ca(w[:, O_AQ:O_AK]), Wg=ca(w[:, O_AG:O_IQ]), Wb=Wb_, Wm=ca(w[:, O_MQ:O_END]),
                  Wo=ca(np.asarray(w_out, f32)[0]), ng=ca(np.asarray(hgrn_norm_g, f32).reshape(1, 128)),
                  lng=ca(np.asarray(ln_g, f32).reshape(1, D)), lnb=ca(np.asarray(ln_b, f32).reshape(1, D)))
    qs_ = np.float32(128.0 ** -0.5)
    P = np.arange(128)
    hq_h, hq_q = P // 8, P % 8
    qs_q, qs_s = P // 16, P % 16
    consts["c_sel8"] = (np.arange(8)[:, None] == hq_q[None, :]).astype(f32)
    consts["c_hm"] = (hq_h[:, None] == np.arange(16)[None, :]).astype(f32)
    consts["c_bq8"] = (hq_q[:, None] == np.arange(8)[None, :]).astype(f32)
    consts["c_bq16"] = (qs_q[:, None] == np.arange(8)[None, :]).astype(f32)
    sameq = (qs_q[:, None] == qs_q[None, :])
    consts["c_BQ1"] = sameq.astype(f32)
    consts["c_BQm"] = (sameq / 16.0).astype(f32)
    consts["c_LT"] = (sameq & (qs_s[:, None] < qs_s[None, :])).astype(f32)
    consts["c_negm"] = np.where((qs_s[:, None] == 0) & (np.arange(8)[None, :] <= qs_q[:, None]), 0.0, NEG).astype(f32)
    consts["c_aiota"] = (qs_s[:, None] * 8 + (np.arange(1024)[None, :] // 128) + 1).astype(f32)
    consts["c_iota512"] = np.tile((np.arange(512) - 256).astype(f32)[None, :], (128, 1))
    esel = np.zeros((8, 8, 128), f32)
    eq = np.zeros((8, 8, 8), f32)
    for q_ in range(8):
        esel[q_, q_, :] = 1.0
        eq[:, q_, q_] = 1.0
    consts["c_esel"] = esel.reshape(8, 1024)
    consts["c_eq"] = eq.reshape(8, 64)
    consts["c_bd8"] = np.repeat((np.arange(8)[:, None] == np.arange(8)[None, :]).astype(f32), 128, axis=1)
    pw2 = np.zeros(60, f32)
    for k in range(30):
        pw2[k] = 2.0 ** -(k + 1)
        pw2[30 + k] = 2.0 ** -(k + 2) if k < 29 else 2.0 ** -30
    consts["c_pow2"] = np.tile(pw2[None, :], (128, 1))
    consts["c_i32"] = np.concatenate([np.full((128, 256), 255), np.full((128, 256), 8), np.full((128, 256), 16)], 1).astype(np.int32)
    ck_flat = np.asarray(cache_k, f32)[0].reshape(-1, 1024)
    cv_flat = np.asarray(cache_v, f32)[0].reshape(-1, 1024)
    cidx_flat = np.asarray(cache_idx_k, f32)[0].reshape(1280, 8192)
    consts.update(ck=ck_flat, cv=cv_flat, cidx=cidx_flat)
    shared = dict(Wk=Wk, Wv=Wv, W3=W3, Wmk=ca(np.asarray(w_mem_k, f32)[0]), Wmv=ca(np.asarray(w_mem_v, f32)[0]),
                  lbl=ca(np.asarray(lb_logits, f32)), ropeN=ropeN, **consts)
    pos_s = PAST + np.arange(8)
    c128s, s128s = _rope_tab(pos_s, 16)
    c64s, s64s = _rope_tab(pos_s, 8)
    ropeS = ca(np.concatenate([c128s, s128s, c64s, s64s, c128s * qs_, s128s * qs_], 1).astype(f32))
    x_sample = np.asarray(x_sample, f32)
    in_maps = []
    for c in range(8):
        b, h = c // 2, c % 2
        posq = np.zeros((128, NO + 1), f32)
        for i in range(NO):
            posq[:, i] = (2 * i + h) * 128 + np.arange(128)
        posq[:, NO] = h
        m = dict(shared)
        own = np.concatenate([(2 * i + h) * 128 + np.arange(128) for i in range(NO)])
        ropeO = ca(np.concatenate([cc128[own] * qs_, ss128[own] * qs_, cc64[own], ss64[own]], 1).astype(f32))
        m.update(xTn=ca(x_prompt[b].T), memT=ca(np.asarray(mem_prompt, f32)[b].T), posq=posq,
                 xTo=ca(x_prompt[b][own].T), xo=ca(x_prompt[b][own]), ropeO=ropeO,
                 xsT=ca(x_sample[c].T), xs=ca(x_sample[c]), ropeS=ropeS,
                 st_h=ca(np.asarray(state_hgrn, f32)[0, c]),
                 ptab=ca(np.asarray(page_table)[c].astype(np.int32).reshape(1, 128)),
                 cmk_d=ca(np.asarray(cache_mem_k, f32)[0, c].reshape(256, 512)),
                 cmv_d=ca(np.asarray(cache_mem_v, f32)[0, c].reshape(256, 512)))
        in_maps.append(m)
    import os
    stage = int(os.environ.get("KSTAGE", "99"))
    Sched.LIMIT = int(os.environ.get("KLIMIT", str(10 ** 9)))
    ncores = int(os.environ.get("KCORES", "8"))
    if "nc" not in _NC_CACHE:
        _NC_CACHE["nc"] = build_program(stage)
    nc = _NC_CACHE["nc"]
    res = run_bass_kernel_spmd(nc, in_maps[:ncores], core_ids=list(range(ncores)))
    if ncores < 8:
        res.results.extend([res.results[0]] * (8 - ncores))
    R = res.results
    _NC_CACHE["R"] = R
    B = 4
    y_p = np.zeros((B, SEQ, D), f32)
    if "o_y" in R[0]:
        for c in range(8):
            b, h = c // 2, c % 2
            oy = R[c]["o_y"]
            for i in range(NO):
                n = 2 * i + h
                y_p[b, n * 128:(n + 1) * 128] = oy[i * 128:(i + 1) * 128]
    y_s = np.zeros((8, 8, D), f32)
    k_p = np.stack([R[2 * b]["o_k"].reshape(SEQ, 8, 128) for b in range(B)])[None]
    v_p = np.stack([R[2 * b]["o_v"].reshape(SEQ, 8, 128) for b in range(B)])[None]
    ik_p = np.stack([R[2 * b]["o_ik"] for b in range(B)])[None]
    hg_p = np.stack([R[2 * b]["o_hg"].transpose(1, 0, 2) for b in range(B)])[None]
    mk_p = np.stack([R[2 * b]["o_mk"].reshape(256, 4, 128) for b in range(B)])[None]
    mv_p = np.stack([R[2 * b]["o_mv"].reshape(256, 4, 128) for b in range(B)])[None]
    k_s = np.stack([R[c]["o_ks"].reshape(8, 8, 128) for c in range(8)])[None].astype(f32)
    v_s = np.stack([R[c]["o_vs"].reshape(8, 8, 128) for c in range(8)])[None].astype(f32)
    ik_s = np.stack([R[c]["o_iks"] for c in range(8)])[None].astype(f32)
    hg_s = np.stack([R[c]["o_hgs"].transpose(1, 0, 2) for c in range(8)])[None].astype(f32)
    y_s = np.stack([R[c]["o_ys"] for c in range(8)]).astype(f32)
    return (y_p, y_s, k_p.astype(f32), v_p.astype(f32), ik_p.astype(f32), hg_p.astype(f32), mk_p.astype(f32),
            mv_p.astype(f32), k_s, v_s, ik_s, hg_s)
```
